# Optimizing a Trainium2 kernel written in Bass

```python
import jax, jax.numpy as jnp
from jax import lax
import numpy as np

D_MODEL = 1024
BATCH = 32
SEQ = 2048
DEPTH = 1
DEC_BATCH = 1
DEC_SEQ = 16384
PAST_LEN = 128

MLA_HEADS = 16
QK_NOPE = 64
QK_ROPE = 32
V_HEAD = 64
Q_LORA = 384
KV_LORA = 256
MLA_WIDTH = MLA_HEADS * V_HEAD
ROPE_THETA = 10000.0
Q_BLOCK = 128
SSM_HEADS = 16
SSM_HEAD_DIM = 64
D_SSM = SSM_HEADS * SSM_HEAD_DIM
SSM_GROUPS = 2
D_STATE = 64
SSM_CONV = 3
CHUNK = 128
D_XBC = D_SSM + 2 * SSM_GROUPS * D_STATE
MIX_WIDTH = MLA_WIDTH + D_SSM
D_IN = Q_LORA + (KV_LORA + QK_ROPE) + D_SSM + D_XBC + 2 * SSM_HEADS
SPLITS = (Q_LORA,
          Q_LORA + KV_LORA + QK_ROPE,
          Q_LORA + KV_LORA + QK_ROPE + D_SSM,
          Q_LORA + KV_LORA + QK_ROPE + D_SSM + D_XBC)
D_FF = 2816
FFN_CONV = 3
EPS = 1e-6

kernel_name = "hymba_mla_ssd_convffn_encoder"


def rmsnorm(x, w):
    xf = x.astype(jnp.float32)
    y = xf * lax.rsqrt(jnp.mean(xf * xf, axis=-1, keepdims=True) + EPS)
    return (y * w.astype(jnp.float32)).astype(x.dtype)


def dwconv_centred(x, w, b):
    K = w.shape[0]
    p = K // 2
    L = x.shape[1]
    xp = jnp.pad(x, ((0, 0), (p, p), (0, 0)))
    y = xp[:, 0:L] * w[0]
    for k in range(1, K):
        y = y + xp[:, k:k + L] * w[k]
    return y + b


def rope_tables(L):
    inv = ROPE_THETA ** (-jnp.arange(0, QK_ROPE, 2, dtype=jnp.float32) / QK_ROPE)
    ang = jnp.arange(L, dtype=jnp.float32)[:, None] * inv[None, :]
    return jnp.cos(ang), jnp.sin(ang)


def apply_rope(x, cos, sin):
    x1, x2 = jnp.split(x, 2, axis=-1)
    return jnp.concatenate([x1 * cos - x2 * sin, x2 * cos + x1 * sin], axis=-1).astype(x.dtype)


def mla(q_lat, kv_lat, q_a_norm, kv_a_norm, w_q_b, w_kv_b):
    b, L, _ = q_lat.shape
    q = (rmsnorm(q_lat, q_a_norm) @ w_q_b).reshape(b, L, MLA_HEADS, QK_NOPE + QK_ROPE)
    c_kv, k_rope = kv_lat[..., :KV_LORA], kv_lat[..., KV_LORA:]
    kv = (rmsnorm(c_kv, kv_a_norm) @ w_kv_b).reshape(b, L, MLA_HEADS, QK_NOPE + V_HEAD)
    k_nope, v = kv[..., :QK_NOPE], kv[..., QK_NOPE:]
    cos, sin = rope_tables(L)
    q_nope = q[..., :QK_NOPE]
    q_rope = apply_rope(q[..., QK_NOPE:], cos[:, None, :], sin[:, None, :])
    k_rope = apply_rope(k_rope, cos, sin)
    scale = (QK_NOPE + QK_ROPE) ** -0.5
    nb = L // Q_BLOCK

    def blocks(t):
        return jnp.moveaxis(t.reshape(b, nb, Q_BLOCK, *t.shape[2:]), 1, 0)

    def attend(qs):
        qn, qr = qs
        s = (jnp.einsum('bqhd,bkhd->bhqk', qn, k_nope, preferred_element_type=jnp.float32)
             + jnp.einsum('bqhr,bkr->bhqk', qr, k_rope, preferred_element_type=jnp.float32))
        p = jax.nn.softmax(s * scale, axis=-1)
        return jnp.einsum('bhqk,bkhv->bqhv', p.astype(v.dtype), v)

    o = lax.map(attend, (blocks(q_nope), blocks(q_rope)))
    return jnp.moveaxis(o, 0, 1).reshape(b, L, MLA_WIDTH)


def ssd_chunked(x, dt, A, Bm, Cm):
    b, L, H, P = x.shape
    G, N = Bm.shape[2], Bm.shape[3]
    hg = H // G
    c = L // CHUNK
    f32 = jnp.float32
    xf = x.astype(f32).reshape(b, c, CHUNK, G, hg, P)
    dtc = dt.reshape(b, c, CHUNK, G, hg)
    Bc = Bm.astype(f32).reshape(b, c, CHUNK, G, N)
    Cc = Cm.astype(f32).reshape(b, c, CHUNK, G, N)
    xdt = xf * dtc[..., None]
    acs = jnp.cumsum(dtc * A.reshape(G, hg), axis=2)
    acs_t = jnp.moveaxis(acs, 2, -1)
    seg = acs_t[..., :, None] - acs_t[..., None, :]
    lower = jnp.tril(jnp.ones((CHUNK, CHUNK), dtype=bool))
    Lm = jnp.exp(jnp.where(lower, seg, -jnp.inf))
    CB = jnp.einsum('bctgn,bcsgn->bcgts', Cc, Bc)
    y_diag = jnp.einsum('bcgts,bcghts,bcsghp->bctghp', CB, Lm, xdt)
    decay_to_end = jnp.exp(acs[:, :, -1:] - acs)
    states = jnp.einsum('bcsgn,bcsgh,bcsghp->bcghpn', Bc, decay_to_end, xdt)
    chunk_decay = jnp.exp(acs[:, :, -1])

    def step(h, inp):
        s_k, d_k = inp
        return h * d_k[..., None, None] + s_k, h

    h0 = jnp.zeros((b, G, hg, P, N), f32)
    _, h_prev = lax.scan(step, h0, (jnp.moveaxis(states, 1, 0), jnp.moveaxis(chunk_decay, 1, 0)))
    h_prev = jnp.moveaxis(h_prev, 0, 1)
    y_off = jnp.einsum('bctgn,bcghpn,bctgh->bctghp', Cc, h_prev, jnp.exp(acs))
    return (y_diag + y_off).reshape(b, L, H, P)


def ssd_mixer(z, xbc, dt_raw, conv_w, conv_b, dt_bias_f, dt_bias_b, a_log_f, a_log_b, d_skip, ssm_norm):
    b, L, _ = z.shape
    f32 = jnp.float32
    GN = SSM_GROUPS * D_STATE
    xbc = jax.nn.silu(dwconv_centred(xbc, conv_w, conv_b))
    xs = xbc[..., :D_SSM].reshape(b, L, SSM_HEADS, SSM_HEAD_DIM)
    Bm = xbc[..., D_SSM:D_SSM + GN].reshape(b, L, SSM_GROUPS, D_STATE)
    Cm = xbc[..., D_SSM + GN:].reshape(b, L, SSM_GROUPS, D_STATE)
    dt_f = jax.nn.softplus(dt_raw[..., :SSM_HEADS].astype(f32) + dt_bias_f.astype(f32))
    dt_b = jax.nn.softplus(dt_raw[..., SSM_HEADS:].astype(f32) + dt_bias_b.astype(f32))
    A_f = -jnp.exp(a_log_f.astype(f32))
    A_b = -jnp.exp(a_log_b.astype(f32))
    flip = lambda t: jnp.flip(t, axis=1)
    y_f = ssd_chunked(xs, dt_f, A_f, Bm, Cm)
    y_b = flip(ssd_chunked(flip(xs), flip(dt_b), A_b, flip(Bm), flip(Cm)))
    y = y_f + y_b + xs.astype(f32) * d_skip.astype(f32)[:, None]
    y = y.reshape(b, L, D_SSM).astype(z.dtype) * jax.nn.silu(z)
    gs = D_SSM // SSM_GROUPS
    y = rmsnorm(y.reshape(b, L, SSM_GROUPS, gs), ssm_norm.reshape(SSM_GROUPS, gs))
    return y.reshape(b, L, D_SSM)


def conv_ffn(h, w_gate, w_up, conv_w, conv_b, w_down):
    g = dwconv_centred(h @ w_gate, conv_w, conv_b)
    return (jax.nn.silu(g) * (h @ w_up)) @ w_down


def encoder(x, norm1, w_in, q_a_norm, kv_a_norm, w_q_b, w_kv_b, conv_w, conv_b,
            dt_bias_f, dt_bias_b, a_log_f, a_log_b, d_skip, ssm_norm, w_out,
            norm2, w_gate, w_up, ffn_conv_w, ffn_conv_b, w_down, final_norm):
    for l in range(DEPTH):
        h = rmsnorm(x, norm1[l])
        proj = h @ w_in[l]
        q_lat, kv_lat, z, xbc, dt_raw = jnp.split(proj, SPLITS, axis=-1)
        attn = mla(q_lat, kv_lat, q_a_norm[l], kv_a_norm[l], w_q_b[l], w_kv_b[l])
        ssm = ssd_mixer(z, xbc, dt_raw, conv_w[l], conv_b[l], dt_bias_f[l], dt_bias_b[l],
                        a_log_f[l], a_log_b[l], d_skip[l], ssm_norm[l])
        x = x + jnp.concatenate([attn, ssm], axis=-1) @ w_out[l]
        x = x + conv_ffn(rmsnorm(x, norm2[l]), w_gate[l], w_up[l], ffn_conv_w[l], ffn_conv_b[l], w_down[l])
    return rmsnorm(x, final_norm)


def setup_inputs(seed: int = 0) -> dict:
    key = jax.random.key(seed)
    ks = jax.random.split(key, 24)
    f32 = jnp.float32
    nrm = lambda k, shape, fan_in: jax.random.normal(k, shape, f32) * (fan_in ** -0.5)
    gain = lambda k, shape: 1.0 + 0.01 * jax.random.normal(k, shape, f32)
    dt0 = jnp.exp(jax.random.uniform(ks[10], (DEPTH, SSM_HEADS), f32, np.log(1e-3), np.log(1e-1)))
    dt1 = jnp.exp(jax.random.uniform(ks[11], (DEPTH, SSM_HEADS), f32, np.log(1e-3), np.log(1e-1)))
    inv_softplus = lambda d: d + jnp.log(-jnp.expm1(-d))
    return {
        "x_prompt": jax.random.normal(ks[0], (BATCH, SEQ, D_MODEL), f32),
        "x_sample": jax.random.normal(ks[1], (DEC_BATCH, DEC_SEQ, D_MODEL), f32),
        "norm1": gain(ks[2], (DEPTH, D_MODEL)),
        "w_in": nrm(ks[3], (DEPTH, D_MODEL, D_IN), D_MODEL),
        "q_a_norm": gain(ks[4], (DEPTH, Q_LORA)),
        "kv_a_norm": gain(ks[5], (DEPTH, KV_LORA)),
        "w_q_b": nrm(ks[6], (DEPTH, Q_LORA, MLA_HEADS * (QK_NOPE + QK_ROPE)), Q_LORA),
        "w_kv_b": nrm(ks[7], (DEPTH, KV_LORA, MLA_HEADS * (QK_NOPE + V_HEAD)), KV_LORA),
        "conv_w": nrm(ks[8], (DEPTH, SSM_CONV, D_XBC), SSM_CONV),
        "conv_b": 0.01 * jax.random.normal(ks[9], (DEPTH, D_XBC), f32),
        "dt_bias_f": inv_softplus(dt0),
        "dt_bias_b": inv_softplus(dt1),
        "a_log_f": jnp.log(jax.random.uniform(ks[12], (DEPTH, SSM_HEADS), f32, 1.0, 16.0)),
        "a_log_b": jnp.log(jax.random.uniform(ks[13], (DEPTH, SSM_HEADS), f32, 1.0, 16.0)),
        "d_skip": gain(ks[14], (DEPTH, SSM_HEADS)),
        "ssm_norm": gain(ks[15], (DEPTH, D_SSM)),
        "w_out": nrm(ks[16], (DEPTH, MIX_WIDTH, D_MODEL), MIX_WIDTH),
        "norm2": gain(ks[17], (DEPTH, D_MODEL)),
        "w_gate": nrm(ks[18], (DEPTH, D_MODEL, D_FF), D_MODEL),
        "w_up": nrm(ks[19], (DEPTH, D_MODEL, D_FF), D_MODEL),
        "ffn_conv_w": nrm(ks[20], (DEPTH, FFN_CONV, D_FF), FFN_CONV),
        "ffn_conv_b": 0.01 * jax.random.normal(ks[21], (DEPTH, D_FF), f32),
        "w_down": nrm(ks[22], (DEPTH, D_FF, D_MODEL), D_FF),
        "final_norm": gain(ks[23], (D_MODEL,)),
    }


def reference(x_prompt, x_sample, norm1, w_in, q_a_norm, kv_a_norm, w_q_b, w_kv_b, conv_w, conv_b,
              dt_bias_f, dt_bias_b, a_log_f, a_log_b, d_skip, ssm_norm, w_out,
              norm2, w_gate, w_up, ffn_conv_w, ffn_conv_b, w_down, final_norm):
    y_prompt = encoder(x_prompt, norm1, w_in, q_a_norm, kv_a_norm, w_q_b, w_kv_b, conv_w, conv_b,
                       dt_bias_f, dt_bias_b, a_log_f, a_log_b, d_skip, ssm_norm, w_out,
                       norm2, w_gate, w_up, ffn_conv_w, ffn_conv_b, w_down, final_norm)
    y_sample = encoder(x_sample, norm1, w_in, q_a_norm, kv_a_norm, w_q_b, w_kv_b, conv_w, conv_b,
                       dt_bias_f, dt_bias_b, a_log_f, a_log_b, d_skip, ssm_norm, w_out,
                       norm2, w_gate, w_up, ffn_conv_w, ffn_conv_b, w_down, final_norm)
    return (y_prompt, y_sample)
```

```python
import os
from contextlib import ExitStack
import numpy as np
import ml_dtypes
import concourse.bass as bass
import concourse.mybir as mybir
from concourse.bass_utils import run_bass_kernel_spmd

F32 = mybir.dt.float32
BF16 = mybir.dt.bfloat16
AF = mybir.ActivationFunctionType
ALU = mybir.AluOpType

D = 1024
KC = 8
H = 16
QL, KVL, RO = 384, 256, 32
DSSM, DXBC, NST = 1024, 1280, 64
DIN = 3008
DFF = 2816
FC = 22
EPS = 1e-6
NEG = -30000.0
O_Q, O_CKV, O_KR, O_Z, O_XBC, O_DT = 0, 384, 640, 672, 1696, 2976
SCALE = 96.0 ** -0.5


class Buf:
    def __init__(self, name, t, is_dram=False):
        self.name = name
        self.t = t
        self.is_dram = is_dram
        self.w = None
        self.r = []
        self.dsem = None
        self.dcnt = 0

    def __getitem__(self, idx):
        return self.t[idx]


class K:
    ENG = ("pe", "act", "dve", "pool", "sp")

    def __init__(self, nc, es, n_dma_sems=46, n_sw_sems=50):
        self.nc = nc
        self.q = {e: [] for e in self.ENG}
        self.cnt = {e: 0 for e in self.ENG}
        self.waited = {e: {} for e in self.ENG}
        self.sem = {e: es.enter_context(nc.semaphore("s_" + e)) for e in ("pe", "act", "dve", "pool")}
        self.dma_pool = [es.enter_context(nc.semaphore("d%d" % i)) for i in range(n_dma_sems + n_sw_sems)]
        self.dma_free = list(range(n_dma_sems))
        self.sw_free = list(range(n_dma_sems, n_dma_sems + n_sw_sems))
        self.dma_val = [0] * (n_dma_sems + n_sw_sems)
        self.live_sw = []
        self.live = []
        self.n_ops = 0

    def sb(self, es, name, shape, dt):
        self.uid = getattr(self, "uid", 0) + 1
        name = "%s_u%d" % (name, self.uid)
        return Buf(name, es.enter_context(self.nc.sbuf_tensor(name, list(shape), dt)))

    def ps(self, es, name, shape, dt):
        self.uid = getattr(self, "uid", 0) + 1
        name = "%s_u%d" % (name, self.uid)
        b = Buf(name, es.enter_context(self.nc.psum_tensor(name, list(shape), dt)))
        b.is_psum = True
        return b

    def _need(self, eng, dep, out):
        kind, s, v = dep
        if kind == "pe" and eng == "pe":
            return
        key = (kind, s)
        if self.waited[eng].get(key, -1) >= v:
            return
        self.waited[eng][key] = v
        out.append(dep)

    def op(self, eng, fn, reads=(), writes=(), dma=False):
        deps = []
        for b in reads:
            if b.w is not None:
                self._need(eng, b.w, deps)
            if getattr(b, "is_psum", False):
                for r in b.r:
                    if r[0] != eng:
                        self._need(eng, r, deps)
        for b in writes:
            if b.w is not None and (dma or b.w[0] != eng):
                self._need(eng, b.w, deps)
            for r in b.r:
                if dma or r[0] != eng:
                    self._need(eng, r, deps)
        if dma:
            owner = None
            for b in list(writes) + list(reads):
                if not b.is_dram:
                    owner = b
                    break
            if owner is None:
                owner = (list(writes) + list(reads))[0]
            if eng == "pool":
                if getattr(owner, "swsem", None) is None:
                    owner.swsem = self.sw_free.pop()
                    owner.swcnt = 0
                    self.live_sw.append(owner)
                owner.swcnt += 16
                tok = ("dma", owner.swsem, owner.swcnt)
                semh, val = self.dma_pool[owner.swsem], 16
            else:
                if owner.dsem is None:
                    owner.dsem = self.dma_free.pop()
                    owner.dcnt = self.dma_val[owner.dsem]
                    self.live.append(owner)
                owner.dcnt += 16
                tok = ("dma", owner.dsem, owner.dcnt)
                semh, val = self.dma_pool[owner.dsem], 16
        else:
            self.cnt[eng] += 1
            tok = (eng, None, self.cnt[eng])
            semh, val = self.sem[eng], 1
        self.q[eng].append((deps, fn, semh, val))
        self.n_ops += 1
        for b in reads:
            b.r.append(tok)
            if len(b.r) > 64:
                b.r = b.r[-64:]
        for b in writes:
            b.w = tok
            b.r = []
        return tok

    def barrier(self):
        toks = [(e, None, self.cnt[e]) for e in ("pe", "act", "dve", "pool") if self.cnt[e]]
        for b in self.live:
            toks.append(("dma", b.dsem, b.dcnt))
        for b in self.live_sw:
            toks.append(("dma", b.swsem, b.swcnt))
        self.live_sw = []
        for e in self.ENG:
            deps = []
            for t in toks:
                if t[0] == e:
                    continue
                self._need(e, t, deps)
            if deps:
                self.q[e].append((deps, None, None, 0))
        for b in self.live:
            self.dma_val[b.dsem] = b.dcnt
            self.dma_free.append(b.dsem)
            b.dsem = None
        self.live = []

    def emit(self):
        with self.nc.Block() as block:
            def run(eng_name):
                def f(h):
                    for deps, fn, semh, val in self.q[eng_name]:
                        for kind, s, v in deps:
                            h.wait_ge(self.dma_pool[s] if kind == "dma" else self.sem[kind], v)
                        if fn is not None:
                            fn(h).then_inc(semh, val)
                return f
            block.tensor(run("pe"))
            block.scalar(run("act"))
            block.vector(run("dve"))
            block.gpsimd(run("pool"))
            block.sync(run("sp"))

    def dma(self, out, in_, eng="sp"):
        (ob, oa), (ib, ia) = out, in_
        return self.op(eng, lambda h: h.dma_start(out=oa, in_=ia), reads=[ib], writes=[ob], dma=True)

    def mm(self, out, lhsT, rhs, start=True, stop=True):
        (ob, oa), (lb, la), (rb, ra) = out, lhsT, rhs
        return self.op("pe", lambda h: h.matmul(oa, la, ra, start=start, stop=stop), reads=[lb, rb], writes=[ob])

    def tr(self, out, in_, ident):
        (ob, oa), (ib, ia), (db, da) = out, in_, ident
        return self.op("pe", lambda h: h.transpose(oa, ia, da), reads=[ib, db], writes=[ob])

    def act(self, out, in_, func, bias=None, scale=1.0, accum=None):
        (ob, oa), (ib, ia) = out, in_
        reads, writes = [ib], [ob]
        kw = {}
        if bias is not None:
            if isinstance(bias, tuple):
                reads.append(bias[0]); kw["bias"] = bias[1]
            else:
                kw["bias"] = bias
        if isinstance(scale, tuple):
            reads.append(scale[0]); kw["scale"] = scale[1]
        else:
            kw["scale"] = scale
        if accum is not None:
            writes.append(accum[0]); kw["accum_out"] = accum[1]
        return self.op("act", lambda h: h.activation(out=oa, in_=ia, func=func, **kw), reads=reads, writes=writes)

    def tt(self, out, in0, in1, op, eng="dve"):
        (ob, oa), (ab, aa), (bb, ba) = out, in0, in1
        return self.op(eng, lambda h: h.tensor_tensor(out=oa, in0=aa, in1=ba, op=op), reads=[ab, bb], writes=[ob])

    def ts(self, out, in0, s1, op0, s2=None, op1=None, eng="dve", accum=None):
        (ob, oa), (ab, aa) = out, in0
        reads, writes = [ab], [ob]
        if isinstance(s1, tuple):
            reads.append(s1[0]); s1 = s1[1]
        if isinstance(s2, tuple):
            reads.append(s2[0]); s2 = s2[1]
        kw = {}
        if op1 is not None:
            kw["op1"] = op1
        if accum is not None:
            writes.append(accum[0]); kw["accum_out"] = accum[1]
        return self.op(eng, lambda h: h.tensor_scalar(oa, aa, s1, s2, op0, **kw), reads=reads, writes=writes)

    def cp(self, out, in_, eng="dve"):
        (ob, oa), (ib, ia) = out, in_
        return self.op(eng, lambda h: h.tensor_copy(out=oa, in_=ia), reads=[ib], writes=[ob])

    def memset(self, out, val, eng="dve"):
        (ob, oa) = out
        return self.op(eng, lambda h: h.memset(oa, val), writes=[ob])

    def recip(self, out, in_):
        (ob, oa), (ib, ia) = out, in_
        return self.op("dve", lambda h: h.reciprocal(out=oa, in_=ia), reads=[ib], writes=[ob])


def V(buf, ap=None):
    return (buf, buf.t[:] if ap is None else ap)


class Cfg:
    def __init__(self, nc_cores=8, sp=4, t=2048):
        self.NC = nc_cores
        self.SP = sp
        self.T = t
        self.NSEG = sp + 1
        self.NCH = t // 128
        self.NB = t // 512
        self.LS = nc_cores * t


class Rot:
    def __init__(self, items):
        self.items = items
        self.i = 0

    def nxt(self):
        b = self.items[self.i % len(self.items)]
        self.i += 1
        return b


def D_(ap):
    return (None, ap)


def _dma(k, out, in_, eng="sp", slow=False):
    (ob, oa), (ib, ia) = out, in_
    reads = [ib] if ib is not None else []
    writes = [ob] if ob is not None else []
    if not reads and not writes:
        raise ValueError("dram->dram untracked")
    if slow:
        return k.op(eng, lambda h: h.dma_start(out=oa, in_=ia, allow_slow_non_contiguous=True), reads=reads, writes=writes, dma=True)
    return k.op(eng, lambda h: h.dma_start(out=oa, in_=ia), reads=reads, writes=writes, dma=True)


K.dma = _dma


def prep_weight(k, st, dst_fn, src, K_rows, cols, row_gain=None, col_gain=None):
    nk = K_rows // 128
    CB = 1408
    if row_gain is not None:
        rg = st["rg"].nxt()
        k.dma(V(rg, rg[:, 0:nk]), D_(row_gain.rearrange("(c p) -> p c", p=128)), slow=True)
    for c0 in range(0, cols, CB):
        cw = min(CB, cols - c0)
        if col_gain is not None:
            cg = st["cg"].nxt()
            k.dma(V(cg, cg[:, 0:cw]), D_(col_gain[c0:c0 + cw].partition_broadcast(128)))
        for kc in range(nk):
            s32 = st["s32"].nxt()
            k.dma(V(s32, s32[:, 0:cw]), D_(src[kc * 128:(kc + 1) * 128, c0:c0 + cw]))
            cur = V(s32, s32[:, 0:cw])
            if col_gain is not None:
                k.tt(cur, cur, V(cg, cg[:, 0:cw]), ALU.mult, eng="pool")
            db, da = dst_fn(kc, c0, cw)
            if db is None:
                sbf = st["sbf"].nxt()
                o = V(sbf, sbf[:, 0:cw])
            else:
                o = (db, da)
            if row_gain is not None:
                k.act(o, cur, AF.Copy, scale=V(rg, rg[:, kc:kc + 1]))
            else:
                k.act(o, cur, AF.Copy)
            if db is None:
                if len(da.shape) == 3:
                    o = (o[0], o[1].rearrange("p (m c) -> p m c", c=128))
                k.dma(D_(da), o, eng="pool")


def prep_stage(k, es):
    return {
        "s32": Rot([k.sb(es, "p0s32_%d" % i, [128, 1408], F32) for i in range(3)]),
        "sbf": Rot([k.sb(es, "p0sbf_%d" % i, [128, 1408], BF16) for i in range(3)]),
        "cg": Rot([k.sb(es, "p0cg_%d" % i, [128, 1408], F32) for i in range(2)]),
        "rg": Rot([k.sb(es, "p0rg_%d" % i, [128, 24], F32) for i in range(2)]),
    }


def load_w(k, buf, dram_ap):
    n = dram_ap.shape[1]
    step = max(1, n // 4)
    for c in range(0, n, step):
        e = min(n, c + step)
        k.dma(V(buf, buf[:, c:e]), D_(dram_ap[:, c:e]))


def rmsnorm_tile(k, P, xt, rows, hn, dim_scale):
    ss = P["ss"].nxt()
    junk = P["junk"].nxt()
    k.act(V(junk, junk[0:rows, :]), V(xt, xt[0:rows, :]), AF.Square, scale=dim_scale, accum=V(ss, ss[0:rows, 0:1]))
    k.act(V(ss, ss[0:rows, 1:2]), V(ss, ss[0:rows, 0:1]), AF.Sqrt, bias=V(P["eps"], P["eps"][0:rows, 0:1]))
    k.recip(V(ss, ss[0:rows, 1:2]), V(ss, ss[0:rows, 1:2]))
    k.ts(V(hn, hn[0:rows, :]), V(xt, xt[0:rows, :]), V(ss, ss[0:rows, 1:2]), ALU.mult)


def transpose_tile(k, P, hn, rows, dst_buf, dst_ap_fn):
    tp = P["tp"].nxt()
    for j in range(8):
        k.tr(V(tp, tp[:, j * 128:j * 128 + rows]), V(hn, hn[0:rows, j * 128:(j + 1) * 128]),
             V(P["ident"], P["ident"][0:rows, 0:rows]))
    src = tp[:, :].rearrange("p (j t) -> p j t", t=128)[:, :, 0:rows]
    k.cp(V(dst_buf, dst_ap_fn), V(tp, src))


def phase_p1a(k, cfg, W, C, segs):
    nc = k.nc
    with ExitStack() as es:
        Wq = k.sb(es, "Wq", [128, 8, QL], BF16)
        Wc = k.sb(es, "Wc", [128, 8, KVL], BF16)
        Wkr = k.sb(es, "Wkr", [128, 8, 96], BF16)
        Wks = k.sb(es, "Wks", [128, 8, 96], BF16)
        Wqb = k.sb(es, "Wqb", [128, 3, H * 96], BF16)
        Wqs = k.sb(es, "Wqs", [128, 3, H * 96], BF16)
        Wkb = k.sb(es, "Wkb", [128, 2, H * 64], BF16)
        Wvb = k.sb(es, "Wvb", [128, 2, H * 64], BF16)
        ident = k.sb(es, "ident_sb", [128, 128], BF16)
        onesb = k.sb(es, "ones_sb", [128, 128], BF16)
        eps_t = k.sb(es, "eps_sb", [128, 1], F32)
        es_prep = ExitStack()
        st = prep_stage(k, es_prep)
        k.dma(V(ident), D_(C["identb"]))
        k.dma(V(onesb), D_(C["onesb"]))
        n1 = W["norm1"]
        win = W["w_in"]
        prep_weight(k, st, lambda kc, c0, cw: (Wq, Wq[:, kc, c0:c0 + cw]), win[:, O_Q:O_Q + QL], D, QL, row_gain=n1)
        prep_weight(k, st, lambda kc, c0, cw: (Wc, Wc[:, kc, c0:c0 + cw]), win[:, O_CKV:O_CKV + KVL], D, KVL, row_gain=n1)
        k.memset(V(Wkr), 0.0)
        k.memset(V(Wks), 0.0)
        prep_weight(k, st, lambda kc, c0, cw: (Wkr, Wkr[:, kc, 64:96]), win[:, O_KR:O_KR + 32], D, 32, row_gain=n1)
        prep_weight(k, st, lambda kc, c0, cw: (Wks, Wks[:, kc, 64:80]), win[:, O_KR + 16:O_KR + 32], D, 16, row_gain=n1)
        prep_weight(k, st, lambda kc, c0, cw: (Wks, Wks[:, kc, 80:96]), win[:, O_KR:O_KR + 16], D, 16, row_gain=n1)
        prep_weight(k, st, lambda kc, c0, cw: (Wqb, Wqb[:, kc, c0:c0 + cw]), W["w_q_b"], QL, H * 96, row_gain=W["q_a_norm"])
        k.memset(V(Wqs), 0.0, eng="pool")
        wqb3 = W["w_q_b"].rearrange("k (h c) -> k h c", c=96)
        for hh in range(H):
            prep_weight(k, st, lambda kc, c0, cw, hh=hh: (Wqs, Wqs[:, kc, hh * 96 + 64:hh * 96 + 80]),
                        W["w_q_b"][:, hh * 96 + 80:hh * 96 + 96], QL, 16, row_gain=W["q_a_norm"])
            prep_weight(k, st, lambda kc, c0, cw, hh=hh: (Wqs, Wqs[:, kc, hh * 96 + 80:hh * 96 + 96]),
                        W["w_q_b"][:, hh * 96 + 64:hh * 96 + 80], QL, 16, row_gain=W["q_a_norm"])
        for hh in range(H):
            prep_weight(k, st, lambda kc, c0, cw, hh=hh: (Wkb, Wkb[:, kc, hh * 64:(hh + 1) * 64]),
                        W["w_kv_b"][:, hh * 128:hh * 128 + 64], KVL, 64, row_gain=W["kv_a_norm"])
            prep_weight(k, st, lambda kc, c0, cw, hh=hh: (Wvb, Wvb[:, kc, hh * 64:(hh + 1) * 64]),
                        W["w_kv_b"][:, hh * 128 + 64:hh * 128 + 128], KVL, 64, row_gain=W["kv_a_norm"])
        k.barrier()
        es_prep.close()

        P = {
            "ss": Rot([k.sb(es, "ss%d" % i, [128, 2], F32) for i in range(3)]),
            "junk": Rot([k.sb(es, "junk%d" % i, [128, D], BF16) for i in range(2)]),
            "tp": Rot([k.ps(es, "tp%d" % i, [128, D], BF16) for i in range(1)]),
            "ident": ident,
            "eps": eps_t,
        }
        k.memset(V(P["eps"]), EPS)
        xt = Rot([k.sb(es, "xt%d" % i, [128, D], F32) for i in range(9)])
        hn = Rot([k.sb(es, "hn%d" % i, [128, D], BF16) for i in range(2)])
        hT = Rot([k.sb(es, "hT%d" % i, [128, 8, 512], BF16) for i in range(2)])
        pb = Rot([k.ps(es, "pb%d" % i, [128, 512], F32) for i in range(7)])
        sq = Rot([k.sb(es, "sq%d" % i, [128, 512], BF16) for i in range(3)])
        rbc = Rot([k.sb(es, "rbc%d" % i, [128, 512], F32) for i in range(2)])
        qln = Rot([k.sb(es, "qln%d" % i, [128, 3, 512], BF16) for i in range(2)])
        ckn = Rot([k.sb(es, "ckn%d" % i, [128, 2, 512], BF16) for i in range(2)])
        cosb = Rot([k.sb(es, "cosb%d" % i, [96, 512], F32) for i in range(3)])
        sinb = Rot([k.sb(es, "sinb%d" % i, [96, 512], F32) for i in range(3)])
        t1 = Rot([k.sb(es, "t1_%d" % i, [96, 512], F32) for i in range(3)])
        t2 = Rot([k.sb(es, "t2_%d" % i, [96, 512], F32) for i in range(3)])
        qo = Rot([k.sb(es, "qo%d" % i, [96, H, 512], BF16) for i in range(1)])
        ko = Rot([k.sb(es, "ko%d" % i, [96, H, 512], BF16) for i in range(1)])
        krt = Rot([k.sb(es, "krt%d" % i, [96, 512], BF16) for i in range(2)])
        vo = Rot([k.sb(es, "vo%d" % i, [128, H, 128], BF16) for i in range(2)])
        for b_ in vo.items:
            k.memset(V(b_), 1.0, eng="pool")

        def fm_rmsnorm(ps_list, dim, dst, nchunk):
            sqs = []
            for m in range(nchunk):
                s_ = sq.nxt()
                k.act(V(s_), V(ps_list[m]), AF.Square, scale=float(dim) ** -0.5)
                sqs.append(s_)
            pss = pb.nxt()
            for m in range(nchunk):
                k.mm(V(pss), V(onesb), V(sqs[m]), start=(m == 0), stop=(m == nchunk - 1))
            r_ = rbc.nxt()
            k.act(V(r_), V(pss), AF.Sqrt, bias=V(P["eps"]))
            k.recip(V(r_), V(r_))
            for m in range(nchunk):
                k.tt(V(dst, dst[:, m, :]), V(ps_list[m]), V(r_), ALU.mult)

        blocks = [(sg, b) for sg in segs for b in range((sg.get("n", cfg.T) + 511) // 512)]

        loaded = {}

        def load_blk(bi):
            sg, b = blocks[bi]
            c0 = b * 512
            xs_l = []
            for ti in range(4):
                x_ = xt.nxt()
                r0 = c0 + ti * 128
                k.dma(V(x_), D_(sg["x"][r0:r0 + 128, :]))
                xs_l.append(x_)
            cs_, sn_ = cosb.nxt(), sinb.nxt()
            k.dma(V(cs_), D_(sg["cos"][:, c0:c0 + 512]))
            k.dma(V(sn_), D_(sg["sin"][:, c0:c0 + 512]))
            loaded[bi] = (xs_l, cs_, sn_)

        def build_hT(bi):
            if bi not in loaded:
                load_blk(bi)
            xs_l, cs_, sn_ = loaded.pop(bi)
            if bi + 1 < len(blocks) and (bi + 1) not in loaded:
                load_blk(bi + 1)
            h_ = hT.nxt()
            for ti in range(4):
                n_ = hn.nxt()
                rmsnorm_tile(k, P, xs_l[ti], 128, n_, 1.0 / 32.0)
                transpose_tile(k, P, n_, 128, h_, h_[:, :, ti * 128:(ti + 1) * 128])
            return h_, cs_, sn_

        nxt_blk = build_hT(0)
        for bi, (sg, b) in enumerate(blocks):
            if True:
                do_q, do_kv = sg.get("do_q", True), sg.get("do_kv", True)
                c0 = b * 512
                h_, cs_, sn_ = nxt_blk
                def rope_rows(pa, ps_, dst):
                    a_, b2 = t1.nxt(), t2.nxt()
                    k.tt(V(a_, a_[64:96, :]), V(pa, pa[64:96, :]), V(cs_, cs_[64:96, :]), ALU.mult)
                    k.tt(V(b2, b2[64:96, :]), V(ps_, ps_[64:96, :]), V(sn_, sn_[64:96, :]), ALU.mult)
                    k.tt(dst, V(a_, a_[64:96, :]), V(b2, b2[64:96, :]), ALU.add, eng="pool")

                if do_q:
                    pq = [pb.nxt() for _ in range(3)]
                    for m in range(3):
                        for kc in range(KC):
                            k.mm(V(pq[m]), V(Wq, Wq[:, kc, m * 128:(m + 1) * 128]), V(h_, h_[:, kc, :]),
                                 start=(kc == 0), stop=(kc == KC - 1))
                if do_kv:
                    pc = [pb.nxt() for _ in range(2)]
                    for m in range(2):
                        for kc in range(KC):
                            k.mm(V(pc[m]), V(Wc, Wc[:, kc, m * 128:(m + 1) * 128]), V(h_, h_[:, kc, :]),
                                 start=(kc == 0), stop=(kc == KC - 1))
                if bi + 1 < len(blocks):
                    nxt_blk = build_hT(bi + 1)
                if do_q:
                    ql = qln.nxt()
                    fm_rmsnorm(pq, QL, ql, 3)
                if do_kv:
                    pka, pks = pb.nxt(), pb.nxt()
                    for kc in range(KC):
                        k.mm(V(pka, pka[0:96, :]), V(Wkr, Wkr[:, kc, :]), V(h_, h_[:, kc, :]), start=(kc == 0), stop=(kc == KC - 1))
                    for kc in range(KC):
                        k.mm(V(pks, pks[0:96, :]), V(Wks, Wks[:, kc, :]), V(h_, h_[:, kc, :]), start=(kc == 0), stop=(kc == KC - 1))
                    cn = ckn.nxt()
                    fm_rmsnorm(pc, KVL, cn, 2)
                    kr = krt.nxt()
                    rope_rows(pka, pks, V(kr, kr[64:96, :]))

                q_ = qo.nxt() if do_q else None
                for hh in range(H if do_q else 0):
                    pa, ps_ = pb.nxt(), pb.nxt()
                    for m in range(3):
                        k.mm(V(pa, pa[0:96, :]), V(Wqb, Wqb[:, m, hh * 96:(hh + 1) * 96]), V(ql, ql[:, m, :]),
                             start=(m == 0), stop=(m == 2))
                    for m in range(3):
                        k.mm(V(ps_, ps_[0:96, :]), V(Wqs, Wqs[:, m, hh * 96:(hh + 1) * 96]), V(ql, ql[:, m, :]),
                             start=(m == 0), stop=(m == 2))
                    if hh % 2 == 0:
                        k.act(V(q_, q_[0:64, hh, :]), V(pa, pa[0:64, :]), AF.Copy)
                    else:
                        k.cp(V(q_, q_[0:64, hh, :]), V(pa, pa[0:64, :]))
                    rope_rows(pa, ps_, V(q_, q_[64:96, hh, :]))
                if do_q:
                    k.dma(D_(sg["qt"][:, :, c0:c0 + 512].rearrange("h p t -> p h t")), V(q_), eng="pool")
                if not do_kv:
                    continue
                k_ = ko.nxt()
                for hh in range(H):
                    pk = pb.nxt()
                    for m in range(2):
                        k.mm(V(pk, pk[0:64, :]), V(Wkb, Wkb[:, m, hh * 64:(hh + 1) * 64]), V(cn, cn[:, m, :]),
                             start=(m == 0), stop=(m == 1))
                    if hh % 2 == 0 and do_q:
                        k.act(V(k_, k_[0:64, hh, :]), V(pk, pk[0:64, :]), AF.Copy)
                    elif hh % 4 == 0:
                        k.act(V(k_, k_[0:64, hh, :]), V(pk, pk[0:64, :]), AF.Copy)
                    else:
                        k.cp(V(k_, k_[0:64, hh, :]), V(pk, pk[0:64, :]))
                k.cp(V(k_, k_[64:96, :, :]), V(kr, kr[64:96, :].unsqueeze(1).broadcast_to([32, H, 512])), eng="pool")
                k.dma(D_(sg["kt"][:, :, c0:c0 + 512].rearrange("h p t -> p h t")), V(k_), eng="pool")
                for ti in range(4):
                    v_ = vo.nxt()
                    for half in range(2):
                        pv = pb.nxt()
                        for m in range(2):
                            k.mm(V(pv), V(cn, cn[:, m, ti * 128:(ti + 1) * 128]), V(Wvb, Wvb[:, m, half * 512:(half + 1) * 512]),
                                 start=(m == 0), stop=(m == 1))
                        if half == 0:
                            k.act(V(v_, v_[:, half * 8:half * 8 + 8, 0:64]),
                                  V(pv, pv[:, :].rearrange("p (j v) -> p j v", v=64)), AF.Copy)
                        else:
                            k.cp(V(v_, v_[:, half * 8:half * 8 + 8, 0:64]),
                                 V(pv, pv[:, :].rearrange("p (j v) -> p j v", v=64)))
                    r0 = c0 + ti * 128
                    k.dma(D_(sg["va"][r0:r0 + 128, :, :]), V(v_), eng="pool")
        k.barrier()


WNAMES = ["norm1", "w_in", "q_a_norm", "kv_a_norm", "w_q_b", "w_kv_b", "conv_w", "conv_b", "dt_bias_f", "dt_bias_b",
          "a_log_f", "a_log_b", "d_skip", "ssm_norm", "w_out", "norm2", "w_gate", "w_up", "ffn_conv_w", "ffn_conv_b",
          "w_down", "final_norm"]


def rope_tables(pos):
    pos = np.asarray(pos, dtype=np.float32)
    inv = (np.float32(10000.0) ** (-(np.arange(0, RO, 2, dtype=np.float32)) / np.float32(RO))).astype(np.float32)
    ang = (pos[:, None] * inv[None, :]).astype(np.float32)
    c, s = np.cos(ang).astype(np.float32).T, np.sin(ang).astype(np.float32).T
    cos = np.ones((96, len(pos)), np.float32)
    sin = np.zeros((96, len(pos)), np.float32)
    cos[64:80], cos[80:96] = c, c
    sin[64:80], sin[80:96] = -s, s
    return cos, sin


def host_consts():
    r = np.arange(128)
    cst = np.zeros((128, 7, 128), np.float32)
    cst[:, 0, :] = (r[:, None] <= r[None, :])
    cst[:, 1, :] = (r[:, None] < r[None, :])
    cst[:, 2, :] = 1.0
    cst[:, 3, :] = np.eye(128)
    cst[:, 4, :] = np.where(r[None, :] < r[:, None], NEG, 0.0)
    cst[:, 5, :] = np.where(r[None, :] > r[:, None], NEG, 0.0)
    cst[:, 6, :] = -(r[:, None] < r[None, :]).astype(np.float32)
    return {
        "identb": np.eye(128, dtype=np.float32).astype(ml_dtypes.bfloat16),
        "onesb": np.ones((128, 128), np.float32).astype(ml_dtypes.bfloat16),
        "cst32": cst,
    }


def declare_weights(nc, shapes):
    W = {}
    for n in WNAMES:
        shp = list(shapes[n])
        W[n] = nc.dram_tensor(n, shp, F32, kind="ExternalInput").ap()
    return W


def wviews(W):
    o = {}
    for n, ap in W.items():
        o[n] = ap if n == "final_norm" else ap[0]
    return o


def phase_p2(k, cfg, jobs):
    with ExitStack() as es:
        maxk = max(j["Tk"] for j in jobs)
        maxq = max(j["Tq"] for j in jobs)
        qb = Rot([k.sb(es, "aq%d" % i, [96, maxq], BF16) for i in range(2)])
        kb = Rot([k.sb(es, "ak%d" % i, [96, maxk], BF16) for i in range(2)])
        vb = Rot([k.sb(es, "av%d" % i, [128, maxk // 128, 128], BF16) for i in range(2)])
        pS = Rot([k.ps(es, "pS%d" % i, [128, 1024], F32) for i in range(3)])
        pO = Rot([k.ps(es, "pO%d" % i, [128, 512], F32) for i in range(2)])
        pt = Rot([k.sb(es, "apt%d" % i, [128, 1024], BF16) for i in range(3)])
        rc = Rot([k.sb(es, "arc%d" % i, [64, 512], F32) for i in range(2)])
        ao = Rot([k.sb(es, "aao%d" % i, [64, 512], BF16) for i in range(3)])

        def load(j):
            Tq, Tk = j["Tq"], j["Tk"]
            q_, k_, v_ = qb.nxt(), kb.nxt(), vb.nxt()
            k.dma(V(q_, q_[:, 0:Tq]), D_(j["qt"]))
            for c in range(0, Tk, 4096):
                e = min(Tk, c + 4096)
                k.dma(V(k_, k_[:, c:e]), D_(j["kt"][:, c:e]))
            for c in range(0, Tk, 2048):
                e = min(Tk, c + 2048)
                k.dma(V(v_, v_[:, c // 128:e // 128, :]), D_(j["va"][c:e, :].rearrange("(t p) v -> p t v", p=128)))
            return q_, k_, v_

        its = []
        for ji, j in enumerate(jobs):
            for qi in range((j["Tq"] + 511) // 512):
                for kp in range(j["Tk"] // 256):
                    its.append((ji, qi, kp))
        bufs = {0: load(jobs[0])}
        sbuf = {}

        def emit_scores(i):
            ji, qi, kp = its[i]
            if ji not in bufs:
                bufs[ji] = load(jobs[ji])
            q_, k_, v_ = bufs[ji]
            qw = min(512, jobs[ji]["Tq"] - qi * 512)
            s_ = pS.nxt()
            for t in range(2):
                kt_ = 2 * kp + t
                k.mm(V(s_, s_[:, t * 512:t * 512 + qw]), V(k_, k_[:, kt_ * 128:(kt_ + 1) * 128]), V(q_, q_[:, qi * 512:qi * 512 + qw]))
            sbuf[i] = s_

        emit_scores(0)
        o_ = None
        for i, (ji, qi, kp) in enumerate(its):
            j = jobs[ji]
            if qi == 0 and kp == 0 and ji + 1 < len(jobs) and (ji + 1) not in bufs:
                bufs[ji + 1] = load(jobs[ji + 1])
            if i + 1 < len(its):
                emit_scores(i + 1)
            q_, k_, v_ = bufs[ji]
            nkp = j["Tk"] // 256
            if kp == 0:
                o_ = pO.nxt()
            s_ = sbuf.pop(i)
            p_ = pt.nxt()
            qw = min(512, j["Tq"] - qi * 512)
            k.act(V(p_, p_[:, :].rearrange("p (t c) -> p t c", c=512)[:, :, 0:qw]),
                  V(s_, s_[:, :].rearrange("p (t c) -> p t c", c=512)[:, :, 0:qw]), AF.Exp, scale=SCALE)
            for t in range(2):
                kt_ = 2 * kp + t
                k.mm(V(o_, o_[:, 0:qw]), V(v_, v_[:, kt_, :]), V(p_, p_[:, t * 512:t * 512 + qw]), start=(kt_ == 0), stop=(kt_ == 2 * nkp - 1))
            if kp == nkp - 1:
                r_ = rc.nxt()
                k.recip(V(r_, r_[:, 0:qw]), V(o_, o_[64:128, 0:qw]))
                a_ = ao.nxt()
                k.tt(V(a_, a_[:, 0:qw]), V(o_, o_[0:64, 0:qw]), V(r_, r_[:, 0:qw]), ALU.mult)
                k.dma(D_(j["at"][:, qi * 512:qi * 512 + qw]), V(a_, a_[:, 0:qw]), eng="pool")
                if qi == (j["Tq"] + 511) // 512 - 1:
                    bufs.pop(ji, None)
        k.barrier()


def phase_p3(k, cfg, W, C, segs, nkc=16):
    with ExitStack() as es:
        st = prep_stage(k, es)
        Wo = k.sb(es, "Wo", [128, nkc, D], BF16)
        ident = k.sb(es, "ident_sb3", [128, 128], BF16)
        k.dma(V(ident), D_(C["identb"]))
        prep_weight(k, st, lambda kc, c0, cw: (Wo, Wo[:, kc, c0:c0 + cw]), W["w_out"][0:1024, :], 1024, D)
        if nkc == 16:
            prep_weight(k, st, lambda kc, c0, cw: (Wo, Wo[:, 8 + kc, c0:c0 + cw]), W["w_out"][1024:2048, :], 1024, D,
                        row_gain=W["ssm_norm"])
        P = {
            "ss": Rot([k.sb(es, "ss3_%d" % i, [128, 2], F32) for i in range(3)]),
            "junk": Rot([k.sb(es, "junk3_%d" % i, [128, D], BF16) for i in range(2)]),
            "tp": Rot([k.ps(es, "tp3_%d" % i, [128, D], BF16) for i in range(2)]),
            "ident": ident,
            "eps": k.sb(es, "eps3", [128, 1], F32),
        }
        k.memset(V(P["eps"]), EPS)
        zt = k.sb(es, "zt3", [128, 8, 2], BF16)
        k.memset(V(zt), 0.0)
        mixT = Rot([k.sb(es, "mixT%d" % i, [128, nkc, 512], BF16) for i in range(2)])
        xt = Rot([k.sb(es, "xt3_%d" % i, [128, D], F32) for i in range(3)])
        x1 = Rot([k.sb(es, "x1_%d" % i, [128, D], F32) for i in range(3)])
        hn = Rot([k.sb(es, "hn3_%d" % i, [128, D], BF16) for i in range(2)])
        h2 = Rot([k.sb(es, "h2_%d" % i, [128, 8, 128], BF16) for i in range(3)])
        px = Rot([k.ps(es, "px%d" % i, [128, D], F32) for i in range(3)])
        for sg in segs:
            T = sg.get("n", cfg.T)
            k.dma(D_(sg["h2t"][:, :, 0:1]), V(zt, zt[:, :, 0:1]), eng="pool", slow=True)
            k.dma(D_(sg["h2t"][:, :, T + 1:T + 2]), V(zt, zt[:, :, 1:2]), eng="pool", slow=True)
            for c0 in range(0, T, 512):
                bwid = min(512, T - c0)
                m_ = mixT.nxt()
                for i, mx in enumerate(sg["mix"]):
                    k.dma(V(m_, m_[:, 8 * i:8 * i + 8, 0:bwid]), D_(mx.rearrange("(c p) t -> p c t", p=128)[:, :, c0:c0 + bwid]))
                for ti in range(bwid // 128):
                    r0 = c0 + ti * 128
                    x_ = xt.nxt()
                    k.dma(V(x_), D_(sg["x"][r0:r0 + 128, :]))
                    p_ = px.nxt()
                    for n in range(2):
                        for kc in range(nkc):
                            k.mm(V(p_, p_[:, n * 512:(n + 1) * 512]), V(m_, m_[:, kc, ti * 128:(ti + 1) * 128]),
                                 V(Wo, Wo[:, kc, n * 512:(n + 1) * 512]), start=(kc == 0), stop=(kc == nkc - 1))
                    y_ = x1.nxt()
                    k.tt(V(y_), V(p_), V(x_), ALU.add)
                    k.dma(D_(sg["x1"][r0:r0 + 128, :]), V(y_), eng="pool")
                    n_ = hn.nxt()
                    rmsnorm_tile(k, P, y_, 128, n_, 1.0 / 32.0)
                    h_ = h2.nxt()
                    transpose_tile(k, P, n_, 128, h_, h_[:, :, :])
                    k.dma(D_(sg["h2t"][:, :, 1 + r0:1 + r0 + 128]), V(h_), eng="pool")
        k.barrier()


FB = 510


def phase_p4(k, cfg, W, C, wg_scr, segs):
    T = cfg.T
    with ExitStack() as es:
        Wu = k.sb(es, "Wu", [128, 8, DFF], BF16)
        Wd = k.sb(es, "Wd", [128, FC, D], BF16)
        bg = k.sb(es, "bg", [128, FC], F32)
        cw3 = k.sb(es, "cw3", [128, 3, FC], F32)
        gain = k.sb(es, "fgain", [128, D], F32)
        eps = k.sb(es, "eps4", [128, 1], F32)
        es_prep = ExitStack()
        st = prep_stage(k, es_prep)
        n2 = W["norm2"]

        def dst(kc, c0, cw):
            return (None, wg_scr[c0 // 128:(c0 + cw) // 128, :, kc, :].rearrange("m p c -> p m c"))
        prep_weight(k, st, dst, W["w_gate"], D, DFF, row_gain=n2)
        prep_weight(k, st, lambda kc, c0, cw: (Wu, Wu[:, kc, c0:c0 + cw]), W["w_up"], D, DFF, row_gain=n2)
        prep_weight(k, st, lambda kc, c0, cw: (Wd, Wd[:, kc, c0:c0 + cw]), W["w_down"], DFF, D)
        k.dma(V(bg), D_(W["ffn_conv_b"].rearrange("(c p) -> p c", p=128)), slow=True)
        for tap in range(3):
            k.dma(V(cw3, cw3[:, tap, :]), D_(W["ffn_conv_w"][tap].rearrange("(c p) -> p c", p=128)), slow=True)
        k.dma(V(gain), D_(W["final_norm"].partition_broadcast(128)))
        k.memset(V(eps), EPS)
        k.barrier()
        es_prep.close()
        hT = Rot([k.sb(es, "h2T%d" % i, [128, 8, T + 2], BF16) for i in range(1)])
        vf4 = k.sb(es, "vf4", [128, 2], F32)
        wg = Rot([k.sb(es, "wg%d" % i, [128, 8, 128], BF16) for i in range(3)])
        pg = Rot([k.ps(es, "pg%d" % i, [128, 512], F32) for i in range(2)])
        pu = Rot([k.ps(es, "pu%d" % i, [128, 512], F32) for i in range(2)])
        pd = Rot([k.ps(es, "pd%d" % i, [128, D], F32) for i in range(2)])
        cv = Rot([k.sb(es, "cv%d" % i, [128, 512], F32) for i in range(3)])
        sg_ = Rot([k.sb(es, "sgl%d" % i, [128, 512], F32) for i in range(2)])
        aT = Rot([k.sb(es, "aT%d" % i, [128, FC, 512], BF16) for i in range(1)])
        x1 = Rot([k.sb(es, "x14_%d" % i, [128, D], F32) for i in range(2)])
        ss = Rot([k.sb(es, "ss4_%d" % i, [128, 2], F32) for i in range(3)])
        junk = Rot([k.sb(es, "junk4_%d" % i, [128, D], BF16) for i in range(2)])
        for sg in segs:
            h_ = hT.nxt()
            for c in range(0, T + 2, 1024):
                e = min(T + 2, c + 1024)
                k.dma(V(h_, h_[:, :, c:e]), D_(sg["h2t"][:, :, c:e]))
            if sg.get("vflag") is not None:
                k.dma(V(vf4), D_(sg["vflag"]))
                k.ts(V(h_, h_[:, :, 0:1]), V(h_, h_[:, :, 0:1]), V(vf4, vf4[:, 0:1]), ALU.mult)
                k.ts(V(h_, h_[:, :, T + 1:T + 2]), V(h_, h_[:, :, T + 1:T + 2]), V(vf4, vf4[:, 1:2]), ALU.mult)
            for v0 in range(0, T, FB):
                bw = min(FB, T - v0)
                a_ = aT.nxt()
                for m in range(FC):
                    w_ = wg.nxt()
                    k.dma(V(w_), D_(wg_scr[m]))
                    g_, u_ = pg.nxt(), pu.nxt()
                    for kc in range(KC):
                        k.mm(V(g_, g_[:, 0:bw + 2]), V(w_, w_[:, kc, :]), V(h_, h_[:, kc, v0:v0 + bw + 2]),
                             start=(kc == 0), stop=(kc == KC - 1))
                    for kc in range(KC):
                        k.mm(V(u_, u_[:, 0:bw]), V(Wu, Wu[:, kc, m * 128:(m + 1) * 128]), V(h_, h_[:, kc, v0 + 1:v0 + 1 + bw]),
                             start=(kc == 0), stop=(kc == KC - 1))
                    c_ = cv.nxt()
                    k.ts(V(c_, c_[:, 0:bw]), V(g_, g_[:, 0:bw]), V(cw3, cw3[:, 0, m:m + 1]), ALU.mult)
                    for tap in (1, 2):
                        k.op("dve", lambda hh, c_=c_, g_=g_, tap=tap, m=m, bw=bw: hh.scalar_tensor_tensor(
                            out=c_[:, 0:bw], in0=g_[:, tap:tap + bw], scalar=cw3[:, tap, m:m + 1], in1=c_[:, 0:bw],
                            op0=ALU.mult, op1=ALU.add), reads=[g_, cw3, c_], writes=[c_])
                    s_ = sg_.nxt()
                    k.act(V(s_, s_[:, 0:bw]), V(c_, c_[:, 0:bw]), AF.Silu, bias=V(bg, bg[:, m:m + 1]))
                    k.tt(V(a_, a_[:, m, 0:bw]), V(s_, s_[:, 0:bw]), V(u_, u_[:, 0:bw]), ALU.mult)
                for i0_ in range(0, bw, 128):
                    rows = min(128, bw - i0_)
                    r0 = v0 + i0_
                    x_ = x1.nxt()
                    k.dma(V(x_, x_[0:rows, :]), D_(sg["x1"][r0:r0 + rows, :]))
                    p_ = pd.nxt()
                    for n in range(2):
                        for m in range(FC):
                            k.mm(V(p_, p_[0:rows, n * 512:(n + 1) * 512]), V(a_, a_[:, m, i0_:i0_ + rows]),
                                 V(Wd, Wd[:, m, n * 512:(n + 1) * 512]), start=(m == 0), stop=(m == FC - 1))
                    k.tt(V(x_, x_[0:rows, :]), V(p_, p_[0:rows, :]), V(x_, x_[0:rows, :]), ALU.add)
                    s2, jk = ss.nxt(), junk.nxt()
                    k.act(V(jk, jk[0:rows, :]), V(x_, x_[0:rows, :]), AF.Square, scale=1.0 / 32.0, accum=V(s2, s2[0:rows, 0:1]))
                    k.act(V(s2, s2[0:rows, 1:2]), V(s2, s2[0:rows, 0:1]), AF.Sqrt, bias=V(eps, eps[0:rows, :]))
                    k.recip(V(s2, s2[0:rows, 1:2]), V(s2, s2[0:rows, 1:2]))
                    k.act(V(x_, x_[0:rows, :]), V(x_, x_[0:rows, :]), AF.Copy, scale=V(s2, s2[0:rows, 1:2]))
                    k.tt(V(x_, x_[0:rows, :]), V(x_, x_[0:rows, :]), V(gain, gain[0:rows, :]), ALU.mult, eng="pool")
                    k.dma(D_(sg["out"][r0:r0 + rows, :]), V(x_, x_[0:rows, :]), eng="pool")
        k.barrier()


def build_program(cfg, shapes, cst_arrays):
    nc = bass.Bass("TRN2", target_bir_lowering=False)
    T, NSEG, SP, LS, NCs = cfg.T, cfg.NSEG, cfg.SP, cfg.LS, cfg.NC
    W = wviews(declare_weights(nc, shapes))
    C = {n: nc.dram_tensor(n, list(a.shape), F32 if a.dtype == np.float32 else BF16, kind="ExternalInput").ap()
         for n, a in cst_arrays.items()}
    x_own = nc.dram_tensor("x_own", [NSEG * T, D], F32, kind="ExternalInput").ap()
    x_sg = nc.dram_tensor("x_sg", [LS, D], F32, kind="ExternalInput").ap()
    y_own = nc.dram_tensor("y_own", [NSEG * T, D], F32, kind="ExternalOutput").ap()

    def scr(name, shape, dt):
        return nc.dram_tensor(name, list(shape), dt, kind="Internal").ap()
    QT = scr("QT", [NSEG, H, 96, T], BF16)
    KT = scr("KT", [max(SP, 1), H, 96, T], BF16)
    VA = scr("VA", [max(SP, 1), T, H, 128], BF16)
    KTS = scr("KTS", [H, 96, LS], BF16)
    VAS = scr("VAS", [LS, H, 128], BF16)
    QTD = scr("QTD", [H, 96, T], BF16)
    KTD = scr("KTD", [H, 96, T], BF16)
    VAD = scr("VAD", [T, H, 128], BF16)
    AT = scr("AT", [NSEG, D, T], BF16)
    X1 = scr("X1", [NSEG * T, D], F32)
    H2T = scr("H2T", [NSEG, 128, 8, T + 2], BF16)
    WG = scr("WG", [FC, 128, 8, 128], BF16)
    with ExitStack() as es:
        k = K(nc, es)
        segs = []
        for s_ in range(SP):
            segs.append(dict(x=x_own[s_ * T:(s_ + 1) * T, :], cos=C["cosp"], sin=C["sinp"], qt=QT[s_], kt=KT[s_], va=VA[s_]))
        segs.append(dict(x=x_own[SP * T:(SP + 1) * T, :], cos=C["coso"], sin=C["sino"], qt=QT[SP], kt=KTD, va=VAD))
        for c in range(NCs):
            segs.append(dict(x=x_sg[c * T:(c + 1) * T, :], cos=C["cosg"][:, c * T:(c + 1) * T], sin=C["sing"][:, c * T:(c + 1) * T],
                             qt=QTD, kt=KTS[:, :, c * T:(c + 1) * T], va=VAS[c * T:(c + 1) * T, :, :]))
        phase_p1a(k, cfg, W, C, segs)
        jobs = []
        for s_ in range(NSEG):
            for hh in range(H):
                if s_ < SP:
                    jobs.append(dict(qt=QT[s_, hh], kt=KT[s_, hh], va=VA[s_, :, hh, :], at=AT[s_, hh * 64:(hh + 1) * 64, :], Tq=T, Tk=T))
                else:
                    jobs.append(dict(qt=QT[s_, hh], kt=KTS[hh], va=VAS[:, hh, :], at=AT[s_, hh * 64:(hh + 1) * 64, :], Tq=T, Tk=LS))
        phase_p2(k, cfg, jobs)
        segs3 = [dict(x=x_own[s_ * T:(s_ + 1) * T, :], mix=[AT[s_]], x1=X1[s_ * T:(s_ + 1) * T, :], h2t=H2T[s_]) for s_ in range(NSEG)]
        phase_p3(k, cfg, W, C, segs3, nkc=8)
        segs4 = [dict(h2t=H2T[s_], x1=X1[s_ * T:(s_ + 1) * T, :], out=y_own[s_ * T:(s_ + 1) * T, :]) for s_ in range(NSEG)]
        phase_p4(k, cfg, W, C, WG, segs4)
        n_ops = k.n_ops
        k.emit()
    return nc, n_ops


def run_cfg(cfg, inputs, x_prompt, x_sample):
    T, SP, NCs = cfg.T, cfg.SP, cfg.NC
    cst = host_consts()
    cst["cosp"], cst["sinp"] = rope_tables(np.arange(T))
    cst["cosg"], cst["sing"] = rope_tables(np.arange(cfg.LS))
    cst["coso"], cst["sino"] = rope_tables(np.arange(T))
    shapes = {n: inputs[n].shape for n in WNAMES}
    nc, n_ops = build_program(cfg, shapes, cst)
    in_maps = []
    for c in range(NCs):
        m = {n: np.ascontiguousarray(inputs[n], dtype=np.float32) for n in WNAMES}
        m.update(cst)
        co, so = rope_tables(np.arange(c * T, (c + 1) * T))
        m["coso"], m["sino"] = co, so
        parts = [x_prompt[c * SP + s_] for s_ in range(SP)] + [x_sample[c * T:(c + 1) * T]]
        m["x_own"] = np.ascontiguousarray(np.concatenate(parts, 0), dtype=np.float32)
        m["x_sg"] = np.ascontiguousarray(x_sample, dtype=np.float32)
        in_maps.append(m)
    res = run_bass_kernel_spmd(nc, in_maps, core_ids=list(range(NCs)))
    yp = np.zeros((NCs * SP, T, D), np.float32)
    ys = np.zeros((cfg.LS, D), np.float32)
    for c in range(NCs):
        y = res.results[c]["y_own"]
        for s_ in range(SP):
            yp[c * SP + s_] = y[s_ * T:(s_ + 1) * T]
        ys[c * T:(c + 1) * T] = y[SP * T:(SP + 1) * T]
    return yp, ys


def kernel(**inputs):
    inputs = {n: np.asarray(v) for n, v in inputs.items()}
    cfg = Cfg(8, 4, 2048)
    yp, ys = run_cfg(cfg, inputs, inputs["x_prompt"], inputs["x_sample"][0])
    return yp, ys[None]


def phase_p1c(k, cfg, W, C, segs):
    maxn = max(sg["n"] for sg in segs)
    with ExitStack() as es:
        Wx = k.sb(es, "Wx", [128, 8, 3, DXBC], BF16)
        Wz = k.sb(es, "Wz", [128, 8, DSSM], BF16)
        Wdt = k.sb(es, "Wdt", [128, 8, 32], BF16)
        ident = k.sb(es, "ident_c", [128, 128], BF16)
        onesb = k.sb(es, "ones_c", [128, 128], BF16)
        cst = k.sb(es, "cst_c", [128, 7, 128], F32)
        cbb = k.sb(es, "cbb", [1, DXBC], BF16)
        cb32 = k.sb(es, "cb32", [1, DXBC], F32)
        cbf = k.sb(es, "cbf", [64, 4], F32)
        dtb = k.sb(es, "dtb", [128, 32], F32)
        Abc = k.sb(es, "Abc", [128, 32], F32)
        eps = k.sb(es, "eps_c", [128, 1], F32)
        k.dma(V(ident), D_(C["identb"]))
        k.dma(V(onesb), D_(C["onesb"]))
        k.dma(V(cst), D_(C["cst32"]))
        k.dma(V(cb32), D_(W["conv_b"].rearrange("(o c) -> o c", o=1)))
        k.cp(V(cbb), V(cb32))
        k.dma(V(cbf), D_(W["conv_b"][1024:1280].rearrange("(i p) -> p i", p=64)), slow=True)
        k.dma(V(dtb, dtb[:, 0:16]), D_(W["dt_bias_f"].partition_broadcast(128)))
        k.dma(V(dtb, dtb[:, 16:32]), D_(W["dt_bias_b"].partition_broadcast(128)))
        k.dma(V(Abc, Abc[:, 0:16]), D_(W["a_log_f"].partition_broadcast(128)))
        k.dma(V(Abc, Abc[:, 16:32]), D_(W["a_log_b"].partition_broadcast(128)))
        k.act(V(Abc), V(Abc), AF.Exp)
        k.ts(V(Abc), V(Abc), -1.0, ALU.mult)
        k.memset(V(eps), EPS)
        es_prep = ExitStack()
        st = prep_stage(k, es_prep)
        n1, win = W["norm1"], W["w_in"]
        for tap in range(3):
            prep_weight(k, st, lambda kc, c0, cw, tap=tap: (Wx, Wx[:, kc, tap, c0:c0 + cw]), win[:, O_XBC:O_XBC + DXBC], D, DXBC,
                        row_gain=n1, col_gain=W["conv_w"][tap])
        prep_weight(k, st, lambda kc, c0, cw: (Wz, Wz[:, kc, c0:c0 + cw]), win[:, O_Z:O_Z + DSSM], D, DSSM, row_gain=n1)
        prep_weight(k, st, lambda kc, c0, cw: (Wdt, Wdt[:, kc, c0:c0 + cw]), win[:, O_DT:O_DT + 32], D, 32, row_gain=n1)
        k.barrier()
        es_prep.close()
        P = {
            "ss": Rot([k.sb(es, "ssc%d" % i, [128, 2], F32) for i in range(3)]),
            "junk": Rot([k.sb(es, "junkc%d" % i, [128, D], BF16) for i in range(2)]),
            "tp": Rot([k.ps(es, "tpc%d" % i, [128, D], BF16) for i in range(1)]),
            "ident": ident, "eps": eps,
        }
        hT = k.sb(es, "hTc", [128, 8, maxn + 2], BF16)
        xt = Rot([k.sb(es, "xtc%d" % i, [128, D], F32) for i in range(3)])
        hn = Rot([k.sb(es, "hnc%d" % i, [128, D], BF16) for i in range(2)])
        big = Rot([k.ps(es, "bigc%d" % i, [128, D], F32) for i in range(2)])
        sml = Rot([k.ps(es, "smlc%d" % i, [128, 512], F32) for i in range(3)])
        xsb = Rot([k.sb(es, "xsb%d" % i, [128, D], BF16) for i in range(4)])
        zsb = Rot([k.sb(es, "zsb%d" % i, [128, D], BF16) for i in range(2)])
        btk = Rot([k.sb(es, "btk%d" % i, [128, 128], BF16) for i in range(3)])
        dts = Rot([k.sb(es, "dts%d" % i, [128, 160], F32) for i in range(3)])
        smo = Rot([k.sb(es, "smo%d" % i, [128, 96], F32) for i in range(2)])
        bw = Rot([k.sb(es, "bw%d" % i, [128, H, 64], BF16) for i in range(6)])
        so = Rot([k.sb(es, "so%d" % i, [64, D], F32) for i in range(2)])
        bco = Rot([k.sb(es, "bco%d" % i, [64, 512], BF16) for i in range(3)])
        for sg in segs:
            n, lite = sg["n"], sg["lite"]
            nch = n // 128
            for ti in range(nch):
                x_ = xt.nxt()
                k.dma(V(x_), D_(sg["x"][ti * 128:(ti + 1) * 128, :]))
                n_ = hn.nxt()
                rmsnorm_tile(k, P, x_, 128, n_, 1.0 / 32.0)
                transpose_tile(k, P, n_, 128, hT, hT[:, :, 1 + ti * 128:1 + (ti + 1) * 128])
            x_ = xt.nxt()
            k.dma(V(x_, x_[0:2, :]), D_(sg["xh"]))
            n_ = hn.nxt()
            rmsnorm_tile(k, P, x_, 2, n_, 1.0 / 32.0)
            tp = P["tp"].nxt()
            for j in range(8):
                k.tr(V(tp, tp[:, j * 128:j * 128 + 2]), V(n_, n_[0:2, j * 128:(j + 1) * 128]), V(ident, ident[0:2, 0:2]))
            tpv = tp[:, :].rearrange("p (j t) -> p j t", t=128)
            k.cp(V(hT, hT[:, :, 0:1]), V(tp, tpv[:, :, 0:1]))
            k.cp(V(hT, hT[:, :, n + 1:n + 2]), V(tp, tpv[:, :, 1:2]))
            def proj(c):
                cb0 = 128 * c
                px = big.nxt()
                for nb in range(2):
                    i = 0
                    for tap in range(3):
                        for kc in range(KC):
                            k.mm(V(px, px[:, nb * 512:(nb + 1) * 512]), V(hT, hT[:, kc, cb0 + tap:cb0 + tap + 128]),
                                 V(Wx, Wx[:, kc, tap, nb * 512:(nb + 1) * 512]), start=(i == 0), stop=False)
                            i += 1
                    k.mm(V(px, px[:, nb * 512:(nb + 1) * 512]), V(onesb, onesb[0:1, 0:128]), V(cbb, cbb[0:1, nb * 512:(nb + 1) * 512]),
                         start=False, stop=True)
                xs_ = xsb.nxt()
                k.act(V(xs_), V(px), AF.Silu)
                if not lite:
                    k.dma(D_(sg["xs"][c * 128:(c + 1) * 128, :]), V(xs_), eng="pool")
                    pz = big.nxt()
                    for nb in range(2):
                        for kc in range(KC):
                            k.mm(V(pz, pz[:, nb * 512:(nb + 1) * 512]), V(hT, hT[:, kc, cb0 + 1:cb0 + 129]),
                                 V(Wz, Wz[:, kc, nb * 512:(nb + 1) * 512]), start=(kc == 0), stop=(kc == KC - 1))
                    z_ = zsb.nxt()
                    k.act(V(z_), V(pz), AF.Silu)
                    k.dma(D_(sg["zs"][c * 128:(c + 1) * 128, :]), V(z_), eng="pool")
                pm = sml.nxt()
                i = 0
                for tap in range(3):
                    for kc in range(KC):
                        k.mm(V(pm, pm[:, 0:128]), V(hT, hT[:, kc, cb0 + tap:cb0 + tap + 128]), V(Wx, Wx[:, kc, tap, 1024:1152]),
                             start=(i == 0), stop=False)
                        i += 1
                k.mm(V(pm, pm[:, 0:128]), V(onesb, onesb[0:1, 0:128]), V(cbb, cbb[0:1, 1024:1152]), start=False, stop=True)
                for kc in range(KC):
                    k.mm(V(pm, pm[:, 128:160]), V(hT, hT[:, kc, cb0 + 1:cb0 + 129]), V(Wdt, Wdt[:, kc, :]),
                         start=(kc == 0), stop=(kc == KC - 1))
                bt_ = btk.nxt()
                k.act(V(bt_), V(pm, pm[:, 0:128]), AF.Silu)
                d_ = dts.nxt()
                k.tt(V(d_, d_[:, 0:32]), V(pm, pm[:, 128:160]), V(dtb), ALU.add)
                return xs_, bt_, d_

            def rest(c, xs_, bt_, d_):
                k.act(V(d_, d_[:, 0:32]), V(d_, d_[:, 0:32]), AF.Exp)
                k.act(V(d_, d_[:, 0:32]), V(d_, d_[:, 0:32]), AF.Ln, bias=1.0)
                k.act(V(d_, d_[:, 32:64]), V(d_, d_[:, 0:32]), AF.Ln)
                sm_ = smo.nxt()
                k.tt(V(sm_, sm_[:, 0:32]), V(d_, d_[:, 0:32]), V(Abc), ALU.mult)
                pc = sml.nxt()
                k.mm(V(pc, pc[:, 0:16]), V(cst, cst[:, 0, :]), V(sm_, sm_[:, 0:16]))
                k.mm(V(pc, pc[:, 16:32]), V(cst, cst[:, 1, :]), V(sm_, sm_[:, 16:32]))
                k.mm(V(pc, pc[:, 32:64]), V(cst, cst[:, 2, :]), V(sm_, sm_[:, 0:32]))
                k.tt(V(sm_, sm_[:, 32:48]), V(d_, d_[:, 32:48]), V(pc, pc[:, 0:16]), ALU.subtract)
                k.tt(V(sm_, sm_[:, 48:64]), V(d_, d_[:, 48:64]), V(pc, pc[:, 16:32]), ALU.add)
                k.cp(V(sm_, sm_[:, 64:96]), V(pc, pc[:, 32:64]))
                k.tt(V(d_, d_[:, 96:112]), V(sm_, sm_[:, 64:80]), V(sm_, sm_[:, 32:48]), ALU.add)
                k.cp(V(d_, d_[:, 112:128]), V(sm_, sm_[:, 48:64]))
                k.act(V(d_, d_[:, 128:160]), V(d_, d_[:, 96:128]), AF.Exp)
                k.dma(D_(sg["sm"][c]), V(sm_), eng="pool")
                btv = bt_[:, :].rearrange("p (g n) -> p g n", g=2).unsqueeze(2).broadcast_to([128, 2, 8, 64])
                bws = []
                for d in range(2):
                    b_ = bw.nxt()
                    wv = d_[:, 128 + 16 * d:144 + 16 * d].rearrange("p (g j) -> p g j", g=2).unsqueeze(3).broadcast_to([128, 2, 8, 64])
                    k.tt(V(b_, b_[:, :, :].rearrange("p (g j) n -> p g j n", g=2)), V(bt_, btv), V(d_, wv), ALU.mult, eng="pool")
                    bws.append(b_)
                return bws

            def restB(c, xs_, bws):
                for d in range(2):
                    b_ = bws[d]
                    s_ = so.nxt()
                    for half in range(2):
                        pS = sml.nxt()
                        for j in range(8):
                            hh = half * 8 + j
                            k.mm(V(pS, pS[0:64, j * 64:(j + 1) * 64]), V(b_, b_[:, hh, :]), V(xs_, xs_[:, hh * 64:(hh + 1) * 64]))
                        k.cp(V(s_, s_[:, half * 512:(half + 1) * 512]), V(pS, pS[0:64, :]))
                    k.dma(D_(sg["sst"][c, d]), V(s_), eng="pool")

            nxt_h = proj(0)
            pend = None
            for c in range(nch):
                cur = nxt_h
                if c + 1 < nch:
                    nxt_h = proj(c + 1)
                bws = rest(c, *cur)
                if pend is not None:
                    restB(*pend)
                pend = (c, cur[0], bws)
            restB(*pend)
            if lite:
                continue
            for c0 in range(0, n, 512):
                bwid = min(512, n - c0)
                for idx in range(4):
                    pb_ = sml.nxt()
                    i = 0
                    for tap in range(3):
                        for kc in range(KC):
                            k.mm(V(pb_, pb_[0:64, 0:bwid]), V(Wx, Wx[:, kc, tap, 1024 + idx * 64:1088 + idx * 64]),
                                 V(hT, hT[:, kc, c0 + tap:c0 + tap + bwid]), start=(i == 0), stop=(i == 23))
                            i += 1
                    o_ = bco.nxt()
                    k.act(V(o_, o_[:, 0:bwid]), V(pb_, pb_[0:64, 0:bwid]), AF.Silu, bias=V(cbf, cbf[:, idx:idx + 1]))
                    k.dma(D_(sg["bct"][idx, :, c0:c0 + bwid]), V(o_, o_[:, 0:bwid]), eng="pool")
        k.barrier()


def phase_p1b(k, cfg, W, C, segs, glob=None):
    maxn = max(sg["n"] for sg in segs)
    maxc = maxn // 128
    with ExitStack() as es:
        ident = k.sb(es, "ident_b", [128, 128], BF16)
        cst = k.sb(es, "cst_b", [128, 7, 128], F32)
        dsk = k.sb(es, "dsk", [128, H], F32)
        eps = k.sb(es, "eps_b", [128, 1], F32)
        zcol = k.sb(es, "zcol", [128, 1], F32)
        k.dma(V(ident), D_(C["identb"]))
        k.dma(V(cst), D_(C["cst32"]))
        mskb = k.sb(es, "mskb", [128, 2, 128], BF16)
        k.cp(V(mskb), V(cst, cst[:, 4:6, :]))
        Tb = k.sb(es, "Tb", [128, 2, 128], BF16)
        k.cp(V(Tb, Tb[:, 0, :]), V(cst, cst[:, 0, :]))
        k.cp(V(Tb, Tb[:, 1, :]), V(cst, cst[:, 6, :]))
        ahi = k.sb(es, "ahi", [128, maxc, 32], BF16)
        alo = k.sb(es, "alo", [128, maxc, 32], BF16)
        atmp = k.sb(es, "atmp", [128, maxc, 32], F32)
        k.dma(V(dsk), D_(W["d_skip"].partition_broadcast(128)))
        k.memset(V(eps), EPS)
        k.memset(V(zcol), 0.0)
        P = {
            "ss": Rot([k.sb(es, "ssb%d" % i, [128, 2], F32) for i in range(3)]),
            "junk": Rot([k.sb(es, "junkb%d" % i, [128, D], BF16) for i in range(2)]),
            "tp": Rot([k.ps(es, "tpb%d" % i, [128, D], BF16) for i in range(1)]),
            "ident": ident, "eps": eps,
        }
        smb = k.sb(es, "smb", [128, maxc, 96], F32)
        bct = k.sb(es, "bctb", [64, 4, maxn], BF16)
        dall = k.sb(es, "dall", [64, maxc, 32], F32)
        hbin = k.sb(es, "hbin", [64, maxc, D], BF16)
        hb = k.sb(es, "hb", [64, D], F32)
        hf = k.sb(es, "hf", [64, D], F32)
        hfb = Rot([k.sb(es, "hfb%d" % i, [64, D], BF16) for i in range(2)])
        sld = Rot([k.sb(es, "sld%d" % i, [64, D], F32) for i in range(3)])
        sld2 = Rot([k.sb(es, "sld2_%d" % i, [64, D], F32) for i in range(3)])
        xsb = Rot([k.sb(es, "xsB%d" % i, [128, D], BF16) for i in range(2)])
        zsb = Rot([k.sb(es, "zsB%d" % i, [128, D], BF16) for i in range(2)])
        gs = Rot([k.sb(es, "gs%d" % i, [128, 2, 128], F32) for i in range(2)])
        lp = Rot([k.sb(es, "lp%d" % i, [128, 2, 128], F32) for i in range(4)])
        ee = Rot([k.sb(es, "ee%d" % i, [64, 2, 128], F32) for i in range(4)])
        mt = Rot([k.sb(es, "mt%d" % i, [128, 2, 128], BF16) for i in range(4)])
        cp_ = Rot([k.sb(es, "cpp%d" % i, [64, 2, 128], BF16) for i in range(4)])
        y1j = Rot([k.sb(es, "y1j%d" % i, [128, 512], F32) for i in range(2)])
        y1 = Rot([k.sb(es, "y1_%d" % i, [128, D], F32) for i in range(2)])
        xsd = Rot([k.sb(es, "xsd%d" % i, [128, D], BF16) for i in range(2)])
        yn = Rot([k.sb(es, "yn%d" % i, [128, D], BF16) for i in range(2)])
        yT = Rot([k.sb(es, "yT%d" % i, [128, 8, 128], BF16) for i in range(2)])
        gm = k.sb(es, "gmask", [64, 2, 128], F32)
        gsm = k.sb(es, "gsm", [64, 128, 32], F32)
        gd = Rot([k.sb(es, "gd%d" % i, [64, 16], F32) for i in range(6)])
        vfl = k.sb(es, "vfl", [64, 2], F32)
        pY = Rot([k.ps(es, "pY%d" % i, [128, D], F32) for i in range(1)])
        pR = Rot([k.ps(es, "pR%d" % i, [128, 512], F32) for i in range(4)])
        pG = Rot([k.ps(es, "pG%d" % i, [128, 256], F32) for i in range(1)])

        def decay_mul(h_, dv):
            k.tt(V(h_, h_[:, :].rearrange("p (h q) -> p h q", q=64)), V(h_, h_[:, :].rearrange("p (h q) -> p h q", q=64)),
                 (dv[0], dv[1].unsqueeze(2).broadcast_to([64, H, 64])), ALU.mult)

        for sg in segs:
            n = sg["n"]
            nch = n // 128
            if sg["init"]:
                NG = glob["NG"]
                k.dma(V(gm, gm[:, :, 0:NG]), D_(glob["mask"]))
                k.dma(V(gsm, gsm[:, 0:NG, :]), D_(glob["sm"][:, 0:64, 64:96].rearrange("c p f -> p c f")), slow=True)
                k.memset(V(hf), 0.0)
                k.memset(V(hb), 0.0, eng="pool")
                for step in range(NG):
                    for d, h_, eng_ in ((0, hf, "dve"), (1, hb, "dve")):
                        kk = step if d == 0 else NG - 1 - step
                        g_ = gd.nxt()
                        k.act(V(g_), V(gsm, gsm[:, kk, 16 * d:16 * d + 16]), AF.Exp, scale=V(gm, gm[:, d, kk:kk + 1]))
                        hv = h_[:, :].rearrange("p (h q) -> p h q", q=64)
                        k.tt(V(h_, hv), V(h_, hv), (g_, g_[:, :].unsqueeze(2).broadcast_to([64, H, 64])), ALU.mult, eng=eng_)
                        s_ = (sld if d == 0 else sld2).nxt()
                        k.dma(V(s_), D_(glob["sst"][kk, d]))
                        if eng_ == "dve":
                            k.op(eng_, lambda hh, s_=s_, h_=h_, d=d, kk=kk: hh.scalar_tensor_tensor(
                                out=h_[:, :], in0=s_[:, :], scalar=gm[:, d, kk:kk + 1], in1=h_[:, :], op0=ALU.mult, op1=ALU.add),
                                reads=[s_, gm, h_], writes=[h_])
                        else:
                            k.ts(V(s_), V(s_), V(gm, gm[:, d, kk:kk + 1]), ALU.mult, eng=eng_)
                            k.tt(V(h_), V(h_), V(s_), ALU.add, eng=eng_)
            else:
                k.memset(V(hf), 0.0)
                k.memset(V(hb), 0.0)
            if sg["vflag"] is not None:
                k.dma(V(vfl), D_(sg["vflag"]))
            k.dma(V(smb, smb[:, 0:nch, :]), D_(sg["sm"].rearrange("c p f -> p c f")))
            k.dma(V(bct, bct[:, :, 0:n]), D_(sg["bct"].rearrange("i p t -> p i t")))
            k.act(V(dall, dall[:, 0:nch, :]), V(smb, smb[0:64, 0:nch, 64:96]), AF.Exp)
            k.cp(V(ahi, ahi[:, 0:nch, :]), V(smb, smb[:, 0:nch, 0:32]))
            k.tt(V(atmp, atmp[:, 0:nch, :]), V(smb, smb[:, 0:nch, 0:32]), V(ahi, ahi[:, 0:nch, :]), ALU.subtract)
            k.cp(V(alo, alo[:, 0:nch, :]), V(atmp, atmp[:, 0:nch, :]))

            def load_state(kk, d):
                s_ = sld.nxt()
                k.dma(V(s_), D_(sg["sst"][kk, d]))
                ne = sg.get("nedge", 1)
                if sg["vflag"] is not None and (kk < ne or kk >= nch - ne):
                    col = 0 if kk < ne else 1
                    k.ts(V(s_), V(s_), V(vfl, vfl[:, col:col + 1]), ALU.mult)
                return s_

            for kk in range(nch - 1, -1, -1):
                k.act(V(hbin, hbin[:, kk, :]), V(hb), AF.Copy)
                if kk > 0:
                    s_ = load_state(kk, 1)
                    decay_mul(hb, V(dall, dall[:, kk, 16:32]))
                    k.tt(V(hb), V(hb), V(s_), ALU.add)
            its = [(kk, d, pr) for kk in range(nch) for d in range(2) for pr in range(8)]
            ctx = {}
            rbuf = {}

            def emit_R(i):
                kk, d, pr = its[i]
                t0 = kk * 128
                if d == 0 and pr == 0:
                    xs_, z_ = xsb.nxt(), zsb.nxt()
                    k.dma(V(xs_), D_(sg["xs"][t0:t0 + 128, :]))
                    k.dma(V(z_), D_(sg["zs"][t0:t0 + 128, :]))
                    g_ = pG.nxt()
                    for g in range(2):
                        k.mm(V(g_, g_[:, g * 128:(g + 1) * 128]), V(bct, bct[:, g, t0:t0 + 128]), V(bct, bct[:, 2 + g, t0:t0 + 128]))
                    gs_ = gs.nxt()
                    k.cp(V(gs_), V(g_, g_[:, :].rearrange("p (g t) -> p g t", g=2)))
                    xd_ = xsd.nxt()
                    k.tt(V(xd_, xd_[:, :].rearrange("p (h q) -> p h q", q=64)), V(xs_, xs_[:, :].rearrange("p (h q) -> p h q", q=64)),
                         V(dsk, dsk[:, :].unsqueeze(2).broadcast_to([128, H, 64])), ALU.mult, eng="pool")
                    ctx[kk] = dict(xs=xs_, z=z_, gs=gs_, xd=xd_)
                r_ = pR.nxt()
                for j in range(2):
                    hh = 2 * pr + j
                    hcol = ahi[:, kk, 16 * d + hh:16 * d + hh + 1].broadcast_to([128, 128])
                    lcol = alo[:, kk, 16 * d + hh:16 * d + hh + 1].broadcast_to([128, 128])
                    rm = r_[:, j * 128:(j + 1) * 128]
                    ru = r_[:, 256 + j * 128:256 + (j + 1) * 128]
                    k.mm(V(r_, rm), V(ident), V(mskb, mskb[:, d, :]), start=True, stop=False)
                    k.mm(V(r_, rm), V(ahi, hcol), V(Tb, Tb[:, d, :]), start=False, stop=False)
                    k.mm(V(r_, rm), V(alo, lcol), V(Tb, Tb[:, d, :]), start=False, stop=True)
                    k.mm(V(r_, ru), V(ahi, hcol), V(Tb, Tb[:, d, :]), start=True, stop=False)
                    k.mm(V(r_, ru), V(alo, lcol), V(Tb, Tb[:, d, :]), start=False, stop=True)
                rbuf[i] = r_

            emit_R(0)
            if len(its) > 1:
                emit_R(1)
            for i, (kk, d, pr) in enumerate(its):
                t0 = kk * 128
                if i + 2 < len(its):
                    emit_R(i + 2)
                cx = ctx[kk]
                xs_, z_, gs_ = cx["xs"], cx["z"], cx["gs"]
                if d == 0 and pr == 0:
                    hfb_ = hfb.nxt()
                    k.cp(V(hfb_), V(hf))
                    y_ = pY.nxt()
                    for nb in range(2):
                        k.mm(V(y_, y_[:, nb * 512:(nb + 1) * 512]), V(ident), V(cx["xd"], cx["xd"][:, nb * 512:(nb + 1) * 512]), start=True, stop=False)
                    cx["hfb"], cx["y"] = hfb_, y_
                hfb_, y_ = cx["hfb"], cx["y"]
                r_ = rbuf.pop(i)
                lp_, ee_ = lp.nxt(), ee.nxt()
                for j in range(2):
                    hh = 2 * pr + j
                    k.act(V(lp_, lp_[:, j, :]), V(r_, r_[:, j * 128:(j + 1) * 128]), AF.Exp,
                          bias=V(smb, smb[:, kk, 32 + 16 * d + hh:32 + 16 * d + hh + 1]))
                    eb = V(zcol, zcol[0:64, 0:1]) if d == 0 else V(smb, smb[0:64, kk, 80 + hh:80 + hh + 1])
                    k.act(V(ee_, ee_[:, j, :]), V(r_, r_[0:64, 256 + j * 128:256 + (j + 1) * 128]), AF.Exp, bias=eb)
                g = pr // 4
                mt_, c_ = mt.nxt(), cp_.nxt()
                k.tt(V(mt_), V(lp_), V(gs_, gs_[:, g:g + 1, :].broadcast_to([128, 2, 128])), ALU.mult)
                k.tt(V(c_), V(ee_), V(bct, bct[:, 2 + g:3 + g, t0:t0 + 128].broadcast_to([64, 2, 128])), ALU.mult)
                hst = hfb_ if d == 0 else hbin
                for j in range(2):
                    hh = 2 * pr + j
                    ysl = y_[:, hh * 64:(hh + 1) * 64]
                    k.mm(V(y_, ysl), V(mt_, mt_[:, j, :]), V(xs_, xs_[:, hh * 64:(hh + 1) * 64]), start=False, stop=False)
                    hs_ap = hfb_[:, hh * 64:(hh + 1) * 64] if d == 0 else hbin[:, kk, hh * 64:(hh + 1) * 64]
                    k.mm(V(y_, ysl), V(c_, c_[:, j, :]), V(hst, hs_ap), start=False, stop=(d == 1 and hh in (7, 15)))
                if not (d == 1 and pr == 7):
                    continue
                s_ = load_state(kk, 0)
                decay_mul(hf, V(dall, dall[:, kk, 0:16]))
                k.tt(V(hf), V(hf), V(s_), ALU.add)
                a_ = y1.nxt()
                k.tt(V(a_), V(y_), V(z_), ALU.mult)
                s2, n_ = P["ss"].nxt(), yn.nxt()
                s3 = P["ss"].nxt()
                for g in range(2):
                    jk = y1j.nxt()
                    k.op("dve", lambda hh_, a_=a_, jk=jk, s2=s2, g=g: hh_.scalar_tensor_tensor(
                        out=jk[:, 0:512], in0=a_[:, g * 512:(g + 1) * 512], scalar=1.0 / 512.0, in1=a_[:, g * 512:(g + 1) * 512],
                        op0=ALU.mult, op1=ALU.mult, accum_out=s2[:, g:g + 1]), reads=[a_], writes=[jk, s2])
                k.act(V(s3), V(s2), AF.Ln, bias=V(eps))
                k.act(V(s3), V(s3), AF.Exp, scale=-0.5)
                for g in range(2):
                    k.ts(V(n_, n_[:, g * 512:(g + 1) * 512]), V(a_, a_[:, g * 512:(g + 1) * 512]), V(s3, s3[:, g:g + 1]), ALU.mult)
                t_ = yT.nxt()
                transpose_tile(k, P, n_, 128, t_, t_[:, :, :])
                k.dma(D_(sg["yt"].rearrange("(c p) t -> p c t", p=128)[:, :, t0:t0 + 128]), V(t_), eng="pool")
                del ctx[kk]
        k.barrier()


EXT = 128


def build_program(cfg, shapes, cst_arrays):
    nc = bass.Bass("TRN2", target_bir_lowering=False)
    T, NSEG, SP, LS, NCs = cfg.T, cfg.NSEG, cfg.SP, cfg.LS, cfg.NC
    NS = T + 2 * EXT
    NSP = ((NS + 511) // 512) * 512
    NG = LS // 128
    W = wviews(declare_weights(nc, shapes))
    C = {n: nc.dram_tensor(n, list(a.shape), F32 if a.dtype == np.float32 else BF16, kind="ExternalInput").ap()
         for n, a in cst_arrays.items()}
    x_own = nc.dram_tensor("x_own", [SP * T + NSP, D], F32, kind="ExternalInput").ap()
    xh_own = nc.dram_tensor("xh_own", [NSEG, 2, D], F32, kind="ExternalInput").ap()
    x_sg = nc.dram_tensor("x_sg", [LS, D], F32, kind="ExternalInput").ap()
    xh_sg = nc.dram_tensor("xh_sg", [NCs, 2, D], F32, kind="ExternalInput").ap()
    gmask = nc.dram_tensor("gmask", [64, 2, NG], F32, kind="ExternalInput").ap()
    vflag = nc.dram_tensor("vflag", [128, 2], F32, kind="ExternalInput").ap()
    y_own = nc.dram_tensor("y_own", [NSEG * T, D], F32, kind="ExternalOutput").ap()

    def scr(name, shape, dt):
        return nc.dram_tensor(name, list(shape), dt, kind="Internal").ap()
    seg_n = [T] * SP + [NS]
    seg_off = [s_ * T for s_ in range(SP)] + [SP * T]
    QT = [scr("QT%d" % s_, [H, 96, (NSP if s_ == SP else seg_n[s_])], BF16) for s_ in range(NSEG)]
    KT = scr("KT", [max(SP, 1), H, 96, T], BF16)
    VA = scr("VA", [max(SP, 1), T, H, 128], BF16)
    KTS = scr("KTS", [H, 96, LS], BF16)
    VAS = scr("VAS", [LS, H, 128], BF16)
    AT = [scr("AT%d" % s_, [D, seg_n[s_]], BF16) for s_ in range(NSEG)]
    YT = [scr("YT%d" % s_, [D, seg_n[s_]], BF16) for s_ in range(NSEG)]
    XS = [scr("XS%d" % s_, [seg_n[s_], D], BF16) for s_ in range(NSEG)]
    ZS = [scr("ZS%d" % s_, [seg_n[s_], D], BF16) for s_ in range(NSEG)]
    SST = [scr("SST%d" % s_, [seg_n[s_] // 128, 2, 64, D], F32) for s_ in range(NSEG)]
    SM = [scr("SM%d" % s_, [seg_n[s_] // 128, 128, 96], F32) for s_ in range(NSEG)]
    BCT = [scr("BCT%d" % s_, [4, 64, seg_n[s_]], BF16) for s_ in range(NSEG)]
    SSTG = scr("SSTG", [NG, 2, 64, D], F32)
    SMG = scr("SMG", [NG, 128, 96], F32)
    X1 = [scr("X1_%d" % s_, [seg_n[s_], D], F32) for s_ in range(NSEG)]
    H2T = [scr("H2T%d" % s_, [128, 8, seg_n[s_] + 2], BF16) for s_ in range(NSEG)]
    WG = scr("WG", [FC, 128, 8, 128], BF16)
    with ExitStack() as es:
        k = K(nc, es)
        xo = [x_own[seg_off[s_]:seg_off[s_] + seg_n[s_], :] for s_ in range(NSEG)]
        segs = []
        for s_ in range(SP):
            segs.append(dict(x=xo[s_], n=T, cos=C["cosp"], sin=C["sinp"], qt=QT[s_], kt=KT[s_], va=VA[s_]))
        segs.append(dict(x=x_own[SP * T:SP * T + NSP, :], n=NSP, cos=C["coso"], sin=C["sino"], qt=QT[SP], do_kv=False))
        for c in range(NCs):
            segs.append(dict(x=x_sg[c * T:(c + 1) * T, :], n=T, cos=C["cosg"][:, c * T:(c + 1) * T], sin=C["sing"][:, c * T:(c + 1) * T],
                             do_q=False, kt=KTS[:, :, c * T:(c + 1) * T], va=VAS[c * T:(c + 1) * T, :, :]))
        phase_p1a(k, cfg, W, C, segs)
        segc = [dict(x=xo[s_], xh=xh_own[s_], n=seg_n[s_], lite=False, xs=XS[s_], zs=ZS[s_], sst=SST[s_], sm=SM[s_], bct=BCT[s_])
                for s_ in range(NSEG)]
        for c in range(NCs):
            segc.append(dict(x=x_sg[c * T:(c + 1) * T, :], xh=xh_sg[c], n=T, lite=True,
                             sst=SSTG[c * (T // 128):(c + 1) * (T // 128)], sm=SMG[c * (T // 128):(c + 1) * (T // 128)]))
        phase_p1c(k, cfg, W, C, segc)
        segb = []
        for s_ in range(NSEG):
            segb.append(dict(n=seg_n[s_], xs=XS[s_], zs=ZS[s_], sst=SST[s_], sm=SM[s_], bct=BCT[s_], yt=YT[s_],
                             init=(s_ == SP), vflag=(vflag[0:64, :] if s_ == SP else None), nedge=EXT // 128))
        phase_p1b(k, cfg, W, C, segb, glob=dict(sst=SSTG, sm=SMG, mask=gmask, NG=NG))
        jobs = []
        for s_ in range(NSEG):
            for hh in range(H):
                if s_ < SP:
                    jobs.append(dict(qt=QT[s_][hh], kt=KT[s_, hh], va=VA[s_, :, hh, :], at=AT[s_][hh * 64:(hh + 1) * 64, :], Tq=T, Tk=T))
                else:
                    jobs.append(dict(qt=QT[s_][hh][:, 0:NS], kt=KTS[hh], va=VAS[:, hh, :], at=AT[s_][hh * 64:(hh + 1) * 64, :], Tq=NS, Tk=LS))
        phase_p2(k, cfg, jobs)
        segs3 = [dict(x=xo[s_], n=seg_n[s_], mix=[AT[s_], YT[s_]], x1=X1[s_], h2t=H2T[s_]) for s_ in range(NSEG)]
        phase_p3(k, cfg, W, C, segs3, nkc=16)
        segs4 = []
        for s_ in range(NSEG):
            if s_ < SP:
                segs4.append(dict(h2t=H2T[s_], x1=X1[s_], out=y_own[s_ * T:(s_ + 1) * T, :]))
            else:
                segs4.append(dict(h2t=H2T[s_][:, :, EXT:EXT + T + 2], x1=X1[s_][EXT:EXT + T, :], out=y_own[s_ * T:(s_ + 1) * T, :],
                                  vflag=vflag))
        phase_p4(k, cfg, W, C, WG, segs4)
        n_ops = k.n_ops
        k.emit()
    return nc, n_ops


def run_cfg(cfg, inputs, x_prompt, x_sample):
    T, SP, NCs, LS = cfg.T, cfg.SP, cfg.NC, cfg.LS
    NS = T + 2 * EXT
    NSP = ((NS + 511) // 512) * 512
    NG = LS // 128
    cst = host_consts()
    cst["cosp"], cst["sinp"] = rope_tables(np.arange(T))
    cst["cosg"], cst["sing"] = rope_tables(np.arange(LS))
    cst["coso"], cst["sino"] = rope_tables(np.arange(NSP))
    shapes = {n: inputs[n].shape for n in WNAMES}
    nc, n_ops = build_program(cfg, shapes, cst)
    xs32 = np.ascontiguousarray(x_sample, dtype=np.float32)
    xpad = np.zeros((LS + 2 * EXT + 2, D), np.float32)
    xpad[EXT + 1:EXT + 1 + LS] = xs32
    zero_row = np.zeros((D,), np.float32)
    xh_sg = np.stack([np.stack([xs32[c * T - 1] if c > 0 else zero_row, xs32[(c + 1) * T] if c < NCs - 1 else zero_row])
                      for c in range(NCs)])
    in_maps = []
    for c in range(NCs):
        m = {n: np.ascontiguousarray(inputs[n], dtype=np.float32) for n in WNAMES}
        m.update(cst)
        lo = c * T - EXT
        co, so = rope_tables(np.arange(lo, lo + NSP))
        m["coso"], m["sino"] = co, so
        parts = [x_prompt[c * SP + s_] for s_ in range(SP)] + [xpad[lo + EXT + 1:lo + EXT + 1 + NS], np.zeros((NSP - NS, D), np.float32)]
        m["x_own"] = np.ascontiguousarray(np.concatenate(parts, 0), dtype=np.float32)
        xh = np.zeros((SP + 1, 2, D), np.float32)
        xh[SP, 0] = xpad[lo + EXT]
        xh[SP, 1] = xpad[lo + EXT + 1 + NS]
        m["xh_own"] = xh
        m["x_sg"] = xs32
        m["xh_sg"] = xh_sg
        kk = np.arange(NG)
        gm = np.zeros((64, 2, NG), np.float32)
        gm[:, 0, :] = (kk < (lo // 128 if lo >= 0 else -((-lo) // 128)))[None, :]
        gm[:, 1, :] = (kk >= (lo + NS) // 128)[None, :]
        m["gmask"] = gm
        vf = np.ones((128, 2), np.float32)
        if c == 0:
            vf[:, 0] = 0.0
        if c == NCs - 1:
            vf[:, 1] = 0.0
        m["vflag"] = vf
        in_maps.append(m)
    res = run_bass_kernel_spmd(nc, in_maps, core_ids=list(range(NCs)))
    yp = np.zeros((NCs * SP, T, D), np.float32)
    ys = np.zeros((LS, D), np.float32)
    for c in range(NCs):
        y = res.results[c]["y_own"]
        for s_ in range(SP):
            yp[c * SP + s_] = y[s_ * T:(s_ + 1) * T]
        ys[c * T:(c + 1) * T] = y[SP * T:(SP + 1) * T]
    return yp, ys
```

```python
import os
from contextlib import ExitStack
import numpy as np
import ml_dtypes
import concourse.bass as bass
import concourse.mybir as mybir
from concourse.bass_utils import run_bass_kernel_spmd

F32 = mybir.dt.float32
BF16 = mybir.dt.bfloat16
AF = mybir.ActivationFunctionType
ALU = mybir.AluOpType

D = 1024
KC = 8
H = 16
QL, KVL, RO = 384, 256, 32
DSSM, DXBC, NST = 1024, 1280, 64
DIN = 3008
DFF = 2816
FC = 22
EPS = 1e-6
NEG = -30000.0
O_Q, O_CKV, O_KR, O_Z, O_XBC, O_DT = 0, 384, 640, 672, 1696, 2976
SCALE = 96.0 ** -0.5


class Buf:
    def __init__(self, name, t, is_dram=False):
        self.name = name
        self.t = t
        self.is_dram = is_dram
        self.w = None
        self.r = []
        self.dsem = None
        self.dcnt = 0

    def __getitem__(self, idx):
        return self.t[idx]


class K:
    ENG = ("pe", "act", "dve", "pool", "sp")

    def __init__(self, nc, es, n_dma_sems=46, n_sw_sems=50):
        self.nc = nc
        self.q = {e: [] for e in self.ENG}
        self.cnt = {e: 0 for e in self.ENG}
        self.waited = {e: {} for e in self.ENG}
        self.sem = {e: es.enter_context(nc.semaphore("s_" + e)) for e in ("pe", "act", "dve", "pool")}
        self.dma_pool = [es.enter_context(nc.semaphore("d%d" % i)) for i in range(n_dma_sems + n_sw_sems)]
        self.dma_free = list(range(n_dma_sems))
        self.sw_free = list(range(n_dma_sems, n_dma_sems + n_sw_sems))
        self.dma_val = [0] * (n_dma_sems + n_sw_sems)
        self.live_sw = []
        self.live = []
        self.n_ops = 0

    def sb(self, es, name, shape, dt):
        self.uid = getattr(self, "uid", 0) + 1
        name = "%s_u%d" % (name, self.uid)
        return Buf(name, es.enter_context(self.nc.sbuf_tensor(name, list(shape), dt)))

    def ps(self, es, name, shape, dt):
        self.uid = getattr(self, "uid", 0) + 1
        name = "%s_u%d" % (name, self.uid)
        b = Buf(name, es.enter_context(self.nc.psum_tensor(name, list(shape), dt)))
        b.is_psum = True
        return b

    def _need(self, eng, dep, out):
        kind, s, v = dep
        if kind == "pe" and eng == "pe":
            return
        key = (kind, s)
        if self.waited[eng].get(key, -1) >= v:
            return
        self.waited[eng][key] = v
        out.append(dep)

    def op(self, eng, fn, reads=(), writes=(), dma=False):
        deps = []
        for b in reads:
            if b.w is not None:
                self._need(eng, b.w, deps)
            if getattr(b, "is_psum", False):
                for r in b.r:
                    if r[0] != eng:
                        self._need(eng, r, deps)
        for b in writes:
            if b.w is not None and (dma or b.w[0] != eng):
                self._need(eng, b.w, deps)
            for r in b.r:
                if dma or r[0] != eng:
                    self._need(eng, r, deps)
        if dma:
            owner = None
            for b in list(writes) + list(reads):
                if not b.is_dram:
                    owner = b
                    break
            if owner is None:
                owner = (list(writes) + list(reads))[0]
            if eng == "pool":
                if getattr(owner, "swsem", None) is None:
                    owner.swsem = self.sw_free.pop()
                    owner.swcnt = 0
                    self.live_sw.append(owner)
                owner.swcnt += 16
                tok = ("dma", owner.swsem, owner.swcnt)
                semh, val = self.dma_pool[owner.swsem], 16
            else:
                if owner.dsem is None:
                    owner.dsem = self.dma_free.pop()
                    owner.dcnt = self.dma_val[owner.dsem]
                    self.live.append(owner)
                owner.dcnt += 16
                tok = ("dma", owner.dsem, owner.dcnt)
                semh, val = self.dma_pool[owner.dsem], 16
        else:
            self.cnt[eng] += 1
            tok = (eng, None, self.cnt[eng])
            semh, val = self.sem[eng], 1
        self.q[eng].append((deps, fn, semh, val))
        self.n_ops += 1
        for b in reads:
            b.r.append(tok)
            if len(b.r) > 64:
                b.r = b.r[-64:]
        for b in writes:
            b.w = tok
            b.r = []
        return tok

    def barrier(self):
        toks = [(e, None, self.cnt[e]) for e in ("pe", "act", "dve", "pool") if self.cnt[e]]
        for b in self.live:
            toks.append(("dma", b.dsem, b.dcnt))
        for b in self.live_sw:
            toks.append(("dma", b.swsem, b.swcnt))
        self.live_sw = []
        for e in self.ENG:
            deps = []
            for t in toks:
                if t[0] == e:
                    continue
                self._need(e, t, deps)
            if deps:
                self.q[e].append((deps, None, None, 0))
        for b in self.live:
            self.dma_val[b.dsem] = b.dcnt
            self.dma_free.append(b.dsem)
            b.dsem = None
        self.live = []

    def emit(self):
        with self.nc.Block() as block:
            def run(eng_name):
                def f(h):
                    for deps, fn, semh, val in self.q[eng_name]:
                        for kind, s, v in deps:
                            h.wait_ge(self.dma_pool[s] if kind == "dma" else self.sem[kind], v)
                        if fn is not None:
                            fn(h).then_inc(semh, val)
                return f
            block.tensor(run("pe"))
            block.scalar(run("act"))
            block.vector(run("dve"))
            block.gpsimd(run("pool"))
            block.sync(run("sp"))

    def dma(self, out, in_, eng="sp"):
        (ob, oa), (ib, ia) = out, in_
        return self.op(eng, lambda h: h.dma_start(out=oa, in_=ia), reads=[ib], writes=[ob], dma=True)

    def mm(self, out, lhsT, rhs, start=True, stop=True):
        (ob, oa), (lb, la), (rb, ra) = out, lhsT, rhs
        return self.op("pe", lambda h: h.matmul(oa, la, ra, start=start, stop=stop), reads=[lb, rb], writes=[ob])

    def tr(self, out, in_, ident):
        (ob, oa), (ib, ia), (db, da) = out, in_, ident
        return self.op("pe", lambda h: h.transpose(oa, ia, da), reads=[ib, db], writes=[ob])

    def act(self, out, in_, func, bias=None, scale=1.0, accum=None):
        (ob, oa), (ib, ia) = out, in_
        reads, writes = [ib], [ob]
        kw = {}
        if bias is not None:
            if isinstance(bias, tuple):
                reads.append(bias[0]); kw["bias"] = bias[1]
            else:
                kw["bias"] = bias
        if isinstance(scale, tuple):
            reads.append(scale[0]); kw["scale"] = scale[1]
        else:
            kw["scale"] = scale
        if accum is not None:
            writes.append(accum[0]); kw["accum_out"] = accum[1]
        return self.op("act", lambda h: h.activation(out=oa, in_=ia, func=func, **kw), reads=reads, writes=writes)

    def tt(self, out, in0, in1, op, eng="dve"):
        (ob, oa), (ab, aa), (bb, ba) = out, in0, in1
        return self.op(eng, lambda h: h.tensor_tensor(out=oa, in0=aa, in1=ba, op=op), reads=[ab, bb], writes=[ob])

    def ts(self, out, in0, s1, op0, s2=None, op1=None, eng="dve", accum=None):
        (ob, oa), (ab, aa) = out, in0
        reads, writes = [ab], [ob]
        if isinstance(s1, tuple):
            reads.append(s1[0]); s1 = s1[1]
        if isinstance(s2, tuple):
            reads.append(s2[0]); s2 = s2[1]
        kw = {}
        if op1 is not None:
            kw["op1"] = op1
        if accum is not None:
            writes.append(accum[0]); kw["accum_out"] = accum[1]
        return self.op(eng, lambda h: h.tensor_scalar(oa, aa, s1, s2, op0, **kw), reads=reads, writes=writes)

    def cp(self, out, in_, eng="dve"):
        (ob, oa), (ib, ia) = out, in_
        return self.op(eng, lambda h: h.tensor_copy(out=oa, in_=ia), reads=[ib], writes=[ob])

    def memset(self, out, val, eng="dve"):
        (ob, oa) = out
        return self.op(eng, lambda h: h.memset(oa, val), writes=[ob])

    def recip(self, out, in_):
        (ob, oa), (ib, ia) = out, in_
        return self.op("dve", lambda h: h.reciprocal(out=oa, in_=ia), reads=[ib], writes=[ob])


def V(buf, ap=None):
    return (buf, buf.t[:] if ap is None else ap)


class Cfg:
    def __init__(self, nc_cores=8, sp=4, t=2048):
        self.NC = nc_cores
        self.SP = sp
        self.T = t
        self.NSEG = sp + 1
        self.NCH = t // 128
        self.NB = t // 512
        self.LS = nc_cores * t


class Rot:
    def __init__(self, items):
        self.items = items
        self.i = 0

    def nxt(self):
        b = self.items[self.i % len(self.items)]
        self.i += 1
        return b


def D_(ap):
    return (None, ap)


def _dma(k, out, in_, eng="sp", slow=False):
    (ob, oa), (ib, ia) = out, in_
    reads = [ib] if ib is not None else []
    writes = [ob] if ob is not None else []
    if not reads and not writes:
        raise ValueError("dram->dram untracked")
    if slow:
        return k.op(eng, lambda h: h.dma_start(out=oa, in_=ia, allow_slow_non_contiguous=True), reads=reads, writes=writes, dma=True)
    return k.op(eng, lambda h: h.dma_start(out=oa, in_=ia), reads=reads, writes=writes, dma=True)


K.dma = _dma


def prep_weight(k, st, dst_fn, src, K_rows, cols, row_gain=None, col_gain=None):
    nk = K_rows // 128
    CB = 1408
    if row_gain is not None:
        rg = st["rg"].nxt()
        k.dma(V(rg, rg[:, 0:nk]), D_(row_gain.rearrange("(c p) -> p c", p=128)), slow=True)
    for c0 in range(0, cols, CB):
        cw = min(CB, cols - c0)
        if col_gain is not None:
            cg = st["cg"].nxt()
            k.dma(V(cg, cg[:, 0:cw]), D_(col_gain[c0:c0 + cw].partition_broadcast(128)))
        for kc in range(nk):
            s32 = st["s32"].nxt()
            k.dma(V(s32, s32[:, 0:cw]), D_(src[kc * 128:(kc + 1) * 128, c0:c0 + cw]))
            cur = V(s32, s32[:, 0:cw])
            if col_gain is not None:
                k.tt(cur, cur, V(cg, cg[:, 0:cw]), ALU.mult, eng="pool")
            db, da = dst_fn(kc, c0, cw)
            if db is None:
                sbf = st["sbf"].nxt()
                o = V(sbf, sbf[:, 0:cw])
            else:
                o = (db, da)
            if row_gain is not None:
                k.act(o, cur, AF.Copy, scale=V(rg, rg[:, kc:kc + 1]))
            else:
                k.act(o, cur, AF.Copy)
            if db is None:
                if len(da.shape) == 3:
                    o = (o[0], o[1].rearrange("p (m c) -> p m c", c=128))
                k.dma(D_(da), o, eng="pool")


def prep_stage(k, es):
    return {
        "s32": Rot([k.sb(es, "p0s32_%d" % i, [128, 1408], F32) for i in range(3)]),
        "sbf": Rot([k.sb(es, "p0sbf_%d" % i, [128, 1408], BF16) for i in range(3)]),
        "cg": Rot([k.sb(es, "p0cg_%d" % i, [128, 1408], F32) for i in range(2)]),
        "rg": Rot([k.sb(es, "p0rg_%d" % i, [128, 24], F32) for i in range(2)]),
    }


def load_w(k, buf, dram_ap):
    n = dram_ap.shape[1]
    step = max(1, n // 4)
    for c in range(0, n, step):
        e = min(n, c + step)
        k.dma(V(buf, buf[:, c:e]), D_(dram_ap[:, c:e]))


def rmsnorm_tile(k, P, xt, rows, hn, dim_scale):
    ss = P["ss"].nxt()
    junk = P["junk"].nxt()
    k.act(V(junk, junk[0:rows, :]), V(xt, xt[0:rows, :]), AF.Square, scale=dim_scale, accum=V(ss, ss[0:rows, 0:1]))
    k.act(V(ss, ss[0:rows, 1:2]), V(ss, ss[0:rows, 0:1]), AF.Sqrt, bias=V(P["eps"], P["eps"][0:rows, 0:1]))
    k.recip(V(ss, ss[0:rows, 1:2]), V(ss, ss[0:rows, 1:2]))
    k.ts(V(hn, hn[0:rows, :]), V(xt, xt[0:rows, :]), V(ss, ss[0:rows, 1:2]), ALU.mult)


def transpose_tile(k, P, hn, rows, dst_buf, dst_ap_fn):
    tp = P["tp"].nxt()
    for j in range(8):
        k.tr(V(tp, tp[:, j * 128:j * 128 + rows]), V(hn, hn[0:rows, j * 128:(j + 1) * 128]),
             V(P["ident"], P["ident"][0:rows, 0:rows]))
    src = tp[:, :].rearrange("p (j t) -> p j t", t=128)[:, :, 0:rows]
    k.cp(V(dst_buf, dst_ap_fn), V(tp, src))


def phase_p1a(k, cfg, W, C, segs):
    nc = k.nc
    with ExitStack() as es:
        Wq = k.sb(es, "Wq", [128, 8, QL], BF16)
        Wc = k.sb(es, "Wc", [128, 8, KVL], BF16)
        Wkr = k.sb(es, "Wkr", [128, 8, 96], BF16)
        Wks = k.sb(es, "Wks", [128, 8, 96], BF16)
        Wqb = k.sb(es, "Wqb", [128, 3, H * 96], BF16)
        Wqs = k.sb(es, "Wqs", [128, 3, H * 96], BF16)
        Wkb = k.sb(es, "Wkb", [128, 2, H * 64], BF16)
        Wvb = k.sb(es, "Wvb", [128, 2, H * 64], BF16)
        ident = k.sb(es, "ident_sb", [128, 128], BF16)
        onesb = k.sb(es, "ones_sb", [128, 128], BF16)
        eps_t = k.sb(es, "eps_sb", [128, 1], F32)
        es_prep = ExitStack()
        st = prep_stage(k, es_prep)
        k.dma(V(ident), D_(C["identb"]))
        k.dma(V(onesb), D_(C["onesb"]))
        n1 = W["norm1"]
        win = W["w_in"]
        prep_weight(k, st, lambda kc, c0, cw: (Wq, Wq[:, kc, c0:c0 + cw]), win[:, O_Q:O_Q + QL], D, QL, row_gain=n1)
        prep_weight(k, st, lambda kc, c0, cw: (Wc, Wc[:, kc, c0:c0 + cw]), win[:, O_CKV:O_CKV + KVL], D, KVL, row_gain=n1)
        k.memset(V(Wkr), 0.0)
        k.memset(V(Wks), 0.0)
        prep_weight(k, st, lambda kc, c0, cw: (Wkr, Wkr[:, kc, 64:96]), win[:, O_KR:O_KR + 32], D, 32, row_gain=n1)
        prep_weight(k, st, lambda kc, c0, cw: (Wks, Wks[:, kc, 64:80]), win[:, O_KR + 16:O_KR + 32], D, 16, row_gain=n1)
        prep_weight(k, st, lambda kc, c0, cw: (Wks, Wks[:, kc, 80:96]), win[:, O_KR:O_KR + 16], D, 16, row_gain=n1)
        prep_weight(k, st, lambda kc, c0, cw: (Wqb, Wqb[:, kc, c0:c0 + cw]), W["w_q_b"], QL, H * 96, row_gain=W["q_a_norm"])
        k.memset(V(Wqs), 0.0, eng="pool")
        wqb3 = W["w_q_b"].rearrange("k (h c) -> k h c", c=96)
        for hh in range(H):
            prep_weight(k, st, lambda kc, c0, cw, hh=hh: (Wqs, Wqs[:, kc, hh * 96 + 64:hh * 96 + 80]),
                        W["w_q_b"][:, hh * 96 + 80:hh * 96 + 96], QL, 16, row_gain=W["q_a_norm"])
            prep_weight(k, st, lambda kc, c0, cw, hh=hh: (Wqs, Wqs[:, kc, hh * 96 + 80:hh * 96 + 96]),
                        W["w_q_b"][:, hh * 96 + 64:hh * 96 + 80], QL, 16, row_gain=W["q_a_norm"])
        for hh in range(H):
            prep_weight(k, st, lambda kc, c0, cw, hh=hh: (Wkb, Wkb[:, kc, hh * 64:(hh + 1) * 64]),
                        W["w_kv_b"][:, hh * 128:hh * 128 + 64], KVL, 64, row_gain=W["kv_a_norm"])
            prep_weight(k, st, lambda kc, c0, cw, hh=hh: (Wvb, Wvb[:, kc, hh * 64:(hh + 1) * 64]),
                        W["w_kv_b"][:, hh * 128 + 64:hh * 128 + 128], KVL, 64, row_gain=W["kv_a_norm"])
        k.barrier()
        es_prep.close()

        P = {
            "ss": Rot([k.sb(es, "ss%d" % i, [128, 2], F32) for i in range(3)]),
            "junk": Rot([k.sb(es, "junk%d" % i, [128, D], BF16) for i in range(2)]),
            "tp": Rot([k.ps(es, "tp%d" % i, [128, D], BF16) for i in range(1)]),
            "ident": ident,
            "eps": eps_t,
        }
        k.memset(V(P["eps"]), EPS)
        xt = Rot([k.sb(es, "xt%d" % i, [128, D], F32) for i in range(9)])
        hn = Rot([k.sb(es, "hn%d" % i, [128, D], BF16) for i in range(2)])
        hT = Rot([k.sb(es, "hT%d" % i, [128, 8, 512], BF16) for i in range(2)])
        pb = Rot([k.ps(es, "pb%d" % i, [128, 512], F32) for i in range(7)])
        sq = Rot([k.sb(es, "sq%d" % i, [128, 512], BF16) for i in range(3)])
        rbc = Rot([k.sb(es, "rbc%d" % i, [128, 512], F32) for i in range(2)])
        qln = Rot([k.sb(es, "qln%d" % i, [128, 3, 512], BF16) for i in range(2)])
        ckn = Rot([k.sb(es, "ckn%d" % i, [128, 2, 512], BF16) for i in range(2)])
        cosb = Rot([k.sb(es, "cosb%d" % i, [96, 512], F32) for i in range(3)])
        sinb = Rot([k.sb(es, "sinb%d" % i, [96, 512], F32) for i in range(3)])
        t1 = Rot([k.sb(es, "t1_%d" % i, [96, 512], F32) for i in range(3)])
        t2 = Rot([k.sb(es, "t2_%d" % i, [96, 512], F32) for i in range(3)])
        qo = Rot([k.sb(es, "qo%d" % i, [96, H, 512], BF16) for i in range(1)])
        ko = Rot([k.sb(es, "ko%d" % i, [96, H, 512], BF16) for i in range(1)])
        krt = Rot([k.sb(es, "krt%d" % i, [96, 512], BF16) for i in range(2)])
        vo = Rot([k.sb(es, "vo%d" % i, [128, H, 128], BF16) for i in range(2)])
        for b_ in vo.items:
            k.memset(V(b_), 1.0, eng="pool")

        def fm_rmsnorm(ps_list, dim, dst, nchunk):
            sqs = []
            for m in range(nchunk):
                s_ = sq.nxt()
                k.act(V(s_), V(ps_list[m]), AF.Square, scale=float(dim) ** -0.5)
                sqs.append(s_)
            pss = pb.nxt()
            for m in range(nchunk):
                k.mm(V(pss), V(onesb), V(sqs[m]), start=(m == 0), stop=(m == nchunk - 1))
            r_ = rbc.nxt()
            k.act(V(r_), V(pss), AF.Sqrt, bias=V(P["eps"]))
            k.recip(V(r_), V(r_))
            for m in range(nchunk):
                k.tt(V(dst, dst[:, m, :]), V(ps_list[m]), V(r_), ALU.mult)

        blocks = [(sg, b) for sg in segs for b in range((sg.get("n", cfg.T) + 511) // 512)]

        loaded = {}

        def load_blk(bi):
            sg, b = blocks[bi]
            c0 = b * 512
            xs_l = []
            for ti in range(4):
                x_ = xt.nxt()
                r0 = c0 + ti * 128
                k.dma(V(x_), D_(sg["x"][r0:r0 + 128, :]))
                xs_l.append(x_)
            cs_, sn_ = cosb.nxt(), sinb.nxt()
            k.dma(V(cs_), D_(sg["cos"][:, c0:c0 + 512]))
            k.dma(V(sn_), D_(sg["sin"][:, c0:c0 + 512]))
            loaded[bi] = (xs_l, cs_, sn_)

        def build_hT(bi):
            if bi not in loaded:
                load_blk(bi)
            xs_l, cs_, sn_ = loaded.pop(bi)
            if bi + 1 < len(blocks) and (bi + 1) not in loaded:
                load_blk(bi + 1)
            h_ = hT.nxt()
            for ti in range(4):
                n_ = hn.nxt()
                rmsnorm_tile(k, P, xs_l[ti], 128, n_, 1.0 / 32.0)
                transpose_tile(k, P, n_, 128, h_, h_[:, :, ti * 128:(ti + 1) * 128])
            return h_, cs_, sn_

        nxt_blk = build_hT(0)
        for bi, (sg, b) in enumerate(blocks):
            if True:
                do_q, do_kv = sg.get("do_q", True), sg.get("do_kv", True)
                c0 = b * 512
                h_, cs_, sn_ = nxt_blk
                def rope_rows(pa, ps_, dst):
                    a_, b2 = t1.nxt(), t2.nxt()
                    k.tt(V(a_, a_[64:96, :]), V(pa, pa[64:96, :]), V(cs_, cs_[64:96, :]), ALU.mult)
                    k.tt(V(b2, b2[64:96, :]), V(ps_, ps_[64:96, :]), V(sn_, sn_[64:96, :]), ALU.mult)
                    k.tt(dst, V(a_, a_[64:96, :]), V(b2, b2[64:96, :]), ALU.add)

                if do_q:
                    pq = [pb.nxt() for _ in range(3)]
                    for m in range(3):
                        for kc in range(KC):
                            k.mm(V(pq[m]), V(Wq, Wq[:, kc, m * 128:(m + 1) * 128]), V(h_, h_[:, kc, :]),
                                 start=(kc == 0), stop=(kc == KC - 1))
                if do_kv:
                    pc = [pb.nxt() for _ in range(2)]
                    for m in range(2):
                        for kc in range(KC):
                            k.mm(V(pc[m]), V(Wc, Wc[:, kc, m * 128:(m + 1) * 128]), V(h_, h_[:, kc, :]),
                                 start=(kc == 0), stop=(kc == KC - 1))
                if bi + 1 < len(blocks):
                    nxt_blk = build_hT(bi + 1)
                if do_q:
                    ql = qln.nxt()
                    fm_rmsnorm(pq, QL, ql, 3)
                if do_kv:
                    pka, pks = pb.nxt(), pb.nxt()
                    for kc in range(KC):
                        k.mm(V(pka, pka[0:96, :]), V(Wkr, Wkr[:, kc, :]), V(h_, h_[:, kc, :]), start=(kc == 0), stop=(kc == KC - 1))
                    for kc in range(KC):
                        k.mm(V(pks, pks[0:96, :]), V(Wks, Wks[:, kc, :]), V(h_, h_[:, kc, :]), start=(kc == 0), stop=(kc == KC - 1))
                    cn = ckn.nxt()
                    fm_rmsnorm(pc, KVL, cn, 2)
                    kr = krt.nxt()
                    rope_rows(pka, pks, V(kr, kr[64:96, :]))

                q_ = qo.nxt() if do_q else None
                for hh in range(H if do_q else 0):
                    pa, ps_ = pb.nxt(), pb.nxt()
                    for m in range(3):
                        k.mm(V(pa, pa[0:96, :]), V(Wqb, Wqb[:, m, hh * 96:(hh + 1) * 96]), V(ql, ql[:, m, :]),
                             start=(m == 0), stop=(m == 2))
                    for m in range(3):
                        k.mm(V(ps_, ps_[0:96, :]), V(Wqs, Wqs[:, m, hh * 96:(hh + 1) * 96]), V(ql, ql[:, m, :]),
                             start=(m == 0), stop=(m == 2))
                    if hh % 2 == 0:
                        k.act(V(q_, q_[0:64, hh, :]), V(pa, pa[0:64, :]), AF.Copy)
                    else:
                        k.cp(V(q_, q_[0:64, hh, :]), V(pa, pa[0:64, :]))
                    rope_rows(pa, ps_, V(q_, q_[64:96, hh, :]))
                if do_q:
                    k.dma(D_(sg["qt"][:, :, c0:c0 + 512].rearrange("h p t -> p h t")), V(q_), eng="pool")
                if not do_kv:
                    continue
                k_ = ko.nxt()
                for hh in range(H):
                    pk = pb.nxt()
                    for m in range(2):
                        k.mm(V(pk, pk[0:64, :]), V(Wkb, Wkb[:, m, hh * 64:(hh + 1) * 64]), V(cn, cn[:, m, :]),
                             start=(m == 0), stop=(m == 1))
                    if hh % 2 == 0 and do_q:
                        k.act(V(k_, k_[0:64, hh, :]), V(pk, pk[0:64, :]), AF.Copy)
                    elif hh % 4 == 0:
                        k.act(V(k_, k_[0:64, hh, :]), V(pk, pk[0:64, :]), AF.Copy)
                    else:
                        k.cp(V(k_, k_[0:64, hh, :]), V(pk, pk[0:64, :]))
                k.dma(D_(sg["kt"][:, 0:64, c0:c0 + 512].rearrange("h p t -> p h t")), V(k_, k_[0:64, :, :]), eng="pool")
                k.dma(D_(sg["kt"][0, 64:96, c0:c0 + 512]), V(kr, kr[64:96, :]), eng="pool")
                for ti in range(4):
                    v_ = vo.nxt()
                    for half in range(2):
                        pv = pb.nxt()
                        for m in range(2):
                            k.mm(V(pv), V(cn, cn[:, m, ti * 128:(ti + 1) * 128]), V(Wvb, Wvb[:, m, half * 512:(half + 1) * 512]),
                                 start=(m == 0), stop=(m == 1))
                        if half == 0:
                            k.act(V(v_, v_[:, half * 8:half * 8 + 8, 0:64]),
                                  V(pv, pv[:, :].rearrange("p (j v) -> p j v", v=64)), AF.Copy)
                        else:
                            k.cp(V(v_, v_[:, half * 8:half * 8 + 8, 0:64]),
                                 V(pv, pv[:, :].rearrange("p (j v) -> p j v", v=64)))
                    r0 = c0 + ti * 128
                    k.dma(D_(sg["va"][r0:r0 + 128, :, :]), V(v_), eng="pool")
        k.barrier()


WNAMES = ["norm1", "w_in", "q_a_norm", "kv_a_norm", "w_q_b", "w_kv_b", "conv_w", "conv_b", "dt_bias_f", "dt_bias_b",
          "a_log_f", "a_log_b", "d_skip", "ssm_norm", "w_out", "norm2", "w_gate", "w_up", "ffn_conv_w", "ffn_conv_b",
          "w_down", "final_norm"]


def rope_tables(pos):
    pos = np.asarray(pos, dtype=np.float32)
    inv = (np.float32(10000.0) ** (-(np.arange(0, RO, 2, dtype=np.float32)) / np.float32(RO))).astype(np.float32)
    ang = (pos[:, None] * inv[None, :]).astype(np.float32)
    c, s = np.cos(ang).astype(np.float32).T, np.sin(ang).astype(np.float32).T
    cos = np.ones((96, len(pos)), np.float32)
    sin = np.zeros((96, len(pos)), np.float32)
    cos[64:80], cos[80:96] = c, c
    sin[64:80], sin[80:96] = -s, s
    return cos, sin


def host_consts():
    r = np.arange(128)
    cst = np.zeros((128, 7, 128), np.float32)
    cst[:, 0, :] = (r[:, None] <= r[None, :])
    cst[:, 1, :] = (r[:, None] < r[None, :])
    cst[:, 2, :] = 1.0
    cst[:, 3, :] = np.eye(128)
    cst[:, 4, :] = np.where(r[None, :] < r[:, None], NEG, 0.0)
    cst[:, 5, :] = np.where(r[None, :] > r[:, None], NEG, 0.0)
    cst[:, 6, :] = -(r[:, None] < r[None, :]).astype(np.float32)
    return {
        "identb": np.eye(128, dtype=np.float32).astype(ml_dtypes.bfloat16),
        "onesb": np.ones((128, 128), np.float32).astype(ml_dtypes.bfloat16),
        "cst32": cst,
    }


def declare_weights(nc, shapes):
    W = {}
    for n in WNAMES:
        shp = list(shapes[n])
        W[n] = nc.dram_tensor(n, shp, F32, kind="ExternalInput").ap()
    return W


def wviews(W):
    o = {}
    for n, ap in W.items():
        o[n] = ap if n == "final_norm" else ap[0]
    return o


def phase_p2(k, cfg, jobs):
    with ExitStack() as es:
        maxk = max(j["Tk"] for j in jobs)
        maxq = max(j["Tq"] for j in jobs)
        qb = Rot([k.sb(es, "aq%d" % i, [96, maxq], BF16) for i in range(2)])
        kb = Rot([k.sb(es, "ak%d" % i, [96, maxk], BF16) for i in range(2)])
        vb = Rot([k.sb(es, "av%d" % i, [128, maxk // 128, 128], BF16) for i in range(2)])
        pS = Rot([k.ps(es, "pS%d" % i, [128, 1024], F32) for i in range(3)])
        pO = Rot([k.ps(es, "pO%d" % i, [128, 512], F32) for i in range(2)])
        pt = Rot([k.sb(es, "apt%d" % i, [128, 1024], BF16) for i in range(3)])
        rc = Rot([k.sb(es, "arc%d" % i, [64, 512], F32) for i in range(2)])
        ao = Rot([k.sb(es, "aao%d" % i, [64, 512], BF16) for i in range(3)])

        def load(j):
            Tq, Tk = j["Tq"], j["Tk"]
            q_, k_, v_ = qb.nxt(), kb.nxt(), vb.nxt()
            k.dma(V(q_, q_[:, 0:Tq]), D_(j["qt"]))
            for c in range(0, Tk, 4096):
                e = min(Tk, c + 4096)
                if "kr" in j:
                    k.dma(V(k_, k_[0:64, c:e]), D_(j["kt"][0:64, c:e]))
                    k.dma(V(k_, k_[64:96, c:e]), D_(j["kr"][:, c:e]))
                else:
                    k.dma(V(k_, k_[:, c:e]), D_(j["kt"][:, c:e]))
            for c in range(0, Tk, 2048):
                e = min(Tk, c + 2048)
                k.dma(V(v_, v_[:, c // 128:e // 128, :]), D_(j["va"][c:e, :].rearrange("(t p) v -> p t v", p=128)))
            return q_, k_, v_

        its = []
        for ji, j in enumerate(jobs):
            for qi in range((j["Tq"] + 511) // 512):
                for kp in range(j["Tk"] // 256):
                    its.append((ji, qi, kp))
        bufs = {0: load(jobs[0])}
        sbuf = {}

        def emit_scores(i):
            ji, qi, kp = its[i]
            if ji not in bufs:
                bufs[ji] = load(jobs[ji])
            q_, k_, v_ = bufs[ji]
            qw = min(512, jobs[ji]["Tq"] - qi * 512)
            s_ = pS.nxt()
            for t in range(2):
                kt_ = 2 * kp + t
                k.mm(V(s_, s_[:, t * 512:t * 512 + qw]), V(k_, k_[:, kt_ * 128:(kt_ + 1) * 128]), V(q_, q_[:, qi * 512:qi * 512 + qw]))
            sbuf[i] = s_

        emit_scores(0)
        o_ = None
        for i, (ji, qi, kp) in enumerate(its):
            j = jobs[ji]
            if qi == 0 and kp == 0 and ji + 1 < len(jobs) and (ji + 1) not in bufs:
                bufs[ji + 1] = load(jobs[ji + 1])
            if i + 1 < len(its):
                emit_scores(i + 1)
            q_, k_, v_ = bufs[ji]
            nkp = j["Tk"] // 256
            if kp == 0:
                o_ = pO.nxt()
            s_ = sbuf.pop(i)
            p_ = pt.nxt()
            qw = min(512, j["Tq"] - qi * 512)
            k.act(V(p_, p_[:, :].rearrange("p (t c) -> p t c", c=512)[:, :, 0:qw]),
                  V(s_, s_[:, :].rearrange("p (t c) -> p t c", c=512)[:, :, 0:qw]), AF.Exp, scale=SCALE)
            for t in range(2):
                kt_ = 2 * kp + t
                k.mm(V(o_, o_[:, 0:qw]), V(v_, v_[:, kt_, :]), V(p_, p_[:, t * 512:t * 512 + qw]), start=(kt_ == 0), stop=(kt_ == 2 * nkp - 1))
            if kp == nkp - 1:
                r_ = rc.nxt()
                k.recip(V(r_, r_[:, 0:qw]), V(o_, o_[64:128, 0:qw]))
                a_ = ao.nxt()
                k.tt(V(a_, a_[:, 0:qw]), V(o_, o_[0:64, 0:qw]), V(r_, r_[:, 0:qw]), ALU.mult)
                k.dma(D_(j["at"][:, qi * 512:qi * 512 + qw]), V(a_, a_[:, 0:qw]), eng="pool")
                if qi == (j["Tq"] + 511) // 512 - 1:
                    bufs.pop(ji, None)
        k.barrier()


def phase_p3(k, cfg, W, C, segs, nkc=16):
    with ExitStack() as es:
        st = prep_stage(k, es)
        Wo = k.sb(es, "Wo", [128, nkc, D], BF16)
        ident = k.sb(es, "ident_sb3", [128, 128], BF16)
        k.dma(V(ident), D_(C["identb"]))
        prep_weight(k, st, lambda kc, c0, cw: (Wo, Wo[:, kc, c0:c0 + cw]), W["w_out"][0:1024, :], 1024, D)
        if nkc == 16:
            prep_weight(k, st, lambda kc, c0, cw: (Wo, Wo[:, 8 + kc, c0:c0 + cw]), W["w_out"][1024:2048, :], 1024, D,
                        row_gain=W["ssm_norm"])
        P = {
            "ss": Rot([k.sb(es, "ss3_%d" % i, [128, 2], F32) for i in range(3)]),
            "junk": Rot([k.sb(es, "junk3_%d" % i, [128, D], BF16) for i in range(2)]),
            "tp": Rot([k.ps(es, "tp3_%d" % i, [128, D], BF16) for i in range(2)]),
            "ident": ident,
            "eps": k.sb(es, "eps3", [128, 1], F32),
        }
        k.memset(V(P["eps"]), EPS)
        zt = k.sb(es, "zt3", [128, 8, 2], BF16)
        k.memset(V(zt), 0.0)
        mixT = Rot([k.sb(es, "mixT%d" % i, [128, nkc, 512], BF16) for i in range(2)])
        xt = Rot([k.sb(es, "xt3_%d" % i, [128, D], F32) for i in range(3)])
        x1 = Rot([k.sb(es, "x1_%d" % i, [128, D], F32) for i in range(3)])
        hn = Rot([k.sb(es, "hn3_%d" % i, [128, D], BF16) for i in range(2)])
        h2 = Rot([k.sb(es, "h2_%d" % i, [128, 8, 128], BF16) for i in range(3)])
        px = Rot([k.ps(es, "px%d" % i, [128, D], F32) for i in range(3)])
        for sg in segs:
            T = sg.get("n", cfg.T)
            k.dma(D_(sg["h2t"][:, :, 0:1]), V(zt, zt[:, :, 0:1]), eng="pool", slow=True)
            k.dma(D_(sg["h2t"][:, :, T + 1:T + 2]), V(zt, zt[:, :, 1:2]), eng="pool", slow=True)
            for c0 in range(0, T, 512):
                bwid = min(512, T - c0)
                m_ = mixT.nxt()
                for i, mx in enumerate(sg["mix"]):
                    k.dma(V(m_, m_[:, 8 * i:8 * i + 8, 0:bwid]), D_(mx.rearrange("(c p) t -> p c t", p=128)[:, :, c0:c0 + bwid]))
                for ti in range(bwid // 128):
                    r0 = c0 + ti * 128
                    x_ = xt.nxt()
                    k.dma(V(x_), D_(sg["x"][r0:r0 + 128, :]))
                    p_ = px.nxt()
                    for n in range(2):
                        for kc in range(nkc):
                            k.mm(V(p_, p_[:, n * 512:(n + 1) * 512]), V(m_, m_[:, kc, ti * 128:(ti + 1) * 128]),
                                 V(Wo, Wo[:, kc, n * 512:(n + 1) * 512]), start=(kc == 0), stop=(kc == nkc - 1))
                    y_ = x1.nxt()
                    k.tt(V(y_), V(p_), V(x_), ALU.add)
                    k.dma(D_(sg["x1"][r0:r0 + 128, :]), V(y_), eng="pool")
                    n_ = hn.nxt()
                    rmsnorm_tile(k, P, y_, 128, n_, 1.0 / 32.0)
                    h_ = h2.nxt()
                    transpose_tile(k, P, n_, 128, h_, h_[:, :, :])
                    k.dma(D_(sg["h2t"][:, :, 1 + r0:1 + r0 + 128]), V(h_), eng="pool")
        k.barrier()


FB = 510


def phase_p4(k, cfg, W, C, wg_scr, segs):
    T = cfg.T
    with ExitStack() as es:
        Wu = k.sb(es, "Wu", [128, 8, DFF], BF16)
        Wd = k.sb(es, "Wd", [128, FC, D], BF16)
        bg = k.sb(es, "bg", [128, FC], F32)
        cw3 = k.sb(es, "cw3", [128, 3, FC], F32)
        gain = k.sb(es, "fgain", [128, D], F32)
        eps = k.sb(es, "eps4", [128, 1], F32)
        es_prep = ExitStack()
        st = prep_stage(k, es_prep)
        n2 = W["norm2"]

        def dst(kc, c0, cw):
            return (None, wg_scr[c0 // 128:(c0 + cw) // 128, :, kc, :].rearrange("m p c -> p m c"))
        prep_weight(k, st, dst, W["w_gate"], D, DFF, row_gain=n2)
        prep_weight(k, st, lambda kc, c0, cw: (Wu, Wu[:, kc, c0:c0 + cw]), W["w_up"], D, DFF, row_gain=n2)
        prep_weight(k, st, lambda kc, c0, cw: (Wd, Wd[:, kc, c0:c0 + cw]), W["w_down"], DFF, D)
        k.dma(V(bg), D_(W["ffn_conv_b"].rearrange("(c p) -> p c", p=128)), slow=True)
        for tap in range(3):
            k.dma(V(cw3, cw3[:, tap, :]), D_(W["ffn_conv_w"][tap].rearrange("(c p) -> p c", p=128)), slow=True)
        k.dma(V(gain), D_(W["final_norm"].partition_broadcast(128)))
        k.memset(V(eps), EPS)
        k.barrier()
        es_prep.close()
        hT = Rot([k.sb(es, "h2T%d" % i, [128, 8, T + 2], BF16) for i in range(1)])
        vf4 = k.sb(es, "vf4", [128, 2], F32)
        wg = Rot([k.sb(es, "wg%d" % i, [128, 8, 128], BF16) for i in range(3)])
        pg = Rot([k.ps(es, "pg%d" % i, [128, 512], F32) for i in range(2)])
        pu = Rot([k.ps(es, "pu%d" % i, [128, 512], F32) for i in range(2)])
        pd = Rot([k.ps(es, "pd%d" % i, [128, D], F32) for i in range(2)])
        cv = Rot([k.sb(es, "cv%d" % i, [128, 512], F32) for i in range(3)])
        sg_ = Rot([k.sb(es, "sgl%d" % i, [128, 512], F32) for i in range(2)])
        aT = Rot([k.sb(es, "aT%d" % i, [128, FC, 512], BF16) for i in range(1)])
        x1 = Rot([k.sb(es, "x14_%d" % i, [128, D], F32) for i in range(2)])
        ss = Rot([k.sb(es, "ss4_%d" % i, [128, 2], F32) for i in range(3)])
        junk = Rot([k.sb(es, "junk4_%d" % i, [128, D], BF16) for i in range(2)])
        for sg in segs:
            h_ = hT.nxt()
            for c in range(0, T + 2, 1024):
                e = min(T + 2, c + 1024)
                k.dma(V(h_, h_[:, :, c:e]), D_(sg["h2t"][:, :, c:e]))
            if sg.get("vflag") is not None:
                k.dma(V(vf4), D_(sg["vflag"]))
                k.ts(V(h_, h_[:, :, 0:1]), V(h_, h_[:, :, 0:1]), V(vf4, vf4[:, 0:1]), ALU.mult)
                k.ts(V(h_, h_[:, :, T + 1:T + 2]), V(h_, h_[:, :, T + 1:T + 2]), V(vf4, vf4[:, 1:2]), ALU.mult)
            for v0 in range(0, T, FB):
                bw = min(FB, T - v0)
                a_ = aT.nxt()
                for m in range(FC):
                    w_ = wg.nxt()
                    k.dma(V(w_), D_(wg_scr[m]))
                    g_, u_ = pg.nxt(), pu.nxt()
                    for kc in range(KC):
                        k.mm(V(g_, g_[:, 0:bw + 2]), V(w_, w_[:, kc, :]), V(h_, h_[:, kc, v0:v0 + bw + 2]),
                             start=(kc == 0), stop=(kc == KC - 1))
                    for kc in range(KC):
                        k.mm(V(u_, u_[:, 0:bw]), V(Wu, Wu[:, kc, m * 128:(m + 1) * 128]), V(h_, h_[:, kc, v0 + 1:v0 + 1 + bw]),
                             start=(kc == 0), stop=(kc == KC - 1))
                    c_ = cv.nxt()
                    k.ts(V(c_, c_[:, 0:bw]), V(g_, g_[:, 0:bw]), V(cw3, cw3[:, 0, m:m + 1]), ALU.mult)
                    for tap in (1, 2):
                        k.op("dve", lambda hh, c_=c_, g_=g_, tap=tap, m=m, bw=bw: hh.scalar_tensor_tensor(
                            out=c_[:, 0:bw], in0=g_[:, tap:tap + bw], scalar=cw3[:, tap, m:m + 1], in1=c_[:, 0:bw],
                            op0=ALU.mult, op1=ALU.add), reads=[g_, cw3, c_], writes=[c_])
                    s_ = sg_.nxt()
                    k.act(V(s_, s_[:, 0:bw]), V(c_, c_[:, 0:bw]), AF.Silu, bias=V(bg, bg[:, m:m + 1]))
                    k.tt(V(a_, a_[:, m, 0:bw]), V(s_, s_[:, 0:bw]), V(u_, u_[:, 0:bw]), ALU.mult)
                for i0_ in range(0, bw, 128):
                    rows = min(128, bw - i0_)
                    r0 = v0 + i0_
                    x_ = x1.nxt()
                    k.dma(V(x_, x_[0:rows, :]), D_(sg["x1"][r0:r0 + rows, :]))
                    p_ = pd.nxt()
                    for n in range(2):
                        for m in range(FC):
                            k.mm(V(p_, p_[0:rows, n * 512:(n + 1) * 512]), V(a_, a_[:, m, i0_:i0_ + rows]),
                                 V(Wd, Wd[:, m, n * 512:(n + 1) * 512]), start=(m == 0), stop=(m == FC - 1))
                    k.tt(V(x_, x_[0:rows, :]), V(p_, p_[0:rows, :]), V(x_, x_[0:rows, :]), ALU.add)
                    s2, jk = ss.nxt(), junk.nxt()
                    k.act(V(jk, jk[0:rows, :]), V(x_, x_[0:rows, :]), AF.Square, scale=1.0 / 32.0, accum=V(s2, s2[0:rows, 0:1]))
                    k.act(V(s2, s2[0:rows, 1:2]), V(s2, s2[0:rows, 0:1]), AF.Sqrt, bias=V(eps, eps[0:rows, :]))
                    k.recip(V(s2, s2[0:rows, 1:2]), V(s2, s2[0:rows, 1:2]))
                    k.act(V(x_, x_[0:rows, :]), V(x_, x_[0:rows, :]), AF.Copy, scale=V(s2, s2[0:rows, 1:2]))
                    k.tt(V(x_, x_[0:rows, :]), V(x_, x_[0:rows, :]), V(gain, gain[0:rows, :]), ALU.mult, eng="pool")
                    k.dma(D_(sg["out"][r0:r0 + rows, :]), V(x_, x_[0:rows, :]), eng="pool")
        k.barrier()


def build_program(cfg, shapes, cst_arrays):
    nc = bass.Bass("TRN2", target_bir_lowering=False)
    T, NSEG, SP, LS, NCs = cfg.T, cfg.NSEG, cfg.SP, cfg.LS, cfg.NC
    W = wviews(declare_weights(nc, shapes))
    C = {n: nc.dram_tensor(n, list(a.shape), F32 if a.dtype == np.float32 else BF16, kind="ExternalInput").ap()
         for n, a in cst_arrays.items()}
    x_own = nc.dram_tensor("x_own", [NSEG * T, D], F32, kind="ExternalInput").ap()
    x_sg = nc.dram_tensor("x_sg", [LS, D], F32, kind="ExternalInput").ap()
    y_own = nc.dram_tensor("y_own", [NSEG * T, D], F32, kind="ExternalOutput").ap()

    def scr(name, shape, dt):
        return nc.dram_tensor(name, list(shape), dt, kind="Internal").ap()
    QT = scr("QT", [NSEG, H, 96, T], BF16)
    KT = scr("KT", [max(SP, 1), H, 96, T], BF16)
    VA = scr("VA", [max(SP, 1), T, H, 128], BF16)
    KTS = scr("KTS", [H, 96, LS], BF16)
    VAS = scr("VAS", [LS, H, 128], BF16)
    QTD = scr("QTD", [H, 96, T], BF16)
    KTD = scr("KTD", [H, 96, T], BF16)
    VAD = scr("VAD", [T, H, 128], BF16)
    AT = scr("AT", [NSEG, D, T], BF16)
    X1 = scr("X1", [NSEG * T, D], F32)
    H2T = scr("H2T", [NSEG, 128, 8, T + 2], BF16)
    WG = scr("WG", [FC, 128, 8, 128], BF16)
    with ExitStack() as es:
        k = K(nc, es)
        segs = []
        for s_ in range(SP):
            segs.append(dict(x=x_own[s_ * T:(s_ + 1) * T, :], cos=C["cosp"], sin=C["sinp"], qt=QT[s_], kt=KT[s_], va=VA[s_]))
        segs.append(dict(x=x_own[SP * T:(SP + 1) * T, :], cos=C["coso"], sin=C["sino"], qt=QT[SP], kt=KTD, va=VAD))
        for c in range(NCs):
            segs.append(dict(x=x_sg[c * T:(c + 1) * T, :], cos=C["cosg"][:, c * T:(c + 1) * T], sin=C["sing"][:, c * T:(c + 1) * T],
                             qt=QTD, kt=KTS[:, :, c * T:(c + 1) * T], va=VAS[c * T:(c + 1) * T, :, :]))
        phase_p1a(k, cfg, W, C, segs)
        jobs = []
        for s_ in range(NSEG):
            for hh in range(H):
                if s_ < SP:
                    jobs.append(dict(qt=QT[s_, hh], kt=KT[s_, hh], va=VA[s_, :, hh, :], at=AT[s_, hh * 64:(hh + 1) * 64, :], Tq=T, Tk=T))
                else:
                    jobs.append(dict(qt=QT[s_, hh], kt=KTS[hh], va=VAS[:, hh, :], at=AT[s_, hh * 64:(hh + 1) * 64, :], Tq=T, Tk=LS))
        phase_p2(k, cfg, jobs)
        segs3 = [dict(x=x_own[s_ * T:(s_ + 1) * T, :], mix=[AT[s_]], x1=X1[s_ * T:(s_ + 1) * T, :], h2t=H2T[s_]) for s_ in range(NSEG)]
        phase_p3(k, cfg, W, C, segs3, nkc=8)
        segs4 = [dict(h2t=H2T[s_], x1=X1[s_ * T:(s_ + 1) * T, :], out=y_own[s_ * T:(s_ + 1) * T, :]) for s_ in range(NSEG)]
        phase_p4(k, cfg, W, C, WG, segs4)
        n_ops = k.n_ops
        k.emit()
    return nc, n_ops


def run_cfg(cfg, inputs, x_prompt, x_sample):
    T, SP, NCs = cfg.T, cfg.SP, cfg.NC
    cst = host_consts()
    cst["cosp"], cst["sinp"] = rope_tables(np.arange(T))
    cst["cosg"], cst["sing"] = rope_tables(np.arange(cfg.LS))
    cst["coso"], cst["sino"] = rope_tables(np.arange(T))
    shapes = {n: inputs[n].shape for n in WNAMES}
    nc, n_ops = build_program(cfg, shapes, cst)
    in_maps = []
    for c in range(NCs):
        m = {n: np.ascontiguousarray(inputs[n], dtype=np.float32) for n in WNAMES}
        m.update(cst)
        co, so = rope_tables(np.arange(c * T, (c + 1) * T))
        m["coso"], m["sino"] = co, so
        parts = [x_prompt[c * SP + s_] for s_ in range(SP)] + [x_sample[c * T:(c + 1) * T]]
        m["x_own"] = np.ascontiguousarray(np.concatenate(parts, 0), dtype=np.float32)
        m["x_sg"] = np.ascontiguousarray(x_sample, dtype=np.float32)
        in_maps.append(m)
    res = run_bass_kernel_spmd(nc, in_maps, core_ids=list(range(NCs)))
    yp = np.zeros((NCs * SP, T, D), np.float32)
    ys = np.zeros((cfg.LS, D), np.float32)
    for c in range(NCs):
        y = res.results[c]["y_own"]
        for s_ in range(SP):
            yp[c * SP + s_] = y[s_ * T:(s_ + 1) * T]
        ys[c * T:(c + 1) * T] = y[SP * T:(SP + 1) * T]
    return yp, ys


def kernel(**inputs):
    inputs = {n: np.asarray(v) for n, v in inputs.items()}
    cfg = Cfg(8, 4, 2048)
    yp, ys = run_cfg(cfg, inputs, inputs["x_prompt"], inputs["x_sample"][0])
    return yp, ys[None]


def phase_p1c(k, cfg, W, C, segs):
    maxn = max(sg["n"] for sg in segs)
    with ExitStack() as es:
        Wx = k.sb(es, "Wx", [128, 8, 3, DXBC], BF16)
        Wz = k.sb(es, "Wz", [128, 8, DSSM], BF16)
        Wdt = k.sb(es, "Wdt", [128, 8, 32], BF16)
        ident = k.sb(es, "ident_c", [128, 128], BF16)
        onesb = k.sb(es, "ones_c", [128, 128], BF16)
        cst = k.sb(es, "cst_c", [128, 7, 128], F32)
        cbb = k.sb(es, "cbb", [1, DXBC], BF16)
        cb32 = k.sb(es, "cb32", [1, DXBC], F32)
        cbf = k.sb(es, "cbf", [64, 4], F32)
        dtb = k.sb(es, "dtb", [128, 32], F32)
        Abc = k.sb(es, "Abc", [128, 32], F32)
        eps = k.sb(es, "eps_c", [128, 1], F32)
        k.dma(V(ident), D_(C["identb"]))
        k.dma(V(onesb), D_(C["onesb"]))
        k.dma(V(cst), D_(C["cst32"]))
        k.dma(V(cb32), D_(W["conv_b"].rearrange("(o c) -> o c", o=1)))
        k.cp(V(cbb), V(cb32))
        k.dma(V(cbf), D_(W["conv_b"][1024:1280].rearrange("(i p) -> p i", p=64)), slow=True)
        k.dma(V(dtb, dtb[:, 0:16]), D_(W["dt_bias_f"].partition_broadcast(128)))
        k.dma(V(dtb, dtb[:, 16:32]), D_(W["dt_bias_b"].partition_broadcast(128)))
        k.dma(V(Abc, Abc[:, 0:16]), D_(W["a_log_f"].partition_broadcast(128)))
        k.dma(V(Abc, Abc[:, 16:32]), D_(W["a_log_b"].partition_broadcast(128)))
        k.act(V(Abc), V(Abc), AF.Exp)
        k.ts(V(Abc), V(Abc), -1.0, ALU.mult)
        k.memset(V(eps), EPS)
        es_prep = ExitStack()
        st = prep_stage(k, es_prep)
        n1, win = W["norm1"], W["w_in"]
        for tap in range(3):
            prep_weight(k, st, lambda kc, c0, cw, tap=tap: (Wx, Wx[:, kc, tap, c0:c0 + cw]), win[:, O_XBC:O_XBC + DXBC], D, DXBC,
                        row_gain=n1, col_gain=W["conv_w"][tap])
        prep_weight(k, st, lambda kc, c0, cw: (Wz, Wz[:, kc, c0:c0 + cw]), win[:, O_Z:O_Z + DSSM], D, DSSM, row_gain=n1)
        prep_weight(k, st, lambda kc, c0, cw: (Wdt, Wdt[:, kc, c0:c0 + cw]), win[:, O_DT:O_DT + 32], D, 32, row_gain=n1)
        k.barrier()
        es_prep.close()
        P = {
            "ss": Rot([k.sb(es, "ssc%d" % i, [128, 2], F32) for i in range(3)]),
            "junk": Rot([k.sb(es, "junkc%d" % i, [128, D], BF16) for i in range(2)]),
            "tp": Rot([k.ps(es, "tpc%d" % i, [128, D], BF16) for i in range(1)]),
            "ident": ident, "eps": eps,
        }
        hT = k.sb(es, "hTc", [128, 8, maxn + 2], BF16)
        xt = Rot([k.sb(es, "xtc%d" % i, [128, D], F32) for i in range(3)])
        hn = Rot([k.sb(es, "hnc%d" % i, [128, D], BF16) for i in range(2)])
        big = Rot([k.ps(es, "bigc%d" % i, [128, D], F32) for i in range(2)])
        sml = Rot([k.ps(es, "smlc%d" % i, [128, 512], F32) for i in range(3)])
        xsb = Rot([k.sb(es, "xsb%d" % i, [128, D], BF16) for i in range(4)])
        zsb = Rot([k.sb(es, "zsb%d" % i, [128, D], BF16) for i in range(2)])
        btk = Rot([k.sb(es, "btk%d" % i, [128, 128], BF16) for i in range(3)])
        dts = Rot([k.sb(es, "dts%d" % i, [128, 160], F32) for i in range(3)])
        smo = Rot([k.sb(es, "smo%d" % i, [128, 96], F32) for i in range(2)])
        bw = Rot([k.sb(es, "bw%d" % i, [128, H, 64], BF16) for i in range(6)])
        so = Rot([k.sb(es, "so%d" % i, [64, D], F32) for i in range(2)])
        bco = Rot([k.sb(es, "bco%d" % i, [64, 512], BF16) for i in range(3)])
        for sg in segs:
            n, lite = sg["n"], sg["lite"]
            nch = n // 128
            for ti in range(nch):
                x_ = xt.nxt()
                k.dma(V(x_), D_(sg["x"][ti * 128:(ti + 1) * 128, :]))
                n_ = hn.nxt()
                rmsnorm_tile(k, P, x_, 128, n_, 1.0 / 32.0)
                transpose_tile(k, P, n_, 128, hT, hT[:, :, 1 + ti * 128:1 + (ti + 1) * 128])
            x_ = xt.nxt()
            k.dma(V(x_, x_[0:2, :]), D_(sg["xh"]))
            n_ = hn.nxt()
            rmsnorm_tile(k, P, x_, 2, n_, 1.0 / 32.0)
            tp = P["tp"].nxt()
            for j in range(8):
                k.tr(V(tp, tp[:, j * 128:j * 128 + 2]), V(n_, n_[0:2, j * 128:(j + 1) * 128]), V(ident, ident[0:2, 0:2]))
            tpv = tp[:, :].rearrange("p (j t) -> p j t", t=128)
            k.cp(V(hT, hT[:, :, 0:1]), V(tp, tpv[:, :, 0:1]))
            k.cp(V(hT, hT[:, :, n + 1:n + 2]), V(tp, tpv[:, :, 1:2]))
            def proj(c):
                cb0 = 128 * c
                px = big.nxt()
                for nb in range(2):
                    i = 0
                    for tap in range(3):
                        for kc in range(KC):
                            k.mm(V(px, px[:, nb * 512:(nb + 1) * 512]), V(hT, hT[:, kc, cb0 + tap:cb0 + tap + 128]),
                                 V(Wx, Wx[:, kc, tap, nb * 512:(nb + 1) * 512]), start=(i == 0), stop=False)
                            i += 1
                    k.mm(V(px, px[:, nb * 512:(nb + 1) * 512]), V(onesb, onesb[0:1, 0:128]), V(cbb, cbb[0:1, nb * 512:(nb + 1) * 512]),
                         start=False, stop=True)
                xs_ = xsb.nxt()
                k.act(V(xs_), V(px), AF.Silu)
                if not lite:
                    k.dma(D_(sg["xs"][c * 128:(c + 1) * 128, :]), V(xs_), eng="pool")
                    pz = big.nxt()
                    for nb in range(2):
                        for kc in range(KC):
                            k.mm(V(pz, pz[:, nb * 512:(nb + 1) * 512]), V(hT, hT[:, kc, cb0 + 1:cb0 + 129]),
                                 V(Wz, Wz[:, kc, nb * 512:(nb + 1) * 512]), start=(kc == 0), stop=(kc == KC - 1))
                    z_ = zsb.nxt()
                    k.act(V(z_), V(pz), AF.Silu)
                    k.dma(D_(sg["zs"][c * 128:(c + 1) * 128, :]), V(z_), eng="pool")
                pm = sml.nxt()
                i = 0
                for tap in range(3):
                    for kc in range(KC):
                        k.mm(V(pm, pm[:, 0:128]), V(hT, hT[:, kc, cb0 + tap:cb0 + tap + 128]), V(Wx, Wx[:, kc, tap, 1024:1152]),
                             start=(i == 0), stop=False)
                        i += 1
                k.mm(V(pm, pm[:, 0:128]), V(onesb, onesb[0:1, 0:128]), V(cbb, cbb[0:1, 1024:1152]), start=False, stop=True)
                for kc in range(KC):
                    k.mm(V(pm, pm[:, 128:160]), V(hT, hT[:, kc, cb0 + 1:cb0 + 129]), V(Wdt, Wdt[:, kc, :]),
                         start=(kc == 0), stop=(kc == KC - 1))
                bt_ = btk.nxt()
                k.act(V(bt_), V(pm, pm[:, 0:128]), AF.Silu)
                d_ = dts.nxt()
                k.tt(V(d_, d_[:, 0:32]), V(pm, pm[:, 128:160]), V(dtb), ALU.add)
                return xs_, bt_, d_

            def rest(c, xs_, bt_, d_):
                k.act(V(d_, d_[:, 0:32]), V(d_, d_[:, 0:32]), AF.Exp)
                k.act(V(d_, d_[:, 0:32]), V(d_, d_[:, 0:32]), AF.Ln, bias=1.0)
                k.act(V(d_, d_[:, 32:64]), V(d_, d_[:, 0:32]), AF.Ln)
                sm_ = smo.nxt()
                k.tt(V(sm_, sm_[:, 0:32]), V(d_, d_[:, 0:32]), V(Abc), ALU.mult)
                pc = sml.nxt()
                k.mm(V(pc, pc[:, 0:16]), V(cst, cst[:, 0, :]), V(sm_, sm_[:, 0:16]))
                k.mm(V(pc, pc[:, 16:32]), V(cst, cst[:, 1, :]), V(sm_, sm_[:, 16:32]))
                k.mm(V(pc, pc[:, 32:64]), V(cst, cst[:, 2, :]), V(sm_, sm_[:, 0:32]))
                k.tt(V(sm_, sm_[:, 32:48]), V(d_, d_[:, 32:48]), V(pc, pc[:, 0:16]), ALU.subtract)
                k.tt(V(sm_, sm_[:, 48:64]), V(d_, d_[:, 48:64]), V(pc, pc[:, 16:32]), ALU.add)
                k.cp(V(sm_, sm_[:, 64:96]), V(pc, pc[:, 32:64]))
                k.tt(V(d_, d_[:, 96:112]), V(sm_, sm_[:, 64:80]), V(sm_, sm_[:, 32:48]), ALU.add)
                k.cp(V(d_, d_[:, 112:128]), V(sm_, sm_[:, 48:64]))
                k.act(V(d_, d_[:, 128:160]), V(d_, d_[:, 96:128]), AF.Exp)
                k.dma(D_(sg["sm"][c]), V(sm_), eng="pool")
                btv = bt_[:, :].rearrange("p (g n) -> p g n", g=2).unsqueeze(2).broadcast_to([128, 2, 8, 64])
                bws = []
                for d in range(2):
                    b_ = bw.nxt()
                    wv = d_[:, 128 + 16 * d:144 + 16 * d].rearrange("p (g j) -> p g j", g=2).unsqueeze(3).broadcast_to([128, 2, 8, 64])
                    k.tt(V(b_, b_[:, :, :].rearrange("p (g j) n -> p g j n", g=2)), V(bt_, btv), V(d_, wv), ALU.mult, eng="pool")
                    bws.append(b_)
                return bws

            def restB(c, xs_, bws):
                for d in range(2):
                    b_ = bws[d]
                    s_ = so.nxt()
                    for half in range(2):
                        pS = sml.nxt()
                        for j in range(8):
                            hh = half * 8 + j
                            k.mm(V(pS, pS[0:64, j * 64:(j + 1) * 64]), V(b_, b_[:, hh, :]), V(xs_, xs_[:, hh * 64:(hh + 1) * 64]))
                        k.cp(V(s_, s_[:, half * 512:(half + 1) * 512]), V(pS, pS[0:64, :]))
                    k.dma(D_(sg["sst"][c, d]), V(s_), eng="pool")

            nxt_h = proj(0)
            pend = None
            for c in range(nch):
                cur = nxt_h
                if c + 1 < nch:
                    nxt_h = proj(c + 1)
                bws = rest(c, *cur)
                if pend is not None:
                    restB(*pend)
                pend = (c, cur[0], bws)
            restB(*pend)
            if lite:
                continue
            for c0 in range(0, n, 512):
                bwid = min(512, n - c0)
                for idx in range(4):
                    pb_ = sml.nxt()
                    i = 0
                    for tap in range(3):
                        for kc in range(KC):
                            k.mm(V(pb_, pb_[0:64, 0:bwid]), V(Wx, Wx[:, kc, tap, 1024 + idx * 64:1088 + idx * 64]),
                                 V(hT, hT[:, kc, c0 + tap:c0 + tap + bwid]), start=(i == 0), stop=(i == 23))
                            i += 1
                    o_ = bco.nxt()
                    k.act(V(o_, o_[:, 0:bwid]), V(pb_, pb_[0:64, 0:bwid]), AF.Silu, bias=V(cbf, cbf[:, idx:idx + 1]))
                    k.dma(D_(sg["bct"][idx, :, c0:c0 + bwid]), V(o_, o_[:, 0:bwid]), eng="pool")
        k.barrier()


def phase_p1b(k, cfg, W, C, segs, glob=None):
    maxn = max(sg["n"] for sg in segs)
    maxc = maxn // 128
    with ExitStack() as es:
        ident = k.sb(es, "ident_b", [128, 128], BF16)
        cst = k.sb(es, "cst_b", [128, 7, 128], F32)
        dsk = k.sb(es, "dsk", [128, H], F32)
        eps = k.sb(es, "eps_b", [128, 1], F32)
        zcol = k.sb(es, "zcol", [128, 1], F32)
        k.dma(V(ident), D_(C["identb"]))
        k.dma(V(cst), D_(C["cst32"]))
        mskb = k.sb(es, "mskb", [128, 2, 128], BF16)
        k.cp(V(mskb), V(cst, cst[:, 4:6, :]))
        Tb = k.sb(es, "Tb", [128, 2, 128], BF16)
        k.cp(V(Tb, Tb[:, 0, :]), V(cst, cst[:, 0, :]))
        k.cp(V(Tb, Tb[:, 1, :]), V(cst, cst[:, 6, :]))
        ahi = k.sb(es, "ahi", [128, maxc, 32], BF16)
        alo = k.sb(es, "alo", [128, maxc, 32], BF16)
        atmp = k.sb(es, "atmp", [128, maxc, 32], F32)
        k.dma(V(dsk), D_(W["d_skip"].partition_broadcast(128)))
        k.memset(V(eps), EPS)
        k.memset(V(zcol), 0.0)
        P = {
            "ss": Rot([k.sb(es, "ssb%d" % i, [128, 2], F32) for i in range(3)]),
            "junk": Rot([k.sb(es, "junkb%d" % i, [128, D], BF16) for i in range(2)]),
            "tp": Rot([k.ps(es, "tpb%d" % i, [128, D], BF16) for i in range(1)]),
            "ident": ident, "eps": eps,
        }
        smb = k.sb(es, "smb", [128, maxc, 96], F32)
        bct = k.sb(es, "bctb", [64, 4, maxn], BF16)
        dall = k.sb(es, "dall", [64, maxc, 32], F32)
        hbin = k.sb(es, "hbin", [64, maxc, D], BF16)
        hb = k.sb(es, "hb", [64, D], F32)
        hf = k.sb(es, "hf", [64, D], F32)
        hfb = Rot([k.sb(es, "hfb%d" % i, [64, D], BF16) for i in range(2)])
        sld = Rot([k.sb(es, "sld%d" % i, [64, D], F32) for i in range(3)])
        sld2 = Rot([k.sb(es, "sld2_%d" % i, [64, D], F32) for i in range(3)])
        xsb = Rot([k.sb(es, "xsB%d" % i, [128, D], BF16) for i in range(2)])
        zsb = Rot([k.sb(es, "zsB%d" % i, [128, D], BF16) for i in range(2)])
        gs = Rot([k.sb(es, "gs%d" % i, [128, 2, 128], F32) for i in range(2)])
        lp = Rot([k.sb(es, "lp%d" % i, [128, 2, 128], F32) for i in range(4)])
        ee = Rot([k.sb(es, "ee%d" % i, [64, 2, 128], F32) for i in range(4)])
        mt = Rot([k.sb(es, "mt%d" % i, [128, 2, 128], BF16) for i in range(4)])
        cp_ = Rot([k.sb(es, "cpp%d" % i, [64, 2, 128], BF16) for i in range(4)])
        y1j = Rot([k.sb(es, "y1j%d" % i, [128, 512], F32) for i in range(2)])
        y1 = Rot([k.sb(es, "y1_%d" % i, [128, D], F32) for i in range(2)])
        xsd = Rot([k.sb(es, "xsd%d" % i, [128, D], BF16) for i in range(2)])
        yn = Rot([k.sb(es, "yn%d" % i, [128, D], BF16) for i in range(2)])
        yT = Rot([k.sb(es, "yT%d" % i, [128, 8, 128], BF16) for i in range(2)])
        gm = k.sb(es, "gmask", [64, 2, 128], F32)
        gsm = k.sb(es, "gsm", [64, 128, 32], F32)
        gd = Rot([k.sb(es, "gd%d" % i, [64, 16], F32) for i in range(6)])
        vfl = k.sb(es, "vfl", [64, 2], F32)
        pY = Rot([k.ps(es, "pY%d" % i, [128, D], F32) for i in range(1)])
        pR = Rot([k.ps(es, "pR%d" % i, [128, 512], F32) for i in range(4)])
        pG = Rot([k.ps(es, "pG%d" % i, [128, 256], F32) for i in range(1)])

        def decay_mul(h_, dv):
            k.tt(V(h_, h_[:, :].rearrange("p (h q) -> p h q", q=64)), V(h_, h_[:, :].rearrange("p (h q) -> p h q", q=64)),
                 (dv[0], dv[1].unsqueeze(2).broadcast_to([64, H, 64])), ALU.mult)

        for sg in segs:
            n = sg["n"]
            nch = n // 128
            if sg["init"]:
                NG = glob["NG"]
                k.dma(V(gm, gm[:, :, 0:NG]), D_(glob["mask"]))
                k.dma(V(gsm, gsm[:, 0:NG, :]), D_(glob["sm"][:, 0:64, 64:96].rearrange("c p f -> p c f")), slow=True)
                k.memset(V(hf), 0.0)
                k.memset(V(hb), 0.0, eng="pool")
                for step in range(NG):
                    for d, h_, eng_ in ((0, hf, "dve"), (1, hb, "dve")):
                        kk = step if d == 0 else NG - 1 - step
                        g_ = gd.nxt()
                        k.act(V(g_), V(gsm, gsm[:, kk, 16 * d:16 * d + 16]), AF.Exp, scale=V(gm, gm[:, d, kk:kk + 1]))
                        hv = h_[:, :].rearrange("p (h q) -> p h q", q=64)
                        k.tt(V(h_, hv), V(h_, hv), (g_, g_[:, :].unsqueeze(2).broadcast_to([64, H, 64])), ALU.mult, eng=eng_)
                        s_ = (sld if d == 0 else sld2).nxt()
                        k.dma(V(s_), D_(glob["sst"][kk, d]))
                        if eng_ == "dve":
                            k.op(eng_, lambda hh, s_=s_, h_=h_, d=d, kk=kk: hh.scalar_tensor_tensor(
                                out=h_[:, :], in0=s_[:, :], scalar=gm[:, d, kk:kk + 1], in1=h_[:, :], op0=ALU.mult, op1=ALU.add),
                                reads=[s_, gm, h_], writes=[h_])
                        else:
                            k.ts(V(s_), V(s_), V(gm, gm[:, d, kk:kk + 1]), ALU.mult, eng=eng_)
                            k.tt(V(h_), V(h_), V(s_), ALU.add, eng=eng_)
            else:
                k.memset(V(hf), 0.0)
                k.memset(V(hb), 0.0)
            if sg["vflag"] is not None:
                k.dma(V(vfl), D_(sg["vflag"]))
            k.dma(V(smb, smb[:, 0:nch, :]), D_(sg["sm"].rearrange("c p f -> p c f")))
            k.dma(V(bct, bct[:, :, 0:n]), D_(sg["bct"].rearrange("i p t -> p i t")))
            k.act(V(dall, dall[:, 0:nch, :]), V(smb, smb[0:64, 0:nch, 64:96]), AF.Exp)
            k.cp(V(ahi, ahi[:, 0:nch, :]), V(smb, smb[:, 0:nch, 0:32]))
            k.tt(V(atmp, atmp[:, 0:nch, :]), V(smb, smb[:, 0:nch, 0:32]), V(ahi, ahi[:, 0:nch, :]), ALU.subtract)
            k.cp(V(alo, alo[:, 0:nch, :]), V(atmp, atmp[:, 0:nch, :]))

            def load_state(kk, d):
                s_ = sld.nxt()
                k.dma(V(s_), D_(sg["sst"][kk, d]))
                ne = sg.get("nedge", 1)
                if sg["vflag"] is not None and (kk < ne or kk >= nch - ne):
                    col = 0 if kk < ne else 1
                    k.ts(V(s_), V(s_), V(vfl, vfl[:, col:col + 1]), ALU.mult)
                return s_

            for kk in range(nch - 1, -1, -1):
                k.act(V(hbin, hbin[:, kk, :]), V(hb), AF.Copy)
                if kk > 0:
                    s_ = load_state(kk, 1)
                    decay_mul(hb, V(dall, dall[:, kk, 16:32]))
                    k.tt(V(hb), V(hb), V(s_), ALU.add)
            its = [(kk, d, pr) for kk in range(nch) for d in range(2) for pr in range(8)]
            ctx = {}
            rbuf = {}

            def emit_R(i):
                kk, d, pr = its[i]
                t0 = kk * 128
                if d == 0 and pr == 0:
                    xs_, z_ = xsb.nxt(), zsb.nxt()
                    k.dma(V(xs_), D_(sg["xs"][t0:t0 + 128, :]))
                    k.dma(V(z_), D_(sg["zs"][t0:t0 + 128, :]))
                    g_ = pG.nxt()
                    for g in range(2):
                        k.mm(V(g_, g_[:, g * 128:(g + 1) * 128]), V(bct, bct[:, g, t0:t0 + 128]), V(bct, bct[:, 2 + g, t0:t0 + 128]))
                    gs_ = gs.nxt()
                    k.cp(V(gs_), V(g_, g_[:, :].rearrange("p (g t) -> p g t", g=2)))
                    xd_ = xsd.nxt()
                    k.tt(V(xd_, xd_[:, :].rearrange("p (h q) -> p h q", q=64)), V(xs_, xs_[:, :].rearrange("p (h q) -> p h q", q=64)),
                         V(dsk, dsk[:, :].unsqueeze(2).broadcast_to([128, H, 64])), ALU.mult, eng="pool")
                    ctx[kk] = dict(xs=xs_, z=z_, gs=gs_, xd=xd_)
                r_ = pR.nxt()
                for j in range(2):
                    hh = 2 * pr + j
                    hcol = ahi[:, kk, 16 * d + hh:16 * d + hh + 1].broadcast_to([128, 128])
                    lcol = alo[:, kk, 16 * d + hh:16 * d + hh + 1].broadcast_to([128, 128])
                    rm = r_[:, j * 128:(j + 1) * 128]
                    ru = r_[:, 256 + j * 128:256 + (j + 1) * 128]
                    k.mm(V(r_, rm), V(ident), V(mskb, mskb[:, d, :]), start=True, stop=False)
                    k.mm(V(r_, rm), V(ahi, hcol), V(Tb, Tb[:, d, :]), start=False, stop=False)
                    k.mm(V(r_, rm), V(alo, lcol), V(Tb, Tb[:, d, :]), start=False, stop=True)
                    k.mm(V(r_, ru), V(ahi, hcol), V(Tb, Tb[:, d, :]), start=True, stop=False)
                    k.mm(V(r_, ru), V(alo, lcol), V(Tb, Tb[:, d, :]), start=False, stop=True)
                rbuf[i] = r_

            emit_R(0)
            if len(its) > 1:
                emit_R(1)
            for i, (kk, d, pr) in enumerate(its):
                t0 = kk * 128
                if i + 2 < len(its):
                    emit_R(i + 2)
                cx = ctx[kk]
                xs_, z_, gs_ = cx["xs"], cx["z"], cx["gs"]
                if d == 0 and pr == 0:
                    hfb_ = hfb.nxt()
                    k.cp(V(hfb_), V(hf))
                    y_ = pY.nxt()
                    for nb in range(2):
                        k.mm(V(y_, y_[:, nb * 512:(nb + 1) * 512]), V(ident), V(cx["xd"], cx["xd"][:, nb * 512:(nb + 1) * 512]), start=True, stop=False)
                    cx["hfb"], cx["y"] = hfb_, y_
                hfb_, y_ = cx["hfb"], cx["y"]
                r_ = rbuf.pop(i)
                lp_, ee_ = lp.nxt(), ee.nxt()
                for j in range(2):
                    hh = 2 * pr + j
                    k.act(V(lp_, lp_[:, j, :]), V(r_, r_[:, j * 128:(j + 1) * 128]), AF.Exp,
                          bias=V(smb, smb[:, kk, 32 + 16 * d + hh:32 + 16 * d + hh + 1]))
                    eb = V(zcol, zcol[0:64, 0:1]) if d == 0 else V(smb, smb[0:64, kk, 80 + hh:80 + hh + 1])
                    k.act(V(ee_, ee_[:, j, :]), V(r_, r_[0:64, 256 + j * 128:256 + (j + 1) * 128]), AF.Exp, bias=eb)
                g = pr // 4
                mt_, c_ = mt.nxt(), cp_.nxt()
                k.tt(V(mt_), V(lp_), V(gs_, gs_[:, g:g + 1, :].broadcast_to([128, 2, 128])), ALU.mult)
                k.tt(V(c_), V(ee_), V(bct, bct[:, 2 + g:3 + g, t0:t0 + 128].broadcast_to([64, 2, 128])), ALU.mult)
                hst = hfb_ if d == 0 else hbin
                for j in range(2):
                    hh = 2 * pr + j
                    ysl = y_[:, hh * 64:(hh + 1) * 64]
                    k.mm(V(y_, ysl), V(mt_, mt_[:, j, :]), V(xs_, xs_[:, hh * 64:(hh + 1) * 64]), start=False, stop=False)
                    hs_ap = hfb_[:, hh * 64:(hh + 1) * 64] if d == 0 else hbin[:, kk, hh * 64:(hh + 1) * 64]
                    k.mm(V(y_, ysl), V(c_, c_[:, j, :]), V(hst, hs_ap), start=False, stop=(d == 1 and hh in (7, 15)))
                if not (d == 1 and pr == 7):
                    continue
                s_ = load_state(kk, 0)
                decay_mul(hf, V(dall, dall[:, kk, 0:16]))
                k.tt(V(hf), V(hf), V(s_), ALU.add)
                a_ = y1.nxt()
                k.tt(V(a_), V(y_), V(z_), ALU.mult)
                s2, n_ = P["ss"].nxt(), yn.nxt()
                s3 = P["ss"].nxt()
                for g in range(2):
                    jk = y1j.nxt()
                    k.op("dve", lambda hh_, a_=a_, jk=jk, s2=s2, g=g: hh_.scalar_tensor_tensor(
                        out=jk[:, 0:512], in0=a_[:, g * 512:(g + 1) * 512], scalar=1.0 / 512.0, in1=a_[:, g * 512:(g + 1) * 512],
                        op0=ALU.mult, op1=ALU.mult, accum_out=s2[:, g:g + 1]), reads=[a_], writes=[jk, s2])
                k.act(V(s3), V(s2), AF.Ln, bias=V(eps))
                k.act(V(s3), V(s3), AF.Exp, scale=-0.5)
                for g in range(2):
                    k.ts(V(n_, n_[:, g * 512:(g + 1) * 512]), V(a_, a_[:, g * 512:(g + 1) * 512]), V(s3, s3[:, g:g + 1]), ALU.mult)
                t_ = yT.nxt()
                transpose_tile(k, P, n_, 128, t_, t_[:, :, :])
                k.dma(D_(sg["yt"].rearrange("(c p) t -> p c t", p=128)[:, :, t0:t0 + 128]), V(t_), eng="pool")
                del ctx[kk]
        k.barrier()


EXT = 128


def build_program(cfg, shapes, cst_arrays):
    nc = bass.Bass("TRN2", target_bir_lowering=False)
    T, NSEG, SP, LS, NCs = cfg.T, cfg.NSEG, cfg.SP, cfg.LS, cfg.NC
    NS = T + 2 * EXT
    NSP = ((NS + 511) // 512) * 512
    NG = LS // 128
    W = wviews(declare_weights(nc, shapes))
    C = {n: nc.dram_tensor(n, list(a.shape), F32 if a.dtype == np.float32 else BF16, kind="ExternalInput").ap()
         for n, a in cst_arrays.items()}
    x_own = nc.dram_tensor("x_own", [SP * T + NSP, D], F32, kind="ExternalInput").ap()
    xh_own = nc.dram_tensor("xh_own", [NSEG, 2, D], F32, kind="ExternalInput").ap()
    x_sg = nc.dram_tensor("x_sg", [LS, D], F32, kind="ExternalInput").ap()
    xh_sg = nc.dram_tensor("xh_sg", [NCs, 2, D], F32, kind="ExternalInput").ap()
    gmask = nc.dram_tensor("gmask", [64, 2, NG], F32, kind="ExternalInput").ap()
    vflag = nc.dram_tensor("vflag", [128, 2], F32, kind="ExternalInput").ap()
    y_own = nc.dram_tensor("y_own", [NSEG * T, D], F32, kind="ExternalOutput").ap()

    def scr(name, shape, dt):
        return nc.dram_tensor(name, list(shape), dt, kind="Internal").ap()
    seg_n = [T] * SP + [NS]
    seg_off = [s_ * T for s_ in range(SP)] + [SP * T]
    QT = [scr("QT%d" % s_, [H, 96, (NSP if s_ == SP else seg_n[s_])], BF16) for s_ in range(NSEG)]
    KT = scr("KT", [max(SP, 1), H, 96, T], BF16)
    VA = scr("VA", [max(SP, 1), T, H, 128], BF16)
    KTS = scr("KTS", [H, 96, LS], BF16)
    VAS = scr("VAS", [LS, H, 128], BF16)
    AT = [scr("AT%d" % s_, [D, seg_n[s_]], BF16) for s_ in range(NSEG)]
    YT = [scr("YT%d" % s_, [D, seg_n[s_]], BF16) for s_ in range(NSEG)]
    XS = [scr("XS%d" % s_, [seg_n[s_], D], BF16) for s_ in range(NSEG)]
    ZS = [scr("ZS%d" % s_, [seg_n[s_], D], BF16) for s_ in range(NSEG)]
    SST = [scr("SST%d" % s_, [seg_n[s_] // 128, 2, 64, D], F32) for s_ in range(NSEG)]
    SM = [scr("SM%d" % s_, [seg_n[s_] // 128, 128, 96], F32) for s_ in range(NSEG)]
    BCT = [scr("BCT%d" % s_, [4, 64, seg_n[s_]], BF16) for s_ in range(NSEG)]
    SSTG = scr("SSTG", [NG, 2, 64, D], F32)
    SMG = scr("SMG", [NG, 128, 96], F32)
    X1 = [scr("X1_%d" % s_, [seg_n[s_], D], F32) for s_ in range(NSEG)]
    H2T = [scr("H2T%d" % s_, [128, 8, seg_n[s_] + 2], BF16) for s_ in range(NSEG)]
    WG = scr("WG", [FC, 128, 8, 128], BF16)
    with ExitStack() as es:
        k = K(nc, es)
        xo = [x_own[seg_off[s_]:seg_off[s_] + seg_n[s_], :] for s_ in range(NSEG)]
        segs = []
        for s_ in range(SP):
            segs.append(dict(x=xo[s_], n=T, cos=C["cosp"], sin=C["sinp"], qt=QT[s_], kt=KT[s_], va=VA[s_]))
        segs.append(dict(x=x_own[SP * T:SP * T + NSP, :], n=NSP, cos=C["coso"], sin=C["sino"], qt=QT[SP], do_kv=False))
        for c in range(NCs):
            segs.append(dict(x=x_sg[c * T:(c + 1) * T, :], n=T, cos=C["cosg"][:, c * T:(c + 1) * T], sin=C["sing"][:, c * T:(c + 1) * T],
                             do_q=False, kt=KTS[:, :, c * T:(c + 1) * T], va=VAS[c * T:(c + 1) * T, :, :]))
        phase_p1a(k, cfg, W, C, segs)
        segc = [dict(x=xo[s_], xh=xh_own[s_], n=seg_n[s_], lite=False, xs=XS[s_], zs=ZS[s_], sst=SST[s_], sm=SM[s_], bct=BCT[s_])
                for s_ in range(NSEG)]
        for c in range(NCs):
            segc.append(dict(x=x_sg[c * T:(c + 1) * T, :], xh=xh_sg[c], n=T, lite=True,
                             sst=SSTG[c * (T // 128):(c + 1) * (T // 128)], sm=SMG[c * (T // 128):(c + 1) * (T // 128)]))
        phase_p1c(k, cfg, W, C, segc)
        segb = []
        for s_ in range(NSEG):
            segb.append(dict(n=seg_n[s_], xs=XS[s_], zs=ZS[s_], sst=SST[s_], sm=SM[s_], bct=BCT[s_], yt=YT[s_],
                             init=(s_ == SP), vflag=(vflag[0:64, :] if s_ == SP else None), nedge=EXT // 128))
        phase_p1b(k, cfg, W, C, segb, glob=dict(sst=SSTG, sm=SMG, mask=gmask, NG=NG))
        jobs = []
        for s_ in range(NSEG):
            for hh in range(H):
                if s_ < SP:
                    jobs.append(dict(qt=QT[s_][hh], kt=KT[s_, hh], kr=KT[s_, 0, 64:96, :], va=VA[s_, :, hh, :], at=AT[s_][hh * 64:(hh + 1) * 64, :], Tq=T, Tk=T))
                else:
                    jobs.append(dict(qt=QT[s_][hh][:, 0:NS], kt=KTS[hh], kr=KTS[0, 64:96, :], va=VAS[:, hh, :], at=AT[s_][hh * 64:(hh + 1) * 64, :], Tq=NS, Tk=LS))
        phase_p2(k, cfg, jobs)
        segs3 = [dict(x=xo[s_], n=seg_n[s_], mix=[AT[s_], YT[s_]], x1=X1[s_], h2t=H2T[s_]) for s_ in range(NSEG)]
        phase_p3(k, cfg, W, C, segs3, nkc=16)
        segs4 = []
        for s_ in range(NSEG):
            if s_ < SP:
                segs4.append(dict(h2t=H2T[s_], x1=X1[s_], out=y_own[s_ * T:(s_ + 1) * T, :]))
            else:
                segs4.append(dict(h2t=H2T[s_][:, :, EXT:EXT + T + 2], x1=X1[s_][EXT:EXT + T, :], out=y_own[s_ * T:(s_ + 1) * T, :],
                                  vflag=vflag))
        phase_p4(k, cfg, W, C, WG, segs4)
        n_ops = k.n_ops
        k.emit()
    return nc, n_ops


def run_cfg(cfg, inputs, x_prompt, x_sample):
    T, SP, NCs, LS = cfg.T, cfg.SP, cfg.NC, cfg.LS
    NS = T + 2 * EXT
    NSP = ((NS + 511) // 512) * 512
    NG = LS // 128
    cst = host_consts()
    cst["cosp"], cst["sinp"] = rope_tables(np.arange(T))
    cst["cosg"], cst["sing"] = rope_tables(np.arange(LS))
    cst["coso"], cst["sino"] = rope_tables(np.arange(NSP))
    shapes = {n: inputs[n].shape for n in WNAMES}
    nc, n_ops = build_program(cfg, shapes, cst)
    xs32 = np.ascontiguousarray(x_sample, dtype=np.float32)
    xpad = np.zeros((LS + 2 * EXT + 2, D), np.float32)
    xpad[EXT + 1:EXT + 1 + LS] = xs32
    zero_row = np.zeros((D,), np.float32)
    xh_sg = np.stack([np.stack([xs32[c * T - 1] if c > 0 else zero_row, xs32[(c + 1) * T] if c < NCs - 1 else zero_row])
                      for c in range(NCs)])
    in_maps = []
    for c in range(NCs):
        m = {n: np.ascontiguousarray(inputs[n], dtype=np.float32) for n in WNAMES}
        m.update(cst)
        lo = c * T - EXT
        co, so = rope_tables(np.arange(lo, lo + NSP))
        m["coso"], m["sino"] = co, so
        parts = [x_prompt[c * SP + s_] for s_ in range(SP)] + [xpad[lo + EXT + 1:lo + EXT + 1 + NS], np.zeros((NSP - NS, D), np.float32)]
        m["x_own"] = np.ascontiguousarray(np.concatenate(parts, 0), dtype=np.float32)
        xh = np.zeros((SP + 1, 2, D), np.float32)
        xh[SP, 0] = xpad[lo + EXT]
        xh[SP, 1] = xpad[lo + EXT + 1 + NS]
        m["xh_own"] = xh
        m["x_sg"] = xs32
        m["xh_sg"] = xh_sg
        kk = np.arange(NG)
        gm = np.zeros((64, 2, NG), np.float32)
        gm[:, 0, :] = (kk < (lo // 128 if lo >= 0 else -((-lo) // 128)))[None, :]
        gm[:, 1, :] = (kk >= (lo + NS) // 128)[None, :]
        m["gmask"] = gm
        vf = np.ones((128, 2), np.float32)
        if c == 0:
            vf[:, 0] = 0.0
        if c == NCs - 1:
            vf[:, 1] = 0.0
        m["vflag"] = vf
        in_maps.append(m)
    res = run_bass_kernel_spmd(nc, in_maps, core_ids=list(range(NCs)))
    yp = np.zeros((NCs * SP, T, D), np.float32)
    ys = np.zeros((LS, D), np.float32)
    for c in range(NCs):
        y = res.results[c]["y_own"]
        for s_ in range(SP):
            yp[c * SP + s_] = y[s_ * T:(s_ + 1) * T]
        ys[c * T:(c + 1) * T] = y[SP * T:(SP + 1) * T]
    return yp, ys
```

```python
import os
from contextlib import ExitStack
import numpy as np
import ml_dtypes
import concourse.bass as bass
import concourse.mybir as mybir
from concourse.bass_utils import run_bass_kernel_spmd

F32 = mybir.dt.float32
BF16 = mybir.dt.bfloat16
AF = mybir.ActivationFunctionType
ALU = mybir.AluOpType

D = 1024
KC = 8
H = 16
QL, KVL, RO = 384, 256, 32
DSSM, DXBC, NST = 1024, 1280, 64
DIN = 3008
DFF = 2816
FC = 22
EPS = 1e-6
NEG = -30000.0
O_Q, O_CKV, O_KR, O_Z, O_XBC, O_DT = 0, 384, 640, 672, 1696, 2976
SCALE = 96.0 ** -0.5


class Buf:
    def __init__(self, name, t, is_dram=False):
        self.name = name
        self.t = t
        self.is_dram = is_dram
        self.w = None
        self.r = []
        self.dsem = None
        self.dcnt = 0

    def __getitem__(self, idx):
        return self.t[idx]


class K:
    ENG = ("pe", "act", "dve", "pool", "sp")

    def __init__(self, nc, es, n_dma_sems=46, n_sw_sems=50):
        self.nc = nc
        self.q = {e: [] for e in self.ENG}
        self.cnt = {e: 0 for e in self.ENG}
        self.waited = {e: {} for e in self.ENG}
        self.sem = {e: es.enter_context(nc.semaphore("s_" + e)) for e in ("pe", "act", "dve", "pool")}
        self.dma_pool = [es.enter_context(nc.semaphore("d%d" % i)) for i in range(n_dma_sems + n_sw_sems)]
        self.dma_free = list(range(n_dma_sems))
        self.sw_free = list(range(n_dma_sems, n_dma_sems + n_sw_sems))
        self.dma_val = [0] * (n_dma_sems + n_sw_sems)
        self.live_sw = []
        self.live = []
        self.n_ops = 0

    def sb(self, es, name, shape, dt):
        self.uid = getattr(self, "uid", 0) + 1
        name = "%s_u%d" % (name, self.uid)
        return Buf(name, es.enter_context(self.nc.sbuf_tensor(name, list(shape), dt)))

    def ps(self, es, name, shape, dt):
        self.uid = getattr(self, "uid", 0) + 1
        name = "%s_u%d" % (name, self.uid)
        b = Buf(name, es.enter_context(self.nc.psum_tensor(name, list(shape), dt)))
        b.is_psum = True
        return b

    def _need(self, eng, dep, out):
        kind, s, v = dep
        if kind == "pe" and eng == "pe":
            return
        key = (kind, s)
        if self.waited[eng].get(key, -1) >= v:
            return
        self.waited[eng][key] = v
        out.append(dep)

    def op(self, eng, fn, reads=(), writes=(), dma=False):
        deps = []
        for b in reads:
            if b.w is not None:
                self._need(eng, b.w, deps)
            if getattr(b, "is_psum", False):
                for r in b.r:
                    if r[0] != eng:
                        self._need(eng, r, deps)
        for b in writes:
            if b.w is not None and (dma or b.w[0] != eng):
                self._need(eng, b.w, deps)
            for r in b.r:
                if dma or r[0] != eng:
                    self._need(eng, r, deps)
        if dma:
            owner = None
            for b in list(writes) + list(reads):
                if not b.is_dram:
                    owner = b
                    break
            if owner is None:
                owner = (list(writes) + list(reads))[0]
            if eng == "pool":
                if getattr(owner, "swsem", None) is None:
                    owner.swsem = self.sw_free.pop()
                    owner.swcnt = 0
                    self.live_sw.append(owner)
                owner.swcnt += 16
                tok = ("dma", owner.swsem, owner.swcnt)
                semh, val = self.dma_pool[owner.swsem], 16
            else:
                if owner.dsem is None:
                    owner.dsem = self.dma_free.pop()
                    owner.dcnt = self.dma_val[owner.dsem]
                    self.live.append(owner)
                owner.dcnt += 16
                tok = ("dma", owner.dsem, owner.dcnt)
                semh, val = self.dma_pool[owner.dsem], 16
        else:
            self.cnt[eng] += 1
            tok = (eng, None, self.cnt[eng])
            semh, val = self.sem[eng], 1
        self.q[eng].append((deps, fn, semh, val))
        self.n_ops += 1
        for b in reads:
            b.r.append(tok)
            if len(b.r) > 64:
                b.r = b.r[-64:]
        for b in writes:
            b.w = tok
            b.r = []
        return tok

    def barrier(self):
        toks = [(e, None, self.cnt[e]) for e in ("pe", "act", "dve", "pool") if self.cnt[e]]
        for b in self.live:
            toks.append(("dma", b.dsem, b.dcnt))
        for b in self.live_sw:
            toks.append(("dma", b.swsem, b.swcnt))
        self.live_sw = []
        for e in self.ENG:
            deps = []
            for t in toks:
                if t[0] == e:
                    continue
                self._need(e, t, deps)
            if deps:
                self.q[e].append((deps, None, None, 0))
        for b in self.live:
            self.dma_val[b.dsem] = b.dcnt
            self.dma_free.append(b.dsem)
            b.dsem = None
        self.live = []

    def emit(self):
        with self.nc.Block() as block:
            def run(eng_name):
                def f(h):
                    for deps, fn, semh, val in self.q[eng_name]:
                        for kind, s, v in deps:
                            h.wait_ge(self.dma_pool[s] if kind == "dma" else self.sem[kind], v)
                        if fn is not None:
                            fn(h).then_inc(semh, val)
                return f
            block.tensor(run("pe"))
            block.scalar(run("act"))
            block.vector(run("dve"))
            block.gpsimd(run("pool"))
            block.sync(run("sp"))

    def dma(self, out, in_, eng="sp"):
        (ob, oa), (ib, ia) = out, in_
        return self.op(eng, lambda h: h.dma_start(out=oa, in_=ia), reads=[ib], writes=[ob], dma=True)

    def mm(self, out, lhsT, rhs, start=True, stop=True):
        (ob, oa), (lb, la), (rb, ra) = out, lhsT, rhs
        return self.op("pe", lambda h: h.matmul(oa, la, ra, start=start, stop=stop), reads=[lb, rb], writes=[ob])

    def tr(self, out, in_, ident):
        (ob, oa), (ib, ia), (db, da) = out, in_, ident
        return self.op("pe", lambda h: h.transpose(oa, ia, da), reads=[ib, db], writes=[ob])

    def act(self, out, in_, func, bias=None, scale=1.0, accum=None):
        (ob, oa), (ib, ia) = out, in_
        reads, writes = [ib], [ob]
        kw = {}
        if bias is not None:
            if isinstance(bias, tuple):
                reads.append(bias[0]); kw["bias"] = bias[1]
            else:
                kw["bias"] = bias
        if isinstance(scale, tuple):
            reads.append(scale[0]); kw["scale"] = scale[1]
        else:
            kw["scale"] = scale
        if accum is not None:
            writes.append(accum[0]); kw["accum_out"] = accum[1]
        return self.op("act", lambda h: h.activation(out=oa, in_=ia, func=func, **kw), reads=reads, writes=writes)

    def tt(self, out, in0, in1, op, eng="dve"):
        (ob, oa), (ab, aa), (bb, ba) = out, in0, in1
        return self.op(eng, lambda h: h.tensor_tensor(out=oa, in0=aa, in1=ba, op=op), reads=[ab, bb], writes=[ob])

    def ts(self, out, in0, s1, op0, s2=None, op1=None, eng="dve", accum=None):
        (ob, oa), (ab, aa) = out, in0
        reads, writes = [ab], [ob]
        if isinstance(s1, tuple):
            reads.append(s1[0]); s1 = s1[1]
        if isinstance(s2, tuple):
            reads.append(s2[0]); s2 = s2[1]
        kw = {}
        if op1 is not None:
            kw["op1"] = op1
        if accum is not None:
            writes.append(accum[0]); kw["accum_out"] = accum[1]
        return self.op(eng, lambda h: h.tensor_scalar(oa, aa, s1, s2, op0, **kw), reads=reads, writes=writes)

    def cp(self, out, in_, eng="dve"):
        (ob, oa), (ib, ia) = out, in_
        return self.op(eng, lambda h: h.tensor_copy(out=oa, in_=ia), reads=[ib], writes=[ob])

    def memset(self, out, val, eng="dve"):
        (ob, oa) = out
        return self.op(eng, lambda h: h.memset(oa, val), writes=[ob])

    def recip(self, out, in_):
        (ob, oa), (ib, ia) = out, in_
        return self.op("dve", lambda h: h.reciprocal(out=oa, in_=ia), reads=[ib], writes=[ob])


def V(buf, ap=None):
    return (buf, buf.t[:] if ap is None else ap)


class Cfg:
    def __init__(self, nc_cores=8, sp=4, t=2048):
        self.NC = nc_cores
        self.SP = sp
        self.T = t
        self.NSEG = sp + 1
        self.NCH = t // 128
        self.NB = t // 512
        self.LS = nc_cores * t


class Rot:
    def __init__(self, items):
        self.items = items
        self.i = 0

    def nxt(self):
        b = self.items[self.i % len(self.items)]
        self.i += 1
        return b


def D_(ap):
    return (None, ap)


def _dma(k, out, in_, eng="sp", slow=False):
    (ob, oa), (ib, ia) = out, in_
    reads = [ib] if ib is not None else []
    writes = [ob] if ob is not None else []
    if not reads and not writes:
        raise ValueError("dram->dram untracked")
    if slow:
        return k.op(eng, lambda h: h.dma_start(out=oa, in_=ia, allow_slow_non_contiguous=True), reads=reads, writes=writes, dma=True)
    return k.op(eng, lambda h: h.dma_start(out=oa, in_=ia), reads=reads, writes=writes, dma=True)


K.dma = _dma


def prep_weight(k, st, dst_fn, src, K_rows, cols, row_gain=None, col_gain=None):
    nk = K_rows // 128
    CB = 1408
    if row_gain is not None:
        rg = st["rg"].nxt()
        k.dma(V(rg, rg[:, 0:nk]), D_(row_gain.rearrange("(c p) -> p c", p=128)), slow=True)
    for c0 in range(0, cols, CB):
        cw = min(CB, cols - c0)
        if col_gain is not None:
            cg = st["cg"].nxt()
            k.dma(V(cg, cg[:, 0:cw]), D_(col_gain[c0:c0 + cw].partition_broadcast(128)))
        for kc in range(nk):
            s32 = st["s32"].nxt()
            k.dma(V(s32, s32[:, 0:cw]), D_(src[kc * 128:(kc + 1) * 128, c0:c0 + cw]))
            cur = V(s32, s32[:, 0:cw])
            if col_gain is not None:
                k.tt(cur, cur, V(cg, cg[:, 0:cw]), ALU.mult, eng="pool")
            db, da = dst_fn(kc, c0, cw)
            if db is None:
                sbf = st["sbf"].nxt()
                o = V(sbf, sbf[:, 0:cw])
            else:
                o = (db, da)
            if row_gain is not None:
                k.act(o, cur, AF.Copy, scale=V(rg, rg[:, kc:kc + 1]))
            else:
                k.act(o, cur, AF.Copy)
            if db is None:
                if len(da.shape) == 3:
                    o = (o[0], o[1].rearrange("p (m c) -> p m c", c=128))
                k.dma(D_(da), o, eng="pool")


def prep_stage(k, es):
    return {
        "s32": Rot([k.sb(es, "p0s32_%d" % i, [128, 1408], F32) for i in range(3)]),
        "sbf": Rot([k.sb(es, "p0sbf_%d" % i, [128, 1408], BF16) for i in range(3)]),
        "cg": Rot([k.sb(es, "p0cg_%d" % i, [128, 1408], F32) for i in range(2)]),
        "rg": Rot([k.sb(es, "p0rg_%d" % i, [128, 24], F32) for i in range(2)]),
    }


def load_w(k, buf, dram_ap):
    n = dram_ap.shape[1]
    step = max(1, n // 4)
    for c in range(0, n, step):
        e = min(n, c + step)
        k.dma(V(buf, buf[:, c:e]), D_(dram_ap[:, c:e]))


def rmsnorm_tile(k, P, xt, rows, hn, dim_scale):
    ss = P["ss"].nxt()
    junk = P["junk"].nxt()
    k.act(V(junk, junk[0:rows, :]), V(xt, xt[0:rows, :]), AF.Square, scale=dim_scale, accum=V(ss, ss[0:rows, 0:1]))
    k.act(V(ss, ss[0:rows, 1:2]), V(ss, ss[0:rows, 0:1]), AF.Sqrt, bias=V(P["eps"], P["eps"][0:rows, 0:1]))
    k.recip(V(ss, ss[0:rows, 1:2]), V(ss, ss[0:rows, 1:2]))
    k.ts(V(hn, hn[0:rows, :]), V(xt, xt[0:rows, :]), V(ss, ss[0:rows, 1:2]), ALU.mult)


def transpose_tile(k, P, hn, rows, dst_buf, dst_ap_fn):
    tp = P["tp"].nxt()
    for j in range(8):
        k.tr(V(tp, tp[:, j * 128:j * 128 + rows]), V(hn, hn[0:rows, j * 128:(j + 1) * 128]),
             V(P["ident"], P["ident"][0:rows, 0:rows]))
    src = tp[:, :].rearrange("p (j t) -> p j t", t=128)[:, :, 0:rows]
    k.cp(V(dst_buf, dst_ap_fn), V(tp, src))


def phase_p1a(k, cfg, W, C, segs):
    nc = k.nc
    with ExitStack() as es:
        Wq = k.sb(es, "Wq", [128, 8, QL], BF16)
        Wc = k.sb(es, "Wc", [128, 8, KVL], BF16)
        Wkr = k.sb(es, "Wkr", [128, 8, 96], BF16)
        Wks = k.sb(es, "Wks", [128, 8, 96], BF16)
        Wqb = k.sb(es, "Wqb", [128, 3, H * 96], BF16)
        Wqs = k.sb(es, "Wqs", [128, 3, H * 96], BF16)
        Wkb = k.sb(es, "Wkb", [128, 2, H * 64], BF16)
        Wvb = k.sb(es, "Wvb", [128, 2, H * 64], BF16)
        ident = k.sb(es, "ident_sb", [128, 128], BF16)
        onesb = k.sb(es, "ones_sb", [128, 128], BF16)
        eps_t = k.sb(es, "eps_sb", [128, 1], F32)
        es_prep = ExitStack()
        st = prep_stage(k, es_prep)
        k.dma(V(ident), D_(C["identb"]))
        k.dma(V(onesb), D_(C["onesb"]))
        n1 = W["norm1"]
        win = W["w_in"]
        prep_weight(k, st, lambda kc, c0, cw: (Wq, Wq[:, kc, c0:c0 + cw]), win[:, O_Q:O_Q + QL], D, QL, row_gain=n1)
        prep_weight(k, st, lambda kc, c0, cw: (Wc, Wc[:, kc, c0:c0 + cw]), win[:, O_CKV:O_CKV + KVL], D, KVL, row_gain=n1)
        k.memset(V(Wkr), 0.0)
        k.memset(V(Wks), 0.0)
        prep_weight(k, st, lambda kc, c0, cw: (Wkr, Wkr[:, kc, 64:96]), win[:, O_KR:O_KR + 32], D, 32, row_gain=n1)
        prep_weight(k, st, lambda kc, c0, cw: (Wks, Wks[:, kc, 64:80]), win[:, O_KR + 16:O_KR + 32], D, 16, row_gain=n1)
        prep_weight(k, st, lambda kc, c0, cw: (Wks, Wks[:, kc, 80:96]), win[:, O_KR:O_KR + 16], D, 16, row_gain=n1)
        prep_weight(k, st, lambda kc, c0, cw: (Wqb, Wqb[:, kc, c0:c0 + cw]), W["w_q_b"], QL, H * 96, row_gain=W["q_a_norm"])
        k.memset(V(Wqs), 0.0, eng="pool")
        wqb3 = W["w_q_b"].rearrange("k (h c) -> k h c", c=96)
        for hh in range(H):
            prep_weight(k, st, lambda kc, c0, cw, hh=hh: (Wqs, Wqs[:, kc, hh * 96 + 64:hh * 96 + 80]),
                        W["w_q_b"][:, hh * 96 + 80:hh * 96 + 96], QL, 16, row_gain=W["q_a_norm"])
            prep_weight(k, st, lambda kc, c0, cw, hh=hh: (Wqs, Wqs[:, kc, hh * 96 + 80:hh * 96 + 96]),
                        W["w_q_b"][:, hh * 96 + 64:hh * 96 + 80], QL, 16, row_gain=W["q_a_norm"])
        for hh in range(H):
            prep_weight(k, st, lambda kc, c0, cw, hh=hh: (Wkb, Wkb[:, kc, hh * 64:(hh + 1) * 64]),
                        W["w_kv_b"][:, hh * 128:hh * 128 + 64], KVL, 64, row_gain=W["kv_a_norm"])
            prep_weight(k, st, lambda kc, c0, cw, hh=hh: (Wvb, Wvb[:, kc, hh * 64:(hh + 1) * 64]),
                        W["w_kv_b"][:, hh * 128 + 64:hh * 128 + 128], KVL, 64, row_gain=W["kv_a_norm"])
        k.barrier()
        es_prep.close()

        P = {
            "ss": Rot([k.sb(es, "ss%d" % i, [128, 2], F32) for i in range(3)]),
            "junk": Rot([k.sb(es, "junk%d" % i, [128, D], BF16) for i in range(2)]),
            "tp": Rot([k.ps(es, "tp%d" % i, [128, D], BF16) for i in range(1)]),
            "ident": ident,
            "eps": eps_t,
        }
        k.memset(V(P["eps"]), EPS)
        xt = Rot([k.sb(es, "xt%d" % i, [128, D], F32) for i in range(9)])
        hn = Rot([k.sb(es, "hn%d" % i, [128, D], BF16) for i in range(2)])
        hT = Rot([k.sb(es, "hT%d" % i, [128, 8, 512], BF16) for i in range(2)])
        pb = Rot([k.ps(es, "pb%d" % i, [128, 512], F32) for i in range(7)])
        sq = Rot([k.sb(es, "sq%d" % i, [128, 512], BF16) for i in range(3)])
        rbc = Rot([k.sb(es, "rbc%d" % i, [128, 512], F32) for i in range(2)])
        qln = Rot([k.sb(es, "qln%d" % i, [128, 3, 512], BF16) for i in range(2)])
        ckn = Rot([k.sb(es, "ckn%d" % i, [128, 2, 512], BF16) for i in range(2)])
        cosb = Rot([k.sb(es, "cosb%d" % i, [96, 512], F32) for i in range(3)])
        sinb = Rot([k.sb(es, "sinb%d" % i, [96, 512], F32) for i in range(3)])
        t1 = Rot([k.sb(es, "t1_%d" % i, [96, 512], F32) for i in range(3)])
        t2 = Rot([k.sb(es, "t2_%d" % i, [96, 512], F32) for i in range(3)])
        qo = Rot([k.sb(es, "qo%d" % i, [96, H, 512], BF16) for i in range(1)])
        ko = Rot([k.sb(es, "ko%d" % i, [96, H, 512], BF16) for i in range(1)])
        krt = Rot([k.sb(es, "krt%d" % i, [96, 512], BF16) for i in range(2)])
        vo = Rot([k.sb(es, "vo%d" % i, [128, H, 128], BF16) for i in range(2)])
        for b_ in vo.items:
            k.memset(V(b_), 1.0, eng="pool")

        def fm_rmsnorm(ps_list, dim, dst, nchunk):
            sqs = []
            for m in range(nchunk):
                s_ = sq.nxt()
                k.act(V(s_), V(ps_list[m]), AF.Square, scale=float(dim) ** -0.5)
                sqs.append(s_)
            pss = pb.nxt()
            for m in range(nchunk):
                k.mm(V(pss), V(onesb), V(sqs[m]), start=(m == 0), stop=(m == nchunk - 1))
            r_ = rbc.nxt()
            k.act(V(r_), V(pss), AF.Sqrt, bias=V(P["eps"]))
            k.recip(V(r_), V(r_))
            for m in range(nchunk):
                k.tt(V(dst, dst[:, m, :]), V(ps_list[m]), V(r_), ALU.mult)

        blocks = [(sg, b) for sg in segs for b in range((sg.get("n", cfg.T) + 511) // 512)]

        loaded = {}

        def load_blk(bi):
            sg, b = blocks[bi]
            c0 = b * 512
            xs_l = []
            for ti in range(4):
                x_ = xt.nxt()
                r0 = c0 + ti * 128
                k.dma(V(x_), D_(sg["x"][r0:r0 + 128, :]))
                xs_l.append(x_)
            cs_, sn_ = cosb.nxt(), sinb.nxt()
            k.dma(V(cs_), D_(sg["cos"][:, c0:c0 + 512]))
            k.dma(V(sn_), D_(sg["sin"][:, c0:c0 + 512]))
            loaded[bi] = (xs_l, cs_, sn_)

        def build_hT(bi):
            if bi not in loaded:
                load_blk(bi)
            xs_l, cs_, sn_ = loaded.pop(bi)
            if bi + 1 < len(blocks) and (bi + 1) not in loaded:
                load_blk(bi + 1)
            h_ = hT.nxt()
            for ti in range(4):
                n_ = hn.nxt()
                rmsnorm_tile(k, P, xs_l[ti], 128, n_, 1.0 / 32.0)
                transpose_tile(k, P, n_, 128, h_, h_[:, :, ti * 128:(ti + 1) * 128])
            return h_, cs_, sn_

        nxt_blk = build_hT(0)
        for bi, (sg, b) in enumerate(blocks):
            if True:
                do_q, do_kv = sg.get("do_q", True), sg.get("do_kv", True)
                c0 = b * 512
                h_, cs_, sn_ = nxt_blk
                def rope_rows(pa, ps_, dst):
                    a_, b2 = t1.nxt(), t2.nxt()
                    k.tt(V(a_, a_[64:96, :]), V(pa, pa[64:96, :]), V(cs_, cs_[64:96, :]), ALU.mult)
                    k.tt(V(b2, b2[64:96, :]), V(ps_, ps_[64:96, :]), V(sn_, sn_[64:96, :]), ALU.mult)
                    k.tt(dst, V(a_, a_[64:96, :]), V(b2, b2[64:96, :]), ALU.add)

                if do_q:
                    pq = [pb.nxt() for _ in range(3)]
                    for m in range(3):
                        for kc in range(KC):
                            k.mm(V(pq[m]), V(Wq, Wq[:, kc, m * 128:(m + 1) * 128]), V(h_, h_[:, kc, :]),
                                 start=(kc == 0), stop=(kc == KC - 1))
                if do_kv:
                    pc = [pb.nxt() for _ in range(2)]
                    for m in range(2):
                        for kc in range(KC):
                            k.mm(V(pc[m]), V(Wc, Wc[:, kc, m * 128:(m + 1) * 128]), V(h_, h_[:, kc, :]),
                                 start=(kc == 0), stop=(kc == KC - 1))
                if bi + 1 < len(blocks):
                    nxt_blk = build_hT(bi + 1)
                if do_q:
                    ql = qln.nxt()
                    fm_rmsnorm(pq, QL, ql, 3)
                if do_kv:
                    pka, pks = pb.nxt(), pb.nxt()
                    for kc in range(KC):
                        k.mm(V(pka, pka[0:96, :]), V(Wkr, Wkr[:, kc, :]), V(h_, h_[:, kc, :]), start=(kc == 0), stop=(kc == KC - 1))
                    for kc in range(KC):
                        k.mm(V(pks, pks[0:96, :]), V(Wks, Wks[:, kc, :]), V(h_, h_[:, kc, :]), start=(kc == 0), stop=(kc == KC - 1))
                    cn = ckn.nxt()
                    fm_rmsnorm(pc, KVL, cn, 2)
                    kr = krt.nxt()
                    rope_rows(pka, pks, V(kr, kr[64:96, :]))

                q_ = qo.nxt() if do_q else None
                for hh in range(H if do_q else 0):
                    pa, ps_ = pb.nxt(), pb.nxt()
                    for m in range(3):
                        k.mm(V(pa, pa[0:96, :]), V(Wqb, Wqb[:, m, hh * 96:(hh + 1) * 96]), V(ql, ql[:, m, :]),
                             start=(m == 0), stop=(m == 2))
                    for m in range(3):
                        k.mm(V(ps_, ps_[0:96, :]), V(Wqs, Wqs[:, m, hh * 96:(hh + 1) * 96]), V(ql, ql[:, m, :]),
                             start=(m == 0), stop=(m == 2))
                    if hh % 2 == 0:
                        k.act(V(q_, q_[0:64, hh, :]), V(pa, pa[0:64, :]), AF.Copy)
                    else:
                        k.cp(V(q_, q_[0:64, hh, :]), V(pa, pa[0:64, :]))
                    rope_rows(pa, ps_, V(q_, q_[64:96, hh, :]))
                if do_q:
                    k.dma(D_(sg["qt"][:, :, c0:c0 + 512].rearrange("h p t -> p h t")), V(q_), eng="pool")
                if not do_kv:
                    continue
                k_ = ko.nxt()
                for hh in range(H):
                    pk = pb.nxt()
                    for m in range(2):
                        k.mm(V(pk, pk[0:64, :]), V(Wkb, Wkb[:, m, hh * 64:(hh + 1) * 64]), V(cn, cn[:, m, :]),
                             start=(m == 0), stop=(m == 1))
                    if hh % 2 == 0 and do_q:
                        k.act(V(k_, k_[0:64, hh, :]), V(pk, pk[0:64, :]), AF.Copy)
                    elif hh % 4 == 0:
                        k.act(V(k_, k_[0:64, hh, :]), V(pk, pk[0:64, :]), AF.Copy)
                    else:
                        k.cp(V(k_, k_[0:64, hh, :]), V(pk, pk[0:64, :]))
                k.dma(D_(sg["kt"][:, 0:64, c0:c0 + 512].rearrange("h p t -> p h t")), V(k_, k_[0:64, :, :]), eng="pool")
                k.dma(D_(sg["kt"][0, 64:96, c0:c0 + 512]), V(kr, kr[64:96, :]), eng="pool")
                for ti in range(4):
                    v_ = vo.nxt()
                    for half in range(2):
                        pv = pb.nxt()
                        for m in range(2):
                            k.mm(V(pv), V(cn, cn[:, m, ti * 128:(ti + 1) * 128]), V(Wvb, Wvb[:, m, half * 512:(half + 1) * 512]),
                                 start=(m == 0), stop=(m == 1))
                        if half == 0:
                            k.act(V(v_, v_[:, half * 8:half * 8 + 8, 0:64]),
                                  V(pv, pv[:, :].rearrange("p (j v) -> p j v", v=64)), AF.Copy)
                        else:
                            k.cp(V(v_, v_[:, half * 8:half * 8 + 8, 0:64]),
                                 V(pv, pv[:, :].rearrange("p (j v) -> p j v", v=64)))
                    r0 = c0 + ti * 128
                    k.dma(D_(sg["va"][r0:r0 + 128, :, :]), V(v_), eng="pool")
        k.barrier()


WNAMES = ["norm1", "w_in", "q_a_norm", "kv_a_norm", "w_q_b", "w_kv_b", "conv_w", "conv_b", "dt_bias_f", "dt_bias_b",
          "a_log_f", "a_log_b", "d_skip", "ssm_norm", "w_out", "norm2", "w_gate", "w_up", "ffn_conv_w", "ffn_conv_b",
          "w_down", "final_norm"]


def rope_tables(pos):
    pos = np.asarray(pos, dtype=np.float32)
    inv = (np.float32(10000.0) ** (-(np.arange(0, RO, 2, dtype=np.float32)) / np.float32(RO))).astype(np.float32)
    ang = (pos[:, None] * inv[None, :]).astype(np.float32)
    c, s = np.cos(ang).astype(np.float32).T, np.sin(ang).astype(np.float32).T
    cos = np.ones((96, len(pos)), np.float32)
    sin = np.zeros((96, len(pos)), np.float32)
    cos[64:80], cos[80:96] = c, c
    sin[64:80], sin[80:96] = -s, s
    return cos, sin


def host_consts():
    r = np.arange(128)
    cst = np.zeros((128, 7, 128), np.float32)
    cst[:, 0, :] = (r[:, None] <= r[None, :])
    cst[:, 1, :] = (r[:, None] < r[None, :])
    cst[:, 2, :] = 1.0
    cst[:, 3, :] = np.eye(128)
    cst[:, 4, :] = np.where(r[None, :] < r[:, None], NEG, 0.0)
    cst[:, 5, :] = np.where(r[None, :] > r[:, None], NEG, 0.0)
    cst[:, 6, :] = -(r[:, None] < r[None, :]).astype(np.float32)
    return {
        "identb": np.eye(128, dtype=np.float32).astype(ml_dtypes.bfloat16),
        "onesb": np.ones((128, 128), np.float32).astype(ml_dtypes.bfloat16),
        "cst32": cst,
    }


def declare_weights(nc, shapes):
    W = {}
    for n in WNAMES:
        shp = list(shapes[n])
        W[n] = nc.dram_tensor(n, shp, F32, kind="ExternalInput").ap()
    return W


def wviews(W):
    o = {}
    for n, ap in W.items():
        o[n] = ap if n == "final_norm" else ap[0]
    return o


def phase_p2(k, cfg, jobs):
    with ExitStack() as es:
        maxk = max(j["Tk"] for j in jobs)
        maxq = max(j["Tq"] for j in jobs)
        qb = Rot([k.sb(es, "aq%d" % i, [96, maxq], BF16) for i in range(2)])
        kb = Rot([k.sb(es, "ak%d" % i, [96, maxk], BF16) for i in range(2)])
        vb = Rot([k.sb(es, "av%d" % i, [128, maxk // 128, 128], BF16) for i in range(2)])
        pS = Rot([k.ps(es, "pS%d" % i, [128, 1024], F32) for i in range(3)])
        pO = Rot([k.ps(es, "pO%d" % i, [128, 512], F32) for i in range(2)])
        pt = Rot([k.sb(es, "apt%d" % i, [128, 1024], BF16) for i in range(3)])
        rc = Rot([k.sb(es, "arc%d" % i, [64, 512], F32) for i in range(2)])
        ao = Rot([k.sb(es, "aao%d" % i, [64, 512], BF16) for i in range(3)])

        def load(j):
            Tq, Tk = j["Tq"], j["Tk"]
            q_, k_, v_ = qb.nxt(), kb.nxt(), vb.nxt()
            k.dma(V(q_, q_[:, 0:Tq]), D_(j["qt"]))
            for c in range(0, Tk, 4096):
                e = min(Tk, c + 4096)
                if "kr" in j:
                    k.dma(V(k_, k_[0:64, c:e]), D_(j["kt"][0:64, c:e]))
                    k.dma(V(k_, k_[64:96, c:e]), D_(j["kr"][:, c:e]))
                else:
                    k.dma(V(k_, k_[:, c:e]), D_(j["kt"][:, c:e]))
            for c in range(0, Tk, 2048):
                e = min(Tk, c + 2048)
                k.dma(V(v_, v_[:, c // 128:e // 128, :]), D_(j["va"][c:e, :].rearrange("(t p) v -> p t v", p=128)))
            return q_, k_, v_

        its = []
        for ji, j in enumerate(jobs):
            for qi in range((j["Tq"] + 511) // 512):
                for kp in range(j["Tk"] // 256):
                    its.append((ji, qi, kp))
        bufs = {0: load(jobs[0])}
        sbuf = {}

        def emit_scores(i):
            ji, qi, kp = its[i]
            if ji not in bufs:
                bufs[ji] = load(jobs[ji])
            q_, k_, v_ = bufs[ji]
            qw = min(512, jobs[ji]["Tq"] - qi * 512)
            s_ = pS.nxt()
            for t in range(2):
                kt_ = 2 * kp + t
                k.mm(V(s_, s_[:, t * 512:t * 512 + qw]), V(k_, k_[:, kt_ * 128:(kt_ + 1) * 128]), V(q_, q_[:, qi * 512:qi * 512 + qw]))
            sbuf[i] = s_

        emit_scores(0)
        o_ = None
        for i, (ji, qi, kp) in enumerate(its):
            j = jobs[ji]
            if qi == 0 and kp == 0 and ji + 1 < len(jobs) and (ji + 1) not in bufs:
                bufs[ji + 1] = load(jobs[ji + 1])
            if i + 1 < len(its):
                emit_scores(i + 1)
            q_, k_, v_ = bufs[ji]
            nkp = j["Tk"] // 256
            if kp == 0:
                o_ = pO.nxt()
            s_ = sbuf.pop(i)
            p_ = pt.nxt()
            qw = min(512, j["Tq"] - qi * 512)
            k.act(V(p_, p_[:, :].rearrange("p (t c) -> p t c", c=512)[:, :, 0:qw]),
                  V(s_, s_[:, :].rearrange("p (t c) -> p t c", c=512)[:, :, 0:qw]), AF.Exp, scale=SCALE)
            for t in range(2):
                kt_ = 2 * kp + t
                k.mm(V(o_, o_[:, 0:qw]), V(v_, v_[:, kt_, :]), V(p_, p_[:, t * 512:t * 512 + qw]), start=(kt_ == 0), stop=(kt_ == 2 * nkp - 1))
            if kp == nkp - 1:
                r_ = rc.nxt()
                k.recip(V(r_, r_[:, 0:qw]), V(o_, o_[64:128, 0:qw]))
                a_ = ao.nxt()
                k.tt(V(a_, a_[:, 0:qw]), V(o_, o_[0:64, 0:qw]), V(r_, r_[:, 0:qw]), ALU.mult)
                k.dma(D_(j["at"][:, qi * 512:qi * 512 + qw]), V(a_, a_[:, 0:qw]), eng="pool")
                if qi == (j["Tq"] + 511) // 512 - 1:
                    bufs.pop(ji, None)
        k.barrier()


def phase_p3(k, cfg, W, C, segs, nkc=16):
    with ExitStack() as es:
        st = prep_stage(k, es)
        Wo = k.sb(es, "Wo", [128, nkc, D], BF16)
        ident = k.sb(es, "ident_sb3", [128, 128], BF16)
        k.dma(V(ident), D_(C["identb"]))
        prep_weight(k, st, lambda kc, c0, cw: (Wo, Wo[:, kc, c0:c0 + cw]), W["w_out"][0:1024, :], 1024, D)
        if nkc == 16:
            prep_weight(k, st, lambda kc, c0, cw: (Wo, Wo[:, 8 + kc, c0:c0 + cw]), W["w_out"][1024:2048, :], 1024, D,
                        row_gain=W["ssm_norm"])
        P = {
            "ss": Rot([k.sb(es, "ss3_%d" % i, [128, 2], F32) for i in range(3)]),
            "junk": Rot([k.sb(es, "junk3_%d" % i, [128, D], BF16) for i in range(2)]),
            "tp": Rot([k.ps(es, "tp3_%d" % i, [128, D], BF16) for i in range(2)]),
            "ident": ident,
            "eps": k.sb(es, "eps3", [128, 1], F32),
        }
        k.memset(V(P["eps"]), EPS)
        zt = k.sb(es, "zt3", [128, 8, 2], BF16)
        k.memset(V(zt), 0.0)
        mixT = Rot([k.sb(es, "mixT%d" % i, [128, nkc, 512], BF16) for i in range(2)])
        xt = Rot([k.sb(es, "xt3_%d" % i, [128, D], F32) for i in range(3)])
        x1 = Rot([k.sb(es, "x1_%d" % i, [128, D], F32) for i in range(3)])
        hn = Rot([k.sb(es, "hn3_%d" % i, [128, D], BF16) for i in range(2)])
        h2 = Rot([k.sb(es, "h2_%d" % i, [128, 8, 128], BF16) for i in range(3)])
        px = Rot([k.ps(es, "px%d" % i, [128, D], F32) for i in range(3)])
        for sg in segs:
            T = sg.get("n", cfg.T)
            k.dma(D_(sg["h2t"][:, :, 0:1]), V(zt, zt[:, :, 0:1]), eng="pool", slow=True)
            k.dma(D_(sg["h2t"][:, :, T + 1:T + 2]), V(zt, zt[:, :, 1:2]), eng="pool", slow=True)
            for c0 in range(0, T, 512):
                bwid = min(512, T - c0)
                m_ = mixT.nxt()
                for i, mx in enumerate(sg["mix"]):
                    k.dma(V(m_, m_[:, 8 * i:8 * i + 8, 0:bwid]), D_(mx.rearrange("(c p) t -> p c t", p=128)[:, :, c0:c0 + bwid]))
                for ti in range(bwid // 128):
                    r0 = c0 + ti * 128
                    x_ = xt.nxt()
                    k.dma(V(x_), D_(sg["x"][r0:r0 + 128, :]))
                    p_ = px.nxt()
                    for n in range(2):
                        for kc in range(nkc):
                            k.mm(V(p_, p_[:, n * 512:(n + 1) * 512]), V(m_, m_[:, kc, ti * 128:(ti + 1) * 128]),
                                 V(Wo, Wo[:, kc, n * 512:(n + 1) * 512]), start=(kc == 0), stop=(kc == nkc - 1))
                    y_ = x1.nxt()
                    k.tt(V(y_), V(p_), V(x_), ALU.add)
                    k.dma(D_(sg["x1"][r0:r0 + 128, :]), V(y_), eng="pool")
                    n_ = hn.nxt()
                    rmsnorm_tile(k, P, y_, 128, n_, 1.0 / 32.0)
                    h_ = h2.nxt()
                    transpose_tile(k, P, n_, 128, h_, h_[:, :, :])
                    k.dma(D_(sg["h2t"][:, :, 1 + r0:1 + r0 + 128]), V(h_), eng="pool")
        k.barrier()


FB = 510


def phase_p4(k, cfg, W, C, wg_scr, segs):
    T = cfg.T
    with ExitStack() as es:
        Wu = k.sb(es, "Wu", [128, 8, DFF], BF16)
        Wd = k.sb(es, "Wd", [128, FC, D], BF16)
        bg = k.sb(es, "bg", [128, FC], F32)
        cw3 = k.sb(es, "cw3", [128, 3, FC], F32)
        gain = k.sb(es, "fgain", [128, D], F32)
        eps = k.sb(es, "eps4", [128, 1], F32)
        es_prep = ExitStack()
        st = prep_stage(k, es_prep)
        n2 = W["norm2"]

        def dst(kc, c0, cw):
            return (None, wg_scr[c0 // 128:(c0 + cw) // 128, :, kc, :].rearrange("m p c -> p m c"))
        prep_weight(k, st, dst, W["w_gate"], D, DFF, row_gain=n2)
        prep_weight(k, st, lambda kc, c0, cw: (Wu, Wu[:, kc, c0:c0 + cw]), W["w_up"], D, DFF, row_gain=n2)
        prep_weight(k, st, lambda kc, c0, cw: (Wd, Wd[:, kc, c0:c0 + cw]), W["w_down"], DFF, D)
        k.dma(V(bg), D_(W["ffn_conv_b"].rearrange("(c p) -> p c", p=128)), slow=True)
        for tap in range(3):
            k.dma(V(cw3, cw3[:, tap, :]), D_(W["ffn_conv_w"][tap].rearrange("(c p) -> p c", p=128)), slow=True)
        k.dma(V(gain), D_(W["final_norm"].partition_broadcast(128)))
        k.memset(V(eps), EPS)
        k.barrier()
        es_prep.close()
        hT = Rot([k.sb(es, "h2T%d" % i, [128, 8, T + 2], BF16) for i in range(1)])
        vf4 = k.sb(es, "vf4", [128, 2], F32)
        wg = Rot([k.sb(es, "wg%d" % i, [128, 8, 128], BF16) for i in range(3)])
        pg = Rot([k.ps(es, "pg%d" % i, [128, 512], F32) for i in range(2)])
        pu = Rot([k.ps(es, "pu%d" % i, [128, 512], F32) for i in range(2)])
        pd = Rot([k.ps(es, "pd%d" % i, [128, D], F32) for i in range(2)])
        cv = Rot([k.sb(es, "cv%d" % i, [128, 512], F32) for i in range(3)])
        sg_ = Rot([k.sb(es, "sgl%d" % i, [128, 512], F32) for i in range(2)])
        aT = Rot([k.sb(es, "aT%d" % i, [128, FC, 512], BF16) for i in range(1)])
        x1 = Rot([k.sb(es, "x14_%d" % i, [128, D], F32) for i in range(2)])
        ss = Rot([k.sb(es, "ss4_%d" % i, [128, 2], F32) for i in range(3)])
        junk = Rot([k.sb(es, "junk4_%d" % i, [128, D], BF16) for i in range(2)])
        for sg in segs:
            h_ = hT.nxt()
            for c in range(0, T + 2, 1024):
                e = min(T + 2, c + 1024)
                k.dma(V(h_, h_[:, :, c:e]), D_(sg["h2t"][:, :, c:e]))
            if sg.get("vflag") is not None:
                k.dma(V(vf4), D_(sg["vflag"]))
                k.ts(V(h_, h_[:, :, 0:1]), V(h_, h_[:, :, 0:1]), V(vf4, vf4[:, 0:1]), ALU.mult)
                k.ts(V(h_, h_[:, :, T + 1:T + 2]), V(h_, h_[:, :, T + 1:T + 2]), V(vf4, vf4[:, 1:2]), ALU.mult)
            for v0 in range(0, T, FB):
                bw = min(FB, T - v0)
                a_ = aT.nxt()
                for m in range(FC):
                    w_ = wg.nxt()
                    k.dma(V(w_), D_(wg_scr[m]))
                    g_, u_ = pg.nxt(), pu.nxt()
                    for kc in range(KC):
                        k.mm(V(g_, g_[:, 0:bw + 2]), V(w_, w_[:, kc, :]), V(h_, h_[:, kc, v0:v0 + bw + 2]),
                             start=(kc == 0), stop=(kc == KC - 1))
                    for kc in range(KC):
                        k.mm(V(u_, u_[:, 0:bw]), V(Wu, Wu[:, kc, m * 128:(m + 1) * 128]), V(h_, h_[:, kc, v0 + 1:v0 + 1 + bw]),
                             start=(kc == 0), stop=(kc == KC - 1))
                    c_ = cv.nxt()
                    k.ts(V(c_, c_[:, 0:bw]), V(g_, g_[:, 0:bw]), V(cw3, cw3[:, 0, m:m + 1]), ALU.mult)
                    for tap in (1, 2):
                        k.op("dve", lambda hh, c_=c_, g_=g_, tap=tap, m=m, bw=bw: hh.scalar_tensor_tensor(
                            out=c_[:, 0:bw], in0=g_[:, tap:tap + bw], scalar=cw3[:, tap, m:m + 1], in1=c_[:, 0:bw],
                            op0=ALU.mult, op1=ALU.add), reads=[g_, cw3, c_], writes=[c_])
                    s_ = sg_.nxt()
                    k.act(V(s_, s_[:, 0:bw]), V(c_, c_[:, 0:bw]), AF.Silu, bias=V(bg, bg[:, m:m + 1]))
                    k.tt(V(a_, a_[:, m, 0:bw]), V(s_, s_[:, 0:bw]), V(u_, u_[:, 0:bw]), ALU.mult)
                for i0_ in range(0, bw, 128):
                    rows = min(128, bw - i0_)
                    r0 = v0 + i0_
                    x_ = x1.nxt()
                    k.dma(V(x_, x_[0:rows, :]), D_(sg["x1"][r0:r0 + rows, :]))
                    p_ = pd.nxt()
                    for n in range(2):
                        for m in range(FC):
                            k.mm(V(p_, p_[0:rows, n * 512:(n + 1) * 512]), V(a_, a_[:, m, i0_:i0_ + rows]),
                                 V(Wd, Wd[:, m, n * 512:(n + 1) * 512]), start=(m == 0), stop=(m == FC - 1))
                    k.tt(V(x_, x_[0:rows, :]), V(p_, p_[0:rows, :]), V(x_, x_[0:rows, :]), ALU.add)
                    s2, jk = ss.nxt(), junk.nxt()
                    k.act(V(jk, jk[0:rows, :]), V(x_, x_[0:rows, :]), AF.Square, scale=1.0 / 32.0, accum=V(s2, s2[0:rows, 0:1]))
                    k.act(V(s2, s2[0:rows, 1:2]), V(s2, s2[0:rows, 0:1]), AF.Sqrt, bias=V(eps, eps[0:rows, :]))
                    k.recip(V(s2, s2[0:rows, 1:2]), V(s2, s2[0:rows, 1:2]))
                    k.act(V(x_, x_[0:rows, :]), V(x_, x_[0:rows, :]), AF.Copy, scale=V(s2, s2[0:rows, 1:2]))
                    k.tt(V(x_, x_[0:rows, :]), V(x_, x_[0:rows, :]), V(gain, gain[0:rows, :]), ALU.mult, eng="pool")
                    k.dma(D_(sg["out"][r0:r0 + rows, :]), V(x_, x_[0:rows, :]), eng="pool")
        k.barrier()


def build_program(cfg, shapes, cst_arrays):
    nc = bass.Bass("TRN2", target_bir_lowering=False)
    T, NSEG, SP, LS, NCs = cfg.T, cfg.NSEG, cfg.SP, cfg.LS, cfg.NC
    W = wviews(declare_weights(nc, shapes))
    C = {n: nc.dram_tensor(n, list(a.shape), F32 if a.dtype == np.float32 else BF16, kind="ExternalInput").ap()
         for n, a in cst_arrays.items()}
    x_own = nc.dram_tensor("x_own", [NSEG * T, D], F32, kind="ExternalInput").ap()
    x_sg = nc.dram_tensor("x_sg", [LS, D], F32, kind="ExternalInput").ap()
    y_own = nc.dram_tensor("y_own", [NSEG * T, D], F32, kind="ExternalOutput").ap()

    def scr(name, shape, dt):
        return nc.dram_tensor(name, list(shape), dt, kind="Internal").ap()
    QT = scr("QT", [NSEG, H, 96, T], BF16)
    KT = scr("KT", [max(SP, 1), H, 96, T], BF16)
    VA = scr("VA", [max(SP, 1), T, H, 128], BF16)
    KTS = scr("KTS", [H, 96, LS], BF16)
    VAS = scr("VAS", [LS, H, 128], BF16)
    QTD = scr("QTD", [H, 96, T], BF16)
    KTD = scr("KTD", [H, 96, T], BF16)
    VAD = scr("VAD", [T, H, 128], BF16)
    AT = scr("AT", [NSEG, D, T], BF16)
    X1 = scr("X1", [NSEG * T, D], F32)
    H2T = scr("H2T", [NSEG, 128, 8, T + 2], BF16)
    WG = scr("WG", [FC, 128, 8, 128], BF16)
    with ExitStack() as es:
        k = K(nc, es)
        segs = []
        for s_ in range(SP):
            segs.append(dict(x=x_own[s_ * T:(s_ + 1) * T, :], cos=C["cosp"], sin=C["sinp"], qt=QT[s_], kt=KT[s_], va=VA[s_]))
        segs.append(dict(x=x_own[SP * T:(SP + 1) * T, :], cos=C["coso"], sin=C["sino"], qt=QT[SP], kt=KTD, va=VAD))
        for c in range(NCs):
            segs.append(dict(x=x_sg[c * T:(c + 1) * T, :], cos=C["cosg"][:, c * T:(c + 1) * T], sin=C["sing"][:, c * T:(c + 1) * T],
                             qt=QTD, kt=KTS[:, :, c * T:(c + 1) * T], va=VAS[c * T:(c + 1) * T, :, :]))
        phase_p1a(k, cfg, W, C, segs)
        jobs = []
        for s_ in range(NSEG):
            for hh in range(H):
                if s_ < SP:
                    jobs.append(dict(qt=QT[s_, hh], kt=KT[s_, hh], va=VA[s_, :, hh, :], at=AT[s_, hh * 64:(hh + 1) * 64, :], Tq=T, Tk=T))
                else:
                    jobs.append(dict(qt=QT[s_, hh], kt=KTS[hh], va=VAS[:, hh, :], at=AT[s_, hh * 64:(hh + 1) * 64, :], Tq=T, Tk=LS))
        phase_p2(k, cfg, jobs)
        segs3 = [dict(x=x_own[s_ * T:(s_ + 1) * T, :], mix=[AT[s_]], x1=X1[s_ * T:(s_ + 1) * T, :], h2t=H2T[s_]) for s_ in range(NSEG)]
        phase_p3(k, cfg, W, C, segs3, nkc=8)
        segs4 = [dict(h2t=H2T[s_], x1=X1[s_ * T:(s_ + 1) * T, :], out=y_own[s_ * T:(s_ + 1) * T, :]) for s_ in range(NSEG)]
        phase_p4(k, cfg, W, C, WG, segs4)
        n_ops = k.n_ops
        k.emit()
    return nc, n_ops


def run_cfg(cfg, inputs, x_prompt, x_sample):
    T, SP, NCs = cfg.T, cfg.SP, cfg.NC
    cst = host_consts()
    cst["cosp"], cst["sinp"] = rope_tables(np.arange(T))
    cst["cosg"], cst["sing"] = rope_tables(np.arange(cfg.LS))
    cst["coso"], cst["sino"] = rope_tables(np.arange(T))
    shapes = {n: inputs[n].shape for n in WNAMES}
    nc, n_ops = build_program(cfg, shapes, cst)
    in_maps = []
    for c in range(NCs):
        m = {n: np.ascontiguousarray(inputs[n], dtype=np.float32) for n in WNAMES}
        m.update(cst)
        co, so = rope_tables(np.arange(c * T, (c + 1) * T))
        m["coso"], m["sino"] = co, so
        parts = [x_prompt[c * SP + s_] for s_ in range(SP)] + [x_sample[c * T:(c + 1) * T]]
        m["x_own"] = np.ascontiguousarray(np.concatenate(parts, 0), dtype=np.float32)
        m["x_sg"] = np.ascontiguousarray(x_sample, dtype=np.float32)
        in_maps.append(m)
    res = run_bass_kernel_spmd(nc, in_maps, core_ids=list(range(NCs)))
    yp = np.zeros((NCs * SP, T, D), np.float32)
    ys = np.zeros((cfg.LS, D), np.float32)
    for c in range(NCs):
        y = res.results[c]["y_own"]
        for s_ in range(SP):
            yp[c * SP + s_] = y[s_ * T:(s_ + 1) * T]
        ys[c * T:(c + 1) * T] = y[SP * T:(SP + 1) * T]
    return yp, ys


def kernel(**inputs):
    inputs = {n: np.asarray(v) for n, v in inputs.items()}
    cfg = Cfg(8, 4, 2048)
    yp, ys = run_cfg(cfg, inputs, inputs["x_prompt"], inputs["x_sample"][0])
    return yp, ys[None]


def phase_p1c(k, cfg, W, C, segs):
    maxn = max(sg["n"] for sg in segs)
    with ExitStack() as es:
        Wx = k.sb(es, "WxBC", [128, 8, 3, 256], BF16)
        Wx1 = k.sb(es, "Wx1", [128, 8, DSSM], BF16)
        cwx = k.sb(es, "cwx", [128, 3, 8], F32)
        cbx = k.sb(es, "cbx", [128, 8], F32)
        Wz = k.sb(es, "Wz", [128, 8, DSSM], BF16)
        Wdt = k.sb(es, "Wdt", [128, 8, 32], BF16)
        ident = k.sb(es, "ident_c", [128, 128], BF16)
        onesb = k.sb(es, "ones_c", [128, 128], BF16)
        cst = k.sb(es, "cst_c", [128, 7, 128], F32)
        cbb = k.sb(es, "cbb", [1, DXBC], BF16)
        cb32 = k.sb(es, "cb32", [1, DXBC], F32)
        cbf = k.sb(es, "cbf", [64, 4], F32)
        dtb = k.sb(es, "dtb", [128, 32], F32)
        Abc = k.sb(es, "Abc", [128, 32], F32)
        eps = k.sb(es, "eps_c", [128, 1], F32)
        k.dma(V(ident), D_(C["identb"]))
        k.dma(V(onesb), D_(C["onesb"]))
        k.dma(V(cst), D_(C["cst32"]))
        k.dma(V(cb32), D_(W["conv_b"].rearrange("(o c) -> o c", o=1)))
        k.cp(V(cbb), V(cb32))
        k.dma(V(cbf), D_(W["conv_b"][1024:1280].rearrange("(i p) -> p i", p=64)), slow=True)
        k.dma(V(cbx), D_(W["conv_b"][0:1024].rearrange("(c p) -> p c", p=128)), slow=True)
        for tap in range(3):
            k.dma(V(cwx, cwx[:, tap, :]), D_(W["conv_w"][tap][0:1024].rearrange("(c p) -> p c", p=128)), slow=True)
        k.dma(V(dtb, dtb[:, 0:16]), D_(W["dt_bias_f"].partition_broadcast(128)))
        k.dma(V(dtb, dtb[:, 16:32]), D_(W["dt_bias_b"].partition_broadcast(128)))
        k.dma(V(Abc, Abc[:, 0:16]), D_(W["a_log_f"].partition_broadcast(128)))
        k.dma(V(Abc, Abc[:, 16:32]), D_(W["a_log_b"].partition_broadcast(128)))
        k.act(V(Abc), V(Abc), AF.Exp)
        k.ts(V(Abc), V(Abc), -1.0, ALU.mult)
        k.memset(V(eps), EPS)
        es_prep = ExitStack()
        st = prep_stage(k, es_prep)
        n1, win = W["norm1"], W["w_in"]
        for tap in range(3):
            prep_weight(k, st, lambda kc, c0, cw, tap=tap: (Wx, Wx[:, kc, tap, c0:c0 + cw]), win[:, O_XBC + 1024:O_XBC + DXBC], D, 256,
                        row_gain=n1, col_gain=W["conv_w"][tap][1024:1280])
        prep_weight(k, st, lambda kc, c0, cw: (Wx1, Wx1[:, kc, c0:c0 + cw]), win[:, O_XBC:O_XBC + 1024], D, 1024, row_gain=n1)
        prep_weight(k, st, lambda kc, c0, cw: (Wz, Wz[:, kc, c0:c0 + cw]), win[:, O_Z:O_Z + DSSM], D, DSSM, row_gain=n1)
        prep_weight(k, st, lambda kc, c0, cw: (Wdt, Wdt[:, kc, c0:c0 + cw]), win[:, O_DT:O_DT + 32], D, 32, row_gain=n1)
        k.barrier()
        es_prep.close()
        P = {
            "ss": Rot([k.sb(es, "ssc%d" % i, [128, 2], F32) for i in range(3)]),
            "junk": Rot([k.sb(es, "junkc%d" % i, [128, D], BF16) for i in range(2)]),
            "tp": Rot([k.ps(es, "tpc%d" % i, [128, D], BF16) for i in range(1)]),
            "ident": ident, "eps": eps,
        }
        hT = k.sb(es, "hTc", [128, 8, maxn + 2], BF16)
        xt = Rot([k.sb(es, "xtc%d" % i, [128, D], F32) for i in range(3)])
        hn = Rot([k.sb(es, "hnc%d" % i, [128, D], BF16) for i in range(2)])
        big = Rot([k.ps(es, "bigc%d" % i, [128, D], F32) for i in range(1)])
        pf = Rot([k.ps(es, "pfc%d" % i, [128, 512], F32) for i in range(2)])
        xsT = Rot([k.sb(es, "xsT%d" % i, [128, 8, 384], BF16) for i in range(2)])
        cvt = Rot([k.sb(es, "cvt%d" % i, [128, 384], F32) for i in range(3)])
        sml = Rot([k.ps(es, "smlc%d" % i, [128, 512], F32) for i in range(3)])
        xsb = Rot([k.sb(es, "xsb%d" % i, [128, D], BF16) for i in range(4)])
        zsb = Rot([k.sb(es, "zsb%d" % i, [128, D], BF16) for i in range(2)])
        btk = Rot([k.sb(es, "btk%d" % i, [128, 128], BF16) for i in range(3)])
        dts = Rot([k.sb(es, "dts%d" % i, [128, 160], F32) for i in range(3)])
        smo = Rot([k.sb(es, "smo%d" % i, [128, 96], F32) for i in range(2)])
        bw = Rot([k.sb(es, "bw%d" % i, [128, H, 64], BF16) for i in range(6)])
        so = Rot([k.sb(es, "so%d" % i, [64, D], F32) for i in range(2)])
        bco = Rot([k.sb(es, "bco%d" % i, [64, 512], BF16) for i in range(3)])
        for sg in segs:
            n, lite = sg["n"], sg["lite"]
            nch = n // 128
            for ti in range(nch):
                x_ = xt.nxt()
                k.dma(V(x_), D_(sg["x"][ti * 128:(ti + 1) * 128, :]))
                n_ = hn.nxt()
                rmsnorm_tile(k, P, x_, 128, n_, 1.0 / 32.0)
                transpose_tile(k, P, n_, 128, hT, hT[:, :, 1 + ti * 128:1 + (ti + 1) * 128])
            x_ = xt.nxt()
            k.dma(V(x_, x_[0:2, :]), D_(sg["xh"]))
            n_ = hn.nxt()
            rmsnorm_tile(k, P, x_, 2, n_, 1.0 / 32.0)
            tp = P["tp"].nxt()
            for j in range(8):
                k.tr(V(tp, tp[:, j * 128:j * 128 + 2]), V(n_, n_[0:2, j * 128:(j + 1) * 128]), V(ident, ident[0:2, 0:2]))
            tpv = tp[:, :].rearrange("p (j t) -> p j t", t=128)
            k.cp(V(hT, hT[:, :, 0:1]), V(tp, tpv[:, :, 0:1]))
            k.cp(V(hT, hT[:, :, n + 1:n + 2]), V(tp, tpv[:, :, 1:2]))
            def proj(c):
                cb0 = 128 * c
                xT_, j3 = blkT[c // 3], c % 3
                tpx = P["tp"].nxt()
                for cc in range(8):
                    k.tr(V(tpx, tpx[:, cc * 128:(cc + 1) * 128]), V(xT_, xT_[:, cc, j3 * 128:(j3 + 1) * 128]), V(ident))
                xs_ = xsb.nxt()
                if c % 2 == 0:
                    k.act(V(xs_), V(tpx), AF.Copy)
                else:
                    k.cp(V(xs_), V(tpx))
                if not lite:
                    k.dma(D_(sg["xs"][c * 128:(c + 1) * 128, :]), V(xs_), eng="pool")
                    pz = big.nxt()
                    for nb in range(2):
                        for kc in range(KC):
                            k.mm(V(pz, pz[:, nb * 512:(nb + 1) * 512]), V(hT, hT[:, kc, cb0 + 1:cb0 + 129]),
                                 V(Wz, Wz[:, kc, nb * 512:(nb + 1) * 512]), start=(kc == 0), stop=(kc == KC - 1))
                    z_ = zsb.nxt()
                    k.act(V(z_), V(pz), AF.Silu)
                    k.dma(D_(sg["zs"][c * 128:(c + 1) * 128, :]), V(z_), eng="pool")
                pm = sml.nxt()
                i = 0
                for tap in range(3):
                    for kc in range(KC):
                        k.mm(V(pm, pm[:, 0:128]), V(hT, hT[:, kc, cb0 + tap:cb0 + tap + 128]), V(Wx, Wx[:, kc, tap, 0:128]),
                             start=(i == 0), stop=False)
                        i += 1
                k.mm(V(pm, pm[:, 0:128]), V(onesb, onesb[0:1, 0:128]), V(cbb, cbb[0:1, 1024:1152]), start=False, stop=True)
                for kc in range(KC):
                    k.mm(V(pm, pm[:, 128:160]), V(hT, hT[:, kc, cb0 + 1:cb0 + 129]), V(Wdt, Wdt[:, kc, :]),
                         start=(kc == 0), stop=(kc == KC - 1))
                bt_ = btk.nxt()
                k.act(V(bt_), V(pm, pm[:, 0:128]), AF.Silu)
                d_ = dts.nxt()
                k.tt(V(d_, d_[:, 0:32]), V(pm, pm[:, 128:160]), V(dtb), ALU.add)
                return xs_, bt_, d_

            def rest(c, xs_, bt_, d_):
                k.act(V(d_, d_[:, 0:32]), V(d_, d_[:, 0:32]), AF.Exp)
                k.act(V(d_, d_[:, 0:32]), V(d_, d_[:, 0:32]), AF.Ln, bias=1.0)
                k.act(V(d_, d_[:, 32:64]), V(d_, d_[:, 0:32]), AF.Ln)
                sm_ = smo.nxt()
                k.tt(V(sm_, sm_[:, 0:32]), V(d_, d_[:, 0:32]), V(Abc), ALU.mult)
                pc = sml.nxt()
                k.mm(V(pc, pc[:, 0:16]), V(cst, cst[:, 0, :]), V(sm_, sm_[:, 0:16]))
                k.mm(V(pc, pc[:, 16:32]), V(cst, cst[:, 1, :]), V(sm_, sm_[:, 16:32]))
                k.mm(V(pc, pc[:, 32:64]), V(cst, cst[:, 2, :]), V(sm_, sm_[:, 0:32]))
                k.tt(V(sm_, sm_[:, 32:48]), V(d_, d_[:, 32:48]), V(pc, pc[:, 0:16]), ALU.subtract)
                k.tt(V(sm_, sm_[:, 48:64]), V(d_, d_[:, 48:64]), V(pc, pc[:, 16:32]), ALU.add)
                k.cp(V(sm_, sm_[:, 64:96]), V(pc, pc[:, 32:64]))
                k.tt(V(d_, d_[:, 96:112]), V(sm_, sm_[:, 64:80]), V(sm_, sm_[:, 32:48]), ALU.add)
                k.cp(V(d_, d_[:, 112:128]), V(sm_, sm_[:, 48:64]))
                k.act(V(d_, d_[:, 128:160]), V(d_, d_[:, 96:128]), AF.Exp)
                k.dma(D_(sg["sm"][c]), V(sm_), eng="pool")
                btv = bt_[:, :].rearrange("p (g n) -> p g n", g=2).unsqueeze(2).broadcast_to([128, 2, 8, 64])
                bws = []
                for d in range(2):
                    b_ = bw.nxt()
                    wv = d_[:, 128 + 16 * d:144 + 16 * d].rearrange("p (g j) -> p g j", g=2).unsqueeze(3).broadcast_to([128, 2, 8, 64])
                    k.tt(V(b_, b_[:, :, :].rearrange("p (g j) n -> p g j n", g=2)), V(bt_, btv), V(d_, wv), ALU.mult, eng="pool")
                    bws.append(b_)
                return bws

            def restB(c, xs_, bws):
                for d in range(2):
                    b_ = bws[d]
                    s_ = so.nxt()
                    for half in range(2):
                        pS = sml.nxt()
                        for j in range(8):
                            hh = half * 8 + j
                            k.mm(V(pS, pS[0:64, j * 64:(j + 1) * 64]), V(b_, b_[:, hh, :]), V(xs_, xs_[:, hh * 64:(hh + 1) * 64]))
                        k.cp(V(s_, s_[:, half * 512:(half + 1) * 512]), V(pS, pS[0:64, :]))
                    k.dma(D_(sg["sst"][c, d]), V(s_), eng="pool")

            blkT = {}

            def xs_block(b):
                t0b = 384 * b
                nt = min(384, n - t0b)
                xT_ = xsT.nxt()
                for cc in range(8):
                    f_ = pf.nxt()
                    for kc in range(KC):
                        k.mm(V(f_, f_[:, 0:nt + 2]), V(Wx1, Wx1[:, kc, cc * 128:(cc + 1) * 128]), V(hT, hT[:, kc, t0b:t0b + nt + 2]),
                             start=(kc == 0), stop=(kc == KC - 1))
                    t_ = cvt.nxt()
                    k.ts(V(t_, t_[:, 0:nt]), V(f_, f_[:, 0:nt]), V(cwx, cwx[:, 0, cc:cc + 1]), ALU.mult)
                    for tap in (1, 2):
                        k.op("dve", lambda hh_, t_=t_, f_=f_, tap=tap, cc=cc, nt=nt: hh_.scalar_tensor_tensor(
                            out=t_[:, 0:nt], in0=f_[:, tap:tap + nt], scalar=cwx[:, tap, cc:cc + 1], in1=t_[:, 0:nt],
                            op0=ALU.mult, op1=ALU.add), reads=[f_, cwx, t_], writes=[t_])
                    k.act(V(xT_, xT_[:, cc, 0:nt]), V(t_, t_[:, 0:nt]), AF.Silu, bias=V(cbx, cbx[:, cc:cc + 1]))
                blkT[b] = xT_

            nblk = (nch + 2) // 3
            xs_block(0)
            nxt_h = proj(0)
            pend = None
            for c in range(nch):
                cur = nxt_h
                if c % 3 == 0 and c // 3 + 1 < nblk:
                    xs_block(c // 3 + 1)
                if c + 1 < nch:
                    nxt_h = proj(c + 1)
                bws = rest(c, *cur)
                if pend is not None:
                    restB(*pend)
                pend = (c, cur[0], bws)
            restB(*pend)
            if lite:
                continue
            for c0 in range(0, n, 512):
                bwid = min(512, n - c0)
                for idx in range(4):
                    pb_ = sml.nxt()
                    i = 0
                    for tap in range(3):
                        for kc in range(KC):
                            k.mm(V(pb_, pb_[0:64, 0:bwid]), V(Wx, Wx[:, kc, tap, idx * 64:(idx + 1) * 64]),
                                 V(hT, hT[:, kc, c0 + tap:c0 + tap + bwid]), start=(i == 0), stop=(i == 23))
                            i += 1
                    o_ = bco.nxt()
                    k.act(V(o_, o_[:, 0:bwid]), V(pb_, pb_[0:64, 0:bwid]), AF.Silu, bias=V(cbf, cbf[:, idx:idx + 1]))
                    k.dma(D_(sg["bct"][idx, :, c0:c0 + bwid]), V(o_, o_[:, 0:bwid]), eng="pool")
        k.barrier()


def phase_p1b(k, cfg, W, C, segs, glob=None):
    maxn = max(sg["n"] for sg in segs)
    maxc = maxn // 128
    with ExitStack() as es:
        ident = k.sb(es, "ident_b", [128, 128], BF16)
        cst = k.sb(es, "cst_b", [128, 7, 128], F32)
        dsk = k.sb(es, "dsk", [128, H], F32)
        eps = k.sb(es, "eps_b", [128, 1], F32)
        zcol = k.sb(es, "zcol", [128, 1], F32)
        k.dma(V(ident), D_(C["identb"]))
        k.dma(V(cst), D_(C["cst32"]))
        mskb = k.sb(es, "mskb", [128, 2, 128], BF16)
        k.cp(V(mskb), V(cst, cst[:, 4:6, :]))
        Tb = k.sb(es, "Tb", [128, 2, 128], BF16)
        k.cp(V(Tb, Tb[:, 0, :]), V(cst, cst[:, 0, :]))
        k.cp(V(Tb, Tb[:, 1, :]), V(cst, cst[:, 6, :]))
        ahi = k.sb(es, "ahi", [128, maxc, 32], BF16)
        alo = k.sb(es, "alo", [128, maxc, 32], BF16)
        atmp = k.sb(es, "atmp", [128, maxc, 32], F32)
        k.dma(V(dsk), D_(W["d_skip"].partition_broadcast(128)))
        k.memset(V(eps), EPS)
        k.memset(V(zcol), 0.0)
        P = {
            "ss": Rot([k.sb(es, "ssb%d" % i, [128, 2], F32) for i in range(3)]),
            "junk": Rot([k.sb(es, "junkb%d" % i, [128, D], BF16) for i in range(2)]),
            "tp": Rot([k.ps(es, "tpb%d" % i, [128, D], BF16) for i in range(1)]),
            "ident": ident, "eps": eps,
        }
        smb = k.sb(es, "smb", [128, maxc, 96], F32)
        bct = k.sb(es, "bctb", [64, 4, maxn], BF16)
        dall = k.sb(es, "dall", [64, maxc, 32], F32)
        hbin = k.sb(es, "hbin", [64, maxc, D], BF16)
        hb = k.sb(es, "hb", [64, D], F32)
        hf = k.sb(es, "hf", [64, D], F32)
        hfb = Rot([k.sb(es, "hfb%d" % i, [64, D], BF16) for i in range(2)])
        sld = Rot([k.sb(es, "sld%d" % i, [64, D], F32) for i in range(3)])
        sld2 = Rot([k.sb(es, "sld2_%d" % i, [64, D], F32) for i in range(3)])
        xsb = Rot([k.sb(es, "xsB%d" % i, [128, D], BF16) for i in range(2)])
        zsb = Rot([k.sb(es, "zsB%d" % i, [128, D], BF16) for i in range(2)])
        gs = Rot([k.sb(es, "gs%d" % i, [128, 2, 128], F32) for i in range(2)])
        lp = Rot([k.sb(es, "lp%d" % i, [128, 2, 128], F32) for i in range(4)])
        ee = Rot([k.sb(es, "ee%d" % i, [64, 2, 128], F32) for i in range(4)])
        mt = Rot([k.sb(es, "mt%d" % i, [128, 2, 128], BF16) for i in range(4)])
        cp_ = Rot([k.sb(es, "cpp%d" % i, [64, 2, 128], BF16) for i in range(4)])
        y1j = Rot([k.sb(es, "y1j%d" % i, [128, 512], F32) for i in range(2)])
        y1 = Rot([k.sb(es, "y1_%d" % i, [128, D], F32) for i in range(2)])
        xsd = Rot([k.sb(es, "xsd%d" % i, [128, D], BF16) for i in range(2)])
        yn = Rot([k.sb(es, "yn%d" % i, [128, D], BF16) for i in range(2)])
        yT = Rot([k.sb(es, "yT%d" % i, [128, 8, 128], BF16) for i in range(2)])
        gm = k.sb(es, "gmask", [64, 2, 128], F32)
        gsm = k.sb(es, "gsm", [64, 128, 32], F32)
        gd = Rot([k.sb(es, "gd%d" % i, [64, 16], F32) for i in range(6)])
        vfl = k.sb(es, "vfl", [64, 2], F32)
        pY = Rot([k.ps(es, "pY%d" % i, [128, D], F32) for i in range(1)])
        pR = Rot([k.ps(es, "pR%d" % i, [128, 512], F32) for i in range(4)])
        pG = Rot([k.ps(es, "pG%d" % i, [128, 256], F32) for i in range(1)])

        def decay_mul(h_, dv):
            k.tt(V(h_, h_[:, :].rearrange("p (h q) -> p h q", q=64)), V(h_, h_[:, :].rearrange("p (h q) -> p h q", q=64)),
                 (dv[0], dv[1].unsqueeze(2).broadcast_to([64, H, 64])), ALU.mult)

        for sg in segs:
            n = sg["n"]
            nch = n // 128
            if sg["init"]:
                NG = glob["NG"]
                k.dma(V(gm, gm[:, :, 0:NG]), D_(glob["mask"]))
                k.dma(V(gsm, gsm[:, 0:NG, :]), D_(glob["sm"][:, 0:64, 64:96].rearrange("c p f -> p c f")), slow=True)
                k.memset(V(hf), 0.0)
                k.memset(V(hb), 0.0, eng="pool")
                for step in range(NG):
                    for d, h_, eng_ in ((0, hf, "dve"), (1, hb, "dve")):
                        kk = step if d == 0 else NG - 1 - step
                        g_ = gd.nxt()
                        k.act(V(g_), V(gsm, gsm[:, kk, 16 * d:16 * d + 16]), AF.Exp, scale=V(gm, gm[:, d, kk:kk + 1]))
                        hv = h_[:, :].rearrange("p (h q) -> p h q", q=64)
                        k.tt(V(h_, hv), V(h_, hv), (g_, g_[:, :].unsqueeze(2).broadcast_to([64, H, 64])), ALU.mult, eng=eng_)
                        s_ = (sld if d == 0 else sld2).nxt()
                        k.dma(V(s_), D_(glob["sst"][kk, d]))
                        if eng_ == "dve":
                            k.op(eng_, lambda hh, s_=s_, h_=h_, d=d, kk=kk: hh.scalar_tensor_tensor(
                                out=h_[:, :], in0=s_[:, :], scalar=gm[:, d, kk:kk + 1], in1=h_[:, :], op0=ALU.mult, op1=ALU.add),
                                reads=[s_, gm, h_], writes=[h_])
                        else:
                            k.ts(V(s_), V(s_), V(gm, gm[:, d, kk:kk + 1]), ALU.mult, eng=eng_)
                            k.tt(V(h_), V(h_), V(s_), ALU.add, eng=eng_)
            else:
                k.memset(V(hf), 0.0)
                k.memset(V(hb), 0.0)
            if sg["vflag"] is not None:
                k.dma(V(vfl), D_(sg["vflag"]))
            k.dma(V(smb, smb[:, 0:nch, :]), D_(sg["sm"].rearrange("c p f -> p c f")))
            k.dma(V(bct, bct[:, :, 0:n]), D_(sg["bct"].rearrange("i p t -> p i t")))
            k.act(V(dall, dall[:, 0:nch, :]), V(smb, smb[0:64, 0:nch, 64:96]), AF.Exp)
            k.cp(V(ahi, ahi[:, 0:nch, :]), V(smb, smb[:, 0:nch, 0:32]))
            k.tt(V(atmp, atmp[:, 0:nch, :]), V(smb, smb[:, 0:nch, 0:32]), V(ahi, ahi[:, 0:nch, :]), ALU.subtract)
            k.cp(V(alo, alo[:, 0:nch, :]), V(atmp, atmp[:, 0:nch, :]))

            def load_state(kk, d):
                s_ = sld.nxt()
                k.dma(V(s_), D_(sg["sst"][kk, d]))
                ne = sg.get("nedge", 1)
                if sg["vflag"] is not None and (kk < ne or kk >= nch - ne):
                    col = 0 if kk < ne else 1
                    k.ts(V(s_), V(s_), V(vfl, vfl[:, col:col + 1]), ALU.mult)
                return s_

            for kk in range(nch - 1, -1, -1):
                k.act(V(hbin, hbin[:, kk, :]), V(hb), AF.Copy)
                if kk > 0:
                    s_ = load_state(kk, 1)
                    decay_mul(hb, V(dall, dall[:, kk, 16:32]))
                    k.tt(V(hb), V(hb), V(s_), ALU.add)
            its = [(kk, d, pr) for kk in range(nch) for d in range(2) for pr in range(8)]
            ctx = {}
            rbuf = {}

            def emit_R(i):
                kk, d, pr = its[i]
                t0 = kk * 128
                if d == 0 and pr == 0:
                    xs_, z_ = xsb.nxt(), zsb.nxt()
                    k.dma(V(xs_), D_(sg["xs"][t0:t0 + 128, :]))
                    k.dma(V(z_), D_(sg["zs"][t0:t0 + 128, :]))
                    g_ = pG.nxt()
                    for g in range(2):
                        k.mm(V(g_, g_[:, g * 128:(g + 1) * 128]), V(bct, bct[:, g, t0:t0 + 128]), V(bct, bct[:, 2 + g, t0:t0 + 128]))
                    gs_ = gs.nxt()
                    k.cp(V(gs_), V(g_, g_[:, :].rearrange("p (g t) -> p g t", g=2)))
                    xd_ = xsd.nxt()
                    k.tt(V(xd_, xd_[:, :].rearrange("p (h q) -> p h q", q=64)), V(xs_, xs_[:, :].rearrange("p (h q) -> p h q", q=64)),
                         V(dsk, dsk[:, :].unsqueeze(2).broadcast_to([128, H, 64])), ALU.mult, eng="pool")
                    ctx[kk] = dict(xs=xs_, z=z_, gs=gs_, xd=xd_)
                r_ = pR.nxt()
                for j in range(2):
                    hh = 2 * pr + j
                    hcol = ahi[:, kk, 16 * d + hh:16 * d + hh + 1].broadcast_to([128, 128])
                    lcol = alo[:, kk, 16 * d + hh:16 * d + hh + 1].broadcast_to([128, 128])
                    rm = r_[:, j * 128:(j + 1) * 128]
                    ru = r_[:, 256 + j * 128:256 + (j + 1) * 128]
                    k.mm(V(r_, rm), V(ident), V(mskb, mskb[:, d, :]), start=True, stop=False)
                    k.mm(V(r_, rm), V(ahi, hcol), V(Tb, Tb[:, d, :]), start=False, stop=False)
                    k.mm(V(r_, rm), V(alo, lcol), V(Tb, Tb[:, d, :]), start=False, stop=True)
                    k.mm(V(r_, ru), V(ahi, hcol), V(Tb, Tb[:, d, :]), start=True, stop=False)
                    k.mm(V(r_, ru), V(alo, lcol), V(Tb, Tb[:, d, :]), start=False, stop=True)
                rbuf[i] = r_

            emit_R(0)
            if len(its) > 1:
                emit_R(1)
            for i, (kk, d, pr) in enumerate(its):
                t0 = kk * 128
                if i + 2 < len(its):
                    emit_R(i + 2)
                cx = ctx[kk]
                xs_, z_, gs_ = cx["xs"], cx["z"], cx["gs"]
                if d == 0 and pr == 0:
                    hfb_ = hfb.nxt()
                    k.cp(V(hfb_), V(hf))
                    y_ = pY.nxt()
                    for nb in range(2):
                        k.mm(V(y_, y_[:, nb * 512:(nb + 1) * 512]), V(ident), V(cx["xd"], cx["xd"][:, nb * 512:(nb + 1) * 512]), start=True, stop=False)
                    cx["hfb"], cx["y"] = hfb_, y_
                hfb_, y_ = cx["hfb"], cx["y"]
                r_ = rbuf.pop(i)
                lp_, ee_ = lp.nxt(), ee.nxt()
                for j in range(2):
                    hh = 2 * pr + j
                    k.act(V(lp_, lp_[:, j, :]), V(r_, r_[:, j * 128:(j + 1) * 128]), AF.Exp,
                          bias=V(smb, smb[:, kk, 32 + 16 * d + hh:32 + 16 * d + hh + 1]))
                    eb = V(zcol, zcol[0:64, 0:1]) if d == 0 else V(smb, smb[0:64, kk, 80 + hh:80 + hh + 1])
                    k.act(V(ee_, ee_[:, j, :]), V(r_, r_[0:64, 256 + j * 128:256 + (j + 1) * 128]), AF.Exp, bias=eb)
                g = pr // 4
                mt_, c_ = mt.nxt(), cp_.nxt()
                k.tt(V(mt_), V(lp_), V(gs_, gs_[:, g:g + 1, :].broadcast_to([128, 2, 128])), ALU.mult)
                k.tt(V(c_), V(ee_), V(bct, bct[:, 2 + g:3 + g, t0:t0 + 128].broadcast_to([64, 2, 128])), ALU.mult)
                hst = hfb_ if d == 0 else hbin
                for j in range(2):
                    hh = 2 * pr + j
                    ysl = y_[:, hh * 64:(hh + 1) * 64]
                    k.mm(V(y_, ysl), V(mt_, mt_[:, j, :]), V(xs_, xs_[:, hh * 64:(hh + 1) * 64]), start=False, stop=False)
                    hs_ap = hfb_[:, hh * 64:(hh + 1) * 64] if d == 0 else hbin[:, kk, hh * 64:(hh + 1) * 64]
                    k.mm(V(y_, ysl), V(c_, c_[:, j, :]), V(hst, hs_ap), start=False, stop=(d == 1 and hh in (7, 15)))
                if not (d == 1 and pr == 7):
                    continue
                s_ = load_state(kk, 0)
                decay_mul(hf, V(dall, dall[:, kk, 0:16]))
                k.tt(V(hf), V(hf), V(s_), ALU.add)
                a_ = y1.nxt()
                k.tt(V(a_), V(y_), V(z_), ALU.mult)
                s2, n_ = P["ss"].nxt(), yn.nxt()
                s3 = P["ss"].nxt()
                for g in range(2):
                    jk = y1j.nxt()
                    k.op("dve", lambda hh_, a_=a_, jk=jk, s2=s2, g=g: hh_.scalar_tensor_tensor(
                        out=jk[:, 0:512], in0=a_[:, g * 512:(g + 1) * 512], scalar=1.0 / 512.0, in1=a_[:, g * 512:(g + 1) * 512],
                        op0=ALU.mult, op1=ALU.mult, accum_out=s2[:, g:g + 1]), reads=[a_], writes=[jk, s2])
                k.act(V(s3), V(s2), AF.Ln, bias=V(eps))
                k.act(V(s3), V(s3), AF.Exp, scale=-0.5)
                for g in range(2):
                    k.ts(V(n_, n_[:, g * 512:(g + 1) * 512]), V(a_, a_[:, g * 512:(g + 1) * 512]), V(s3, s3[:, g:g + 1]), ALU.mult)
                t_ = yT.nxt()
                transpose_tile(k, P, n_, 128, t_, t_[:, :, :])
                k.dma(D_(sg["yt"].rearrange("(c p) t -> p c t", p=128)[:, :, t0:t0 + 128]), V(t_), eng="pool")
                del ctx[kk]
        k.barrier()


EXT = 128


def build_program(cfg, shapes, cst_arrays):
    nc = bass.Bass("TRN2", target_bir_lowering=False)
    T, NSEG, SP, LS, NCs = cfg.T, cfg.NSEG, cfg.SP, cfg.LS, cfg.NC
    NS = T + 2 * EXT
    NSP = ((NS + 511) // 512) * 512
    NG = LS // 128
    W = wviews(declare_weights(nc, shapes))
    C = {n: nc.dram_tensor(n, list(a.shape), F32 if a.dtype == np.float32 else BF16, kind="ExternalInput").ap()
         for n, a in cst_arrays.items()}
    x_own = nc.dram_tensor("x_own", [SP * T + NSP, D], F32, kind="ExternalInput").ap()
    xh_own = nc.dram_tensor("xh_own", [NSEG, 2, D], F32, kind="ExternalInput").ap()
    x_sg = nc.dram_tensor("x_sg", [LS, D], F32, kind="ExternalInput").ap()
    xh_sg = nc.dram_tensor("xh_sg", [NCs, 2, D], F32, kind="ExternalInput").ap()
    gmask = nc.dram_tensor("gmask", [64, 2, NG], F32, kind="ExternalInput").ap()
    vflag = nc.dram_tensor("vflag", [128, 2], F32, kind="ExternalInput").ap()
    y_own = nc.dram_tensor("y_own", [NSEG * T, D], F32, kind="ExternalOutput").ap()

    def scr(name, shape, dt):
        return nc.dram_tensor(name, list(shape), dt, kind="Internal").ap()
    seg_n = [T] * SP + [NS]
    seg_off = [s_ * T for s_ in range(SP)] + [SP * T]
    QT = [scr("QT%d" % s_, [H, 96, (NSP if s_ == SP else seg_n[s_])], BF16) for s_ in range(NSEG)]
    KT = scr("KT", [max(SP, 1), H, 96, T], BF16)
    VA = scr("VA", [max(SP, 1), T, H, 128], BF16)
    KTS = scr("KTS", [H, 96, LS], BF16)
    VAS = scr("VAS", [LS, H, 128], BF16)
    AT = [scr("AT%d" % s_, [D, seg_n[s_]], BF16) for s_ in range(NSEG)]
    YT = [scr("YT%d" % s_, [D, seg_n[s_]], BF16) for s_ in range(NSEG)]
    XS = [scr("XS%d" % s_, [seg_n[s_], D], BF16) for s_ in range(NSEG)]
    ZS = [scr("ZS%d" % s_, [seg_n[s_], D], BF16) for s_ in range(NSEG)]
    SST = [scr("SST%d" % s_, [seg_n[s_] // 128, 2, 64, D], F32) for s_ in range(NSEG)]
    SM = [scr("SM%d" % s_, [seg_n[s_] // 128, 128, 96], F32) for s_ in range(NSEG)]
    BCT = [scr("BCT%d" % s_, [4, 64, seg_n[s_]], BF16) for s_ in range(NSEG)]
    SSTG = scr("SSTG", [NG, 2, 64, D], F32)
    SMG = scr("SMG", [NG, 128, 96], F32)
    X1 = [scr("X1_%d" % s_, [seg_n[s_], D], F32) for s_ in range(NSEG)]
    H2T = [scr("H2T%d" % s_, [128, 8, seg_n[s_] + 2], BF16) for s_ in range(NSEG)]
    WG = scr("WG", [FC, 128, 8, 128], BF16)
    with ExitStack() as es:
        k = K(nc, es)
        xo = [x_own[seg_off[s_]:seg_off[s_] + seg_n[s_], :] for s_ in range(NSEG)]
        segs = []
        for s_ in range(SP):
            segs.append(dict(x=xo[s_], n=T, cos=C["cosp"], sin=C["sinp"], qt=QT[s_], kt=KT[s_], va=VA[s_]))
        segs.append(dict(x=x_own[SP * T:SP * T + NSP, :], n=NSP, cos=C["coso"], sin=C["sino"], qt=QT[SP], do_kv=False))
        for c in range(NCs):
            segs.append(dict(x=x_sg[c * T:(c + 1) * T, :], n=T, cos=C["cosg"][:, c * T:(c + 1) * T], sin=C["sing"][:, c * T:(c + 1) * T],
                             do_q=False, kt=KTS[:, :, c * T:(c + 1) * T], va=VAS[c * T:(c + 1) * T, :, :]))
        phase_p1a(k, cfg, W, C, segs)
        segc = [dict(x=xo[s_], xh=xh_own[s_], n=seg_n[s_], lite=False, xs=XS[s_], zs=ZS[s_], sst=SST[s_], sm=SM[s_], bct=BCT[s_])
                for s_ in range(NSEG)]
        for c in range(NCs):
            segc.append(dict(x=x_sg[c * T:(c + 1) * T, :], xh=xh_sg[c], n=T, lite=True,
                             sst=SSTG[c * (T // 128):(c + 1) * (T // 128)], sm=SMG[c * (T // 128):(c + 1) * (T // 128)]))
        phase_p1c(k, cfg, W, C, segc)
        segb = []
        for s_ in range(NSEG):
            segb.append(dict(n=seg_n[s_], xs=XS[s_], zs=ZS[s_], sst=SST[s_], sm=SM[s_], bct=BCT[s_], yt=YT[s_],
                             init=(s_ == SP), vflag=(vflag[0:64, :] if s_ == SP else None), nedge=EXT // 128))
        phase_p1b(k, cfg, W, C, segb, glob=dict(sst=SSTG, sm=SMG, mask=gmask, NG=NG))
        jobs = []
        for s_ in range(NSEG):
            for hh in range(H):
                if s_ < SP:
                    jobs.append(dict(qt=QT[s_][hh], kt=KT[s_, hh], kr=KT[s_, 0, 64:96, :], va=VA[s_, :, hh, :], at=AT[s_][hh * 64:(hh + 1) * 64, :], Tq=T, Tk=T))
                else:
                    jobs.append(dict(qt=QT[s_][hh][:, 0:NS], kt=KTS[hh], kr=KTS[0, 64:96, :], va=VAS[:, hh, :], at=AT[s_][hh * 64:(hh + 1) * 64, :], Tq=NS, Tk=LS))
        phase_p2(k, cfg, jobs)
        segs3 = [dict(x=xo[s_], n=seg_n[s_], mix=[AT[s_], YT[s_]], x1=X1[s_], h2t=H2T[s_]) for s_ in range(NSEG)]
        phase_p3(k, cfg, W, C, segs3, nkc=16)
        segs4 = []
        for s_ in range(NSEG):
            if s_ < SP:
                segs4.append(dict(h2t=H2T[s_], x1=X1[s_], out=y_own[s_ * T:(s_ + 1) * T, :]))
            else:
                segs4.append(dict(h2t=H2T[s_][:, :, EXT:EXT + T + 2], x1=X1[s_][EXT:EXT + T, :], out=y_own[s_ * T:(s_ + 1) * T, :],
                                  vflag=vflag))
        phase_p4(k, cfg, W, C, WG, segs4)
        n_ops = k.n_ops
        k.emit()
    return nc, n_ops


def run_cfg(cfg, inputs, x_prompt, x_sample):
    T, SP, NCs, LS = cfg.T, cfg.SP, cfg.NC, cfg.LS
    NS = T + 2 * EXT
    NSP = ((NS + 511) // 512) * 512
    NG = LS // 128
    cst = host_consts()
    cst["cosp"], cst["sinp"] = rope_tables(np.arange(T))
    cst["cosg"], cst["sing"] = rope_tables(np.arange(LS))
    cst["coso"], cst["sino"] = rope_tables(np.arange(NSP))
    shapes = {n: inputs[n].shape for n in WNAMES}
    nc, n_ops = build_program(cfg, shapes, cst)
    xs32 = np.ascontiguousarray(x_sample, dtype=np.float32)
    xpad = np.zeros((LS + 2 * EXT + 2, D), np.float32)
    xpad[EXT + 1:EXT + 1 + LS] = xs32
    zero_row = np.zeros((D,), np.float32)
    xh_sg = np.stack([np.stack([xs32[c * T - 1] if c > 0 else zero_row, xs32[(c + 1) * T] if c < NCs - 1 else zero_row])
                      for c in range(NCs)])
    in_maps = []
    for c in range(NCs):
        m = {n: np.ascontiguousarray(inputs[n], dtype=np.float32) for n in WNAMES}
        m.update(cst)
        lo = c * T - EXT
        co, so = rope_tables(np.arange(lo, lo + NSP))
        m["coso"], m["sino"] = co, so
        parts = [x_prompt[c * SP + s_] for s_ in range(SP)] + [xpad[lo + EXT + 1:lo + EXT + 1 + NS], np.zeros((NSP - NS, D), np.float32)]
        m["x_own"] = np.ascontiguousarray(np.concatenate(parts, 0), dtype=np.float32)
        xh = np.zeros((SP + 1, 2, D), np.float32)
        xh[SP, 0] = xpad[lo + EXT]
        xh[SP, 1] = xpad[lo + EXT + 1 + NS]
        m["xh_own"] = xh
        m["x_sg"] = xs32
        m["xh_sg"] = xh_sg
        kk = np.arange(NG)
        gm = np.zeros((64, 2, NG), np.float32)
        gm[:, 0, :] = (kk < (lo // 128 if lo >= 0 else -((-lo) // 128)))[None, :]
        gm[:, 1, :] = (kk >= (lo + NS) // 128)[None, :]
        m["gmask"] = gm
        vf = np.ones((128, 2), np.float32)
        if c == 0:
            vf[:, 0] = 0.0
        if c == NCs - 1:
            vf[:, 1] = 0.0
        m["vflag"] = vf
        in_maps.append(m)
    res = run_bass_kernel_spmd(nc, in_maps, core_ids=list(range(NCs)))
    yp = np.zeros((NCs * SP, T, D), np.float32)
    ys = np.zeros((LS, D), np.float32)
    for c in range(NCs):
        y = res.results[c]["y_own"]
        for s_ in range(SP):
            yp[c * SP + s_] = y[s_ * T:(s_ + 1) * T]
        ys[c * T:(c + 1) * T] = y[SP * T:(SP + 1) * T]
    return yp, ys
```

```python
import os
from contextlib import ExitStack
import numpy as np
import ml_dtypes
import concourse.bass as bass
import concourse.mybir as mybir
from concourse.bass_utils import run_bass_kernel_spmd

F32 = mybir.dt.float32
BF16 = mybir.dt.bfloat16
AF = mybir.ActivationFunctionType
ALU = mybir.AluOpType

D = 1024
KC = 8
H = 16
QL, KVL, RO = 384, 256, 32
DSSM, DXBC, NST = 1024, 1280, 64
DIN = 3008
DFF = 2816
FC = 22
EPS = 1e-6
NEG = -30000.0
O_Q, O_CKV, O_KR, O_Z, O_XBC, O_DT = 0, 384, 640, 672, 1696, 2976
SCALE = 96.0 ** -0.5


class Buf:
    def __init__(self, name, t, is_dram=False):
        self.name = name
        self.t = t
        self.is_dram = is_dram
        self.w = None
        self.r = []
        self.dsem = None
        self.dcnt = 0

    def __getitem__(self, idx):
        return self.t[idx]


class K:
    ENG = ("pe", "act", "dve", "pool", "sp")

    def __init__(self, nc, es, n_dma_sems=46, n_sw_sems=50):
        self.nc = nc
        self.q = {e: [] for e in self.ENG}
        self.cnt = {e: 0 for e in self.ENG}
        self.waited = {e: {} for e in self.ENG}
        self.sem = {e: es.enter_context(nc.semaphore("s_" + e)) for e in ("pe", "act", "dve", "pool")}
        self.dma_pool = [es.enter_context(nc.semaphore("d%d" % i)) for i in range(n_dma_sems + n_sw_sems)]
        self.dma_free = list(range(n_dma_sems))
        self.sw_free = list(range(n_dma_sems, n_dma_sems + n_sw_sems))
        self.dma_val = [0] * (n_dma_sems + n_sw_sems)
        self.live_sw = []
        self.live = []
        self.n_ops = 0

    def sb(self, es, name, shape, dt):
        self.uid = getattr(self, "uid", 0) + 1
        name = "%s_u%d" % (name, self.uid)
        return Buf(name, es.enter_context(self.nc.sbuf_tensor(name, list(shape), dt)))

    def ps(self, es, name, shape, dt):
        self.uid = getattr(self, "uid", 0) + 1
        name = "%s_u%d" % (name, self.uid)
        b = Buf(name, es.enter_context(self.nc.psum_tensor(name, list(shape), dt)))
        b.is_psum = True
        return b

    def _need(self, eng, dep, out):
        kind, s, v = dep
        if kind == "pe" and eng == "pe":
            return
        key = (kind, s)
        if self.waited[eng].get(key, -1) >= v:
            return
        self.waited[eng][key] = v
        out.append(dep)

    def op(self, eng, fn, reads=(), writes=(), dma=False):
        deps = []
        for b in reads:
            if b.w is not None:
                self._need(eng, b.w, deps)
            if getattr(b, "is_psum", False):
                for r in b.r:
                    if r[0] != eng:
                        self._need(eng, r, deps)
        for b in writes:
            if b.w is not None and (dma or b.w[0] != eng):
                self._need(eng, b.w, deps)
            for r in b.r:
                if dma or r[0] != eng:
                    self._need(eng, r, deps)
        if dma:
            owner = None
            for b in list(writes) + list(reads):
                if not b.is_dram:
                    owner = b
                    break
            if owner is None:
                owner = (list(writes) + list(reads))[0]
            if eng == "pool":
                if getattr(owner, "swsem", None) is None:
                    owner.swsem = self.sw_free.pop()
                    owner.swcnt = 0
                    self.live_sw.append(owner)
                owner.swcnt += 16
                tok = ("dma", owner.swsem, owner.swcnt)
                semh, val = self.dma_pool[owner.swsem], 16
            else:
                if owner.dsem is None:
                    owner.dsem = self.dma_free.pop()
                    owner.dcnt = self.dma_val[owner.dsem]
                    self.live.append(owner)
                owner.dcnt += 16
                tok = ("dma", owner.dsem, owner.dcnt)
                semh, val = self.dma_pool[owner.dsem], 16
        else:
            self.cnt[eng] += 1
            tok = (eng, None, self.cnt[eng])
            semh, val = self.sem[eng], 1
        self.q[eng].append((deps, fn, semh, val))
        self.n_ops += 1
        for b in reads:
            b.r.append(tok)
            if len(b.r) > 64:
                b.r = b.r[-64:]
        for b in writes:
            b.w = tok
            b.r = []
        return tok

    def barrier(self):
        toks = [(e, None, self.cnt[e]) for e in ("pe", "act", "dve", "pool") if self.cnt[e]]
        for b in self.live:
            toks.append(("dma", b.dsem, b.dcnt))
        for b in self.live_sw:
            toks.append(("dma", b.swsem, b.swcnt))
        self.live_sw = []
        for e in self.ENG:
            deps = []
            for t in toks:
                if t[0] == e:
                    continue
                self._need(e, t, deps)
            if deps:
                self.q[e].append((deps, None, None, 0))
        for b in self.live:
            self.dma_val[b.dsem] = b.dcnt
            self.dma_free.append(b.dsem)
            b.dsem = None
        self.live = []

    def emit(self):
        with self.nc.Block() as block:
            def run(eng_name):
                def f(h):
                    for deps, fn, semh, val in self.q[eng_name]:
                        for kind, s, v in deps:
                            h.wait_ge(self.dma_pool[s] if kind == "dma" else self.sem[kind], v)
                        if fn is not None:
                            fn(h).then_inc(semh, val)
                return f
            block.tensor(run("pe"))
            block.scalar(run("act"))
            block.vector(run("dve"))
            block.gpsimd(run("pool"))
            block.sync(run("sp"))

    def dma(self, out, in_, eng="sp"):
        (ob, oa), (ib, ia) = out, in_
        return self.op(eng, lambda h: h.dma_start(out=oa, in_=ia), reads=[ib], writes=[ob], dma=True)

    def mm(self, out, lhsT, rhs, start=True, stop=True):
        (ob, oa), (lb, la), (rb, ra) = out, lhsT, rhs
        return self.op("pe", lambda h: h.matmul(oa, la, ra, start=start, stop=stop), reads=[lb, rb], writes=[ob])

    def tr(self, out, in_, ident):
        (ob, oa), (ib, ia), (db, da) = out, in_, ident
        return self.op("pe", lambda h: h.transpose(oa, ia, da), reads=[ib, db], writes=[ob])

    def act(self, out, in_, func, bias=None, scale=1.0, accum=None):
        (ob, oa), (ib, ia) = out, in_
        reads, writes = [ib], [ob]
        kw = {}
        if bias is not None:
            if isinstance(bias, tuple):
                reads.append(bias[0]); kw["bias"] = bias[1]
            else:
                kw["bias"] = bias
        if isinstance(scale, tuple):
            reads.append(scale[0]); kw["scale"] = scale[1]
        else:
            kw["scale"] = scale
        if accum is not None:
            writes.append(accum[0]); kw["accum_out"] = accum[1]
        return self.op("act", lambda h: h.activation(out=oa, in_=ia, func=func, **kw), reads=reads, writes=writes)

    def tt(self, out, in0, in1, op, eng="dve"):
        (ob, oa), (ab, aa), (bb, ba) = out, in0, in1
        return self.op(eng, lambda h: h.tensor_tensor(out=oa, in0=aa, in1=ba, op=op), reads=[ab, bb], writes=[ob])

    def ts(self, out, in0, s1, op0, s2=None, op1=None, eng="dve", accum=None):
        (ob, oa), (ab, aa) = out, in0
        reads, writes = [ab], [ob]
        if isinstance(s1, tuple):
            reads.append(s1[0]); s1 = s1[1]
        if isinstance(s2, tuple):
            reads.append(s2[0]); s2 = s2[1]
        kw = {}
        if op1 is not None:
            kw["op1"] = op1
        if accum is not None:
            writes.append(accum[0]); kw["accum_out"] = accum[1]
        return self.op(eng, lambda h: h.tensor_scalar(oa, aa, s1, s2, op0, **kw), reads=reads, writes=writes)

    def cp(self, out, in_, eng="dve"):
        (ob, oa), (ib, ia) = out, in_
        return self.op(eng, lambda h: h.tensor_copy(out=oa, in_=ia), reads=[ib], writes=[ob])

    def memset(self, out, val, eng="dve"):
        (ob, oa) = out
        return self.op(eng, lambda h: h.memset(oa, val), writes=[ob])

    def recip(self, out, in_):
        (ob, oa), (ib, ia) = out, in_
        return self.op("dve", lambda h: h.reciprocal(out=oa, in_=ia), reads=[ib], writes=[ob])


def V(buf, ap=None):
    return (buf, buf.t[:] if ap is None else ap)


class Cfg:
    def __init__(self, nc_cores=8, sp=4, t=2048):
        self.NC = nc_cores
        self.SP = sp
        self.T = t
        self.NSEG = sp + 1
        self.NCH = t // 128
        self.NB = t // 512
        self.LS = nc_cores * t


class Rot:
    def __init__(self, items):
        self.items = items
        self.i = 0

    def nxt(self):
        b = self.items[self.i % len(self.items)]
        self.i += 1
        return b


def D_(ap):
    return (None, ap)


def _dma(k, out, in_, eng="sp", slow=False):
    (ob, oa), (ib, ia) = out, in_
    reads = [ib] if ib is not None else []
    writes = [ob] if ob is not None else []
    if not reads and not writes:
        raise ValueError("dram->dram untracked")
    if slow:
        return k.op(eng, lambda h: h.dma_start(out=oa, in_=ia, allow_slow_non_contiguous=True), reads=reads, writes=writes, dma=True)
    return k.op(eng, lambda h: h.dma_start(out=oa, in_=ia), reads=reads, writes=writes, dma=True)


K.dma = _dma


def prep_weight(k, st, dst_fn, src, K_rows, cols, row_gain=None, col_gain=None):
    nk = K_rows // 128
    CB = 1408
    if row_gain is not None:
        rg = st["rg"].nxt()
        k.dma(V(rg, rg[:, 0:nk]), D_(row_gain.rearrange("(c p) -> p c", p=128)), slow=True)
    for c0 in range(0, cols, CB):
        cw = min(CB, cols - c0)
        if col_gain is not None:
            cg = st["cg"].nxt()
            k.dma(V(cg, cg[:, 0:cw]), D_(col_gain[c0:c0 + cw].partition_broadcast(128)))
        for kc in range(nk):
            s32 = st["s32"].nxt()
            k.dma(V(s32, s32[:, 0:cw]), D_(src[kc * 128:(kc + 1) * 128, c0:c0 + cw]))
            cur = V(s32, s32[:, 0:cw])
            if col_gain is not None:
                k.tt(cur, cur, V(cg, cg[:, 0:cw]), ALU.mult, eng="pool")
            db, da = dst_fn(kc, c0, cw)
            if db is None:
                sbf = st["sbf"].nxt()
                o = V(sbf, sbf[:, 0:cw])
            else:
                o = (db, da)
            if row_gain is not None:
                k.act(o, cur, AF.Copy, scale=V(rg, rg[:, kc:kc + 1]))
            else:
                k.act(o, cur, AF.Copy)
            if db is None:
                if len(da.shape) == 3:
                    o = (o[0], o[1].rearrange("p (m c) -> p m c", c=128))
                k.dma(D_(da), o, eng="pool")


def prep_stage(k, es):
    return {
        "s32": Rot([k.sb(es, "p0s32_%d" % i, [128, 1408], F32) for i in range(3)]),
        "sbf": Rot([k.sb(es, "p0sbf_%d" % i, [128, 1408], BF16) for i in range(3)]),
        "cg": Rot([k.sb(es, "p0cg_%d" % i, [128, 1408], F32) for i in range(2)]),
        "rg": Rot([k.sb(es, "p0rg_%d" % i, [128, 24], F32) for i in range(2)]),
    }


def load_w(k, buf, dram_ap):
    n = dram_ap.shape[1]
    step = max(1, n // 4)
    for c in range(0, n, step):
        e = min(n, c + step)
        k.dma(V(buf, buf[:, c:e]), D_(dram_ap[:, c:e]))


def rmsnorm_tile(k, P, xt, rows, hn, dim_scale):
    ss = P["ss"].nxt()
    junk = P["junk"].nxt()
    k.act(V(junk, junk[0:rows, :]), V(xt, xt[0:rows, :]), AF.Square, scale=dim_scale, accum=V(ss, ss[0:rows, 0:1]))
    k.act(V(ss, ss[0:rows, 1:2]), V(ss, ss[0:rows, 0:1]), AF.Sqrt, bias=V(P["eps"], P["eps"][0:rows, 0:1]))
    k.recip(V(ss, ss[0:rows, 1:2]), V(ss, ss[0:rows, 1:2]))
    k.ts(V(hn, hn[0:rows, :]), V(xt, xt[0:rows, :]), V(ss, ss[0:rows, 1:2]), ALU.mult)


def transpose_tile(k, P, hn, rows, dst_buf, dst_ap_fn):
    tp = P["tp"].nxt()
    for j in range(8):
        k.tr(V(tp, tp[:, j * 128:j * 128 + rows]), V(hn, hn[0:rows, j * 128:(j + 1) * 128]),
             V(P["ident"], P["ident"][0:rows, 0:rows]))
    src = tp[:, :].rearrange("p (j t) -> p j t", t=128)[:, :, 0:rows]
    k.cp(V(dst_buf, dst_ap_fn), V(tp, src))


def phase_p1a(k, cfg, W, C, segs):
    nc = k.nc
    with ExitStack() as es:
        Wq = k.sb(es, "Wq", [128, 8, QL], BF16)
        Wc = k.sb(es, "Wc", [128, 8, KVL], BF16)
        Wkr = k.sb(es, "Wkr", [128, 8, 96], BF16)
        Wks = k.sb(es, "Wks", [128, 8, 96], BF16)
        Wqb = k.sb(es, "Wqb", [128, 3, H * 96], BF16)
        Wqs = k.sb(es, "Wqs", [128, 3, H * 96], BF16)
        Wkb = k.sb(es, "Wkb", [128, 2, H * 64], BF16)
        Wvb = k.sb(es, "Wvb", [128, 2, H * 64], BF16)
        ident = k.sb(es, "ident_sb", [128, 128], BF16)
        onesb = k.sb(es, "ones_sb", [128, 128], BF16)
        eps_t = k.sb(es, "eps_sb", [128, 1], F32)
        es_prep = ExitStack()
        st = prep_stage(k, es_prep)
        k.dma(V(ident), D_(C["identb"]))
        k.dma(V(onesb), D_(C["onesb"]))
        n1 = W["norm1"]
        win = W["w_in"]
        prep_weight(k, st, lambda kc, c0, cw: (Wq, Wq[:, kc, c0:c0 + cw]), win[:, O_Q:O_Q + QL], D, QL, row_gain=n1)
        prep_weight(k, st, lambda kc, c0, cw: (Wc, Wc[:, kc, c0:c0 + cw]), win[:, O_CKV:O_CKV + KVL], D, KVL, row_gain=n1)
        k.memset(V(Wkr), 0.0)
        k.memset(V(Wks), 0.0)
        prep_weight(k, st, lambda kc, c0, cw: (Wkr, Wkr[:, kc, 64:96]), win[:, O_KR:O_KR + 32], D, 32, row_gain=n1)
        prep_weight(k, st, lambda kc, c0, cw: (Wks, Wks[:, kc, 64:80]), win[:, O_KR + 16:O_KR + 32], D, 16, row_gain=n1)
        prep_weight(k, st, lambda kc, c0, cw: (Wks, Wks[:, kc, 80:96]), win[:, O_KR:O_KR + 16], D, 16, row_gain=n1)
        prep_weight(k, st, lambda kc, c0, cw: (Wqb, Wqb[:, kc, c0:c0 + cw]), W["w_q_b"], QL, H * 96, row_gain=W["q_a_norm"])
        k.memset(V(Wqs), 0.0, eng="pool")
        wqb3 = W["w_q_b"].rearrange("k (h c) -> k h c", c=96)
        for hh in range(H):
            prep_weight(k, st, lambda kc, c0, cw, hh=hh: (Wqs, Wqs[:, kc, hh * 96 + 64:hh * 96 + 80]),
                        W["w_q_b"][:, hh * 96 + 80:hh * 96 + 96], QL, 16, row_gain=W["q_a_norm"])
            prep_weight(k, st, lambda kc, c0, cw, hh=hh: (Wqs, Wqs[:, kc, hh * 96 + 80:hh * 96 + 96]),
                        W["w_q_b"][:, hh * 96 + 64:hh * 96 + 80], QL, 16, row_gain=W["q_a_norm"])
        for hh in range(H):
            prep_weight(k, st, lambda kc, c0, cw, hh=hh: (Wkb, Wkb[:, kc, hh * 64:(hh + 1) * 64]),
                        W["w_kv_b"][:, hh * 128:hh * 128 + 64], KVL, 64, row_gain=W["kv_a_norm"])
            prep_weight(k, st, lambda kc, c0, cw, hh=hh: (Wvb, Wvb[:, kc, hh * 64:(hh + 1) * 64]),
                        W["w_kv_b"][:, hh * 128 + 64:hh * 128 + 128], KVL, 64, row_gain=W["kv_a_norm"])
        k.barrier()
        es_prep.close()

        P = {
            "ss": Rot([k.sb(es, "ss%d" % i, [128, 2], F32) for i in range(3)]),
            "junk": Rot([k.sb(es, "junk%d" % i, [128, D], BF16) for i in range(2)]),
            "tp": Rot([k.ps(es, "tp%d" % i, [128, D], BF16) for i in range(1)]),
            "ident": ident,
            "eps": eps_t,
        }
        k.memset(V(P["eps"]), EPS)
        xt = Rot([k.sb(es, "xt%d" % i, [128, D], F32) for i in range(9)])
        hn = Rot([k.sb(es, "hn%d" % i, [128, D], BF16) for i in range(2)])
        hT = Rot([k.sb(es, "hT%d" % i, [128, 8, 512], BF16) for i in range(2)])
        pb = Rot([k.ps(es, "pb%d" % i, [128, 512], F32) for i in range(7)])
        sq = Rot([k.sb(es, "sq%d" % i, [128, 512], BF16) for i in range(3)])
        rbc = Rot([k.sb(es, "rbc%d" % i, [128, 512], F32) for i in range(2)])
        qln = Rot([k.sb(es, "qln%d" % i, [128, 3, 512], BF16) for i in range(2)])
        ckn = Rot([k.sb(es, "ckn%d" % i, [128, 2, 512], BF16) for i in range(2)])
        cosb = Rot([k.sb(es, "cosb%d" % i, [96, 512], F32) for i in range(3)])
        sinb = Rot([k.sb(es, "sinb%d" % i, [96, 512], F32) for i in range(3)])
        t1 = Rot([k.sb(es, "t1_%d" % i, [96, 512], F32) for i in range(3)])
        t2 = Rot([k.sb(es, "t2_%d" % i, [96, 512], F32) for i in range(3)])
        qo = Rot([k.sb(es, "qo%d" % i, [96, H, 512], BF16) for i in range(1)])
        ko = Rot([k.sb(es, "ko%d" % i, [96, H, 512], BF16) for i in range(1)])
        krt = Rot([k.sb(es, "krt%d" % i, [96, 512], BF16) for i in range(2)])
        vo = Rot([k.sb(es, "vo%d" % i, [128, H, 128], BF16) for i in range(2)])
        for b_ in vo.items:
            k.memset(V(b_), 1.0, eng="pool")

        def fm_rmsnorm(ps_list, dim, dst, nchunk):
            sqs = []
            for m in range(nchunk):
                s_ = sq.nxt()
                k.act(V(s_), V(ps_list[m]), AF.Square, scale=float(dim) ** -0.5)
                sqs.append(s_)
            pss = pb.nxt()
            for m in range(nchunk):
                k.mm(V(pss), V(onesb), V(sqs[m]), start=(m == 0), stop=(m == nchunk - 1))
            r_ = rbc.nxt()
            k.act(V(r_), V(pss), AF.Sqrt, bias=V(P["eps"]))
            k.recip(V(r_), V(r_))
            for m in range(nchunk):
                k.tt(V(dst, dst[:, m, :]), V(ps_list[m]), V(r_), ALU.mult)

        blocks = [(sg, b) for sg in segs for b in range((sg.get("n", cfg.T) + 511) // 512)]

        loaded = {}

        def load_blk(bi):
            sg, b = blocks[bi]
            c0 = b * 512
            xs_l = []
            for ti in range(4):
                x_ = xt.nxt()
                r0 = c0 + ti * 128
                k.dma(V(x_), D_(sg["x"][r0:r0 + 128, :]))
                xs_l.append(x_)
            cs_, sn_ = cosb.nxt(), sinb.nxt()
            k.dma(V(cs_), D_(sg["cos"][:, c0:c0 + 512]))
            k.dma(V(sn_), D_(sg["sin"][:, c0:c0 + 512]))
            loaded[bi] = (xs_l, cs_, sn_)

        def build_hT(bi):
            if bi not in loaded:
                load_blk(bi)
            xs_l, cs_, sn_ = loaded.pop(bi)
            if bi + 1 < len(blocks) and (bi + 1) not in loaded:
                load_blk(bi + 1)
            h_ = hT.nxt()
            for ti in range(4):
                n_ = hn.nxt()
                rmsnorm_tile(k, P, xs_l[ti], 128, n_, 1.0 / 32.0)
                transpose_tile(k, P, n_, 128, h_, h_[:, :, ti * 128:(ti + 1) * 128])
            return h_, cs_, sn_

        nxt_blk = build_hT(0)
        for bi, (sg, b) in enumerate(blocks):
            if True:
                do_q, do_kv = sg.get("do_q", True), sg.get("do_kv", True)
                c0 = b * 512
                h_, cs_, sn_ = nxt_blk
                def rope_rows(pa, ps_, dst):
                    a_, b2 = t1.nxt(), t2.nxt()
                    k.tt(V(a_, a_[64:96, :]), V(pa, pa[64:96, :]), V(cs_, cs_[64:96, :]), ALU.mult)
                    k.tt(V(b2, b2[64:96, :]), V(ps_, ps_[64:96, :]), V(sn_, sn_[64:96, :]), ALU.mult)
                    k.tt(dst, V(a_, a_[64:96, :]), V(b2, b2[64:96, :]), ALU.add)

                if do_q:
                    pq = [pb.nxt() for _ in range(3)]
                    for m in range(3):
                        for kc in range(KC):
                            k.mm(V(pq[m]), V(Wq, Wq[:, kc, m * 128:(m + 1) * 128]), V(h_, h_[:, kc, :]),
                                 start=(kc == 0), stop=(kc == KC - 1))
                if do_kv:
                    pc = [pb.nxt() for _ in range(2)]
                    for m in range(2):
                        for kc in range(KC):
                            k.mm(V(pc[m]), V(Wc, Wc[:, kc, m * 128:(m + 1) * 128]), V(h_, h_[:, kc, :]),
                                 start=(kc == 0), stop=(kc == KC - 1))
                if bi + 1 < len(blocks):
                    nxt_blk = build_hT(bi + 1)
                if do_q:
                    ql = qln.nxt()
                    fm_rmsnorm(pq, QL, ql, 3)
                if do_kv:
                    pka, pks = pb.nxt(), pb.nxt()
                    for kc in range(KC):
                        k.mm(V(pka, pka[0:96, :]), V(Wkr, Wkr[:, kc, :]), V(h_, h_[:, kc, :]), start=(kc == 0), stop=(kc == KC - 1))
                    for kc in range(KC):
                        k.mm(V(pks, pks[0:96, :]), V(Wks, Wks[:, kc, :]), V(h_, h_[:, kc, :]), start=(kc == 0), stop=(kc == KC - 1))
                    cn = ckn.nxt()
                    fm_rmsnorm(pc, KVL, cn, 2)
                    kr = krt.nxt()
                    rope_rows(pka, pks, V(kr, kr[64:96, :]))

                q_ = qo.nxt() if do_q else None
                for hh in range(H if do_q else 0):
                    pa, ps_ = pb.nxt(), pb.nxt()
                    for m in range(3):
                        k.mm(V(pa, pa[0:96, :]), V(Wqb, Wqb[:, m, hh * 96:(hh + 1) * 96]), V(ql, ql[:, m, :]),
                             start=(m == 0), stop=(m == 2))
                    for m in range(3):
                        k.mm(V(ps_, ps_[0:96, :]), V(Wqs, Wqs[:, m, hh * 96:(hh + 1) * 96]), V(ql, ql[:, m, :]),
                             start=(m == 0), stop=(m == 2))
                    if hh % 2 == 0:
                        k.act(V(q_, q_[0:64, hh, :]), V(pa, pa[0:64, :]), AF.Copy)
                    else:
                        k.cp(V(q_, q_[0:64, hh, :]), V(pa, pa[0:64, :]))
                    rope_rows(pa, ps_, V(q_, q_[64:96, hh, :]))
                if do_q:
                    k.dma(D_(sg["qt"][:, :, c0:c0 + 512].rearrange("h p t -> p h t")), V(q_), eng="pool")
                if not do_kv:
                    continue
                k_ = ko.nxt()
                for hh in range(H):
                    pk = pb.nxt()
                    for m in range(2):
                        k.mm(V(pk, pk[0:64, :]), V(Wkb, Wkb[:, m, hh * 64:(hh + 1) * 64]), V(cn, cn[:, m, :]),
                             start=(m == 0), stop=(m == 1))
                    if hh % 2 == 0 and do_q:
                        k.act(V(k_, k_[0:64, hh, :]), V(pk, pk[0:64, :]), AF.Copy)
                    elif hh % 4 == 0:
                        k.act(V(k_, k_[0:64, hh, :]), V(pk, pk[0:64, :]), AF.Copy)
                    else:
                        k.cp(V(k_, k_[0:64, hh, :]), V(pk, pk[0:64, :]))
                k.dma(D_(sg["kt"][:, 0:64, c0:c0 + 512].rearrange("h p t -> p h t")), V(k_, k_[0:64, :, :]), eng="pool")
                k.dma(D_(sg["kt"][0, 64:96, c0:c0 + 512]), V(kr, kr[64:96, :]), eng="pool")
                for ti in range(4):
                    v_ = vo.nxt()
                    for half in range(2):
                        pv = pb.nxt()
                        for m in range(2):
                            k.mm(V(pv), V(cn, cn[:, m, ti * 128:(ti + 1) * 128]), V(Wvb, Wvb[:, m, half * 512:(half + 1) * 512]),
                                 start=(m == 0), stop=(m == 1))
                        if half == 0:
                            k.act(V(v_, v_[:, half * 8:half * 8 + 8, 0:64]),
                                  V(pv, pv[:, :].rearrange("p (j v) -> p j v", v=64)), AF.Copy)
                        else:
                            k.cp(V(v_, v_[:, half * 8:half * 8 + 8, 0:64]),
                                 V(pv, pv[:, :].rearrange("p (j v) -> p j v", v=64)))
                    r0 = c0 + ti * 128
                    k.dma(D_(sg["va"][r0:r0 + 128, :, :]), V(v_), eng="pool")
        k.barrier()


WNAMES = ["norm1", "w_in", "q_a_norm", "kv_a_norm", "w_q_b", "w_kv_b", "conv_w", "conv_b", "dt_bias_f", "dt_bias_b",
          "a_log_f", "a_log_b", "d_skip", "ssm_norm", "w_out", "norm2", "w_gate", "w_up", "ffn_conv_w", "ffn_conv_b",
          "w_down", "final_norm"]


def rope_tables(pos):
    pos = np.asarray(pos, dtype=np.float32)
    inv = (np.float32(10000.0) ** (-(np.arange(0, RO, 2, dtype=np.float32)) / np.float32(RO))).astype(np.float32)
    ang = (pos[:, None] * inv[None, :]).astype(np.float32)
    c, s = np.cos(ang).astype(np.float32).T, np.sin(ang).astype(np.float32).T
    cos = np.ones((96, len(pos)), np.float32)
    sin = np.zeros((96, len(pos)), np.float32)
    cos[64:80], cos[80:96] = c, c
    sin[64:80], sin[80:96] = -s, s
    return cos, sin


def host_consts():
    r = np.arange(128)
    cst = np.zeros((128, 7, 128), np.float32)
    cst[:, 0, :] = (r[:, None] <= r[None, :])
    cst[:, 1, :] = (r[:, None] < r[None, :])
    cst[:, 2, :] = 1.0
    cst[:, 3, :] = np.eye(128)
    cst[:, 4, :] = np.where(r[None, :] < r[:, None], NEG, 0.0)
    cst[:, 5, :] = np.where(r[None, :] > r[:, None], NEG, 0.0)
    cst[:, 6, :] = -(r[:, None] < r[None, :]).astype(np.float32)
    return {
        "identb": np.eye(128, dtype=np.float32).astype(ml_dtypes.bfloat16),
        "onesb": np.ones((128, 128), np.float32).astype(ml_dtypes.bfloat16),
        "cst32": cst,
    }


def declare_weights(nc, shapes):
    W = {}
    for n in WNAMES:
        shp = list(shapes[n])
        W[n] = nc.dram_tensor(n, shp, F32, kind="ExternalInput").ap()
    return W


def wviews(W):
    o = {}
    for n, ap in W.items():
        o[n] = ap if n == "final_norm" else ap[0]
    return o


def phase_p2(k, cfg, jobs):
    with ExitStack() as es:
        maxk = max(j["Tk"] for j in jobs)
        maxq = max(j["Tq"] for j in jobs)
        qb = Rot([k.sb(es, "aq%d" % i, [96, maxq], BF16) for i in range(2)])
        kb = Rot([k.sb(es, "ak%d" % i, [96, maxk], BF16) for i in range(2)])
        vb = Rot([k.sb(es, "av%d" % i, [128, maxk // 128, 128], BF16) for i in range(2)])
        pS = Rot([k.ps(es, "pS%d" % i, [128, 1024], F32) for i in range(3)])
        pO = Rot([k.ps(es, "pO%d" % i, [128, 512], F32) for i in range(2)])
        pt = Rot([k.sb(es, "apt%d" % i, [128, 1024], BF16) for i in range(3)])
        rc = Rot([k.sb(es, "arc%d" % i, [64, 512], F32) for i in range(2)])
        ao = Rot([k.sb(es, "aao%d" % i, [64, 512], BF16) for i in range(3)])

        def load(j):
            Tq, Tk = j["Tq"], j["Tk"]
            q_, k_, v_ = qb.nxt(), kb.nxt(), vb.nxt()
            k.dma(V(q_, q_[:, 0:Tq]), D_(j["qt"]))
            for c in range(0, Tk, 4096):
                e = min(Tk, c + 4096)
                if "kr" in j:
                    k.dma(V(k_, k_[0:64, c:e]), D_(j["kt"][0:64, c:e]))
                    k.dma(V(k_, k_[64:96, c:e]), D_(j["kr"][:, c:e]))
                else:
                    k.dma(V(k_, k_[:, c:e]), D_(j["kt"][:, c:e]))
            for c in range(0, Tk, 2048):
                e = min(Tk, c + 2048)
                k.dma(V(v_, v_[:, c // 128:e // 128, :]), D_(j["va"][c:e, :].rearrange("(t p) v -> p t v", p=128)))
            return q_, k_, v_

        its = []
        for ji, j in enumerate(jobs):
            for qi in range((j["Tq"] + 511) // 512):
                for kp in range(j["Tk"] // 256):
                    its.append((ji, qi, kp))
        bufs = {0: load(jobs[0])}
        sbuf = {}

        def emit_scores(i):
            ji, qi, kp = its[i]
            if ji not in bufs:
                bufs[ji] = load(jobs[ji])
            q_, k_, v_ = bufs[ji]
            qw = min(512, jobs[ji]["Tq"] - qi * 512)
            s_ = pS.nxt()
            for t in range(2):
                kt_ = 2 * kp + t
                k.mm(V(s_, s_[:, t * 512:t * 512 + qw]), V(k_, k_[:, kt_ * 128:(kt_ + 1) * 128]), V(q_, q_[:, qi * 512:qi * 512 + qw]))
            sbuf[i] = s_

        emit_scores(0)
        o_ = None
        for i, (ji, qi, kp) in enumerate(its):
            j = jobs[ji]
            if qi == 0 and kp == 0 and ji + 1 < len(jobs) and (ji + 1) not in bufs:
                bufs[ji + 1] = load(jobs[ji + 1])
            if i + 1 < len(its):
                emit_scores(i + 1)
            q_, k_, v_ = bufs[ji]
            nkp = j["Tk"] // 256
            if kp == 0:
                o_ = pO.nxt()
            s_ = sbuf.pop(i)
            p_ = pt.nxt()
            qw = min(512, j["Tq"] - qi * 512)
            k.act(V(p_, p_[:, :].rearrange("p (t c) -> p t c", c=512)[:, :, 0:qw]),
                  V(s_, s_[:, :].rearrange("p (t c) -> p t c", c=512)[:, :, 0:qw]), AF.Exp, scale=SCALE)
            for t in range(2):
                kt_ = 2 * kp + t
                k.mm(V(o_, o_[:, 0:qw]), V(v_, v_[:, kt_, :]), V(p_, p_[:, t * 512:t * 512 + qw]), start=(kt_ == 0), stop=(kt_ == 2 * nkp - 1))
            if kp == nkp - 1:
                r_ = rc.nxt()
                k.recip(V(r_, r_[:, 0:qw]), V(o_, o_[64:128, 0:qw]))
                a_ = ao.nxt()
                k.tt(V(a_, a_[:, 0:qw]), V(o_, o_[0:64, 0:qw]), V(r_, r_[:, 0:qw]), ALU.mult)
                k.dma(D_(j["at"][:, qi * 512:qi * 512 + qw]), V(a_, a_[:, 0:qw]), eng="pool")
                if qi == (j["Tq"] + 511) // 512 - 1:
                    bufs.pop(ji, None)
        k.barrier()


def phase_p3(k, cfg, W, C, segs, nkc=16):
    with ExitStack() as es:
        st = prep_stage(k, es)
        Wo = k.sb(es, "Wo", [128, nkc, D], BF16)
        ident = k.sb(es, "ident_sb3", [128, 128], BF16)
        k.dma(V(ident), D_(C["identb"]))
        prep_weight(k, st, lambda kc, c0, cw: (Wo, Wo[:, kc, c0:c0 + cw]), W["w_out"][0:1024, :], 1024, D)
        if nkc == 16:
            prep_weight(k, st, lambda kc, c0, cw: (Wo, Wo[:, 8 + kc, c0:c0 + cw]), W["w_out"][1024:2048, :], 1024, D,
                        row_gain=W["ssm_norm"])
        P = {
            "ss": Rot([k.sb(es, "ss3_%d" % i, [128, 2], F32) for i in range(3)]),
            "junk": Rot([k.sb(es, "junk3_%d" % i, [128, D], BF16) for i in range(2)]),
            "tp": Rot([k.ps(es, "tp3_%d" % i, [128, D], BF16) for i in range(2)]),
            "ident": ident,
            "eps": k.sb(es, "eps3", [128, 1], F32),
        }
        k.memset(V(P["eps"]), EPS)
        zt = k.sb(es, "zt3", [128, 8, 2], BF16)
        k.memset(V(zt), 0.0)
        mixT = Rot([k.sb(es, "mixT%d" % i, [128, nkc, 512], BF16) for i in range(2)])
        xt = Rot([k.sb(es, "xt3_%d" % i, [128, D], F32) for i in range(3)])
        x1 = Rot([k.sb(es, "x1_%d" % i, [128, D], F32) for i in range(3)])
        hn = Rot([k.sb(es, "hn3_%d" % i, [128, D], BF16) for i in range(2)])
        h2 = Rot([k.sb(es, "h2_%d" % i, [128, 8, 128], BF16) for i in range(3)])
        px = Rot([k.ps(es, "px%d" % i, [128, D], F32) for i in range(3)])
        for sg in segs:
            T = sg.get("n", cfg.T)
            k.dma(D_(sg["h2t"][:, :, 0:1]), V(zt, zt[:, :, 0:1]), eng="pool", slow=True)
            k.dma(D_(sg["h2t"][:, :, T + 1:T + 2]), V(zt, zt[:, :, 1:2]), eng="pool", slow=True)
            for c0 in range(0, T, 512):
                bwid = min(512, T - c0)
                m_ = mixT.nxt()
                for i, mx in enumerate(sg["mix"]):
                    k.dma(V(m_, m_[:, 8 * i:8 * i + 8, 0:bwid]), D_(mx.rearrange("(c p) t -> p c t", p=128)[:, :, c0:c0 + bwid]))
                def tile_mm(ti):
                    r0 = c0 + ti * 128
                    x_ = xt.nxt()
                    k.dma(V(x_), D_(sg["x"][r0:r0 + 128, :]))
                    p_ = px.nxt()
                    for n in range(2):
                        for kc in range(nkc):
                            k.mm(V(p_, p_[:, n * 512:(n + 1) * 512]), V(m_, m_[:, kc, ti * 128:(ti + 1) * 128]),
                                 V(Wo, Wo[:, kc, n * 512:(n + 1) * 512]), start=(kc == 0), stop=(kc == nkc - 1))
                    return r0, x_, p_

                def tile_post(r0, x_, p_):
                    y_ = x1.nxt()
                    k.tt(V(y_), V(p_), V(x_), ALU.add)
                    k.dma(D_(sg["x1"][r0:r0 + 128, :]), V(y_), eng="pool")
                    n_ = hn.nxt()
                    rmsnorm_tile(k, P, y_, 128, n_, 1.0 / 32.0)
                    h_ = h2.nxt()
                    transpose_tile(k, P, n_, 128, h_, h_[:, :, :])
                    k.dma(D_(sg["h2t"][:, :, 1 + r0:1 + r0 + 128]), V(h_), eng="pool")

                prev = None
                for ti in range(bwid // 128):
                    cur = tile_mm(ti)
                    if prev is not None:
                        tile_post(*prev)
                    prev = cur
                tile_post(*prev)
        k.barrier()


FB = 510


def phase_p4(k, cfg, W, C, wg_scr, segs):
    T = cfg.T
    with ExitStack() as es:
        Wu = k.sb(es, "Wu", [128, 8, DFF], BF16)
        Wd = k.sb(es, "Wd", [128, FC, D], BF16)
        bg = k.sb(es, "bg", [128, FC], F32)
        cw3 = k.sb(es, "cw3", [128, 3, FC], F32)
        gain = k.sb(es, "fgain", [128, D], F32)
        eps = k.sb(es, "eps4", [128, 1], F32)
        es_prep = ExitStack()
        st = prep_stage(k, es_prep)
        n2 = W["norm2"]

        def dst(kc, c0, cw):
            return (None, wg_scr[c0 // 128:(c0 + cw) // 128, :, kc, :].rearrange("m p c -> p m c"))
        prep_weight(k, st, dst, W["w_gate"], D, DFF, row_gain=n2)
        prep_weight(k, st, lambda kc, c0, cw: (Wu, Wu[:, kc, c0:c0 + cw]), W["w_up"], D, DFF, row_gain=n2)
        prep_weight(k, st, lambda kc, c0, cw: (Wd, Wd[:, kc, c0:c0 + cw]), W["w_down"], DFF, D)
        k.dma(V(bg), D_(W["ffn_conv_b"].rearrange("(c p) -> p c", p=128)), slow=True)
        for tap in range(3):
            k.dma(V(cw3, cw3[:, tap, :]), D_(W["ffn_conv_w"][tap].rearrange("(c p) -> p c", p=128)), slow=True)
        k.dma(V(gain), D_(W["final_norm"].partition_broadcast(128)))
        k.memset(V(eps), EPS)
        k.barrier()
        es_prep.close()
        hT = Rot([k.sb(es, "h2T%d" % i, [128, 8, T + 2], BF16) for i in range(1)])
        vf4 = k.sb(es, "vf4", [128, 2], F32)
        wg = Rot([k.sb(es, "wg%d" % i, [128, 8, 128], BF16) for i in range(3)])
        pg = Rot([k.ps(es, "pg%d" % i, [128, 512], F32) for i in range(2)])
        pu = Rot([k.ps(es, "pu%d" % i, [128, 512], F32) for i in range(2)])
        pd = Rot([k.ps(es, "pd%d" % i, [128, D], F32) for i in range(2)])
        cv = Rot([k.sb(es, "cv%d" % i, [128, 512], F32) for i in range(3)])
        sg_ = Rot([k.sb(es, "sgl%d" % i, [128, 512], F32) for i in range(2)])
        aT = Rot([k.sb(es, "aT%d" % i, [128, FC, 512], BF16) for i in range(1)])
        x1 = Rot([k.sb(es, "x14_%d" % i, [128, D], F32) for i in range(2)])
        ss = Rot([k.sb(es, "ss4_%d" % i, [128, 2], F32) for i in range(3)])
        junk = Rot([k.sb(es, "junk4_%d" % i, [128, D], BF16) for i in range(2)])
        for sg in segs:
            h_ = hT.nxt()
            for c in range(0, T + 2, 1024):
                e = min(T + 2, c + 1024)
                k.dma(V(h_, h_[:, :, c:e]), D_(sg["h2t"][:, :, c:e]))
            if sg.get("vflag") is not None:
                k.dma(V(vf4), D_(sg["vflag"]))
                k.ts(V(h_, h_[:, :, 0:1]), V(h_, h_[:, :, 0:1]), V(vf4, vf4[:, 0:1]), ALU.mult)
                k.ts(V(h_, h_[:, :, T + 1:T + 2]), V(h_, h_[:, :, T + 1:T + 2]), V(vf4, vf4[:, 1:2]), ALU.mult)
            for v0 in range(0, T, FB):
                bw = min(FB, T - v0)
                a_ = aT.nxt()
                for m in range(FC):
                    w_ = wg.nxt()
                    k.dma(V(w_), D_(wg_scr[m]))
                    g_, u_ = pg.nxt(), pu.nxt()
                    for kc in range(KC):
                        k.mm(V(g_, g_[:, 0:bw + 2]), V(w_, w_[:, kc, :]), V(h_, h_[:, kc, v0:v0 + bw + 2]),
                             start=(kc == 0), stop=(kc == KC - 1))
                    for kc in range(KC):
                        k.mm(V(u_, u_[:, 0:bw]), V(Wu, Wu[:, kc, m * 128:(m + 1) * 128]), V(h_, h_[:, kc, v0 + 1:v0 + 1 + bw]),
                             start=(kc == 0), stop=(kc == KC - 1))
                    c_ = cv.nxt()
                    k.ts(V(c_, c_[:, 0:bw]), V(g_, g_[:, 0:bw]), V(cw3, cw3[:, 0, m:m + 1]), ALU.mult)
                    for tap in (1, 2):
                        k.op("dve", lambda hh, c_=c_, g_=g_, tap=tap, m=m, bw=bw: hh.scalar_tensor_tensor(
                            out=c_[:, 0:bw], in0=g_[:, tap:tap + bw], scalar=cw3[:, tap, m:m + 1], in1=c_[:, 0:bw],
                            op0=ALU.mult, op1=ALU.add), reads=[g_, cw3, c_], writes=[c_])
                    s_ = sg_.nxt()
                    k.act(V(s_, s_[:, 0:bw]), V(c_, c_[:, 0:bw]), AF.Silu, bias=V(bg, bg[:, m:m + 1]))
                    k.tt(V(a_, a_[:, m, 0:bw]), V(s_, s_[:, 0:bw]), V(u_, u_[:, 0:bw]), ALU.mult)
                for i0_ in range(0, bw, 128):
                    rows = min(128, bw - i0_)
                    r0 = v0 + i0_
                    x_ = x1.nxt()
                    k.dma(V(x_, x_[0:rows, :]), D_(sg["x1"][r0:r0 + rows, :]))
                    p_ = pd.nxt()
                    for n in range(2):
                        for m in range(FC):
                            k.mm(V(p_, p_[0:rows, n * 512:(n + 1) * 512]), V(a_, a_[:, m, i0_:i0_ + rows]),
                                 V(Wd, Wd[:, m, n * 512:(n + 1) * 512]), start=(m == 0), stop=(m == FC - 1))
                    k.tt(V(x_, x_[0:rows, :]), V(p_, p_[0:rows, :]), V(x_, x_[0:rows, :]), ALU.add)
                    s2, jk = ss.nxt(), junk.nxt()
                    k.act(V(jk, jk[0:rows, :]), V(x_, x_[0:rows, :]), AF.Square, scale=1.0 / 32.0, accum=V(s2, s2[0:rows, 0:1]))
                    k.act(V(s2, s2[0:rows, 1:2]), V(s2, s2[0:rows, 0:1]), AF.Sqrt, bias=V(eps, eps[0:rows, :]))
                    k.recip(V(s2, s2[0:rows, 1:2]), V(s2, s2[0:rows, 1:2]))
                    k.act(V(x_, x_[0:rows, :]), V(x_, x_[0:rows, :]), AF.Copy, scale=V(s2, s2[0:rows, 1:2]))
                    k.tt(V(x_, x_[0:rows, :]), V(x_, x_[0:rows, :]), V(gain, gain[0:rows, :]), ALU.mult, eng="pool")
                    k.dma(D_(sg["out"][r0:r0 + rows, :]), V(x_, x_[0:rows, :]), eng="pool")
        k.barrier()


def build_program(cfg, shapes, cst_arrays):
    nc = bass.Bass("TRN2", target_bir_lowering=False)
    T, NSEG, SP, LS, NCs = cfg.T, cfg.NSEG, cfg.SP, cfg.LS, cfg.NC
    W = wviews(declare_weights(nc, shapes))
    C = {n: nc.dram_tensor(n, list(a.shape), F32 if a.dtype == np.float32 else BF16, kind="ExternalInput").ap()
         for n, a in cst_arrays.items()}
    x_own = nc.dram_tensor("x_own", [NSEG * T, D], F32, kind="ExternalInput").ap()
    x_sg = nc.dram_tensor("x_sg", [LS, D], F32, kind="ExternalInput").ap()
    y_own = nc.dram_tensor("y_own", [NSEG * T, D], F32, kind="ExternalOutput").ap()

    def scr(name, shape, dt):
        return nc.dram_tensor(name, list(shape), dt, kind="Internal").ap()
    QT = scr("QT", [NSEG, H, 96, T], BF16)
    KT = scr("KT", [max(SP, 1), H, 96, T], BF16)
    VA = scr("VA", [max(SP, 1), T, H, 128], BF16)
    KTS = scr("KTS", [H, 96, LS], BF16)
    VAS = scr("VAS", [LS, H, 128], BF16)
    QTD = scr("QTD", [H, 96, T], BF16)
    KTD = scr("KTD", [H, 96, T], BF16)
    VAD = scr("VAD", [T, H, 128], BF16)
    AT = scr("AT", [NSEG, D, T], BF16)
    X1 = scr("X1", [NSEG * T, D], F32)
    H2T = scr("H2T", [NSEG, 128, 8, T + 2], BF16)
    WG = scr("WG", [FC, 128, 8, 128], BF16)
    with ExitStack() as es:
        k = K(nc, es)
        segs = []
        for s_ in range(SP):
            segs.append(dict(x=x_own[s_ * T:(s_ + 1) * T, :], cos=C["cosp"], sin=C["sinp"], qt=QT[s_], kt=KT[s_], va=VA[s_]))
        segs.append(dict(x=x_own[SP * T:(SP + 1) * T, :], cos=C["coso"], sin=C["sino"], qt=QT[SP], kt=KTD, va=VAD))
        for c in range(NCs):
            segs.append(dict(x=x_sg[c * T:(c + 1) * T, :], cos=C["cosg"][:, c * T:(c + 1) * T], sin=C["sing"][:, c * T:(c + 1) * T],
                             qt=QTD, kt=KTS[:, :, c * T:(c + 1) * T], va=VAS[c * T:(c + 1) * T, :, :]))
        phase_p1a(k, cfg, W, C, segs)
        jobs = []
        for s_ in range(NSEG):
            for hh in range(H):
                if s_ < SP:
                    jobs.append(dict(qt=QT[s_, hh], kt=KT[s_, hh], va=VA[s_, :, hh, :], at=AT[s_, hh * 64:(hh + 1) * 64, :], Tq=T, Tk=T))
                else:
                    jobs.append(dict(qt=QT[s_, hh], kt=KTS[hh], va=VAS[:, hh, :], at=AT[s_, hh * 64:(hh + 1) * 64, :], Tq=T, Tk=LS))
        phase_p2(k, cfg, jobs)
        segs3 = [dict(x=x_own[s_ * T:(s_ + 1) * T, :], mix=[AT[s_]], x1=X1[s_ * T:(s_ + 1) * T, :], h2t=H2T[s_]) for s_ in range(NSEG)]
        phase_p3(k, cfg, W, C, segs3, nkc=8)
        segs4 = [dict(h2t=H2T[s_], x1=X1[s_ * T:(s_ + 1) * T, :], out=y_own[s_ * T:(s_ + 1) * T, :]) for s_ in range(NSEG)]
        phase_p4(k, cfg, W, C, WG, segs4)
        n_ops = k.n_ops
        k.emit()
    return nc, n_ops


def run_cfg(cfg, inputs, x_prompt, x_sample):
    T, SP, NCs = cfg.T, cfg.SP, cfg.NC
    cst = host_consts()
    cst["cosp"], cst["sinp"] = rope_tables(np.arange(T))
    cst["cosg"], cst["sing"] = rope_tables(np.arange(cfg.LS))
    cst["coso"], cst["sino"] = rope_tables(np.arange(T))
    shapes = {n: inputs[n].shape for n in WNAMES}
    nc, n_ops = build_program(cfg, shapes, cst)
    in_maps = []
    for c in range(NCs):
        m = {n: np.ascontiguousarray(inputs[n], dtype=np.float32) for n in WNAMES}
        m.update(cst)
        co, so = rope_tables(np.arange(c * T, (c + 1) * T))
        m["coso"], m["sino"] = co, so
        parts = [x_prompt[c * SP + s_] for s_ in range(SP)] + [x_sample[c * T:(c + 1) * T]]
        m["x_own"] = np.ascontiguousarray(np.concatenate(parts, 0), dtype=np.float32)
        m["x_sg"] = np.ascontiguousarray(x_sample, dtype=np.float32)
        in_maps.append(m)
    res = run_bass_kernel_spmd(nc, in_maps, core_ids=list(range(NCs)))
    yp = np.zeros((NCs * SP, T, D), np.float32)
    ys = np.zeros((cfg.LS, D), np.float32)
    for c in range(NCs):
        y = res.results[c]["y_own"]
        for s_ in range(SP):
            yp[c * SP + s_] = y[s_ * T:(s_ + 1) * T]
        ys[c * T:(c + 1) * T] = y[SP * T:(SP + 1) * T]
    return yp, ys


def kernel(**inputs):
    inputs = {n: np.asarray(v) for n, v in inputs.items()}
    cfg = Cfg(8, 4, 2048)
    yp, ys = run_cfg(cfg, inputs, inputs["x_prompt"], inputs["x_sample"][0])
    return yp, ys[None]


def phase_p1c(k, cfg, W, C, segs):
    maxn = max(sg["n"] for sg in segs)
    with ExitStack() as es:
        Wx = k.sb(es, "WxBC", [128, 8, 3, 256], BF16)
        Wx1 = k.sb(es, "Wx1", [128, 8, DSSM], BF16)
        cwx = k.sb(es, "cwx", [128, 3, 8], F32)
        cbx = k.sb(es, "cbx", [128, 8], F32)
        Wz = k.sb(es, "Wz", [128, 8, DSSM], BF16)
        Wdt = k.sb(es, "Wdt", [128, 8, 32], BF16)
        ident = k.sb(es, "ident_c", [128, 128], BF16)
        onesb = k.sb(es, "ones_c", [128, 128], BF16)
        cst = k.sb(es, "cst_c", [128, 7, 128], F32)
        cbb = k.sb(es, "cbb", [1, DXBC], BF16)
        cb32 = k.sb(es, "cb32", [1, DXBC], F32)
        cbf = k.sb(es, "cbf", [64, 4], F32)
        dtb = k.sb(es, "dtb", [128, 32], F32)
        Abc = k.sb(es, "Abc", [128, 32], F32)
        eps = k.sb(es, "eps_c", [128, 1], F32)
        k.dma(V(ident), D_(C["identb"]))
        k.dma(V(onesb), D_(C["onesb"]))
        k.dma(V(cst), D_(C["cst32"]))
        k.dma(V(cb32), D_(W["conv_b"].rearrange("(o c) -> o c", o=1)))
        k.cp(V(cbb), V(cb32))
        k.dma(V(cbf), D_(W["conv_b"][1024:1280].rearrange("(i p) -> p i", p=64)), slow=True)
        k.dma(V(cbx), D_(W["conv_b"][0:1024].rearrange("(c p) -> p c", p=128)), slow=True)
        for tap in range(3):
            k.dma(V(cwx, cwx[:, tap, :]), D_(W["conv_w"][tap][0:1024].rearrange("(c p) -> p c", p=128)), slow=True)
        k.dma(V(dtb, dtb[:, 0:16]), D_(W["dt_bias_f"].partition_broadcast(128)))
        k.dma(V(dtb, dtb[:, 16:32]), D_(W["dt_bias_b"].partition_broadcast(128)))
        k.dma(V(Abc, Abc[:, 0:16]), D_(W["a_log_f"].partition_broadcast(128)))
        k.dma(V(Abc, Abc[:, 16:32]), D_(W["a_log_b"].partition_broadcast(128)))
        k.act(V(Abc), V(Abc), AF.Exp)
        k.ts(V(Abc), V(Abc), -1.0, ALU.mult)
        k.memset(V(eps), EPS)
        es_prep = ExitStack()
        st = prep_stage(k, es_prep)
        n1, win = W["norm1"], W["w_in"]
        for tap in range(3):
            prep_weight(k, st, lambda kc, c0, cw, tap=tap: (Wx, Wx[:, kc, tap, c0:c0 + cw]), win[:, O_XBC + 1024:O_XBC + DXBC], D, 256,
                        row_gain=n1, col_gain=W["conv_w"][tap][1024:1280])
        prep_weight(k, st, lambda kc, c0, cw: (Wx1, Wx1[:, kc, c0:c0 + cw]), win[:, O_XBC:O_XBC + 1024], D, 1024, row_gain=n1)
        prep_weight(k, st, lambda kc, c0, cw: (Wz, Wz[:, kc, c0:c0 + cw]), win[:, O_Z:O_Z + DSSM], D, DSSM, row_gain=n1)
        prep_weight(k, st, lambda kc, c0, cw: (Wdt, Wdt[:, kc, c0:c0 + cw]), win[:, O_DT:O_DT + 32], D, 32, row_gain=n1)
        k.barrier()
        es_prep.close()
        P = {
            "ss": Rot([k.sb(es, "ssc%d" % i, [128, 2], F32) for i in range(3)]),
            "junk": Rot([k.sb(es, "junkc%d" % i, [128, D], BF16) for i in range(2)]),
            "tp": Rot([k.ps(es, "tpc%d" % i, [128, D], BF16) for i in range(1)]),
            "ident": ident, "eps": eps,
        }
        hT = k.sb(es, "hTc", [128, 8, maxn + 2], BF16)
        xt = Rot([k.sb(es, "xtc%d" % i, [128, D], F32) for i in range(3)])
        hn = Rot([k.sb(es, "hnc%d" % i, [128, D], BF16) for i in range(2)])
        big = Rot([k.ps(es, "bigc%d" % i, [128, D], F32) for i in range(1)])
        pf = Rot([k.ps(es, "pfc%d" % i, [128, 512], F32) for i in range(2)])
        xsT = Rot([k.sb(es, "xsT%d" % i, [128, 8, 384], BF16) for i in range(2)])
        cvt = Rot([k.sb(es, "cvt%d" % i, [128, 384], F32) for i in range(3)])
        sml = Rot([k.ps(es, "smlc%d" % i, [128, 512], F32) for i in range(3)])
        xsb = Rot([k.sb(es, "xsb%d" % i, [128, D], BF16) for i in range(4)])
        zsb = Rot([k.sb(es, "zsb%d" % i, [128, D], BF16) for i in range(2)])
        btk = Rot([k.sb(es, "btk%d" % i, [128, 128], BF16) for i in range(3)])
        dts = Rot([k.sb(es, "dts%d" % i, [128, 160], F32) for i in range(3)])
        smo = Rot([k.sb(es, "smo%d" % i, [128, 96], F32) for i in range(2)])
        bw = Rot([k.sb(es, "bw%d" % i, [128, H, 64], BF16) for i in range(6)])
        so = Rot([k.sb(es, "so%d" % i, [64, D], F32) for i in range(2)])
        bco = Rot([k.sb(es, "bco%d" % i, [64, 512], BF16) for i in range(3)])
        for sg in segs:
            n, lite = sg["n"], sg["lite"]
            nch = n // 128
            for ti in range(nch):
                x_ = xt.nxt()
                k.dma(V(x_), D_(sg["x"][ti * 128:(ti + 1) * 128, :]))
                n_ = hn.nxt()
                rmsnorm_tile(k, P, x_, 128, n_, 1.0 / 32.0)
                transpose_tile(k, P, n_, 128, hT, hT[:, :, 1 + ti * 128:1 + (ti + 1) * 128])
            x_ = xt.nxt()
            k.dma(V(x_, x_[0:2, :]), D_(sg["xh"]))
            n_ = hn.nxt()
            rmsnorm_tile(k, P, x_, 2, n_, 1.0 / 32.0)
            tp = P["tp"].nxt()
            for j in range(8):
                k.tr(V(tp, tp[:, j * 128:j * 128 + 2]), V(n_, n_[0:2, j * 128:(j + 1) * 128]), V(ident, ident[0:2, 0:2]))
            tpv = tp[:, :].rearrange("p (j t) -> p j t", t=128)
            k.cp(V(hT, hT[:, :, 0:1]), V(tp, tpv[:, :, 0:1]))
            k.cp(V(hT, hT[:, :, n + 1:n + 2]), V(tp, tpv[:, :, 1:2]))
            def proj(c):
                cb0 = 128 * c
                xT_, j3 = blkT[c // 3], c % 3
                tpx = P["tp"].nxt()
                for cc in range(8):
                    k.tr(V(tpx, tpx[:, cc * 128:(cc + 1) * 128]), V(xT_, xT_[:, cc, j3 * 128:(j3 + 1) * 128]), V(ident))
                xs_ = xsb.nxt()
                if c % 2 == 0:
                    k.act(V(xs_), V(tpx), AF.Copy)
                else:
                    k.cp(V(xs_), V(tpx))
                if not lite:
                    k.dma(D_(sg["xs"][c * 128:(c + 1) * 128, :]), V(xs_), eng="pool")
                    pz = big.nxt()
                    for nb in range(2):
                        for kc in range(KC):
                            k.mm(V(pz, pz[:, nb * 512:(nb + 1) * 512]), V(hT, hT[:, kc, cb0 + 1:cb0 + 129]),
                                 V(Wz, Wz[:, kc, nb * 512:(nb + 1) * 512]), start=(kc == 0), stop=(kc == KC - 1))
                    z_ = zsb.nxt()
                    k.act(V(z_), V(pz), AF.Silu)
                    k.dma(D_(sg["zs"][c * 128:(c + 1) * 128, :]), V(z_), eng="pool")
                pm = sml.nxt()
                i = 0
                for tap in range(3):
                    for kc in range(KC):
                        k.mm(V(pm, pm[:, 0:128]), V(hT, hT[:, kc, cb0 + tap:cb0 + tap + 128]), V(Wx, Wx[:, kc, tap, 0:128]),
                             start=(i == 0), stop=False)
                        i += 1
                k.mm(V(pm, pm[:, 0:128]), V(onesb, onesb[0:1, 0:128]), V(cbb, cbb[0:1, 1024:1152]), start=False, stop=True)
                for kc in range(KC):
                    k.mm(V(pm, pm[:, 128:160]), V(hT, hT[:, kc, cb0 + 1:cb0 + 129]), V(Wdt, Wdt[:, kc, :]),
                         start=(kc == 0), stop=(kc == KC - 1))
                bt_ = btk.nxt()
                k.act(V(bt_), V(pm, pm[:, 0:128]), AF.Silu)
                d_ = dts.nxt()
                k.tt(V(d_, d_[:, 0:32]), V(pm, pm[:, 128:160]), V(dtb), ALU.add)
                return xs_, bt_, d_

            def rest(c, xs_, bt_, d_):
                k.act(V(d_, d_[:, 0:32]), V(d_, d_[:, 0:32]), AF.Exp)
                k.act(V(d_, d_[:, 0:32]), V(d_, d_[:, 0:32]), AF.Ln, bias=1.0)
                k.act(V(d_, d_[:, 32:64]), V(d_, d_[:, 0:32]), AF.Ln)
                sm_ = smo.nxt()
                k.tt(V(sm_, sm_[:, 0:32]), V(d_, d_[:, 0:32]), V(Abc), ALU.mult)
                pc = sml.nxt()
                k.mm(V(pc, pc[:, 0:16]), V(cst, cst[:, 0, :]), V(sm_, sm_[:, 0:16]))
                k.mm(V(pc, pc[:, 16:32]), V(cst, cst[:, 1, :]), V(sm_, sm_[:, 16:32]))
                k.mm(V(pc, pc[:, 32:64]), V(cst, cst[:, 2, :]), V(sm_, sm_[:, 0:32]))
                k.tt(V(sm_, sm_[:, 32:48]), V(d_, d_[:, 32:48]), V(pc, pc[:, 0:16]), ALU.subtract)
                k.tt(V(sm_, sm_[:, 48:64]), V(d_, d_[:, 48:64]), V(pc, pc[:, 16:32]), ALU.add)
                k.cp(V(sm_, sm_[:, 64:96]), V(pc, pc[:, 32:64]))
                k.tt(V(d_, d_[:, 96:112]), V(sm_, sm_[:, 64:80]), V(sm_, sm_[:, 32:48]), ALU.add)
                k.cp(V(d_, d_[:, 112:128]), V(sm_, sm_[:, 48:64]))
                k.act(V(d_, d_[:, 128:160]), V(d_, d_[:, 96:128]), AF.Exp)
                k.dma(D_(sg["sm"][c]), V(sm_), eng="pool")
                btv = bt_[:, :].rearrange("p (g n) -> p g n", g=2).unsqueeze(2).broadcast_to([128, 2, 8, 64])
                bws = []
                for d in range(2):
                    b_ = bw.nxt()
                    wv = d_[:, 128 + 16 * d:144 + 16 * d].rearrange("p (g j) -> p g j", g=2).unsqueeze(3).broadcast_to([128, 2, 8, 64])
                    k.tt(V(b_, b_[:, :, :].rearrange("p (g j) n -> p g j n", g=2)), V(bt_, btv), V(d_, wv), ALU.mult, eng="pool")
                    bws.append(b_)
                return bws

            def restB(c, xs_, bws):
                for d in range(2):
                    b_ = bws[d]
                    s_ = so.nxt()
                    for half in range(2):
                        pS = sml.nxt()
                        for j in range(8):
                            hh = half * 8 + j
                            k.mm(V(pS, pS[0:64, j * 64:(j + 1) * 64]), V(b_, b_[:, hh, :]), V(xs_, xs_[:, hh * 64:(hh + 1) * 64]))
                        k.cp(V(s_, s_[:, half * 512:(half + 1) * 512]), V(pS, pS[0:64, :]))
                    k.dma(D_(sg["sst"][c, d]), V(s_), eng="pool")

            blkT = {}

            def xs_block(b):
                t0b = 384 * b
                nt = min(384, n - t0b)
                xT_ = xsT.nxt()
                for cc in range(8):
                    f_ = pf.nxt()
                    for kc in range(KC):
                        k.mm(V(f_, f_[:, 0:nt + 2]), V(Wx1, Wx1[:, kc, cc * 128:(cc + 1) * 128]), V(hT, hT[:, kc, t0b:t0b + nt + 2]),
                             start=(kc == 0), stop=(kc == KC - 1))
                    t_ = cvt.nxt()
                    k.ts(V(t_, t_[:, 0:nt]), V(f_, f_[:, 0:nt]), V(cwx, cwx[:, 0, cc:cc + 1]), ALU.mult)
                    for tap in (1, 2):
                        k.op("dve", lambda hh_, t_=t_, f_=f_, tap=tap, cc=cc, nt=nt: hh_.scalar_tensor_tensor(
                            out=t_[:, 0:nt], in0=f_[:, tap:tap + nt], scalar=cwx[:, tap, cc:cc + 1], in1=t_[:, 0:nt],
                            op0=ALU.mult, op1=ALU.add), reads=[f_, cwx, t_], writes=[t_])
                    k.act(V(xT_, xT_[:, cc, 0:nt]), V(t_, t_[:, 0:nt]), AF.Silu, bias=V(cbx, cbx[:, cc:cc + 1]))
                blkT[b] = xT_

            nblk = (nch + 2) // 3
            xs_block(0)
            nxt_h = proj(0)
            pend = None
            for c in range(nch):
                cur = nxt_h
                if c % 3 == 0 and c // 3 + 1 < nblk:
                    xs_block(c // 3 + 1)
                if c + 1 < nch:
                    nxt_h = proj(c + 1)
                bws = rest(c, *cur)
                if pend is not None:
                    restB(*pend)
                pend = (c, cur[0], bws)
            restB(*pend)
            if lite:
                continue
            for c0 in range(0, n, 512):
                bwid = min(512, n - c0)
                for idx in range(4):
                    pb_ = sml.nxt()
                    i = 0
                    for tap in range(3):
                        for kc in range(KC):
                            k.mm(V(pb_, pb_[0:64, 0:bwid]), V(Wx, Wx[:, kc, tap, idx * 64:(idx + 1) * 64]),
                                 V(hT, hT[:, kc, c0 + tap:c0 + tap + bwid]), start=(i == 0), stop=(i == 23))
                            i += 1
                    o_ = bco.nxt()
                    k.act(V(o_, o_[:, 0:bwid]), V(pb_, pb_[0:64, 0:bwid]), AF.Silu, bias=V(cbf, cbf[:, idx:idx + 1]))
                    k.dma(D_(sg["bct"][idx, :, c0:c0 + bwid]), V(o_, o_[:, 0:bwid]), eng="pool")
        k.barrier()


def phase_p1b(k, cfg, W, C, segs, glob=None):
    maxn = max(sg["n"] for sg in segs)
    maxc = maxn // 128
    with ExitStack() as es:
        ident = k.sb(es, "ident_b", [128, 128], BF16)
        cst = k.sb(es, "cst_b", [128, 7, 128], F32)
        dsk = k.sb(es, "dsk", [128, H], F32)
        eps = k.sb(es, "eps_b", [128, 1], F32)
        zcol = k.sb(es, "zcol", [128, 1], F32)
        k.dma(V(ident), D_(C["identb"]))
        k.dma(V(cst), D_(C["cst32"]))
        mskb = k.sb(es, "mskb", [128, 2, 128], BF16)
        k.cp(V(mskb), V(cst, cst[:, 4:6, :]))
        Tb = k.sb(es, "Tb", [128, 2, 128], BF16)
        k.cp(V(Tb, Tb[:, 0, :]), V(cst, cst[:, 0, :]))
        k.cp(V(Tb, Tb[:, 1, :]), V(cst, cst[:, 6, :]))
        ahi = k.sb(es, "ahi", [128, maxc, 32], BF16)
        alo = k.sb(es, "alo", [128, maxc, 32], BF16)
        atmp = k.sb(es, "atmp", [128, maxc, 32], F32)
        k.dma(V(dsk), D_(W["d_skip"].partition_broadcast(128)))
        k.memset(V(eps), EPS)
        k.memset(V(zcol), 0.0)
        P = {
            "ss": Rot([k.sb(es, "ssb%d" % i, [128, 2], F32) for i in range(3)]),
            "junk": Rot([k.sb(es, "junkb%d" % i, [128, D], BF16) for i in range(2)]),
            "tp": Rot([k.ps(es, "tpb%d" % i, [128, D], BF16) for i in range(1)]),
            "ident": ident, "eps": eps,
        }
        smb = k.sb(es, "smb", [128, maxc, 96], F32)
        bct = k.sb(es, "bctb", [64, 4, maxn], BF16)
        dall = k.sb(es, "dall", [64, maxc, 32], F32)
        hbin = k.sb(es, "hbin", [64, maxc, D], BF16)
        hb = k.sb(es, "hb", [64, D], F32)
        hf = k.sb(es, "hf", [64, D], F32)
        hfb = Rot([k.sb(es, "hfb%d" % i, [64, D], BF16) for i in range(2)])
        sld = Rot([k.sb(es, "sld%d" % i, [64, D], F32) for i in range(3)])
        sld2 = Rot([k.sb(es, "sld2_%d" % i, [64, D], F32) for i in range(3)])
        xsb = Rot([k.sb(es, "xsB%d" % i, [128, D], BF16) for i in range(2)])
        zsb = Rot([k.sb(es, "zsB%d" % i, [128, D], BF16) for i in range(2)])
        gs = Rot([k.sb(es, "gs%d" % i, [128, 2, 128], F32) for i in range(2)])
        lp = Rot([k.sb(es, "lp%d" % i, [128, 2, 128], F32) for i in range(4)])
        ee = Rot([k.sb(es, "ee%d" % i, [64, 2, 128], F32) for i in range(4)])
        mt = Rot([k.sb(es, "mt%d" % i, [128, 2, 128], BF16) for i in range(4)])
        cp_ = Rot([k.sb(es, "cpp%d" % i, [64, 2, 128], BF16) for i in range(4)])
        y1j = Rot([k.sb(es, "y1j%d" % i, [128, 512], F32) for i in range(2)])
        y1 = Rot([k.sb(es, "y1_%d" % i, [128, D], F32) for i in range(2)])
        xsd = Rot([k.sb(es, "xsd%d" % i, [128, D], BF16) for i in range(2)])
        yn = Rot([k.sb(es, "yn%d" % i, [128, D], BF16) for i in range(2)])
        yT = Rot([k.sb(es, "yT%d" % i, [128, 8, 128], BF16) for i in range(2)])
        gm = k.sb(es, "gmask", [64, 2, 128], F32)
        gsm = k.sb(es, "gsm", [64, 128, 32], F32)
        gd = Rot([k.sb(es, "gd%d" % i, [64, 16], F32) for i in range(6)])
        vfl = k.sb(es, "vfl", [64, 2], F32)
        pY = Rot([k.ps(es, "pY%d" % i, [128, D], F32) for i in range(1)])
        pR = Rot([k.ps(es, "pR%d" % i, [128, 512], F32) for i in range(4)])
        pG = Rot([k.ps(es, "pG%d" % i, [128, 256], F32) for i in range(1)])

        def decay_mul(h_, dv):
            k.tt(V(h_, h_[:, :].rearrange("p (h q) -> p h q", q=64)), V(h_, h_[:, :].rearrange("p (h q) -> p h q", q=64)),
                 (dv[0], dv[1].unsqueeze(2).broadcast_to([64, H, 64])), ALU.mult)

        for sg in segs:
            n = sg["n"]
            nch = n // 128
            if sg["init"]:
                NG = glob["NG"]
                k.dma(V(gm, gm[:, :, 0:NG]), D_(glob["mask"]))
                k.dma(V(gsm, gsm[:, 0:NG, :]), D_(glob["sm"][:, 0:64, 64:96].rearrange("c p f -> p c f")), slow=True)
                k.memset(V(hf), 0.0)
                k.memset(V(hb), 0.0, eng="pool")
                skip = glob.get("skip", 0)
                for step in range(NG - skip):
                    for d, h_, eng_ in ((0, hf, "dve"), (1, hb, "dve")):
                        kk = step if d == 0 else NG - 1 - step
                        g_ = gd.nxt()
                        k.act(V(g_), V(gsm, gsm[:, kk, 16 * d:16 * d + 16]), AF.Exp, scale=V(gm, gm[:, d, kk:kk + 1]))
                        hv = h_[:, :].rearrange("p (h q) -> p h q", q=64)
                        k.tt(V(h_, hv), V(h_, hv), (g_, g_[:, :].unsqueeze(2).broadcast_to([64, H, 64])), ALU.mult, eng=eng_)
                        s_ = (sld if d == 0 else sld2).nxt()
                        k.dma(V(s_), D_(glob["sst"][kk, d]))
                        if eng_ == "dve":
                            k.op(eng_, lambda hh, s_=s_, h_=h_, d=d, kk=kk: hh.scalar_tensor_tensor(
                                out=h_[:, :], in0=s_[:, :], scalar=gm[:, d, kk:kk + 1], in1=h_[:, :], op0=ALU.mult, op1=ALU.add),
                                reads=[s_, gm, h_], writes=[h_])
                        else:
                            k.ts(V(s_), V(s_), V(gm, gm[:, d, kk:kk + 1]), ALU.mult, eng=eng_)
                            k.tt(V(h_), V(h_), V(s_), ALU.add, eng=eng_)
            else:
                k.memset(V(hf), 0.0)
                k.memset(V(hb), 0.0)
            if sg["vflag"] is not None:
                k.dma(V(vfl), D_(sg["vflag"]))
            k.dma(V(smb, smb[:, 0:nch, :]), D_(sg["sm"].rearrange("c p f -> p c f")))
            k.dma(V(bct, bct[:, :, 0:n]), D_(sg["bct"].rearrange("i p t -> p i t")))
            k.act(V(dall, dall[:, 0:nch, :]), V(smb, smb[0:64, 0:nch, 64:96]), AF.Exp)
            k.cp(V(ahi, ahi[:, 0:nch, :]), V(smb, smb[:, 0:nch, 0:32]))
            k.tt(V(atmp, atmp[:, 0:nch, :]), V(smb, smb[:, 0:nch, 0:32]), V(ahi, ahi[:, 0:nch, :]), ALU.subtract)
            k.cp(V(alo, alo[:, 0:nch, :]), V(atmp, atmp[:, 0:nch, :]))

            def load_state(kk, d):
                s_ = sld.nxt()
                k.dma(V(s_), D_(sg["sst"][kk, d]))
                ne = sg.get("nedge", 1)
                if sg["vflag"] is not None and (kk < ne or kk >= nch - ne):
                    col = 0 if kk < ne else 1
                    k.ts(V(s_), V(s_), V(vfl, vfl[:, col:col + 1]), ALU.mult)
                return s_

            for kk in range(nch - 1, -1, -1):
                k.act(V(hbin, hbin[:, kk, :]), V(hb), AF.Copy)
                if kk > 0:
                    s_ = load_state(kk, 1)
                    decay_mul(hb, V(dall, dall[:, kk, 16:32]))
                    k.tt(V(hb), V(hb), V(s_), ALU.add)
            its = [(kk, d, pr) for kk in range(nch) for d in range(2) for pr in range(8)]
            ctx = {}
            rbuf = {}

            def emit_R(i):
                kk, d, pr = its[i]
                t0 = kk * 128
                if d == 0 and pr == 0:
                    xs_, z_ = xsb.nxt(), zsb.nxt()
                    k.dma(V(xs_), D_(sg["xs"][t0:t0 + 128, :]))
                    k.dma(V(z_), D_(sg["zs"][t0:t0 + 128, :]))
                    g_ = pG.nxt()
                    for g in range(2):
                        k.mm(V(g_, g_[:, g * 128:(g + 1) * 128]), V(bct, bct[:, g, t0:t0 + 128]), V(bct, bct[:, 2 + g, t0:t0 + 128]))
                    gs_ = gs.nxt()
                    k.cp(V(gs_), V(g_, g_[:, :].rearrange("p (g t) -> p g t", g=2)))
                    xd_ = xsd.nxt()
                    k.tt(V(xd_, xd_[:, :].rearrange("p (h q) -> p h q", q=64)), V(xs_, xs_[:, :].rearrange("p (h q) -> p h q", q=64)),
                         V(dsk, dsk[:, :].unsqueeze(2).broadcast_to([128, H, 64])), ALU.mult, eng="pool")
                    ctx[kk] = dict(xs=xs_, z=z_, gs=gs_, xd=xd_)
                r_ = pR.nxt()
                for j in range(2):
                    hh = 2 * pr + j
                    hcol = ahi[:, kk, 16 * d + hh:16 * d + hh + 1].broadcast_to([128, 128])
                    lcol = alo[:, kk, 16 * d + hh:16 * d + hh + 1].broadcast_to([128, 128])
                    rm = r_[:, j * 128:(j + 1) * 128]
                    ru = r_[:, 256 + j * 128:256 + (j + 1) * 128]
                    k.mm(V(r_, rm), V(ident), V(mskb, mskb[:, d, :]), start=True, stop=False)
                    k.mm(V(r_, rm), V(ahi, hcol), V(Tb, Tb[:, d, :]), start=False, stop=False)
                    k.mm(V(r_, rm), V(alo, lcol), V(Tb, Tb[:, d, :]), start=False, stop=True)
                    k.mm(V(r_, ru), V(ahi, hcol), V(Tb, Tb[:, d, :]), start=True, stop=False)
                    k.mm(V(r_, ru), V(alo, lcol), V(Tb, Tb[:, d, :]), start=False, stop=True)
                rbuf[i] = r_

            emit_R(0)
            if len(its) > 1:
                emit_R(1)
            for i, (kk, d, pr) in enumerate(its):
                t0 = kk * 128
                if i + 2 < len(its):
                    emit_R(i + 2)
                cx = ctx[kk]
                xs_, z_, gs_ = cx["xs"], cx["z"], cx["gs"]
                if d == 0 and pr == 0:
                    hfb_ = hfb.nxt()
                    k.cp(V(hfb_), V(hf))
                    y_ = pY.nxt()
                    for nb in range(2):
                        k.mm(V(y_, y_[:, nb * 512:(nb + 1) * 512]), V(ident), V(cx["xd"], cx["xd"][:, nb * 512:(nb + 1) * 512]), start=True, stop=False)
                    cx["hfb"], cx["y"] = hfb_, y_
                hfb_, y_ = cx["hfb"], cx["y"]
                r_ = rbuf.pop(i)
                lp_, ee_ = lp.nxt(), ee.nxt()
                for j in range(2):
                    hh = 2 * pr + j
                    k.act(V(lp_, lp_[:, j, :]), V(r_, r_[:, j * 128:(j + 1) * 128]), AF.Exp,
                          bias=V(smb, smb[:, kk, 32 + 16 * d + hh:32 + 16 * d + hh + 1]))
                    eb = V(zcol, zcol[0:64, 0:1]) if d == 0 else V(smb, smb[0:64, kk, 80 + hh:80 + hh + 1])
                    k.act(V(ee_, ee_[:, j, :]), V(r_, r_[0:64, 256 + j * 128:256 + (j + 1) * 128]), AF.Exp, bias=eb)
                g = pr // 4
                mt_, c_ = mt.nxt(), cp_.nxt()
                k.tt(V(mt_), V(lp_), V(gs_, gs_[:, g:g + 1, :].broadcast_to([128, 2, 128])), ALU.mult)
                k.tt(V(c_), V(ee_), V(bct, bct[:, 2 + g:3 + g, t0:t0 + 128].broadcast_to([64, 2, 128])), ALU.mult)
                hst = hfb_ if d == 0 else hbin
                for j in range(2):
                    hh = 2 * pr + j
                    ysl = y_[:, hh * 64:(hh + 1) * 64]
                    k.mm(V(y_, ysl), V(mt_, mt_[:, j, :]), V(xs_, xs_[:, hh * 64:(hh + 1) * 64]), start=False, stop=False)
                    hs_ap = hfb_[:, hh * 64:(hh + 1) * 64] if d == 0 else hbin[:, kk, hh * 64:(hh + 1) * 64]
                    k.mm(V(y_, ysl), V(c_, c_[:, j, :]), V(hst, hs_ap), start=False, stop=(d == 1 and hh in (7, 15)))
                if not (d == 1 and pr == 7):
                    continue
                s_ = load_state(kk, 0)
                decay_mul(hf, V(dall, dall[:, kk, 0:16]))
                k.tt(V(hf), V(hf), V(s_), ALU.add)
                a_ = y1.nxt()
                k.tt(V(a_), V(y_), V(z_), ALU.mult)
                s2, n_ = P["ss"].nxt(), yn.nxt()
                s3 = P["ss"].nxt()
                for g in range(2):
                    jk = y1j.nxt()
                    k.op("dve", lambda hh_, a_=a_, jk=jk, s2=s2, g=g: hh_.scalar_tensor_tensor(
                        out=jk[:, 0:512], in0=a_[:, g * 512:(g + 1) * 512], scalar=1.0 / 512.0, in1=a_[:, g * 512:(g + 1) * 512],
                        op0=ALU.mult, op1=ALU.mult, accum_out=s2[:, g:g + 1]), reads=[a_], writes=[jk, s2])
                k.act(V(s3), V(s2), AF.Ln, bias=V(eps))
                k.act(V(s3), V(s3), AF.Exp, scale=-0.5)
                for g in range(2):
                    k.ts(V(n_, n_[:, g * 512:(g + 1) * 512]), V(a_, a_[:, g * 512:(g + 1) * 512]), V(s3, s3[:, g:g + 1]), ALU.mult)
                t_ = yT.nxt()
                transpose_tile(k, P, n_, 128, t_, t_[:, :, :])
                k.dma(D_(sg["yt"].rearrange("(c p) t -> p c t", p=128)[:, :, t0:t0 + 128]), V(t_), eng="pool")
                del ctx[kk]
        k.barrier()


EXT = 128


def build_program(cfg, shapes, cst_arrays):
    nc = bass.Bass("TRN2", target_bir_lowering=False)
    T, NSEG, SP, LS, NCs = cfg.T, cfg.NSEG, cfg.SP, cfg.LS, cfg.NC
    NS = T + 2 * EXT
    NSP = ((NS + 511) // 512) * 512
    NG = LS // 128
    W = wviews(declare_weights(nc, shapes))
    C = {n: nc.dram_tensor(n, list(a.shape), F32 if a.dtype == np.float32 else BF16, kind="ExternalInput").ap()
         for n, a in cst_arrays.items()}
    x_own = nc.dram_tensor("x_own", [SP * T + NSP, D], F32, kind="ExternalInput").ap()
    xh_own = nc.dram_tensor("xh_own", [NSEG, 2, D], F32, kind="ExternalInput").ap()
    x_sg = nc.dram_tensor("x_sg", [LS, D], F32, kind="ExternalInput").ap()
    xh_sg = nc.dram_tensor("xh_sg", [NCs, 2, D], F32, kind="ExternalInput").ap()
    gmask = nc.dram_tensor("gmask", [64, 2, NG], F32, kind="ExternalInput").ap()
    vflag = nc.dram_tensor("vflag", [128, 2], F32, kind="ExternalInput").ap()
    y_own = nc.dram_tensor("y_own", [NSEG * T, D], F32, kind="ExternalOutput").ap()

    def scr(name, shape, dt):
        return nc.dram_tensor(name, list(shape), dt, kind="Internal").ap()
    seg_n = [T] * SP + [NS]
    seg_off = [s_ * T for s_ in range(SP)] + [SP * T]
    QT = [scr("QT%d" % s_, [H, 96, (NSP if s_ == SP else seg_n[s_])], BF16) for s_ in range(NSEG)]
    KT = scr("KT", [max(SP, 1), H, 96, T], BF16)
    VA = scr("VA", [max(SP, 1), T, H, 128], BF16)
    KTS = scr("KTS", [H, 96, LS], BF16)
    VAS = scr("VAS", [LS, H, 128], BF16)
    AT = [scr("AT%d" % s_, [D, seg_n[s_]], BF16) for s_ in range(NSEG)]
    YT = [scr("YT%d" % s_, [D, seg_n[s_]], BF16) for s_ in range(NSEG)]
    XS = [scr("XS%d" % s_, [seg_n[s_], D], BF16) for s_ in range(NSEG)]
    ZS = [scr("ZS%d" % s_, [seg_n[s_], D], BF16) for s_ in range(NSEG)]
    SST = [scr("SST%d" % s_, [seg_n[s_] // 128, 2, 64, D], F32) for s_ in range(NSEG)]
    SM = [scr("SM%d" % s_, [seg_n[s_] // 128, 128, 96], F32) for s_ in range(NSEG)]
    BCT = [scr("BCT%d" % s_, [4, 64, seg_n[s_]], BF16) for s_ in range(NSEG)]
    SSTG = scr("SSTG", [NG, 2, 64, D], F32)
    SMG = scr("SMG", [NG, 128, 96], F32)
    X1 = [scr("X1_%d" % s_, [seg_n[s_], D], F32) for s_ in range(NSEG)]
    H2T = [scr("H2T%d" % s_, [128, 8, seg_n[s_] + 2], BF16) for s_ in range(NSEG)]
    WG = scr("WG", [FC, 128, 8, 128], BF16)
    with ExitStack() as es:
        k = K(nc, es)
        xo = [x_own[seg_off[s_]:seg_off[s_] + seg_n[s_], :] for s_ in range(NSEG)]
        segs = []
        for s_ in range(SP):
            segs.append(dict(x=xo[s_], n=T, cos=C["cosp"], sin=C["sinp"], qt=QT[s_], kt=KT[s_], va=VA[s_]))
        segs.append(dict(x=x_own[SP * T:SP * T + NSP, :], n=NSP, cos=C["coso"], sin=C["sino"], qt=QT[SP], do_kv=False))
        for c in range(NCs):
            segs.append(dict(x=x_sg[c * T:(c + 1) * T, :], n=T, cos=C["cosg"][:, c * T:(c + 1) * T], sin=C["sing"][:, c * T:(c + 1) * T],
                             do_q=False, kt=KTS[:, :, c * T:(c + 1) * T], va=VAS[c * T:(c + 1) * T, :, :]))
        phase_p1a(k, cfg, W, C, segs)
        segc = [dict(x=xo[s_], xh=xh_own[s_], n=seg_n[s_], lite=False, xs=XS[s_], zs=ZS[s_], sst=SST[s_], sm=SM[s_], bct=BCT[s_])
                for s_ in range(NSEG)]
        for c in range(NCs):
            segc.append(dict(x=x_sg[c * T:(c + 1) * T, :], xh=xh_sg[c], n=T, lite=True,
                             sst=SSTG[c * (T // 128):(c + 1) * (T // 128)], sm=SMG[c * (T // 128):(c + 1) * (T // 128)]))
        phase_p1c(k, cfg, W, C, segc)
        segb = []
        for s_ in range(NSEG):
            segb.append(dict(n=seg_n[s_], xs=XS[s_], zs=ZS[s_], sst=SST[s_], sm=SM[s_], bct=BCT[s_], yt=YT[s_],
                             init=(s_ == SP), vflag=(vflag[0:64, :] if s_ == SP else None), nedge=EXT // 128))
        phase_p1b(k, cfg, W, C, segb, glob=dict(sst=SSTG, sm=SMG, mask=gmask, NG=NG, skip=(T + EXT) // 128))
        jobs = []
        for s_ in range(NSEG):
            for hh in range(H):
                if s_ < SP:
                    jobs.append(dict(qt=QT[s_][hh], kt=KT[s_, hh], kr=KT[s_, 0, 64:96, :], va=VA[s_, :, hh, :], at=AT[s_][hh * 64:(hh + 1) * 64, :], Tq=T, Tk=T))
                else:
                    jobs.append(dict(qt=QT[s_][hh][:, 0:NS], kt=KTS[hh], kr=KTS[0, 64:96, :], va=VAS[:, hh, :], at=AT[s_][hh * 64:(hh + 1) * 64, :], Tq=NS, Tk=LS))
        phase_p2(k, cfg, jobs)
        segs3 = [dict(x=xo[s_], n=seg_n[s_], mix=[AT[s_], YT[s_]], x1=X1[s_], h2t=H2T[s_]) for s_ in range(NSEG)]
        phase_p3(k, cfg, W, C, segs3, nkc=16)
        segs4 = []
        for s_ in range(NSEG):
            if s_ < SP:
                segs4.append(dict(h2t=H2T[s_], x1=X1[s_], out=y_own[s_ * T:(s_ + 1) * T, :]))
            else:
                segs4.append(dict(h2t=H2T[s_][:, :, EXT:EXT + T + 2], x1=X1[s_][EXT:EXT + T, :], out=y_own[s_ * T:(s_ + 1) * T, :],
                                  vflag=vflag))
        phase_p4(k, cfg, W, C, WG, segs4)
        n_ops = k.n_ops
        k.emit()
    return nc, n_ops


def run_cfg(cfg, inputs, x_prompt, x_sample):
    T, SP, NCs, LS = cfg.T, cfg.SP, cfg.NC, cfg.LS
    NS = T + 2 * EXT
    NSP = ((NS + 511) // 512) * 512
    NG = LS // 128
    cst = host_consts()
    cst["cosp"], cst["sinp"] = rope_tables(np.arange(T))
    cst["cosg"], cst["sing"] = rope_tables(np.arange(LS))
    cst["coso"], cst["sino"] = rope_tables(np.arange(NSP))
    shapes = {n: inputs[n].shape for n in WNAMES}
    nc, n_ops = build_program(cfg, shapes, cst)
    xs32 = np.ascontiguousarray(x_sample, dtype=np.float32)
    xpad = np.zeros((LS + 2 * EXT + 2, D), np.float32)
    xpad[EXT + 1:EXT + 1 + LS] = xs32
    zero_row = np.zeros((D,), np.float32)
    xh_sg = np.stack([np.stack([xs32[c * T - 1] if c > 0 else zero_row, xs32[(c + 1) * T] if c < NCs - 1 else zero_row])
                      for c in range(NCs)])
    in_maps = []
    for c in range(NCs):
        m = {n: np.ascontiguousarray(inputs[n], dtype=np.float32) for n in WNAMES}
        m.update(cst)
        lo = c * T - EXT
        co, so = rope_tables(np.arange(lo, lo + NSP))
        m["coso"], m["sino"] = co, so
        parts = [x_prompt[c * SP + s_] for s_ in range(SP)] + [xpad[lo + EXT + 1:lo + EXT + 1 + NS], np.zeros((NSP - NS, D), np.float32)]
        m["x_own"] = np.ascontiguousarray(np.concatenate(parts, 0), dtype=np.float32)
        xh = np.zeros((SP + 1, 2, D), np.float32)
        xh[SP, 0] = xpad[lo + EXT]
        xh[SP, 1] = xpad[lo + EXT + 1 + NS]
        m["xh_own"] = xh
        m["x_sg"] = xs32
        m["xh_sg"] = xh_sg
        kk = np.arange(NG)
        gm = np.zeros((64, 2, NG), np.float32)
        gm[:, 0, :] = (kk < (lo // 128 if lo >= 0 else -((-lo) // 128)))[None, :]
        gm[:, 1, :] = (kk >= (lo + NS) // 128)[None, :]
        m["gmask"] = gm
        vf = np.ones((128, 2), np.float32)
        if c == 0:
            vf[:, 0] = 0.0
        if c == NCs - 1:
            vf[:, 1] = 0.0
        m["vflag"] = vf
        in_maps.append(m)
    res = run_bass_kernel_spmd(nc, in_maps, core_ids=list(range(NCs)))
    yp = np.zeros((NCs * SP, T, D), np.float32)
    ys = np.zeros((LS, D), np.float32)
    for c in range(NCs):
        y = res.results[c]["y_own"]
        for s_ in range(SP):
            yp[c * SP + s_] = y[s_ * T:(s_ + 1) * T]
        ys[c * T:(c + 1) * T] = y[SP * T:(SP + 1) * T]
    return yp, ys
```

```python
import os
from contextlib import ExitStack
import numpy as np
import ml_dtypes
import concourse.bass as bass
import concourse.mybir as mybir
from concourse.bass_utils import run_bass_kernel_spmd

F32 = mybir.dt.float32
BF16 = mybir.dt.bfloat16
AF = mybir.ActivationFunctionType
ALU = mybir.AluOpType

D = 1024
KC = 8
H = 16
QL, KVL, RO = 384, 256, 32
DSSM, DXBC, NST = 1024, 1280, 64
DIN = 3008
DFF = 2816
FC = 22
EPS = 1e-6
NEG = -30000.0
O_Q, O_CKV, O_KR, O_Z, O_XBC, O_DT = 0, 384, 640, 672, 1696, 2976
SCALE = 96.0 ** -0.5


class Buf:
    def __init__(self, name, t, is_dram=False):
        self.name = name
        self.t = t
        self.is_dram = is_dram
        self.w = None
        self.r = []
        self.dsem = None
        self.dcnt = 0

    def __getitem__(self, idx):
        return self.t[idx]


class K:
    ENG = ("pe", "act", "dve", "pool", "sp")

    def __init__(self, nc, es, n_dma_sems=46, n_sw_sems=50):
        self.nc = nc
        self.q = {e: [] for e in self.ENG}
        self.cnt = {e: 0 for e in self.ENG}
        self.waited = {e: {} for e in self.ENG}
        self.sem = {e: es.enter_context(nc.semaphore("s_" + e)) for e in ("pe", "act", "dve", "pool")}
        self.dma_pool = [es.enter_context(nc.semaphore("d%d" % i)) for i in range(n_dma_sems + n_sw_sems)]
        self.dma_free = list(range(n_dma_sems))
        self.sw_free = list(range(n_dma_sems, n_dma_sems + n_sw_sems))
        self.dma_val = [0] * (n_dma_sems + n_sw_sems)
        self.live_sw = []
        self.live = []
        self.n_ops = 0

    def sb(self, es, name, shape, dt):
        self.uid = getattr(self, "uid", 0) + 1
        name = "%s_u%d" % (name, self.uid)
        return Buf(name, es.enter_context(self.nc.sbuf_tensor(name, list(shape), dt)))

    def ps(self, es, name, shape, dt):
        self.uid = getattr(self, "uid", 0) + 1
        name = "%s_u%d" % (name, self.uid)
        b = Buf(name, es.enter_context(self.nc.psum_tensor(name, list(shape), dt)))
        b.is_psum = True
        return b

    def _need(self, eng, dep, out):
        kind, s, v = dep
        if kind == "pe" and eng == "pe":
            return
        key = (kind, s)
        if self.waited[eng].get(key, -1) >= v:
            return
        self.waited[eng][key] = v
        out.append(dep)

    def op(self, eng, fn, reads=(), writes=(), dma=False):
        deps = []
        for b in reads:
            if b.w is not None:
                self._need(eng, b.w, deps)
            if getattr(b, "is_psum", False):
                for r in b.r:
                    if r[0] != eng:
                        self._need(eng, r, deps)
        for b in writes:
            if b.w is not None and (dma or b.w[0] != eng):
                self._need(eng, b.w, deps)
            for r in b.r:
                if dma or r[0] != eng:
                    self._need(eng, r, deps)
        if dma:
            owner = None
            for b in list(writes) + list(reads):
                if not b.is_dram:
                    owner = b
                    break
            if owner is None:
                owner = (list(writes) + list(reads))[0]
            if eng == "pool":
                if getattr(owner, "swsem", None) is None:
                    owner.swsem = self.sw_free.pop()
                    owner.swcnt = 0
                    self.live_sw.append(owner)
                owner.swcnt += 16
                tok = ("dma", owner.swsem, owner.swcnt)
                semh, val = self.dma_pool[owner.swsem], 16
            else:
                if owner.dsem is None:
                    owner.dsem = self.dma_free.pop()
                    owner.dcnt = self.dma_val[owner.dsem]
                    self.live.append(owner)
                owner.dcnt += 16
                tok = ("dma", owner.dsem, owner.dcnt)
                semh, val = self.dma_pool[owner.dsem], 16
        else:
            self.cnt[eng] += 1
            tok = (eng, None, self.cnt[eng])
            semh, val = self.sem[eng], 1
        self.q[eng].append((deps, fn, semh, val))
        self.n_ops += 1
        for b in reads:
            b.r.append(tok)
            if len(b.r) > 64:
                b.r = b.r[-64:]
        for b in writes:
            b.w = tok
            b.r = []
        return tok

    def barrier(self):
        toks = [(e, None, self.cnt[e]) for e in ("pe", "act", "dve", "pool") if self.cnt[e]]
        for b in self.live:
            toks.append(("dma", b.dsem, b.dcnt))
        for b in self.live_sw:
            toks.append(("dma", b.swsem, b.swcnt))
        self.live_sw = []
        for e in self.ENG:
            deps = []
            for t in toks:
                if t[0] == e:
                    continue
                self._need(e, t, deps)
            if deps:
                self.q[e].append((deps, None, None, 0))
        for b in self.live:
            self.dma_val[b.dsem] = b.dcnt
            self.dma_free.append(b.dsem)
            b.dsem = None
        self.live = []

    def emit(self):
        with self.nc.Block() as block:
            def run(eng_name):
                def f(h):
                    for deps, fn, semh, val in self.q[eng_name]:
                        for kind, s, v in deps:
                            h.wait_ge(self.dma_pool[s] if kind == "dma" else self.sem[kind], v)
                        if fn is not None:
                            fn(h).then_inc(semh, val)
                return f
            block.tensor(run("pe"))
            block.scalar(run("act"))
            block.vector(run("dve"))
            block.gpsimd(run("pool"))
            block.sync(run("sp"))

    def dma(self, out, in_, eng="sp"):
        (ob, oa), (ib, ia) = out, in_
        return self.op(eng, lambda h: h.dma_start(out=oa, in_=ia), reads=[ib], writes=[ob], dma=True)

    def mm(self, out, lhsT, rhs, start=True, stop=True):
        (ob, oa), (lb, la), (rb, ra) = out, lhsT, rhs
        return self.op("pe", lambda h: h.matmul(oa, la, ra, start=start, stop=stop), reads=[lb, rb], writes=[ob])

    def tr(self, out, in_, ident):
        (ob, oa), (ib, ia), (db, da) = out, in_, ident
        return self.op("pe", lambda h: h.transpose(oa, ia, da), reads=[ib, db], writes=[ob])

    def act(self, out, in_, func, bias=None, scale=1.0, accum=None):
        (ob, oa), (ib, ia) = out, in_
        reads, writes = [ib], [ob]
        kw = {}
        if bias is not None:
            if isinstance(bias, tuple):
                reads.append(bias[0]); kw["bias"] = bias[1]
            else:
                kw["bias"] = bias
        if isinstance(scale, tuple):
            reads.append(scale[0]); kw["scale"] = scale[1]
        else:
            kw["scale"] = scale
        if accum is not None:
            writes.append(accum[0]); kw["accum_out"] = accum[1]
        return self.op("act", lambda h: h.activation(out=oa, in_=ia, func=func, **kw), reads=reads, writes=writes)

    def tt(self, out, in0, in1, op, eng="dve"):
        (ob, oa), (ab, aa), (bb, ba) = out, in0, in1
        return self.op(eng, lambda h: h.tensor_tensor(out=oa, in0=aa, in1=ba, op=op), reads=[ab, bb], writes=[ob])

    def ts(self, out, in0, s1, op0, s2=None, op1=None, eng="dve", accum=None):
        (ob, oa), (ab, aa) = out, in0
        reads, writes = [ab], [ob]
        if isinstance(s1, tuple):
            reads.append(s1[0]); s1 = s1[1]
        if isinstance(s2, tuple):
            reads.append(s2[0]); s2 = s2[1]
        kw = {}
        if op1 is not None:
            kw["op1"] = op1
        if accum is not None:
            writes.append(accum[0]); kw["accum_out"] = accum[1]
        return self.op(eng, lambda h: h.tensor_scalar(oa, aa, s1, s2, op0, **kw), reads=reads, writes=writes)

    def cp(self, out, in_, eng="dve"):
        (ob, oa), (ib, ia) = out, in_
        return self.op(eng, lambda h: h.tensor_copy(out=oa, in_=ia), reads=[ib], writes=[ob])

    def memset(self, out, val, eng="dve"):
        (ob, oa) = out
        return self.op(eng, lambda h: h.memset(oa, val), writes=[ob])

    def recip(self, out, in_):
        (ob, oa), (ib, ia) = out, in_
        return self.op("dve", lambda h: h.reciprocal(out=oa, in_=ia), reads=[ib], writes=[ob])


def V(buf, ap=None):
    return (buf, buf.t[:] if ap is None else ap)


class Cfg:
    def __init__(self, nc_cores=8, sp=4, t=2048):
        self.NC = nc_cores
        self.SP = sp
        self.T = t
        self.NSEG = sp + 1
        self.NCH = t // 128
        self.NB = t // 512
        self.LS = nc_cores * t


class Rot:
    def __init__(self, items):
        self.items = items
        self.i = 0

    def nxt(self):
        b = self.items[self.i % len(self.items)]
        self.i += 1
        return b


def D_(ap):
    return (None, ap)


def _dma(k, out, in_, eng="sp", slow=False):
    (ob, oa), (ib, ia) = out, in_
    reads = [ib] if ib is not None else []
    writes = [ob] if ob is not None else []
    if not reads and not writes:
        raise ValueError("dram->dram untracked")
    if slow:
        return k.op(eng, lambda h: h.dma_start(out=oa, in_=ia, allow_slow_non_contiguous=True), reads=reads, writes=writes, dma=True)
    return k.op(eng, lambda h: h.dma_start(out=oa, in_=ia), reads=reads, writes=writes, dma=True)


K.dma = _dma


def prep_weight(k, st, dst_fn, src, K_rows, cols, row_gain=None, col_gain=None):
    nk = K_rows // 128
    CB = 1408
    if row_gain is not None:
        rg = st["rg"].nxt()
        k.dma(V(rg, rg[:, 0:nk]), D_(row_gain.rearrange("(c p) -> p c", p=128)), slow=True)
    for c0 in range(0, cols, CB):
        cw = min(CB, cols - c0)
        if col_gain is not None:
            cg = st["cg"].nxt()
            k.dma(V(cg, cg[:, 0:cw]), D_(col_gain[c0:c0 + cw].partition_broadcast(128)))
        for kc in range(nk):
            s32 = st["s32"].nxt()
            k.dma(V(s32, s32[:, 0:cw]), D_(src[kc * 128:(kc + 1) * 128, c0:c0 + cw]))
            cur = V(s32, s32[:, 0:cw])
            if col_gain is not None:
                k.tt(cur, cur, V(cg, cg[:, 0:cw]), ALU.mult, eng="pool")
            db, da = dst_fn(kc, c0, cw)
            if db is None:
                sbf = st["sbf"].nxt()
                o = V(sbf, sbf[:, 0:cw])
            else:
                o = (db, da)
            if row_gain is not None:
                k.act(o, cur, AF.Copy, scale=V(rg, rg[:, kc:kc + 1]))
            else:
                k.act(o, cur, AF.Copy)
            if db is None:
                if len(da.shape) == 3:
                    o = (o[0], o[1].rearrange("p (m c) -> p m c", c=128))
                k.dma(D_(da), o, eng="pool")


def prep_stage(k, es):
    return {
        "s32": Rot([k.sb(es, "p0s32_%d" % i, [128, 1408], F32) for i in range(3)]),
        "sbf": Rot([k.sb(es, "p0sbf_%d" % i, [128, 1408], BF16) for i in range(3)]),
        "cg": Rot([k.sb(es, "p0cg_%d" % i, [128, 1408], F32) for i in range(2)]),
        "rg": Rot([k.sb(es, "p0rg_%d" % i, [128, 24], F32) for i in range(2)]),
    }


def load_w(k, buf, dram_ap):
    n = dram_ap.shape[1]
    step = max(1, n // 4)
    for c in range(0, n, step):
        e = min(n, c + step)
        k.dma(V(buf, buf[:, c:e]), D_(dram_ap[:, c:e]))


def rmsnorm_tile(k, P, xt, rows, hn, dim_scale):
    ss = P["ss"].nxt()
    junk = P["junk"].nxt()
    k.act(V(junk, junk[0:rows, :]), V(xt, xt[0:rows, :]), AF.Square, scale=dim_scale, accum=V(ss, ss[0:rows, 0:1]))
    k.act(V(ss, ss[0:rows, 1:2]), V(ss, ss[0:rows, 0:1]), AF.Sqrt, bias=V(P["eps"], P["eps"][0:rows, 0:1]))
    k.recip(V(ss, ss[0:rows, 1:2]), V(ss, ss[0:rows, 1:2]))
    k.ts(V(hn, hn[0:rows, :]), V(xt, xt[0:rows, :]), V(ss, ss[0:rows, 1:2]), ALU.mult)


def transpose_tile(k, P, hn, rows, dst_buf, dst_ap_fn):
    tp = P["tp"].nxt()
    for j in range(8):
        k.tr(V(tp, tp[:, j * 128:j * 128 + rows]), V(hn, hn[0:rows, j * 128:(j + 1) * 128]),
             V(P["ident"], P["ident"][0:rows, 0:rows]))
    src = tp[:, :].rearrange("p (j t) -> p j t", t=128)[:, :, 0:rows]
    k.cp(V(dst_buf, dst_ap_fn), V(tp, src))


def phase_p1a(k, cfg, W, C, segs):
    nc = k.nc
    with ExitStack() as es:
        Wq = k.sb(es, "Wq", [128, 8, QL], BF16)
        Wc = k.sb(es, "Wc", [128, 8, KVL], BF16)
        Wkr = k.sb(es, "Wkr", [128, 8, 96], BF16)
        Wks = k.sb(es, "Wks", [128, 8, 96], BF16)
        Wqb = k.sb(es, "Wqb", [128, 3, H * 96], BF16)
        Wqs = k.sb(es, "Wqs", [128, 3, H * 96], BF16)
        Wkb = k.sb(es, "Wkb", [128, 2, H * 64], BF16)
        Wvb = k.sb(es, "Wvb", [128, 2, H * 64], BF16)
        ident = k.sb(es, "ident_sb", [128, 128], BF16)
        onesb = k.sb(es, "ones_sb", [128, 128], BF16)
        eps_t = k.sb(es, "eps_sb", [128, 1], F32)
        es_prep = ExitStack()
        st = prep_stage(k, es_prep)
        k.dma(V(ident), D_(C["identb"]))
        k.dma(V(onesb), D_(C["onesb"]))
        n1 = W["norm1"]
        win = W["w_in"]
        prep_weight(k, st, lambda kc, c0, cw: (Wq, Wq[:, kc, c0:c0 + cw]), win[:, O_Q:O_Q + QL], D, QL, row_gain=n1)
        prep_weight(k, st, lambda kc, c0, cw: (Wc, Wc[:, kc, c0:c0 + cw]), win[:, O_CKV:O_CKV + KVL], D, KVL, row_gain=n1)
        k.memset(V(Wkr), 0.0)
        k.memset(V(Wks), 0.0)
        prep_weight(k, st, lambda kc, c0, cw: (Wkr, Wkr[:, kc, 64:96]), win[:, O_KR:O_KR + 32], D, 32, row_gain=n1)
        prep_weight(k, st, lambda kc, c0, cw: (Wks, Wks[:, kc, 64:80]), win[:, O_KR + 16:O_KR + 32], D, 16, row_gain=n1)
        prep_weight(k, st, lambda kc, c0, cw: (Wks, Wks[:, kc, 80:96]), win[:, O_KR:O_KR + 16], D, 16, row_gain=n1)
        prep_weight(k, st, lambda kc, c0, cw: (Wqb, Wqb[:, kc, c0:c0 + cw]), W["w_q_b"], QL, H * 96, row_gain=W["q_a_norm"])
        k.memset(V(Wqs), 0.0, eng="pool")
        wqb3 = W["w_q_b"].rearrange("k (h c) -> k h c", c=96)
        for hh in range(H):
            prep_weight(k, st, lambda kc, c0, cw, hh=hh: (Wqs, Wqs[:, kc, hh * 96 + 64:hh * 96 + 80]),
                        W["w_q_b"][:, hh * 96 + 80:hh * 96 + 96], QL, 16, row_gain=W["q_a_norm"])
            prep_weight(k, st, lambda kc, c0, cw, hh=hh: (Wqs, Wqs[:, kc, hh * 96 + 80:hh * 96 + 96]),
                        W["w_q_b"][:, hh * 96 + 64:hh * 96 + 80], QL, 16, row_gain=W["q_a_norm"])
        for hh in range(H):
            prep_weight(k, st, lambda kc, c0, cw, hh=hh: (Wkb, Wkb[:, kc, hh * 64:(hh + 1) * 64]),
                        W["w_kv_b"][:, hh * 128:hh * 128 + 64], KVL, 64, row_gain=W["kv_a_norm"])
            prep_weight(k, st, lambda kc, c0, cw, hh=hh: (Wvb, Wvb[:, kc, hh * 64:(hh + 1) * 64]),
                        W["w_kv_b"][:, hh * 128 + 64:hh * 128 + 128], KVL, 64, row_gain=W["kv_a_norm"])
        k.barrier()
        es_prep.close()

        P = {
            "ss": Rot([k.sb(es, "ss%d" % i, [128, 2], F32) for i in range(3)]),
            "junk": Rot([k.sb(es, "junk%d" % i, [128, D], BF16) for i in range(2)]),
            "tp": Rot([k.ps(es, "tp%d" % i, [128, D], BF16) for i in range(1)]),
            "ident": ident,
            "eps": eps_t,
        }
        k.memset(V(P["eps"]), EPS)
        xt = Rot([k.sb(es, "xt%d" % i, [128, D], F32) for i in range(9)])
        hn = Rot([k.sb(es, "hn%d" % i, [128, D], BF16) for i in range(2)])
        hT = Rot([k.sb(es, "hT%d" % i, [128, 8, 512], BF16) for i in range(2)])
        pb = Rot([k.ps(es, "pb%d" % i, [128, 512], F32) for i in range(7)])
        sq = Rot([k.sb(es, "sq%d" % i, [128, 512], BF16) for i in range(3)])
        rbc = Rot([k.sb(es, "rbc%d" % i, [128, 512], F32) for i in range(2)])
        qln = Rot([k.sb(es, "qln%d" % i, [128, 3, 512], BF16) for i in range(2)])
        ckn = Rot([k.sb(es, "ckn%d" % i, [128, 2, 512], BF16) for i in range(2)])
        cosb = Rot([k.sb(es, "cosb%d" % i, [96, 512], F32) for i in range(3)])
        sinb = Rot([k.sb(es, "sinb%d" % i, [96, 512], F32) for i in range(3)])
        t1 = Rot([k.sb(es, "t1_%d" % i, [96, 512], F32) for i in range(3)])
        t2 = Rot([k.sb(es, "t2_%d" % i, [96, 512], F32) for i in range(3)])
        qo = Rot([k.sb(es, "qo%d" % i, [96, H, 512], BF16) for i in range(1)])
        ko = Rot([k.sb(es, "ko%d" % i, [96, H, 512], BF16) for i in range(1)])
        krt = Rot([k.sb(es, "krt%d" % i, [96, 512], BF16) for i in range(2)])
        vo = Rot([k.sb(es, "vo%d" % i, [128, H, 128], BF16) for i in range(2)])
        for b_ in vo.items:
            k.memset(V(b_), 1.0, eng="pool")

        def fm_rmsnorm(ps_list, dim, dst, nchunk):
            sqs = []
            for m in range(nchunk):
                s_ = sq.nxt()
                k.act(V(s_), V(ps_list[m]), AF.Square, scale=float(dim) ** -0.5)
                sqs.append(s_)
            pss = pb.nxt()
            for m in range(nchunk):
                k.mm(V(pss), V(onesb), V(sqs[m]), start=(m == 0), stop=(m == nchunk - 1))
            r_ = rbc.nxt()
            k.act(V(r_), V(pss), AF.Sqrt, bias=V(P["eps"]))
            k.recip(V(r_), V(r_))
            for m in range(nchunk):
                k.tt(V(dst, dst[:, m, :]), V(ps_list[m]), V(r_), ALU.mult)

        blocks = [(sg, b) for sg in segs for b in range((sg.get("n", cfg.T) + 511) // 512)]

        loaded = {}

        def load_blk(bi):
            sg, b = blocks[bi]
            c0 = b * 512
            xs_l = []
            for ti in range(4):
                x_ = xt.nxt()
                r0 = c0 + ti * 128
                k.dma(V(x_), D_(sg["x"][r0:r0 + 128, :]))
                xs_l.append(x_)
            cs_, sn_ = cosb.nxt(), sinb.nxt()
            k.dma(V(cs_), D_(sg["cos"][:, c0:c0 + 512]))
            k.dma(V(sn_), D_(sg["sin"][:, c0:c0 + 512]))
            loaded[bi] = (xs_l, cs_, sn_)

        def build_hT(bi):
            if bi not in loaded:
                load_blk(bi)
            xs_l, cs_, sn_ = loaded.pop(bi)
            if bi + 1 < len(blocks) and (bi + 1) not in loaded:
                load_blk(bi + 1)
            h_ = hT.nxt()
            for ti in range(4):
                n_ = hn.nxt()
                rmsnorm_tile(k, P, xs_l[ti], 128, n_, 1.0 / 32.0)
                transpose_tile(k, P, n_, 128, h_, h_[:, :, ti * 128:(ti + 1) * 128])
            return h_, cs_, sn_

        nxt_blk = build_hT(0)
        for bi, (sg, b) in enumerate(blocks):
            if True:
                do_q, do_kv = sg.get("do_q", True), sg.get("do_kv", True)
                c0 = b * 512
                h_, cs_, sn_ = nxt_blk
                def rope_rows(pa, ps_, dst):
                    a_, b2 = t1.nxt(), t2.nxt()
                    k.tt(V(a_, a_[64:96, :]), V(pa, pa[64:96, :]), V(cs_, cs_[64:96, :]), ALU.mult)
                    k.tt(V(b2, b2[64:96, :]), V(ps_, ps_[64:96, :]), V(sn_, sn_[64:96, :]), ALU.mult)
                    k.tt(dst, V(a_, a_[64:96, :]), V(b2, b2[64:96, :]), ALU.add)

                if do_q:
                    pq = [pb.nxt() for _ in range(3)]
                    for m in range(3):
                        for kc in range(KC):
                            k.mm(V(pq[m]), V(Wq, Wq[:, kc, m * 128:(m + 1) * 128]), V(h_, h_[:, kc, :]),
                                 start=(kc == 0), stop=(kc == KC - 1))
                if do_kv:
                    pc = [pb.nxt() for _ in range(2)]
                    for m in range(2):
                        for kc in range(KC):
                            k.mm(V(pc[m]), V(Wc, Wc[:, kc, m * 128:(m + 1) * 128]), V(h_, h_[:, kc, :]),
                                 start=(kc == 0), stop=(kc == KC - 1))
                if bi + 1 < len(blocks):
                    nxt_blk = build_hT(bi + 1)
                if do_q:
                    ql = qln.nxt()
                    fm_rmsnorm(pq, QL, ql, 3)
                if do_kv:
                    pka, pks = pb.nxt(), pb.nxt()
                    for kc in range(KC):
                        k.mm(V(pka, pka[0:96, :]), V(Wkr, Wkr[:, kc, :]), V(h_, h_[:, kc, :]), start=(kc == 0), stop=(kc == KC - 1))
                    for kc in range(KC):
                        k.mm(V(pks, pks[0:96, :]), V(Wks, Wks[:, kc, :]), V(h_, h_[:, kc, :]), start=(kc == 0), stop=(kc == KC - 1))
                    cn = ckn.nxt()
                    fm_rmsnorm(pc, KVL, cn, 2)
                    kr = krt.nxt()
                    rope_rows(pka, pks, V(kr, kr[64:96, :]))

                q_ = qo.nxt() if do_q else None
                for hh in range(H if do_q else 0):
                    pa, ps_ = pb.nxt(), pb.nxt()
                    for m in range(3):
                        k.mm(V(pa, pa[0:96, :]), V(Wqb, Wqb[:, m, hh * 96:(hh + 1) * 96]), V(ql, ql[:, m, :]),
                             start=(m == 0), stop=(m == 2))
                    for m in range(3):
                        k.mm(V(ps_, ps_[0:96, :]), V(Wqs, Wqs[:, m, hh * 96:(hh + 1) * 96]), V(ql, ql[:, m, :]),
                             start=(m == 0), stop=(m == 2))
                    if hh % 2 == 0:
                        k.act(V(q_, q_[0:64, hh, :]), V(pa, pa[0:64, :]), AF.Copy)
                    else:
                        k.cp(V(q_, q_[0:64, hh, :]), V(pa, pa[0:64, :]))
                    rope_rows(pa, ps_, V(q_, q_[64:96, hh, :]))
                if do_q:
                    k.dma(D_(sg["qt"][:, :, c0:c0 + 512].rearrange("h p t -> p h t")), V(q_), eng="pool")
                if not do_kv:
                    continue
                k_ = ko.nxt()
                for hh in range(H):
                    pk = pb.nxt()
                    for m in range(2):
                        k.mm(V(pk, pk[0:64, :]), V(Wkb, Wkb[:, m, hh * 64:(hh + 1) * 64]), V(cn, cn[:, m, :]),
                             start=(m == 0), stop=(m == 1))
                    if hh % 2 == 0 and do_q:
                        k.act(V(k_, k_[0:64, hh, :]), V(pk, pk[0:64, :]), AF.Copy)
                    elif hh % 4 == 0:
                        k.act(V(k_, k_[0:64, hh, :]), V(pk, pk[0:64, :]), AF.Copy)
                    else:
                        k.cp(V(k_, k_[0:64, hh, :]), V(pk, pk[0:64, :]))
                k.dma(D_(sg["kt"][:, 0:64, c0:c0 + 512].rearrange("h p t -> p h t")), V(k_, k_[0:64, :, :]), eng="pool")
                k.dma(D_(sg["kt"][0, 64:96, c0:c0 + 512]), V(kr, kr[64:96, :]), eng="pool")
                for ti in range(4):
                    v_ = vo.nxt()
                    for half in range(2):
                        pv = pb.nxt()
                        for m in range(2):
                            k.mm(V(pv), V(cn, cn[:, m, ti * 128:(ti + 1) * 128]), V(Wvb, Wvb[:, m, half * 512:(half + 1) * 512]),
                                 start=(m == 0), stop=(m == 1))
                        if half == 0:
                            k.act(V(v_, v_[:, half * 8:half * 8 + 8, 0:64]),
                                  V(pv, pv[:, :].rearrange("p (j v) -> p j v", v=64)), AF.Copy)
                        else:
                            k.cp(V(v_, v_[:, half * 8:half * 8 + 8, 0:64]),
                                 V(pv, pv[:, :].rearrange("p (j v) -> p j v", v=64)))
                    r0 = c0 + ti * 128
                    k.dma(D_(sg["va"][r0:r0 + 128, :, :]), V(v_), eng="pool")
        k.barrier()


WNAMES = ["norm1", "w_in", "q_a_norm", "kv_a_norm", "w_q_b", "w_kv_b", "conv_w", "conv_b", "dt_bias_f", "dt_bias_b",
          "a_log_f", "a_log_b", "d_skip", "ssm_norm", "w_out", "norm2", "w_gate", "w_up", "ffn_conv_w", "ffn_conv_b",
          "w_down", "final_norm"]


def rope_tables(pos):
    pos = np.asarray(pos, dtype=np.float32)
    inv = (np.float32(10000.0) ** (-(np.arange(0, RO, 2, dtype=np.float32)) / np.float32(RO))).astype(np.float32)
    ang = (pos[:, None] * inv[None, :]).astype(np.float32)
    c, s = np.cos(ang).astype(np.float32).T, np.sin(ang).astype(np.float32).T
    cos = np.ones((96, len(pos)), np.float32)
    sin = np.zeros((96, len(pos)), np.float32)
    cos[64:80], cos[80:96] = c, c
    sin[64:80], sin[80:96] = -s, s
    return cos, sin


def host_consts():
    r = np.arange(128)
    cst = np.zeros((128, 7, 128), np.float32)
    cst[:, 0, :] = (r[:, None] <= r[None, :])
    cst[:, 1, :] = (r[:, None] < r[None, :])
    cst[:, 2, :] = 1.0
    cst[:, 3, :] = np.eye(128)
    cst[:, 4, :] = np.where(r[None, :] < r[:, None], NEG, 0.0)
    cst[:, 5, :] = np.where(r[None, :] > r[:, None], NEG, 0.0)
    cst[:, 6, :] = -(r[:, None] < r[None, :]).astype(np.float32)
    return {
        "identb": np.eye(128, dtype=np.float32).astype(ml_dtypes.bfloat16),
        "onesb": np.ones((128, 128), np.float32).astype(ml_dtypes.bfloat16),
        "cst32": cst,
    }


def declare_weights(nc, shapes):
    W = {}
    for n in WNAMES:
        shp = list(shapes[n])
        W[n] = nc.dram_tensor(n, shp, F32, kind="ExternalInput").ap()
    return W


def wviews(W):
    o = {}
    for n, ap in W.items():
        o[n] = ap if n == "final_norm" else ap[0]
    return o


def phase_p2(k, cfg, jobs):
    with ExitStack() as es:
        maxk = max(j["Tk"] for j in jobs)
        maxq = max(j["Tq"] for j in jobs)
        qb = Rot([k.sb(es, "aq%d" % i, [96, maxq], BF16) for i in range(2)])
        kb = Rot([k.sb(es, "ak%d" % i, [96, maxk], BF16) for i in range(2)])
        vb = Rot([k.sb(es, "av%d" % i, [128, maxk // 128, 128], BF16) for i in range(2)])
        pS = Rot([k.ps(es, "pS%d" % i, [128, 1024], F32) for i in range(3)])
        pO = Rot([k.ps(es, "pO%d" % i, [128, 512], F32) for i in range(2)])
        pt = Rot([k.sb(es, "apt%d" % i, [128, 1024], BF16) for i in range(3)])
        rc = Rot([k.sb(es, "arc%d" % i, [64, 512], F32) for i in range(2)])
        ao = Rot([k.sb(es, "aao%d" % i, [64, 512], BF16) for i in range(3)])

        def load(j):
            Tq, Tk = j["Tq"], j["Tk"]
            q_, k_, v_ = qb.nxt(), kb.nxt(), vb.nxt()
            k.dma(V(q_, q_[:, 0:Tq]), D_(j["qt"]))
            for c in range(0, Tk, 4096):
                e = min(Tk, c + 4096)
                if "kr" in j:
                    k.dma(V(k_, k_[0:64, c:e]), D_(j["kt"][0:64, c:e]))
                    k.dma(V(k_, k_[64:96, c:e]), D_(j["kr"][:, c:e]))
                else:
                    k.dma(V(k_, k_[:, c:e]), D_(j["kt"][:, c:e]))
            for c in range(0, Tk, 2048):
                e = min(Tk, c + 2048)
                k.dma(V(v_, v_[:, c // 128:e // 128, :]), D_(j["va"][c:e, :].rearrange("(t p) v -> p t v", p=128)))
            return q_, k_, v_

        its = []
        for ji, j in enumerate(jobs):
            for qi in range((j["Tq"] + 511) // 512):
                for kp in range(j["Tk"] // 256):
                    its.append((ji, qi, kp))
        bufs = {0: load(jobs[0])}
        sbuf = {}

        def emit_scores(i):
            ji, qi, kp = its[i]
            if ji not in bufs:
                bufs[ji] = load(jobs[ji])
            q_, k_, v_ = bufs[ji]
            qw = min(512, jobs[ji]["Tq"] - qi * 512)
            s_ = pS.nxt()
            for t in range(2):
                kt_ = 2 * kp + t
                k.mm(V(s_, s_[:, t * 512:t * 512 + qw]), V(k_, k_[:, kt_ * 128:(kt_ + 1) * 128]), V(q_, q_[:, qi * 512:qi * 512 + qw]))
            sbuf[i] = s_

        emit_scores(0)
        o_ = None
        for i, (ji, qi, kp) in enumerate(its):
            j = jobs[ji]
            if qi == 0 and kp == 0 and ji + 1 < len(jobs) and (ji + 1) not in bufs:
                bufs[ji + 1] = load(jobs[ji + 1])
            if i + 1 < len(its):
                emit_scores(i + 1)
            q_, k_, v_ = bufs[ji]
            nkp = j["Tk"] // 256
            if kp == 0:
                o_ = pO.nxt()
            s_ = sbuf.pop(i)
            p_ = pt.nxt()
            qw = min(512, j["Tq"] - qi * 512)
            k.act(V(p_, p_[:, :].rearrange("p (t c) -> p t c", c=512)[:, :, 0:qw]),
                  V(s_, s_[:, :].rearrange("p (t c) -> p t c", c=512)[:, :, 0:qw]), AF.Exp, scale=SCALE)
            for t in range(2):
                kt_ = 2 * kp + t
                k.mm(V(o_, o_[:, 0:qw]), V(v_, v_[:, kt_, :]), V(p_, p_[:, t * 512:t * 512 + qw]), start=(kt_ == 0), stop=(kt_ == 2 * nkp - 1))
            if kp == nkp - 1:
                r_ = rc.nxt()
                k.recip(V(r_, r_[:, 0:qw]), V(o_, o_[64:128, 0:qw]))
                a_ = ao.nxt()
                k.tt(V(a_, a_[:, 0:qw]), V(o_, o_[0:64, 0:qw]), V(r_, r_[:, 0:qw]), ALU.mult)
                k.dma(D_(j["at"][:, qi * 512:qi * 512 + qw]), V(a_, a_[:, 0:qw]), eng="pool")
                if qi == (j["Tq"] + 511) // 512 - 1:
                    bufs.pop(ji, None)
        k.barrier()


def phase_p3(k, cfg, W, C, segs, nkc=16):
    with ExitStack() as es:
        st = prep_stage(k, es)
        Wo = k.sb(es, "Wo", [128, nkc, D], BF16)
        ident = k.sb(es, "ident_sb3", [128, 128], BF16)
        k.dma(V(ident), D_(C["identb"]))
        prep_weight(k, st, lambda kc, c0, cw: (Wo, Wo[:, kc, c0:c0 + cw]), W["w_out"][0:1024, :], 1024, D)
        if nkc == 16:
            prep_weight(k, st, lambda kc, c0, cw: (Wo, Wo[:, 8 + kc, c0:c0 + cw]), W["w_out"][1024:2048, :], 1024, D,
                        row_gain=W["ssm_norm"])
        P = {
            "ss": Rot([k.sb(es, "ss3_%d" % i, [128, 2], F32) for i in range(3)]),
            "junk": Rot([k.sb(es, "junk3_%d" % i, [128, D], BF16) for i in range(2)]),
            "tp": Rot([k.ps(es, "tp3_%d" % i, [128, D], BF16) for i in range(2)]),
            "ident": ident,
            "eps": k.sb(es, "eps3", [128, 1], F32),
        }
        k.memset(V(P["eps"]), EPS)
        zt = k.sb(es, "zt3", [128, 8, 2], BF16)
        k.memset(V(zt), 0.0)
        mixT = Rot([k.sb(es, "mixT%d" % i, [128, nkc, 512], BF16) for i in range(2)])
        xt = Rot([k.sb(es, "xt3_%d" % i, [128, D], F32) for i in range(3)])
        x1 = Rot([k.sb(es, "x1_%d" % i, [128, D], F32) for i in range(3)])
        hn = Rot([k.sb(es, "hn3_%d" % i, [128, D], BF16) for i in range(2)])
        h2 = Rot([k.sb(es, "h2_%d" % i, [128, 8, 128], BF16) for i in range(3)])
        px = Rot([k.ps(es, "px%d" % i, [128, D], F32) for i in range(3)])
        for sg in segs:
            T = sg.get("n", cfg.T)
            k.dma(D_(sg["h2t"][:, :, 0:1]), V(zt, zt[:, :, 0:1]), eng="pool", slow=True)
            k.dma(D_(sg["h2t"][:, :, T + 1:T + 2]), V(zt, zt[:, :, 1:2]), eng="pool", slow=True)
            for c0 in range(0, T, 512):
                bwid = min(512, T - c0)
                m_ = mixT.nxt()
                for i, mx in enumerate(sg["mix"]):
                    k.dma(V(m_, m_[:, 8 * i:8 * i + 8, 0:bwid]), D_(mx.rearrange("(c p) t -> p c t", p=128)[:, :, c0:c0 + bwid]))
                def tile_mm(ti):
                    r0 = c0 + ti * 128
                    x_ = xt.nxt()
                    k.dma(V(x_), D_(sg["x"][r0:r0 + 128, :]))
                    p_ = px.nxt()
                    for n in range(2):
                        for kc in range(nkc):
                            k.mm(V(p_, p_[:, n * 512:(n + 1) * 512]), V(m_, m_[:, kc, ti * 128:(ti + 1) * 128]),
                                 V(Wo, Wo[:, kc, n * 512:(n + 1) * 512]), start=(kc == 0), stop=(kc == nkc - 1))
                    return r0, x_, p_

                def tile_post(r0, x_, p_):
                    y_ = x1.nxt()
                    k.tt(V(y_), V(p_), V(x_), ALU.add)
                    k.dma(D_(sg["x1"][r0:r0 + 128, :]), V(y_), eng="pool")
                    n_ = hn.nxt()
                    rmsnorm_tile(k, P, y_, 128, n_, 1.0 / 32.0)
                    h_ = h2.nxt()
                    transpose_tile(k, P, n_, 128, h_, h_[:, :, :])
                    k.dma(D_(sg["h2t"][:, :, 1 + r0:1 + r0 + 128]), V(h_), eng="pool")

                prev = None
                for ti in range(bwid // 128):
                    cur = tile_mm(ti)
                    if prev is not None:
                        tile_post(*prev)
                    prev = cur
                tile_post(*prev)
        k.barrier()


FB = 510


def phase_p4(k, cfg, W, C, wg_scr, segs):
    T = cfg.T
    with ExitStack() as es:
        Wu = k.sb(es, "Wu", [128, 8, DFF], BF16)
        Wd = k.sb(es, "Wd", [128, FC, D], BF16)
        bg = k.sb(es, "bg", [128, FC], F32)
        cw3 = k.sb(es, "cw3", [128, 3, FC], F32)
        gain = k.sb(es, "fgain", [128, D], F32)
        eps = k.sb(es, "eps4", [128, 1], F32)
        es_prep = ExitStack()
        st = prep_stage(k, es_prep)
        n2 = W["norm2"]

        def dst(kc, c0, cw):
            return (None, wg_scr[c0 // 128:(c0 + cw) // 128, :, kc, :].rearrange("m p c -> p m c"))
        prep_weight(k, st, dst, W["w_gate"], D, DFF, row_gain=n2)
        prep_weight(k, st, lambda kc, c0, cw: (Wu, Wu[:, kc, c0:c0 + cw]), W["w_up"], D, DFF, row_gain=n2)
        prep_weight(k, st, lambda kc, c0, cw: (Wd, Wd[:, kc, c0:c0 + cw]), W["w_down"], DFF, D)
        k.dma(V(bg), D_(W["ffn_conv_b"].rearrange("(c p) -> p c", p=128)), slow=True)
        for tap in range(3):
            k.dma(V(cw3, cw3[:, tap, :]), D_(W["ffn_conv_w"][tap].rearrange("(c p) -> p c", p=128)), slow=True)
        k.dma(V(gain), D_(W["final_norm"].partition_broadcast(128)))
        k.memset(V(eps), EPS)
        k.barrier()
        es_prep.close()
        hT = Rot([k.sb(es, "h2T%d" % i, [128, 8, T + 2], BF16) for i in range(1)])
        vf4 = k.sb(es, "vf4", [128, 2], F32)
        wg = Rot([k.sb(es, "wg%d" % i, [128, 8, 128], BF16) for i in range(3)])
        pg = Rot([k.ps(es, "pg%d" % i, [128, 512], F32) for i in range(2)])
        pu = Rot([k.ps(es, "pu%d" % i, [128, 512], F32) for i in range(2)])
        pd = Rot([k.ps(es, "pd%d" % i, [128, D], F32) for i in range(2)])
        cv = Rot([k.sb(es, "cv%d" % i, [128, 512], F32) for i in range(3)])
        sg_ = Rot([k.sb(es, "sgl%d" % i, [128, 512], F32) for i in range(2)])
        aT = Rot([k.sb(es, "aT%d" % i, [128, FC, 512], BF16) for i in range(1)])
        x1 = Rot([k.sb(es, "x14_%d" % i, [128, D], F32) for i in range(2)])
        ss = Rot([k.sb(es, "ss4_%d" % i, [128, 2], F32) for i in range(3)])
        junk = Rot([k.sb(es, "junk4_%d" % i, [128, D], BF16) for i in range(2)])
        for sg in segs:
            h_ = hT.nxt()
            for c in range(0, T + 2, 1024):
                e = min(T + 2, c + 1024)
                k.dma(V(h_, h_[:, :, c:e]), D_(sg["h2t"][:, :, c:e]))
            if sg.get("vflag") is not None:
                k.dma(V(vf4), D_(sg["vflag"]))
                k.ts(V(h_, h_[:, :, 0:1]), V(h_, h_[:, :, 0:1]), V(vf4, vf4[:, 0:1]), ALU.mult)
                k.ts(V(h_, h_[:, :, T + 1:T + 2]), V(h_, h_[:, :, T + 1:T + 2]), V(vf4, vf4[:, 1:2]), ALU.mult)
            for v0 in range(0, T, FB):
                bw = min(FB, T - v0)
                a_ = aT.nxt()
                for m in range(FC):
                    w_ = wg.nxt()
                    k.dma(V(w_), D_(wg_scr[m]))
                    g_, u_ = pg.nxt(), pu.nxt()
                    for kc in range(KC):
                        k.mm(V(g_, g_[:, 0:bw + 2]), V(w_, w_[:, kc, :]), V(h_, h_[:, kc, v0:v0 + bw + 2]),
                             start=(kc == 0), stop=(kc == KC - 1))
                    for kc in range(KC):
                        k.mm(V(u_, u_[:, 0:bw]), V(Wu, Wu[:, kc, m * 128:(m + 1) * 128]), V(h_, h_[:, kc, v0 + 1:v0 + 1 + bw]),
                             start=(kc == 0), stop=(kc == KC - 1))
                    c_ = cv.nxt()
                    k.ts(V(c_, c_[:, 0:bw]), V(g_, g_[:, 0:bw]), V(cw3, cw3[:, 0, m:m + 1]), ALU.mult)
                    for tap in (1, 2):
                        k.op("dve", lambda hh, c_=c_, g_=g_, tap=tap, m=m, bw=bw: hh.scalar_tensor_tensor(
                            out=c_[:, 0:bw], in0=g_[:, tap:tap + bw], scalar=cw3[:, tap, m:m + 1], in1=c_[:, 0:bw],
                            op0=ALU.mult, op1=ALU.add), reads=[g_, cw3, c_], writes=[c_])
                    s_ = sg_.nxt()
                    k.act(V(s_, s_[:, 0:bw]), V(c_, c_[:, 0:bw]), AF.Silu, bias=V(bg, bg[:, m:m + 1]))
                    k.tt(V(a_, a_[:, m, 0:bw]), V(s_, s_[:, 0:bw]), V(u_, u_[:, 0:bw]), ALU.mult)
                for i0_ in range(0, bw, 128):
                    rows = min(128, bw - i0_)
                    r0 = v0 + i0_
                    x_ = x1.nxt()
                    k.dma(V(x_, x_[0:rows, :]), D_(sg["x1"][r0:r0 + rows, :]))
                    p_ = pd.nxt()
                    for n in range(2):
                        for m in range(FC):
                            k.mm(V(p_, p_[0:rows, n * 512:(n + 1) * 512]), V(a_, a_[:, m, i0_:i0_ + rows]),
                                 V(Wd, Wd[:, m, n * 512:(n + 1) * 512]), start=(m == 0), stop=(m == FC - 1))
                    k.tt(V(x_, x_[0:rows, :]), V(p_, p_[0:rows, :]), V(x_, x_[0:rows, :]), ALU.add)
                    s2, jk = ss.nxt(), junk.nxt()
                    k.act(V(jk, jk[0:rows, :]), V(x_, x_[0:rows, :]), AF.Square, scale=1.0 / 32.0, accum=V(s2, s2[0:rows, 0:1]))
                    k.act(V(s2, s2[0:rows, 1:2]), V(s2, s2[0:rows, 0:1]), AF.Sqrt, bias=V(eps, eps[0:rows, :]))
                    k.recip(V(s2, s2[0:rows, 1:2]), V(s2, s2[0:rows, 1:2]))
                    k.act(V(x_, x_[0:rows, :]), V(x_, x_[0:rows, :]), AF.Copy, scale=V(s2, s2[0:rows, 1:2]))
                    k.tt(V(x_, x_[0:rows, :]), V(x_, x_[0:rows, :]), V(gain, gain[0:rows, :]), ALU.mult, eng="pool")
                    k.dma(D_(sg["out"][r0:r0 + rows, :]), V(x_, x_[0:rows, :]), eng="pool")
        k.barrier()


def phase_p1c(k, cfg, W, C, segs):
    maxn = max(sg["n"] for sg in segs)
    with ExitStack() as es:
        Wx = k.sb(es, "WxBC", [128, 8, 3, 256], BF16)
        Wx1 = k.sb(es, "Wx1", [128, 8, DSSM], BF16)
        cwx = k.sb(es, "cwx", [128, 3, 8], F32)
        cbx = k.sb(es, "cbx", [128, 8], F32)
        Wz = k.sb(es, "Wz", [128, 8, DSSM], BF16)
        Wdt = k.sb(es, "Wdt", [128, 8, 32], BF16)
        ident = k.sb(es, "ident_c", [128, 128], BF16)
        onesb = k.sb(es, "ones_c", [128, 128], BF16)
        cst = k.sb(es, "cst_c", [128, 7, 128], F32)
        cbb = k.sb(es, "cbb", [1, DXBC], BF16)
        cb32 = k.sb(es, "cb32", [1, DXBC], F32)
        cbf = k.sb(es, "cbf", [64, 4], F32)
        dtb = k.sb(es, "dtb", [128, 32], F32)
        Abc = k.sb(es, "Abc", [128, 32], F32)
        eps = k.sb(es, "eps_c", [128, 1], F32)
        k.dma(V(ident), D_(C["identb"]))
        k.dma(V(onesb), D_(C["onesb"]))
        k.dma(V(cst), D_(C["cst32"]))
        k.dma(V(cb32), D_(W["conv_b"].rearrange("(o c) -> o c", o=1)))
        k.cp(V(cbb), V(cb32))
        k.dma(V(cbf), D_(W["conv_b"][1024:1280].rearrange("(i p) -> p i", p=64)), slow=True)
        k.dma(V(cbx), D_(W["conv_b"][0:1024].rearrange("(c p) -> p c", p=128)), slow=True)
        for tap in range(3):
            k.dma(V(cwx, cwx[:, tap, :]), D_(W["conv_w"][tap][0:1024].rearrange("(c p) -> p c", p=128)), slow=True)
        k.dma(V(dtb, dtb[:, 0:16]), D_(W["dt_bias_f"].partition_broadcast(128)))
        k.dma(V(dtb, dtb[:, 16:32]), D_(W["dt_bias_b"].partition_broadcast(128)))
        k.dma(V(Abc, Abc[:, 0:16]), D_(W["a_log_f"].partition_broadcast(128)))
        k.dma(V(Abc, Abc[:, 16:32]), D_(W["a_log_b"].partition_broadcast(128)))
        k.act(V(Abc), V(Abc), AF.Exp)
        k.ts(V(Abc), V(Abc), -1.0, ALU.mult)
        k.memset(V(eps), EPS)
        es_prep = ExitStack()
        st = prep_stage(k, es_prep)
        n1, win = W["norm1"], W["w_in"]
        for tap in range(3):
            prep_weight(k, st, lambda kc, c0, cw, tap=tap: (Wx, Wx[:, kc, tap, c0:c0 + cw]), win[:, O_XBC + 1024:O_XBC + DXBC], D, 256,
                        row_gain=n1, col_gain=W["conv_w"][tap][1024:1280])
        prep_weight(k, st, lambda kc, c0, cw: (Wx1, Wx1[:, kc, c0:c0 + cw]), win[:, O_XBC:O_XBC + 1024], D, 1024, row_gain=n1)
        prep_weight(k, st, lambda kc, c0, cw: (Wz, Wz[:, kc, c0:c0 + cw]), win[:, O_Z:O_Z + DSSM], D, DSSM, row_gain=n1)
        prep_weight(k, st, lambda kc, c0, cw: (Wdt, Wdt[:, kc, c0:c0 + cw]), win[:, O_DT:O_DT + 32], D, 32, row_gain=n1)
        k.barrier()
        es_prep.close()
        P = {
            "ss": Rot([k.sb(es, "ssc%d" % i, [128, 2], F32) for i in range(3)]),
            "junk": Rot([k.sb(es, "junkc%d" % i, [128, D], BF16) for i in range(2)]),
            "tp": Rot([k.ps(es, "tpc%d" % i, [128, D], BF16) for i in range(1)]),
            "ident": ident, "eps": eps,
        }
        hT = k.sb(es, "hTc", [128, 8, maxn + 2], BF16)
        xt = Rot([k.sb(es, "xtc%d" % i, [128, D], F32) for i in range(3)])
        hn = Rot([k.sb(es, "hnc%d" % i, [128, D], BF16) for i in range(2)])
        big = Rot([k.ps(es, "bigc%d" % i, [128, D], F32) for i in range(1)])
        pf = Rot([k.ps(es, "pfc%d" % i, [128, 512], F32) for i in range(2)])
        xsT = Rot([k.sb(es, "xsT%d" % i, [128, 8, 384], BF16) for i in range(2)])
        cvt = Rot([k.sb(es, "cvt%d" % i, [128, 384], F32) for i in range(3)])
        sml = Rot([k.ps(es, "smlc%d" % i, [128, 512], F32) for i in range(3)])
        xsb = Rot([k.sb(es, "xsb%d" % i, [128, D], BF16) for i in range(4)])
        zsb = Rot([k.sb(es, "zsb%d" % i, [128, D], BF16) for i in range(2)])
        btk = Rot([k.sb(es, "btk%d" % i, [128, 128], BF16) for i in range(3)])
        dts = Rot([k.sb(es, "dts%d" % i, [128, 160], F32) for i in range(3)])
        smo = Rot([k.sb(es, "smo%d" % i, [128, 96], F32) for i in range(2)])
        bw = Rot([k.sb(es, "bw%d" % i, [128, H, 64], BF16) for i in range(6)])
        so = Rot([k.sb(es, "so%d" % i, [64, D], F32) for i in range(2)])
        bco = Rot([k.sb(es, "bco%d" % i, [64, 512], BF16) for i in range(3)])
        for sg in segs:
            n, lite = sg["n"], sg["lite"]
            nch = n // 128
            for ti in range(nch):
                x_ = xt.nxt()
                k.dma(V(x_), D_(sg["x"][ti * 128:(ti + 1) * 128, :]))
                n_ = hn.nxt()
                rmsnorm_tile(k, P, x_, 128, n_, 1.0 / 32.0)
                transpose_tile(k, P, n_, 128, hT, hT[:, :, 1 + ti * 128:1 + (ti + 1) * 128])
            x_ = xt.nxt()
            k.dma(V(x_, x_[0:2, :]), D_(sg["xh"]))
            n_ = hn.nxt()
            rmsnorm_tile(k, P, x_, 2, n_, 1.0 / 32.0)
            tp = P["tp"].nxt()
            for j in range(8):
                k.tr(V(tp, tp[:, j * 128:j * 128 + 2]), V(n_, n_[0:2, j * 128:(j + 1) * 128]), V(ident, ident[0:2, 0:2]))
            tpv = tp[:, :].rearrange("p (j t) -> p j t", t=128)
            k.cp(V(hT, hT[:, :, 0:1]), V(tp, tpv[:, :, 0:1]))
            k.cp(V(hT, hT[:, :, n + 1:n + 2]), V(tp, tpv[:, :, 1:2]))
            def proj(c):
                cb0 = 128 * c
                xT_, j3 = blkT[c // 3], c % 3
                tpx = P["tp"].nxt()
                for cc in range(8):
                    k.tr(V(tpx, tpx[:, cc * 128:(cc + 1) * 128]), V(xT_, xT_[:, cc, j3 * 128:(j3 + 1) * 128]), V(ident))
                xs_ = xsb.nxt()
                if c % 2 == 0:
                    k.act(V(xs_), V(tpx), AF.Copy)
                else:
                    k.cp(V(xs_), V(tpx))
                if not lite:
                    k.dma(D_(sg["xs"][c * 128:(c + 1) * 128, :]), V(xs_), eng="pool")
                    pz = big.nxt()
                    for nb in range(2):
                        for kc in range(KC):
                            k.mm(V(pz, pz[:, nb * 512:(nb + 1) * 512]), V(hT, hT[:, kc, cb0 + 1:cb0 + 129]),
                                 V(Wz, Wz[:, kc, nb * 512:(nb + 1) * 512]), start=(kc == 0), stop=(kc == KC - 1))
                    z_ = zsb.nxt()
                    k.act(V(z_), V(pz), AF.Silu)
                    k.dma(D_(sg["zs"][c * 128:(c + 1) * 128, :]), V(z_), eng="pool")
                pm = sml.nxt()
                i = 0
                for tap in range(3):
                    for kc in range(KC):
                        k.mm(V(pm, pm[:, 0:128]), V(hT, hT[:, kc, cb0 + tap:cb0 + tap + 128]), V(Wx, Wx[:, kc, tap, 0:128]),
                             start=(i == 0), stop=False)
                        i += 1
                k.mm(V(pm, pm[:, 0:128]), V(onesb, onesb[0:1, 0:128]), V(cbb, cbb[0:1, 1024:1152]), start=False, stop=True)
                for kc in range(KC):
                    k.mm(V(pm, pm[:, 128:160]), V(hT, hT[:, kc, cb0 + 1:cb0 + 129]), V(Wdt, Wdt[:, kc, :]),
                         start=(kc == 0), stop=(kc == KC - 1))
                bt_ = btk.nxt()
                k.act(V(bt_), V(pm, pm[:, 0:128]), AF.Silu)
                d_ = dts.nxt()
                k.tt(V(d_, d_[:, 0:32]), V(pm, pm[:, 128:160]), V(dtb), ALU.add)
                return xs_, bt_, d_

            def rest(c, xs_, bt_, d_):
                k.act(V(d_, d_[:, 0:32]), V(d_, d_[:, 0:32]), AF.Exp)
                k.act(V(d_, d_[:, 0:32]), V(d_, d_[:, 0:32]), AF.Ln, bias=1.0)
                k.act(V(d_, d_[:, 32:64]), V(d_, d_[:, 0:32]), AF.Ln)
                sm_ = smo.nxt()
                k.tt(V(sm_, sm_[:, 0:32]), V(d_, d_[:, 0:32]), V(Abc), ALU.mult)
                pc = sml.nxt()
                k.mm(V(pc, pc[:, 0:16]), V(cst, cst[:, 0, :]), V(sm_, sm_[:, 0:16]))
                k.mm(V(pc, pc[:, 16:32]), V(cst, cst[:, 1, :]), V(sm_, sm_[:, 16:32]))
                k.mm(V(pc, pc[:, 32:64]), V(cst, cst[:, 2, :]), V(sm_, sm_[:, 0:32]))
                k.tt(V(sm_, sm_[:, 32:48]), V(d_, d_[:, 32:48]), V(pc, pc[:, 0:16]), ALU.subtract)
                k.tt(V(sm_, sm_[:, 48:64]), V(d_, d_[:, 48:64]), V(pc, pc[:, 16:32]), ALU.add)
                k.cp(V(sm_, sm_[:, 64:96]), V(pc, pc[:, 32:64]))
                k.tt(V(d_, d_[:, 96:112]), V(sm_, sm_[:, 64:80]), V(sm_, sm_[:, 32:48]), ALU.add)
                k.cp(V(d_, d_[:, 112:128]), V(sm_, sm_[:, 48:64]))
                k.act(V(d_, d_[:, 128:160]), V(d_, d_[:, 96:128]), AF.Exp)
                k.dma(D_(sg["sm"][c]), V(sm_), eng="pool")
                btv = bt_[:, :].rearrange("p (g n) -> p g n", g=2).unsqueeze(2).broadcast_to([128, 2, 8, 64])
                bws = []
                for d in range(2):
                    b_ = bw.nxt()
                    wv = d_[:, 128 + 16 * d:144 + 16 * d].rearrange("p (g j) -> p g j", g=2).unsqueeze(3).broadcast_to([128, 2, 8, 64])
                    k.tt(V(b_, b_[:, :, :].rearrange("p (g j) n -> p g j n", g=2)), V(bt_, btv), V(d_, wv), ALU.mult, eng="pool")
                    bws.append(b_)
                return bws

            def restB(c, xs_, bws):
                for d in range(2):
                    b_ = bws[d]
                    s_ = so.nxt()
                    for half in range(2):
                        pS = sml.nxt()
                        for j in range(8):
                            hh = half * 8 + j
                            k.mm(V(pS, pS[0:64, j * 64:(j + 1) * 64]), V(b_, b_[:, hh, :]), V(xs_, xs_[:, hh * 64:(hh + 1) * 64]))
                        k.cp(V(s_, s_[:, half * 512:(half + 1) * 512]), V(pS, pS[0:64, :]))
                    k.dma(D_(sg["sst"][c, d]), V(s_), eng="pool")

            blkT = {}

            def xs_block(b):
                t0b = 384 * b
                nt = min(384, n - t0b)
                xT_ = xsT.nxt()
                for cc in range(8):
                    f_ = pf.nxt()
                    for kc in range(KC):
                        k.mm(V(f_, f_[:, 0:nt + 2]), V(Wx1, Wx1[:, kc, cc * 128:(cc + 1) * 128]), V(hT, hT[:, kc, t0b:t0b + nt + 2]),
                             start=(kc == 0), stop=(kc == KC - 1))
                    t_ = cvt.nxt()
                    k.ts(V(t_, t_[:, 0:nt]), V(f_, f_[:, 0:nt]), V(cwx, cwx[:, 0, cc:cc + 1]), ALU.mult)
                    for tap in (1, 2):
                        k.op("dve", lambda hh_, t_=t_, f_=f_, tap=tap, cc=cc, nt=nt: hh_.scalar_tensor_tensor(
                            out=t_[:, 0:nt], in0=f_[:, tap:tap + nt], scalar=cwx[:, tap, cc:cc + 1], in1=t_[:, 0:nt],
                            op0=ALU.mult, op1=ALU.add), reads=[f_, cwx, t_], writes=[t_])
                    k.act(V(xT_, xT_[:, cc, 0:nt]), V(t_, t_[:, 0:nt]), AF.Silu, bias=V(cbx, cbx[:, cc:cc + 1]))
                blkT[b] = xT_

            nblk = (nch + 2) // 3
            xs_block(0)
            nxt_h = proj(0)
            pend = None
            for c in range(nch):
                cur = nxt_h
                if c % 3 == 0 and c // 3 + 1 < nblk:
                    xs_block(c // 3 + 1)
                if c + 1 < nch:
                    nxt_h = proj(c + 1)
                bws = rest(c, *cur)
                if pend is not None:
                    restB(*pend)
                pend = (c, cur[0], bws)
            restB(*pend)
            if lite:
                continue
            for c0 in range(0, n, 512):
                bwid = min(512, n - c0)
                for idx in range(4):
                    pb_ = sml.nxt()
                    i = 0
                    for tap in range(3):
                        for kc in range(KC):
                            k.mm(V(pb_, pb_[0:64, 0:bwid]), V(Wx, Wx[:, kc, tap, idx * 64:(idx + 1) * 64]),
                                 V(hT, hT[:, kc, c0 + tap:c0 + tap + bwid]), start=(i == 0), stop=(i == 23))
                            i += 1
                    o_ = bco.nxt()
                    k.act(V(o_, o_[:, 0:bwid]), V(pb_, pb_[0:64, 0:bwid]), AF.Silu, bias=V(cbf, cbf[:, idx:idx + 1]))
                    k.dma(D_(sg["bct"][idx, :, c0:c0 + bwid]), V(o_, o_[:, 0:bwid]), eng="pool")
        k.barrier()


def phase_p1b(k, cfg, W, C, segs, glob=None):
    maxn = max(sg["n"] for sg in segs)
    maxc = maxn // 128
    with ExitStack() as es:
        ident = k.sb(es, "ident_b", [128, 128], BF16)
        cst = k.sb(es, "cst_b", [128, 7, 128], F32)
        dsk = k.sb(es, "dsk", [128, H], F32)
        eps = k.sb(es, "eps_b", [128, 1], F32)
        zcol = k.sb(es, "zcol", [128, 1], F32)
        k.dma(V(ident), D_(C["identb"]))
        k.dma(V(cst), D_(C["cst32"]))
        mskb = k.sb(es, "mskb", [128, 2, 128], BF16)
        k.cp(V(mskb), V(cst, cst[:, 4:6, :]))
        Tb = k.sb(es, "Tb", [128, 2, 128], BF16)
        k.cp(V(Tb, Tb[:, 0, :]), V(cst, cst[:, 0, :]))
        k.cp(V(Tb, Tb[:, 1, :]), V(cst, cst[:, 6, :]))
        ahi = k.sb(es, "ahi", [128, maxc, 32], BF16)
        alo = k.sb(es, "alo", [128, maxc, 32], BF16)
        atmp = k.sb(es, "atmp", [128, maxc, 32], F32)
        k.dma(V(dsk), D_(W["d_skip"].partition_broadcast(128)))
        k.memset(V(eps), EPS)
        k.memset(V(zcol), 0.0)
        P = {
            "ss": Rot([k.sb(es, "ssb%d" % i, [128, 2], F32) for i in range(3)]),
            "junk": Rot([k.sb(es, "junkb%d" % i, [128, D], BF16) for i in range(2)]),
            "tp": Rot([k.ps(es, "tpb%d" % i, [128, D], BF16) for i in range(1)]),
            "ident": ident, "eps": eps,
        }
        smb = k.sb(es, "smb", [128, maxc, 96], F32)
        bct = k.sb(es, "bctb", [64, 4, maxn], BF16)
        dall = k.sb(es, "dall", [64, maxc, 32], F32)
        hbin = k.sb(es, "hbin", [64, maxc, D], BF16)
        hb = k.sb(es, "hb", [64, D], F32)
        hf = k.sb(es, "hf", [64, D], F32)
        hfb = Rot([k.sb(es, "hfb%d" % i, [64, D], BF16) for i in range(2)])
        sld = Rot([k.sb(es, "sld%d" % i, [64, D], F32) for i in range(3)])
        sld2 = Rot([k.sb(es, "sld2_%d" % i, [64, D], F32) for i in range(3)])
        xsb = Rot([k.sb(es, "xsB%d" % i, [128, D], BF16) for i in range(2)])
        zsb = Rot([k.sb(es, "zsB%d" % i, [128, D], BF16) for i in range(2)])
        gs = Rot([k.sb(es, "gs%d" % i, [128, 2, 128], F32) for i in range(2)])
        lp = Rot([k.sb(es, "lp%d" % i, [128, 2, 128], F32) for i in range(4)])
        ee = Rot([k.sb(es, "ee%d" % i, [64, 2, 128], F32) for i in range(4)])
        mt = Rot([k.sb(es, "mt%d" % i, [128, 2, 128], BF16) for i in range(4)])
        cp_ = Rot([k.sb(es, "cpp%d" % i, [64, 2, 128], BF16) for i in range(4)])
        y1j = Rot([k.sb(es, "y1j%d" % i, [128, 512], F32) for i in range(2)])
        y1 = Rot([k.sb(es, "y1_%d" % i, [128, D], F32) for i in range(2)])
        xsd = Rot([k.sb(es, "xsd%d" % i, [128, D], BF16) for i in range(2)])
        yn = Rot([k.sb(es, "yn%d" % i, [128, D], BF16) for i in range(2)])
        yT = Rot([k.sb(es, "yT%d" % i, [128, 8, 128], BF16) for i in range(2)])
        gm = k.sb(es, "gmask", [64, 2, 128], F32)
        gsm = k.sb(es, "gsm", [64, 128, 32], F32)
        gd = Rot([k.sb(es, "gd%d" % i, [64, 16], F32) for i in range(6)])
        vfl = k.sb(es, "vfl", [64, 2], F32)
        pY = Rot([k.ps(es, "pY%d" % i, [128, D], F32) for i in range(1)])
        pR = Rot([k.ps(es, "pR%d" % i, [128, 512], F32) for i in range(4)])
        pG = Rot([k.ps(es, "pG%d" % i, [128, 256], F32) for i in range(1)])

        def decay_mul(h_, dv):
            k.tt(V(h_, h_[:, :].rearrange("p (h q) -> p h q", q=64)), V(h_, h_[:, :].rearrange("p (h q) -> p h q", q=64)),
                 (dv[0], dv[1].unsqueeze(2).broadcast_to([64, H, 64])), ALU.mult)

        for sg in segs:
            n = sg["n"]
            nch = n // 128
            if sg["init"]:
                NG = glob["NG"]
                k.dma(V(gm, gm[:, :, 0:NG]), D_(glob["mask"]))
                k.dma(V(gsm, gsm[:, 0:NG, :]), D_(glob["sm"][:, 0:64, 64:96].rearrange("c p f -> p c f")), slow=True)
                k.memset(V(hf), 0.0)
                k.memset(V(hb), 0.0, eng="pool")
                skip = glob.get("skip", 0)
                for step in range(NG - skip):
                    for d, h_, eng_ in ((0, hf, "dve"), (1, hb, "dve")):
                        kk = step if d == 0 else NG - 1 - step
                        g_ = gd.nxt()
                        k.act(V(g_), V(gsm, gsm[:, kk, 16 * d:16 * d + 16]), AF.Exp, scale=V(gm, gm[:, d, kk:kk + 1]))
                        hv = h_[:, :].rearrange("p (h q) -> p h q", q=64)
                        k.tt(V(h_, hv), V(h_, hv), (g_, g_[:, :].unsqueeze(2).broadcast_to([64, H, 64])), ALU.mult, eng=eng_)
                        s_ = (sld if d == 0 else sld2).nxt()
                        k.dma(V(s_), D_(glob["sst"][kk, d]))
                        if eng_ == "dve":
                            k.op(eng_, lambda hh, s_=s_, h_=h_, d=d, kk=kk: hh.scalar_tensor_tensor(
                                out=h_[:, :], in0=s_[:, :], scalar=gm[:, d, kk:kk + 1], in1=h_[:, :], op0=ALU.mult, op1=ALU.add),
                                reads=[s_, gm, h_], writes=[h_])
                        else:
                            k.ts(V(s_), V(s_), V(gm, gm[:, d, kk:kk + 1]), ALU.mult, eng=eng_)
                            k.tt(V(h_), V(h_), V(s_), ALU.add, eng=eng_)
            else:
                k.memset(V(hf), 0.0)
                k.memset(V(hb), 0.0)
            if sg["vflag"] is not None:
                k.dma(V(vfl), D_(sg["vflag"]))
            k.dma(V(smb, smb[:, 0:nch, :]), D_(sg["sm"].rearrange("c p f -> p c f")))
            k.dma(V(bct, bct[:, :, 0:n]), D_(sg["bct"].rearrange("i p t -> p i t")))
            k.act(V(dall, dall[:, 0:nch, :]), V(smb, smb[0:64, 0:nch, 64:96]), AF.Exp)
            k.cp(V(ahi, ahi[:, 0:nch, :]), V(smb, smb[:, 0:nch, 0:32]))
            k.tt(V(atmp, atmp[:, 0:nch, :]), V(smb, smb[:, 0:nch, 0:32]), V(ahi, ahi[:, 0:nch, :]), ALU.subtract)
            k.cp(V(alo, alo[:, 0:nch, :]), V(atmp, atmp[:, 0:nch, :]))

            def load_state(kk, d):
                s_ = sld.nxt()
                k.dma(V(s_), D_(sg["sst"][kk, d]))
                ne = sg.get("nedge", 1)
                if sg["vflag"] is not None and (kk < ne or kk >= nch - ne):
                    col = 0 if kk < ne else 1
                    k.ts(V(s_), V(s_), V(vfl, vfl[:, col:col + 1]), ALU.mult)
                return s_

            for kk in range(nch - 1, -1, -1):
                k.act(V(hbin, hbin[:, kk, :]), V(hb), AF.Copy)
                if kk > 0:
                    s_ = load_state(kk, 1)
                    decay_mul(hb, V(dall, dall[:, kk, 16:32]))
                    k.tt(V(hb), V(hb), V(s_), ALU.add)
            its = [(kk, d, pr) for kk in range(nch) for d in range(2) for pr in range(8)]
            ctx = {}
            rbuf = {}

            def emit_R(i):
                kk, d, pr = its[i]
                t0 = kk * 128
                if d == 0 and pr == 0:
                    xs_, z_ = xsb.nxt(), zsb.nxt()
                    k.dma(V(xs_), D_(sg["xs"][t0:t0 + 128, :]))
                    k.dma(V(z_), D_(sg["zs"][t0:t0 + 128, :]))
                    g_ = pG.nxt()
                    for g in range(2):
                        k.mm(V(g_, g_[:, g * 128:(g + 1) * 128]), V(bct, bct[:, g, t0:t0 + 128]), V(bct, bct[:, 2 + g, t0:t0 + 128]))
                    gs_ = gs.nxt()
                    k.cp(V(gs_), V(g_, g_[:, :].rearrange("p (g t) -> p g t", g=2)))
                    xd_ = xsd.nxt()
                    k.tt(V(xd_, xd_[:, :].rearrange("p (h q) -> p h q", q=64)), V(xs_, xs_[:, :].rearrange("p (h q) -> p h q", q=64)),
                         V(dsk, dsk[:, :].unsqueeze(2).broadcast_to([128, H, 64])), ALU.mult, eng="pool")
                    ctx[kk] = dict(xs=xs_, z=z_, gs=gs_, xd=xd_)
                r_ = pR.nxt()
                for j in range(2):
                    hh = 2 * pr + j
                    hcol = ahi[:, kk, 16 * d + hh:16 * d + hh + 1].broadcast_to([128, 128])
                    lcol = alo[:, kk, 16 * d + hh:16 * d + hh + 1].broadcast_to([128, 128])
                    rm = r_[:, j * 128:(j + 1) * 128]
                    ru = r_[:, 256 + j * 128:256 + (j + 1) * 128]
                    k.mm(V(r_, rm), V(ident), V(mskb, mskb[:, d, :]), start=True, stop=False)
                    k.mm(V(r_, rm), V(ahi, hcol), V(Tb, Tb[:, d, :]), start=False, stop=False)
                    k.mm(V(r_, rm), V(alo, lcol), V(Tb, Tb[:, d, :]), start=False, stop=True)
                    k.mm(V(r_, ru), V(ahi, hcol), V(Tb, Tb[:, d, :]), start=True, stop=False)
                    k.mm(V(r_, ru), V(alo, lcol), V(Tb, Tb[:, d, :]), start=False, stop=True)
                rbuf[i] = r_

            emit_R(0)
            if len(its) > 1:
                emit_R(1)
            for i, (kk, d, pr) in enumerate(its):
                t0 = kk * 128
                if i + 2 < len(its):
                    emit_R(i + 2)
                cx = ctx[kk]
                xs_, z_, gs_ = cx["xs"], cx["z"], cx["gs"]
                if d == 0 and pr == 0:
                    hfb_ = hfb.nxt()
                    k.cp(V(hfb_), V(hf))
                    y_ = pY.nxt()
                    for nb in range(2):
                        k.mm(V(y_, y_[:, nb * 512:(nb + 1) * 512]), V(ident), V(cx["xd"], cx["xd"][:, nb * 512:(nb + 1) * 512]), start=True, stop=False)
                    cx["hfb"], cx["y"] = hfb_, y_
                hfb_, y_ = cx["hfb"], cx["y"]
                r_ = rbuf.pop(i)
                lp_, ee_ = lp.nxt(), ee.nxt()
                for j in range(2):
                    hh = 2 * pr + j
                    k.act(V(lp_, lp_[:, j, :]), V(r_, r_[:, j * 128:(j + 1) * 128]), AF.Exp,
                          bias=V(smb, smb[:, kk, 32 + 16 * d + hh:32 + 16 * d + hh + 1]))
                    eb = V(zcol, zcol[0:64, 0:1]) if d == 0 else V(smb, smb[0:64, kk, 80 + hh:80 + hh + 1])
                    k.act(V(ee_, ee_[:, j, :]), V(r_, r_[0:64, 256 + j * 128:256 + (j + 1) * 128]), AF.Exp, bias=eb)
                g = pr // 4
                mt_, c_ = mt.nxt(), cp_.nxt()
                k.tt(V(mt_), V(lp_), V(gs_, gs_[:, g:g + 1, :].broadcast_to([128, 2, 128])), ALU.mult)
                k.tt(V(c_), V(ee_), V(bct, bct[:, 2 + g:3 + g, t0:t0 + 128].broadcast_to([64, 2, 128])), ALU.mult)
                hst = hfb_ if d == 0 else hbin
                for j in range(2):
                    hh = 2 * pr + j
                    ysl = y_[:, hh * 64:(hh + 1) * 64]
                    k.mm(V(y_, ysl), V(mt_, mt_[:, j, :]), V(xs_, xs_[:, hh * 64:(hh + 1) * 64]), start=False, stop=False)
                    hs_ap = hfb_[:, hh * 64:(hh + 1) * 64] if d == 0 else hbin[:, kk, hh * 64:(hh + 1) * 64]
                    k.mm(V(y_, ysl), V(c_, c_[:, j, :]), V(hst, hs_ap), start=False, stop=(d == 1 and hh in (7, 15)))
                if not (d == 1 and pr == 7):
                    continue
                s_ = load_state(kk, 0)
                decay_mul(hf, V(dall, dall[:, kk, 0:16]))
                k.tt(V(hf), V(hf), V(s_), ALU.add)
                a_ = y1.nxt()
                k.tt(V(a_), V(y_), V(z_), ALU.mult)
                s2, n_ = P["ss"].nxt(), yn.nxt()
                s3 = P["ss"].nxt()
                for g in range(2):
                    jk = y1j.nxt()
                    k.op("dve", lambda hh_, a_=a_, jk=jk, s2=s2, g=g: hh_.scalar_tensor_tensor(
                        out=jk[:, 0:512], in0=a_[:, g * 512:(g + 1) * 512], scalar=1.0 / 512.0, in1=a_[:, g * 512:(g + 1) * 512],
                        op0=ALU.mult, op1=ALU.mult, accum_out=s2[:, g:g + 1]), reads=[a_], writes=[jk, s2])
                k.act(V(s3), V(s2), AF.Ln, bias=V(eps))
                k.act(V(s3), V(s3), AF.Exp, scale=-0.5)
                for g in range(2):
                    k.ts(V(n_, n_[:, g * 512:(g + 1) * 512]), V(a_, a_[:, g * 512:(g + 1) * 512]), V(s3, s3[:, g:g + 1]), ALU.mult)
                t_ = yT.nxt()
                transpose_tile(k, P, n_, 128, t_, t_[:, :, :])
                k.dma(D_(sg["yt"].rearrange("(c p) t -> p c t", p=128)[:, :, t0:t0 + 128]), V(t_), eng="pool")
                del ctx[kk]
        k.barrier()


EXT = 128


def build_program(cfg, shapes, cst_arrays):
    nc = bass.Bass("TRN2", target_bir_lowering=False)
    T, NSEG, SP, LS, NCs = cfg.T, cfg.NSEG, cfg.SP, cfg.LS, cfg.NC
    NS = T + 2 * EXT
    NSP = ((NS + 511) // 512) * 512
    NG = LS // 128
    W = wviews(declare_weights(nc, shapes))
    C = {n: nc.dram_tensor(n, list(a.shape), F32 if a.dtype == np.float32 else BF16, kind="ExternalInput").ap()
         for n, a in cst_arrays.items()}
    x_own = nc.dram_tensor("x_own", [SP * T + NSP, D], F32, kind="ExternalInput").ap()
    xh_own = nc.dram_tensor("xh_own", [NSEG, 2, D], F32, kind="ExternalInput").ap()
    x_sg = nc.dram_tensor("x_sg", [LS, D], F32, kind="ExternalInput").ap()
    xh_sg = nc.dram_tensor("xh_sg", [NCs, 2, D], F32, kind="ExternalInput").ap()
    gmask = nc.dram_tensor("gmask", [64, 2, NG], F32, kind="ExternalInput").ap()
    vflag = nc.dram_tensor("vflag", [128, 2], F32, kind="ExternalInput").ap()
    y_own = nc.dram_tensor("y_own", [NSEG * T, D], F32, kind="ExternalOutput").ap()

    def scr(name, shape, dt):
        return nc.dram_tensor(name, list(shape), dt, kind="Internal").ap()
    seg_n = [T] * SP + [NS]
    seg_off = [s_ * T for s_ in range(SP)] + [SP * T]
    QT = [scr("QT%d" % s_, [H, 96, (NSP if s_ == SP else seg_n[s_])], BF16) for s_ in range(NSEG)]
    KT = scr("KT", [max(SP, 1), H, 96, T], BF16)
    VA = scr("VA", [max(SP, 1), T, H, 128], BF16)
    KTS = scr("KTS", [H, 96, LS], BF16)
    VAS = scr("VAS", [LS, H, 128], BF16)
    AT = [scr("AT%d" % s_, [D, seg_n[s_]], BF16) for s_ in range(NSEG)]
    YT = [scr("YT%d" % s_, [D, seg_n[s_]], BF16) for s_ in range(NSEG)]
    XS = [scr("XS%d" % s_, [seg_n[s_], D], BF16) for s_ in range(NSEG)]
    ZS = [scr("ZS%d" % s_, [seg_n[s_], D], BF16) for s_ in range(NSEG)]
    SST = [scr("SST%d" % s_, [seg_n[s_] // 128, 2, 64, D], F32) for s_ in range(NSEG)]
    SM = [scr("SM%d" % s_, [seg_n[s_] // 128, 128, 96], F32) for s_ in range(NSEG)]
    BCT = [scr("BCT%d" % s_, [4, 64, seg_n[s_]], BF16) for s_ in range(NSEG)]
    SSTG = scr("SSTG", [NG, 2, 64, D], F32)
    SMG = scr("SMG", [NG, 128, 96], F32)
    X1 = [scr("X1_%d" % s_, [seg_n[s_], D], F32) for s_ in range(NSEG)]
    H2T = [scr("H2T%d" % s_, [128, 8, seg_n[s_] + 2], BF16) for s_ in range(NSEG)]
    WG = scr("WG", [FC, 128, 8, 128], BF16)
    with ExitStack() as es:
        k = K(nc, es)
        xo = [x_own[seg_off[s_]:seg_off[s_] + seg_n[s_], :] for s_ in range(NSEG)]
        segs = []
        for s_ in range(SP):
            segs.append(dict(x=xo[s_], n=T, cos=C["cosp"], sin=C["sinp"], qt=QT[s_], kt=KT[s_], va=VA[s_]))
        segs.append(dict(x=x_own[SP * T:SP * T + NSP, :], n=NSP, cos=C["coso"], sin=C["sino"], qt=QT[SP], do_kv=False))
        for c in range(NCs):
            segs.append(dict(x=x_sg[c * T:(c + 1) * T, :], n=T, cos=C["cosg"][:, c * T:(c + 1) * T], sin=C["sing"][:, c * T:(c + 1) * T],
                             do_q=False, kt=KTS[:, :, c * T:(c + 1) * T], va=VAS[c * T:(c + 1) * T, :, :]))
        phase_p1a(k, cfg, W, C, segs)
        segc = [dict(x=xo[s_], xh=xh_own[s_], n=seg_n[s_], lite=False, xs=XS[s_], zs=ZS[s_], sst=SST[s_], sm=SM[s_], bct=BCT[s_])
                for s_ in range(NSEG)]
        for c in range(NCs):
            segc.append(dict(x=x_sg[c * T:(c + 1) * T, :], xh=xh_sg[c], n=T, lite=True,
                             sst=SSTG[c * (T // 128):(c + 1) * (T // 128)], sm=SMG[c * (T // 128):(c + 1) * (T // 128)]))
        phase_p1c(k, cfg, W, C, segc)
        segb = []
        for s_ in range(NSEG):
            segb.append(dict(n=seg_n[s_], xs=XS[s_], zs=ZS[s_], sst=SST[s_], sm=SM[s_], bct=BCT[s_], yt=YT[s_],
                             init=(s_ == SP), vflag=(vflag[0:64, :] if s_ == SP else None), nedge=EXT // 128))
        phase_p1b(k, cfg, W, C, segb, glob=dict(sst=SSTG, sm=SMG, mask=gmask, NG=NG, skip=(T + EXT) // 128))
        jobs = []
        for s_ in range(NSEG):
            for hh in range(H):
                if s_ < SP:
                    jobs.append(dict(qt=QT[s_][hh], kt=KT[s_, hh], kr=KT[s_, 0, 64:96, :], va=VA[s_, :, hh, :], at=AT[s_][hh * 64:(hh + 1) * 64, :], Tq=T, Tk=T))
                else:
                    jobs.append(dict(qt=QT[s_][hh][:, 0:NS], kt=KTS[hh], kr=KTS[0, 64:96, :], va=VAS[:, hh, :], at=AT[s_][hh * 64:(hh + 1) * 64, :], Tq=NS, Tk=LS))
        phase_p2(k, cfg, jobs)
        segs3 = [dict(x=xo[s_], n=seg_n[s_], mix=[AT[s_], YT[s_]], x1=X1[s_], h2t=H2T[s_]) for s_ in range(NSEG)]
        phase_p3(k, cfg, W, C, segs3, nkc=16)
        segs4 = []
        for s_ in range(NSEG):
            if s_ < SP:
                segs4.append(dict(h2t=H2T[s_], x1=X1[s_], out=y_own[s_ * T:(s_ + 1) * T, :]))
            else:
                segs4.append(dict(h2t=H2T[s_][:, :, EXT:EXT + T + 2], x1=X1[s_][EXT:EXT + T, :], out=y_own[s_ * T:(s_ + 1) * T, :],
                                  vflag=vflag))
        phase_p4(k, cfg, W, C, WG, segs4)
        n_ops = k.n_ops
        k.emit()
    return nc, n_ops


def run_cfg(cfg, inputs, x_prompt, x_sample):
    T, SP, NCs, LS = cfg.T, cfg.SP, cfg.NC, cfg.LS
    NS = T + 2 * EXT
    NSP = ((NS + 511) // 512) * 512
    NG = LS // 128
    cst = host_consts()
    cst["cosp"], cst["sinp"] = rope_tables(np.arange(T))
    cst["cosg"], cst["sing"] = rope_tables(np.arange(LS))
    cst["coso"], cst["sino"] = rope_tables(np.arange(NSP))
    shapes = {n: inputs[n].shape for n in WNAMES}
    nc, n_ops = build_program(cfg, shapes, cst)
    xs32 = np.ascontiguousarray(x_sample, dtype=np.float32)
    xpad = np.zeros((LS + 2 * EXT + 2, D), np.float32)
    xpad[EXT + 1:EXT + 1 + LS] = xs32
    zero_row = np.zeros((D,), np.float32)
    xh_sg = np.stack([np.stack([xs32[c * T - 1] if c > 0 else zero_row, xs32[(c + 1) * T] if c < NCs - 1 else zero_row])
                      for c in range(NCs)])
    in_maps = []
    for c in range(NCs):
        m = {n: np.ascontiguousarray(inputs[n], dtype=np.float32) for n in WNAMES}
        m.update(cst)
        lo = c * T - EXT
        co, so = rope_tables(np.arange(lo, lo + NSP))
        m["coso"], m["sino"] = co, so
        parts = [x_prompt[c * SP + s_] for s_ in range(SP)] + [xpad[lo + EXT + 1:lo + EXT + 1 + NS], np.zeros((NSP - NS, D), np.float32)]
        m["x_own"] = np.ascontiguousarray(np.concatenate(parts, 0), dtype=np.float32)
        xh = np.zeros((SP + 1, 2, D), np.float32)
        xh[SP, 0] = xpad[lo + EXT]
        xh[SP, 1] = xpad[lo + EXT + 1 + NS]
        m["xh_own"] = xh
        m["x_sg"] = xs32
        m["xh_sg"] = xh_sg
        kk = np.arange(NG)
        gm = np.zeros((64, 2, NG), np.float32)
        gm[:, 0, :] = (kk < (lo // 128 if lo >= 0 else -((-lo) // 128)))[None, :]
        gm[:, 1, :] = (kk >= (lo + NS) // 128)[None, :]
        m["gmask"] = gm
        vf = np.ones((128, 2), np.float32)
        if c == 0:
            vf[:, 0] = 0.0
        if c == NCs - 1:
            vf[:, 1] = 0.0
        m["vflag"] = vf
        in_maps.append(m)
    res = run_bass_kernel_spmd(nc, in_maps, core_ids=list(range(NCs)))
    yp = np.zeros((NCs * SP, T, D), np.float32)
    ys = np.zeros((LS, D), np.float32)
    for c in range(NCs):
        y = res.results[c]["y_own"]
        for s_ in range(SP):
            yp[c * SP + s_] = y[s_ * T:(s_ + 1) * T]
        ys[c * T:(c + 1) * T] = y[SP * T:(SP + 1) * T]
    return yp, ys


def kernel(**inputs):
    inputs = {n: np.asarray(v) for n, v in inputs.items()}
    cfg = Cfg(8, 4, 2048)
    yp, ys = run_cfg(cfg, inputs, inputs["x_prompt"], inputs["x_sample"][0])
    return yp, ys[None]
```
